# Optimizing a Trainium2 kernel written in Bass

```python
import math
import jax, jax.numpy as jnp
from jax import lax
import numpy as np

D_MODEL = 1024
BATCH = 2
SEQ = 8192
DEPTH = 1
DEC_BATCH = 128
DEC_SEQ = 4
PAST_LEN = 2048
PAGE_SIZE = 128

R_HEADS = 8
R_HD = 64
R_WIDTH = R_HEADS * R_HD
LORA_W = 64
LORA_A = 64
SHIFT_COLS = 3 * R_WIDTH + LORA_W + LORA_A
GN_EPS = 64e-5
A_GROUPS = ((128, 1), (512, 4), (2048, 16))
N_GROUPS = 3
A_HEADS = 4
A_HD = 128
A_QKV_W = N_GROUPS * A_HEADS * A_HD
A_WIDTH = A_HEADS * A_HD
N_IN = SHIFT_COLS + R_WIDTH + 3 * A_QKV_W + A_WIDTH + 2 * D_MODEL
ALPHA = (2 * DEPTH) ** 0.25
BETA = (8 * DEPTH) ** -0.25
LN_EPS = 1e-5
NEG = -1e30

kernel_name = "rwkv7_dilated_attn_hybrid_step"

F32 = jnp.float32


def _split(t, sizes):
    return jnp.split(t, [int(s) for s in np.cumsum(sizes)[:-1]], axis=-1)


def layer_norm(x, g, b):
    xf = x.astype(F32)
    mu = jnp.mean(xf, -1, keepdims=True)
    var = jnp.mean(jnp.square(xf - mu), -1, keepdims=True)
    return ((xf - mu) * lax.rsqrt(var + LN_EPS) * g.astype(F32) + b.astype(F32)).astype(x.dtype)


def rwkv_mix(zs, z_gate, wkv0, w0, w_w2, a0, w_a2, k_k, k_a, r_k, lnx_g, lnx_b):
    B, T, _ = zs.shape
    r, k, v, wd, ad = _split(zs.astype(F32), [R_WIDTH, R_WIDTH, R_WIDTH, LORA_W, LORA_A])
    w = -jax.nn.softplus(-(w0.astype(F32) + jnp.tanh(wd) @ w_w2.astype(F32))) - 0.5
    decay = jnp.exp(-jnp.exp(w))
    a = jax.nn.sigmoid(a0.astype(F32) + ad @ w_a2.astype(F32))
    kk = (k * k_k.astype(F32)).reshape(B, T, R_HEADS, R_HD)
    kk = kk / jnp.maximum(jnp.linalg.norm(kk, axis=-1, keepdims=True), 1e-12)
    k = k * (1.0 + (a - 1.0) * k_a.astype(F32))
    hs = lambda t: t.reshape(B, T, R_HEADS, R_HD)
    r, decay, k, v, a = hs(r), hs(decay), hs(k), hs(v), hs(a)
    xs = tuple(jnp.moveaxis(t, 1, 0) for t in (r, decay, k, v, kk, a))

    def step(S, inp):
        r_t, w_t, k_t, v_t, kk_t, a_t = inp
        sa = jnp.einsum('bhvk,bhk->bhv', S, -kk_t)
        S = (S * w_t[:, :, None, :] + sa[..., None] * (kk_t * a_t)[:, :, None, :]
             + v_t[..., None] * k_t[:, :, None, :])
        return S, jnp.einsum('bhvk,bhk->bhv', S, r_t)

    S_fin, o = lax.scan(step, wkv0.astype(F32), xs)
    o = jnp.moveaxis(o, 0, 1)
    mu = jnp.mean(o, -1, keepdims=True)
    var = jnp.mean(jnp.square(o - mu), -1, keepdims=True)
    o = ((o - mu) * lax.rsqrt(var + GN_EPS)).reshape(B, T, R_WIDTH) * lnx_g.astype(F32) + lnx_b.astype(F32)
    bonus = (jnp.sum(r * k * r_k.astype(F32), -1, keepdims=True) * v).reshape(B, T, R_WIDTH)
    y = (o + bonus) * jax.nn.silu(z_gate.astype(F32))
    return y.astype(zs.dtype), S_fin.astype(wkv0.dtype)


def dilated_attn_prompt(q, k, v, d, span):
    B, S, H, E = q.shape
    L = -(-S // d)
    nb = -(-L // span)
    Lp = nb * span
    pad = Lp * d - S

    def to_blocks(t):
        t = jnp.pad(t, ((0, 0), (0, pad), (0, 0), (0, 0)))
        t = jnp.swapaxes(t.reshape(B, Lp, d, H, E), 1, 2)
        return t.reshape(B, d, nb, span, H, E)

    def with_prev(t):
        prev = jnp.pad(t[:, :, :-1], ((0, 0), (0, 0), (1, 0), (0, 0), (0, 0), (0, 0)))
        return jnp.concatenate([prev, t], axis=3)

    qb = to_blocks(q)
    kc, vc = with_prev(to_blocks(k)), with_prev(to_blocks(v))
    s = jnp.einsum('bdnqhe,bdnkhe->bdnhqk', qb, kc).astype(F32) * (1.0 / math.sqrt(E))
    i = jnp.arange(span)[:, None]
    j = jnp.arange(2 * span)[None, :]
    dist = span + i - j
    valid = (dist >= 0) & (dist <= span)
    first = (jnp.arange(nb)[:, None, None] == 0) & (j[None] < span)
    valid = valid[None] & ~first
    s = jnp.where(valid[None, None, :, None], s, NEG)
    m = jnp.max(s, -1, keepdims=True)
    p = jnp.exp(s - m)
    l = jnp.sum(p, -1)
    o = jnp.einsum('bdnhqk,bdnkhe->bdnqhe', p, vc.astype(F32))
    o = o / jnp.swapaxes(l, 3, 4)[..., None]
    lse = jnp.swapaxes(m[..., 0] + jnp.log(l), 3, 4)

    def from_blocks(t):
        rest = t.shape[4:]
        t = jnp.swapaxes(t.reshape((B, d, Lp) + rest), 1, 2)
        return t.reshape((B, Lp * d) + rest)[:, :S]

    return from_blocks(o), from_blocks(lse)


def dilated_attn_cached(q, k, v, buf, d, span):
    B, T, H, E = q.shape
    Wb = buf.shape[1]
    kall = jnp.concatenate([buf[:, :, 0].astype(k.dtype), k], axis=1)
    vall = jnp.concatenate([buf[:, :, 1].astype(v.dtype), v], axis=1)
    idx = Wb + jnp.arange(T)[:, None] - jnp.arange(span + 1)[None, :] * d
    valid = idx >= 0
    idxc = jnp.maximum(idx, 0)
    kg = jnp.take(kall, idxc, axis=1)
    vg = jnp.take(vall, idxc, axis=1)
    s = jnp.einsum('bthe,btjhe->bthj', q, kg).astype(F32) * (1.0 / math.sqrt(E))
    s = jnp.where(valid[None, :, None, :], s, NEG)
    m = jnp.max(s, -1, keepdims=True)
    p = jnp.exp(s - m)
    l = jnp.sum(p, -1)
    o = jnp.einsum('bthj,btjhe->bthe', p, vg.astype(F32)) / l[..., None]
    return o, m[..., 0] + jnp.log(l)


def layer(x, shift_prev, wkv0, kv_bufs, w_in, b_gate, mu_shift, w0, w_w2, a0, w_a2, k_k, k_a, r_k,
          lnx_g, lnx_b, w_oa, w_ob, w_out, ln_g, ln_b):
    B, T, _ = x.shape
    h = x @ w_in
    zs, z_r, q, k, v, z_a, g_r, g_a = _split(
        h, [SHIFT_COLS, R_WIDTH, A_QKV_W, A_QKV_W, A_QKV_W, A_WIDTH, D_MODEL, D_MODEL])
    prev = jnp.concatenate([shift_prev[:, None].astype(zs.dtype), zs[:, :-1]], axis=1)
    zs_mixed = zs + mu_shift * (prev - zs)
    y_r, wkv_new = rwkv_mix(zs_mixed, z_r, wkv0, w0, w_w2, a0, w_a2, k_k, k_a, r_k, lnx_g, lnx_b)
    q = q.reshape(B, T, N_GROUPS, A_HEADS, A_HD)
    k = k.reshape(B, T, N_GROUPS, A_HEADS, A_HD)
    v = v.reshape(B, T, N_GROUPS, A_HEADS, A_HD)
    outs, lses, new_kv = [], [], []
    for g, (win, dil) in enumerate(A_GROUPS):
        span = win // dil
        qg, kg, vg = q[:, :, g], k[:, :, g], v[:, :, g]
        if kv_bufs is None:
            o, lse = dilated_attn_prompt(qg, kg, vg, dil, span)
            keep = min(win, T)
            new_kv.append(jnp.stack([kg[:, T - keep:], vg[:, T - keep:]], axis=2))
        else:
            o, lse = dilated_attn_cached(qg, kg, vg, kv_bufs[g], dil, span)
            new_kv.append(jnp.stack([kg, vg], axis=2))
        outs.append(o)
        lses.append(lse)
    wts = jax.nn.softmax(jnp.stack(lses, 0), axis=0)
    o = jnp.einsum('gbth,gbthe->bthe', wts, jnp.stack(outs, 0))
    y_a = (o.reshape(B, T, A_WIDTH) * jax.nn.silu(z_a.astype(F32))).astype(x.dtype)
    gate_r = jax.nn.sigmoid(g_r + b_gate[:D_MODEL])
    gate_a = jax.nn.sigmoid(g_a + b_gate[D_MODEL:])
    mix = gate_r * (y_r @ w_oa) + gate_a * (y_a @ w_ob)
    y = layer_norm(ALPHA * x + mix @ w_out, ln_g, ln_b)
    return y, new_kv, wkv_new, zs[:, -1]


def setup_inputs(seed: int = 0) -> dict:
    key = jax.random.key(seed)
    ks = jax.random.split(key, 32)
    n = lambda i, shape: jax.random.normal(ks[i], shape, F32)
    L = DEPTH
    col_scale = np.ones((N_IN,), np.float32)
    col_scale[2 * R_WIDTH:3 * R_WIDTH] = BETA
    v0 = SHIFT_COLS + R_WIDTH + 2 * A_QKV_W
    col_scale[v0:v0 + A_QKV_W] = BETA
    wb = [min(w, PAST_LEN) for (w, _) in A_GROUPS]
    return {
        "x_prompt": n(0, (BATCH, SEQ, D_MODEL)),
        "x_sample": n(1, (DEC_BATCH, DEC_SEQ, D_MODEL)),
        "cache_kv_g1": n(2, (L, DEC_BATCH, wb[0], 2, A_HEADS, A_HD)),
        "cache_kv_g2": n(3, (L, DEC_BATCH, wb[1], 2, A_HEADS, A_HD)),
        "cache_kv_g3": n(4, (L, DEC_BATCH, wb[2], 2, A_HEADS, A_HD)),
        "state_rwkv_wkv": 0.5 * n(5, (L, DEC_BATCH, R_HEADS, R_HD, R_HD)),
        "state_rwkv_shift": n(6, (L, DEC_BATCH, SHIFT_COLS)),
        "w_in": n(7, (L, D_MODEL, N_IN)) * (D_MODEL ** -0.5) * jnp.asarray(col_scale),
        "b_gate": 0.1 * n(8, (L, 2 * D_MODEL)),
        "mu_shift": jax.random.uniform(ks[9], (L, SHIFT_COLS), F32),
        "w0": jax.random.uniform(ks[10], (L, R_WIDTH), F32, -3.0, 1.0),
        "w_w2": n(11, (L, LORA_W, R_WIDTH)) * (LORA_W ** -0.5),
        "a0": 0.5 * n(12, (L, R_WIDTH)),
        "w_a2": n(13, (L, LORA_A, R_WIDTH)) * (LORA_A ** -0.5),
        "k_k": 0.85 + 0.05 * n(14, (L, R_WIDTH)),
        "k_a": 1.0 + 0.05 * n(15, (L, R_WIDTH)),
        "r_k": 0.1 * n(16, (L, R_HEADS, R_HD)),
        "lnx_g": 1.0 + 0.05 * n(17, (L, R_WIDTH)),
        "lnx_b": 0.02 * n(18, (L, R_WIDTH)),
        "w_oa": n(19, (L, R_WIDTH, D_MODEL)) * (R_WIDTH ** -0.5),
        "w_ob": n(20, (L, A_WIDTH, D_MODEL)) * (A_WIDTH ** -0.5),
        "w_out": n(21, (L, D_MODEL, D_MODEL)) * (D_MODEL ** -0.5) * BETA,
        "ln_g": 1.0 + 0.05 * n(22, (L, D_MODEL)),
        "ln_b": 0.02 * n(23, (L, D_MODEL)),
    }


def reference(x_prompt, x_sample, cache_kv_g1, cache_kv_g2, cache_kv_g3, state_rwkv_wkv, state_rwkv_shift,
              w_in, b_gate, mu_shift, w0, w_w2, a0, w_a2, k_k, k_a, r_k, lnx_g, lnx_b,
              w_oa, w_ob, w_out, ln_g, ln_b):
    xp, xs = x_prompt, x_sample
    Bp = xp.shape[0]
    kv1p, kv1s, kv2p, kv2s, kv3p, kv3s = [], [], [], [], [], []
    wkvp, wkvs, shp, shs = [], [], [], []
    for l in range(DEPTH):
        wts = (w_in[l], b_gate[l], mu_shift[l], w0[l], w_w2[l], a0[l], w_a2[l], k_k[l], k_a[l], r_k[l],
               lnx_g[l], lnx_b[l], w_oa[l], w_ob[l], w_out[l], ln_g[l], ln_b[l])
        shift0 = jnp.zeros((Bp, SHIFT_COLS), xp.dtype)
        wkv0 = jnp.zeros((Bp, R_HEADS, R_HD, R_HD), state_rwkv_wkv.dtype)
        xp, kv_p, s_p, sh_p = layer(xp, shift0, wkv0, None, *wts)
        xs, kv_s, s_s, sh_s = layer(xs, state_rwkv_shift[l], state_rwkv_wkv[l],
                                    (cache_kv_g1[l], cache_kv_g2[l], cache_kv_g3[l]), *wts)
        kv1p.append(kv_p[0]); kv2p.append(kv_p[1]); kv3p.append(kv_p[2])
        kv1s.append(kv_s[0]); kv2s.append(kv_s[1]); kv3s.append(kv_s[2])
        wkvp.append(s_p); wkvs.append(s_s); shp.append(sh_p); shs.append(sh_s)
    return (xp, xs, jnp.stack(kv1p), jnp.stack(kv1s), jnp.stack(kv2p), jnp.stack(kv2s),
            jnp.stack(kv3p), jnp.stack(kv3s), jnp.stack(wkvp), jnp.stack(wkvs), jnp.stack(shp), jnp.stack(shs))
```

```python
import contextlib
import numpy as np
import concourse.bass as bass
import concourse.mybir as mybir
from concourse.bass_utils import run_bass_kernel_spmd

F32 = mybir.dt.float32
BF16 = mybir.dt.bfloat16
I32 = mybir.dt.int32
ALU = mybir.AluOpType
AF = mybir.ActivationFunctionType
AX = mybir.AxisListType

NCORES = 8
RING = 20


class Buf:
    __slots__ = ("name", "t", "writers", "readers", "prev_readers", "bank")

    def __init__(self, name, t, bank=None):
        self.name = name
        self.t = t
        self.bank = bank
        self.writers = {}
        self.readers = {}
        self.prev_readers = {}

    def __getitem__(self, idx):
        return self.t[idx]


def _merge(dst, src):
    for k, (s, v) in src.items():
        if k not in dst or dst[k][1] < v:
            dst[k] = (s, v)


class Prog:
    def __init__(self, nc, stack):
        self.nc = nc
        self.stack = stack
        self.eng = {"pe": nc.tensor, "act": nc.scalar, "dve": nc.vector, "pool": nc.gpsimd, "sp": nc.sync}
        self.esem = {}
        self.seq = {}
        self.known = {}
        self.sig = {}
        self.sig_idx = {}
        self.sigcount = {}
        self.last_inst = {}
        for e in self.eng:
            self.esem[e] = stack.enter_context(nc.semaphore("es_" + e))
            self.seq[e] = 0
            self.known[e] = {}
            self.sig[e] = []
            self.sig_idx[e] = []
            self.sigcount[e] = 0
            self.last_inst[e] = None
        self.ring = {}
        self.ring_val = {}
        self.dma_i = {}
        for q in ("sp", "pool", "act"):
            self.ring[q] = [stack.enter_context(nc.semaphore("dq_%s_%d" % (q, i))) for i in range(RING)]
            self.ring_val[q] = [0] * RING
            self.dma_i[q] = 0
        self.nbuf = 0
        self.bank_rd = {}

    def sbuf(self, name, shape, dtype):
        t = self.stack.enter_context(self.nc.sbuf_tensor("sb_" + name, list(shape), dtype))
        return Buf(name, t)

    def psum(self, name, shape, dtype):
        t = self.stack.enter_context(self.nc.psum_tensor(name, list(shape), dtype))
        return Buf(name, t)

    def dram(self, name, shape, dtype, kind="Internal"):
        t = self.nc.dram_tensor(name, list(shape), dtype, kind=kind)
        return Buf(name, t.ap())

    def _deps(self, reads, writes, partial, eng=None):
        deps = {}
        for b in list(reads) + list(writes):
            if b.bank is not None:
                for e2, (k2, ev2) in self.bank_rd.setdefault(b.bank, {}).items():
                    if e2 != eng:
                        _merge(deps, {k2: ev2})
        for b in reads:
            _merge(deps, b.writers)
        for b in writes:
            _merge(deps, b.prev_readers)
            _merge(deps, b.readers)
            if not partial:
                _merge(deps, b.writers)
        return deps

    def _resolve(self, e, idx):
        sig = self.sig[e]
        import bisect
        pos = bisect.bisect_left(self.sig_idx[e], idx)
        if pos < len(sig):
            return self.sig_idx[e][pos], sig[pos]
        last = self.seq[e]
        self.sigcount[e] += 1
        self.last_inst[e].then_inc(self.esem[e], 1)
        self.sig_idx[e].append(last)
        sig.append(self.sigcount[e])
        return last, self.sigcount[e]

    def _wait(self, eng, deps):
        E = self.eng[eng]
        kn = self.known[eng]
        for k, (s, v) in deps.items():
            if eng == "pe" and k == "e_pe":
                continue
            if kn.get(k, 0) >= v:
                continue
            if s is None:
                e = k[2:]
                idx2, cnt = self._resolve(e, v)
                E.wait_ge(self.esem[e], cnt)
                kn[k] = idx2
            else:
                E.wait_ge(s, v)
                kn[k] = v

    def _record(self, ev_key, ev, reads, writes, partial):
        for b in reads:
            _merge(b.readers, {ev_key: ev})
        for b in writes:
            if b.readers or not partial:
                b.prev_readers = b.readers
                b.readers = {}
                b.writers = {ev_key: ev}
            else:
                _merge(b.writers, {ev_key: ev})

    opbudget = None
    opcount = 0

    def op(self, eng, fn, reads=(), writes=(), partial=False):
        Prog.opcount += 1
        if Prog.opbudget is not None and Prog.opcount > Prog.opbudget:
            return None
        deps = self._deps(reads, writes, partial, eng)
        self._wait(eng, deps)
        inst = fn(self.eng[eng])
        self.seq[eng] += 1
        self.last_inst[eng] = inst
        self._record("e_" + eng, (None, self.seq[eng]), reads, writes, partial)
        for b in list(reads) + list(writes):
            if b.bank is not None:
                self.bank_rd.setdefault(b.bank, {})[eng] = ("e_" + eng, (None, self.seq[eng]))
        return inst

    def dma(self, q, out, in_, reads=(), writes=(), partial=True, **kw):
        Prog.opcount += 1
        if Prog.opbudget is not None and Prog.opcount > Prog.opbudget and not kw.pop("always", False):
            return None
        kw.pop("always", None)
        i = self.dma_i[q]
        self.dma_i[q] += 1
        slot = i % RING
        sem = self.ring[q][slot]
        prev = self.ring_val[q][slot]
        key = "d_%s_%d" % (q, slot)
        deps = self._deps(reads, writes, partial)
        if prev > 0:
            _merge(deps, {key: (sem, prev)})
        self._wait(q, deps)
        inst = self.eng[q].dma_start(out=out, in_=in_, **kw)
        inst.then_inc(sem, 16)
        self.ring_val[q][slot] = prev + 16
        self._record(key, (sem, prev + 16), reads, writes, partial)
        return inst

    def finish(self):
        deps = {}
        for q in self.ring:
            for slot in range(RING):
                if self.ring_val[q][slot] > 0:
                    deps["d_%s_%d" % (q, slot)] = (self.ring[q][slot], self.ring_val[q][slot])
        for e in self.eng:
            if self.seq[e] > 0:
                deps["e_" + e] = (None, self.seq[e])
        self._wait("sp", deps)

    def mm(self, out_b, out_ap, lhsT_b, lhsT_ap, rhs_b, rhs_ap, start=True, stop=True, **kw):
        rd = [b for b in (lhsT_b, rhs_b) if b is not None]
        return self.op("pe", lambda e: e.matmul(out_ap, lhsT_ap, rhs_ap, start=start, stop=stop, **kw),
                       reads=rd, writes=[out_b], partial=True)

    def tr(self, out_b, out_ap, in_b, in_ap, ident_b, ident_ap):
        return self.op("pe", lambda e: e.transpose(out_ap, in_ap, ident_ap),
                       reads=[in_b, ident_b], writes=[out_b], partial=True)


D = 1024
SEQ = 8192
NB = 2
R_HEADS = 8
SHIFT_COLS = 1664
RW_COLS = 2176
Q0, K0, V0, ZA0, GR0, GA0 = 2176, 3712, 5248, 6784, 7296, 8320
GROUPS = ((128, 1), (512, 4), (2048, 16))
ALPHA = 2.0 ** 0.25
LN_EPS = 1e-5
GN_EPS = 64e-5
C0 = float(np.exp(-0.5))
NEGM = -30000.0
OWN = 2048
EXT = 8192
SCALE = 1.0 / float(np.sqrt(128.0))

PV_MU = 0
PV_W0 = 13
PV_A0 = 17
PV_KK = 21
PV_KA = 25
PV_RK = 29
PV_LG = 33
PV_LB = 37
PV_BG = 41
PV_OMM = 57
NPV = 70


class Ctx:
    pass


def emit_consts(P, C, din):
    C.ident_f = P.sbuf("ident_f", [128, 128], F32)
    C.ident_b = P.sbuf("ident_b", [128, 128], BF16)
    C.cm = P.sbuf("cm", [128, din["cmask"].shape[1]], F32)
    C.pv = P.sbuf("pv", [128, NPV], F32)
    C.pbias = P.sbuf("pbias", [128, 1], F32)
    P.dma("sp", C.ident_f[:], din["ident"][:, :], writes=[C.ident_f])
    P.dma("pool", C.ident_b[:], din["ident"][:, :], writes=[C.ident_b])
    P.dma("sp", C.cm[:], din["cmask"][:, :], writes=[C.cm])
    P.dma("sp", C.pv[:, 0:PV_OMM], din["pvec"][:, :], writes=[C.pv])
    P.dma("sp", C.pbias[:], din["pbias"][:, :], writes=[C.pbias])
    C.selb = P.sbuf("selb", [128, 64], BF16)
    P.op("pool", lambda e: e.tensor_copy(C.selb[:], C.cm[:, CM_SEL:CM_SEL + 64]), reads=[C.cm], writes=[C.selb])
    C.bones_f = Buf("bones_f", C.cm.t[:, CM_BONES:CM_BONES + 128])
    P.op("dve", lambda e: e.tensor_scalar(C.pv[:, PV_OMM:PV_OMM + 13], C.pv[:, PV_MU:PV_MU + 13], -1.0, 1.0,
                                          ALU.mult, ALU.add), reads=[C.pv], writes=[C.pv])
    P.op("dve", lambda e: e.tensor_copy(C.cm[:, 256:512], C.cm[:, 0:256]), reads=[C.cm], writes=[C.cm])
    P.op("dve", lambda e: e.tensor_scalar(C.cm[:, 256:384], C.cm[:, 256:384], C.pbias[:, 0:1], None, ALU.add),
         reads=[C.cm, C.pbias], writes=[C.cm])


def emit_xT(P, C, x_rows_ap, ntiles, xT, col0, xld, ps_x, cnt):
    for t in range(ntiles):
        xb = xld[cnt[0] % len(xld)]
        px = ps_x[cnt[0] % len(ps_x)]
        P.dma("pool", xb[:], x_rows_ap[t * 128:(t + 1) * 128, :], writes=[xb])
        for k in range(8):
            P.tr(px, px[:, k * 128:(k + 1) * 128], xb, xb[:, k * 128:(k + 1) * 128], C.ident_b, C.ident_b[:])
        dst = xT[:, :, col0 + t * 128: col0 + (t + 1) * 128]
        src = px[:].rearrange("p (k t) -> p k t", k=8)
        if cnt[0] % 2 == 0:
            P.op("dve", lambda e: e.tensor_copy(dst, src), reads=[px], writes=[xT], partial=True)
        else:
            P.op("act", lambda e: e.activation(dst, src, AF.Copy), reads=[px], writes=[xT], partial=True)
        cnt[0] += 1


@contextlib.contextmanager
def scope(P):
    old = P.stack
    with contextlib.ExitStack() as st:
        P.stack = st
        try:
            yield
        finally:
            barrier(P)
            P.stack = old


def barrier(P):
    deps = {}
    for q in P.ring:
        for slot in range(RING):
            if P.ring_val[q][slot] > 0:
                deps["d_%s_%d" % (q, slot)] = (P.ring[q][slot], P.ring_val[q][slot])
    for e in P.eng:
        if P.seq[e] > 0:
            deps["e_" + e] = (None, P.seq[e])
    for e in P.eng:
        P._wait(e, dict(deps))


def carve(C, bank, c0, c1, name, dtype=F32):
    ap = C.bank[bank].t[:, c0:c1]
    if dtype != F32:
        ap = ap.bitcast(dtype)
    return Buf(name, ap, bank=bank)


def evac(P, i, dst_b, dst_ap, src_b, src_ap):
    if i % 2 == 0:
        P.op("dve", lambda e: e.tensor_copy(dst_ap, src_ap), reads=[src_b], writes=[dst_b], partial=True)
    else:
        P.op("act", lambda e: e.activation(dst_ap, src_ap, AF.Copy), reads=[src_b], writes=[dst_b], partial=True)


def emit_attn_prompt(P, C, din, xTA, yaT, dbg=None, dout=None):
    with scope(P):
        wh = P.sbuf("wh", [128, 8, 1280], BF16)
        qT = P.sbuf("qT", [128, 2048], BF16)
        kT = P.sbuf("kT", [128, 4096], BF16)
        vt = P.sbuf("vt", [128, 32, 128], BF16)
        oTg = [P.sbuf("oTg%d" % g, [128, 2048], BF16) for g in range(3)]
        lsB = [P.sbuf("lsB%d" % g, [128, 2048], F32) for g in range(3)]
        sz = P.sbuf("sz", [128, 2048], BF16)
        cw = [P.sbuf("cw%d" % i, [128, 512], F32) for i in range(5)]
        s_sb = [P.sbuf("s_sb%d" % i, [128, 256], F32) for i in range(2)]
        p_sb = [P.sbuf("p_sb%d" % i, [128, 256], BF16) for i in range(2)]
        pT = [P.sbuf("pT%d" % i, [128, 256], BF16) for i in range(2)]
        o_sb = [P.sbuf("o_sb%d" % i, [128, 128], BF16) for i in range(2)]
        st = [P.sbuf("st%d" % i, [128, 8], F32) for i in range(2)]
        kvts = [P.sbuf("kvt%d" % i, [128, 2, 384], F32) for i in range(2)]
        lcol = [P.sbuf("lcol%d" % i, [128, 128], F32) for i in range(2)]
        ps_p = [carve(C, i, 0, 512, "psA_p%d" % i) for i in range(2)]
        ps_s = [carve(C, 2 + i, 0, 256, "psA_s%d" % i) for i in range(2)]
        ps_t = [carve(C, 2 + i, 256, 384, "psA_t%d" % i, BF16) for i in range(2)]
        ps_o = [carve(C, 2 + i, 384, 512, "psA_o%d" % i) for i in range(2)]
        ps_oT = [carve(C, 4 + i, 0, 64, "psA_oT%d" % i, BF16) for i in range(2)]
        ps_l = [carve(C, 4 + i, 128, 256, "psA_l%d" % i) for i in range(2)]
        ec = [0]
        pc = [0]
        blk = [0]

        def proj_fm(col, tok0, ntok, dst_b, dst_ap, src_view=None):
            pp = ps_p[pc[0] % 2]
            pc[0] += 1
            for k in range(8):
                P.mm(pp, pp[:, 0:ntok], wh, wh[:, k, col * 128:(col + 1) * 128], xTA, xTA[:, k, tok0:tok0 + ntok],
                     start=(k == 0), stop=(k == 7))
            src = pp[:, 0:ntok] if src_view is None else src_view(pp)
            evac(P, ec[0], dst_b, dst_ap, pp, src)
            ec[0] += 1

        for h in range(4):
            P.dma("pool", wh[:], din["w_att"][h].rearrange("(k p) c -> p k c", p=128), writes=[wh], partial=False)
            for t in range(4):
                pp = ps_p[pc[0] % 2]
                pc[0] += 1
                for k in range(8):
                    P.mm(pp, pp[:], wh, wh[:, k, 9 * 128:10 * 128], xTA, xTA[:, k, 2048 + t * 512:2048 + (t + 1) * 512],
                         start=(k == 0), stop=(k == 7))
                P.op("act", lambda e: e.activation(sz[:, t * 512:(t + 1) * 512], pp[:], AF.Silu),
                     reads=[pp], writes=[sz], partial=True)
            for tt in range(16):
                kvt = kvts[tt % 2]
                for kv in range(2):
                    pp = ps_p[pc[0] % 2]
                    pc[0] += 1
                    for k in range(8):
                        P.mm(pp, pp[:, 0:384], xTA, xTA[:, k, 2048 + tt * 128:2048 + (tt + 1) * 128], wh,
                             wh[:, k, (3 + 3 * kv) * 128:(6 + 3 * kv) * 128], start=(k == 0), stop=(k == 7))
                    evac(P, ec[0], kvt, kvt[:, kv, :], pp, pp[:, 0:384])
                    ec[0] += 1
                P.dma("sp", dout["kvp3"][tt * 128:(tt + 1) * 128, :, h, :], kvt[:, :, 256:384], reads=[kvt])
                if tt >= 12:
                    P.dma("sp", dout["kvp2"][(tt - 12) * 128:(tt - 11) * 128, :, h, :], kvt[:, :, 128:256], reads=[kvt])
                if tt == 15:
                    P.dma("sp", dout["kvp1"][0:128, :, h, :], kvt[:, :, 0:128], reads=[kvt])
            for g, (win, d) in enumerate(GROUPS):
                L = 2048 // d
                nb = L // 128
                KW = 128 + L
                for t in range(4):
                    mt = 512 // d
                    dst = qT[:].rearrange("p (r m) -> p r m", r=d)[:, :, t * mt:(t + 1) * mt]
                    proj_fm(g, 2048 + t * 512, 512, qT, dst,
                            src_view=lambda pp: pp[:].rearrange("p (m r) -> p r m", r=d))
                kv = kT[:, 0:d * KW].rearrange("p (r m) -> p r m", r=d)
                npt = 128 * d
                for t0 in range(0, npt, 512):
                    n = min(512, npt - t0)
                    dst = kv[:, :, t0 // d:(t0 + n) // d]
                    proj_fm(3 + g, 2048 - npt + t0, n, kT, dst,
                            src_view=lambda pp: pp[:, 0:n].rearrange("p (m r) -> p r m", r=d))
                for t in range(4):
                    mt = 512 // d
                    dst = kv[:, :, 128 + t * mt:128 + (t + 1) * mt]
                    proj_fm(3 + g, 2048 + t * 512, 512, kT, dst,
                            src_view=lambda pp: pp[:].rearrange("p (m r) -> p r m", r=d))
                for r in range(d):
                    for j0 in range(0, 1 + nb, 4):
                        nj = min(4, 1 + nb - j0)
                        pp = ps_p[pc[0] % 2]
                        pc[0] += 1
                        for jj in range(nj):
                            j = j0 + jj
                            tokbase = 2048 - 128 * d + r + j * 128 * d
                            for k in range(8):
                                lhs = xTA[:, k, tokbase:tokbase + 127 * d + 1:d]
                                P.mm(pp, pp[:, jj * 128:(jj + 1) * 128], xTA, lhs, wh, wh[:, k, (6 + g) * 128:(7 + g) * 128],
                                     start=(k == 0), stop=(k == 7))
                        bi = r * (1 + nb) + j0
                        evac(P, ec[0], vt, vt[:, bi:bi + nj, :], pp, pp[:, 0:nj * 128].rearrange("p (j e) -> p j e", j=nj))
                        ec[0] += 1
                def gen_block(r, n, b):
                    S, T_, O_, OT, LB = ps_s[b % 2], ps_t[b % 2], ps_o[b % 2], ps_oT[b % 2], ps_l[b % 2]
                    ss, pb, ptb, ob, stt, lc = s_sb[b % 2], p_sb[b % 2], pT[b % 2], o_sb[b % 2], st[b % 2], lcol[b % 2]
                    qblk = qT[:, r * L + n * 128: r * L + (n + 1) * 128]
                    kblk = kT[:, r * KW + n * 128: r * KW + n * 128 + 256]
                    P.mm(S, S[:], qT, qblk, kT, kblk)
                    mk = C.cm[:, 256:512] if n == 0 else C.cm[:, 0:256]
                    P.op("dve", lambda e: e.scalar_tensor_tensor(ss[:], S[:], SCALE, mk, ALU.mult, ALU.add),
                         reads=[S, C.cm], writes=[ss])
                    yield
                    P.op("dve", lambda e: e.tensor_reduce(stt[:, 0:1], ss[:], AX.X, ALU.max, negate=True),
                         reads=[ss], writes=[stt], partial=True)
                    P.op("act", lambda e: e.activation(pb[:], ss[:], AF.Exp, bias=stt[:, 0:1], scale=1.0,
                                                       accum_out=stt[:, 1:2]),
                         reads=[ss, stt], writes=[pb, stt], partial=True)
                    yield
                    for kb in range(2):
                        P.tr(T_, T_[:, kb * 128:(kb + 1) * 128], pb, pb[:, kb * 128:(kb + 1) * 128], C.ident_b, C.ident_b[:])
                    evac(P, b + 1, ptb, ptb[:], T_, T_[:])
                    yield
                    for kb in range(2):
                        P.mm(O_, O_[:], ptb, ptb[:, kb * 128:(kb + 1) * 128], vt, vt[:, r * (1 + nb) + n + kb, :],
                             start=(kb == 0), stop=(kb == 1))
                    P.op("dve", lambda e: e.reciprocal(stt[:, 2:3], stt[:, 1:2]), reads=[stt], writes=[stt], partial=True)
                    P.op("dve", lambda e: e.tensor_scalar(ob[:], O_[:], stt[:, 2:3], None, ALU.mult),
                         reads=[O_, stt], writes=[ob])
                    yield
                    P.op("act", lambda e: e.activation(stt[:, 3:4], stt[:, 1:2], AF.Ln), reads=[stt], writes=[stt], partial=True)
                    P.op("dve", lambda e: e.tensor_scalar(lc[:], C.cm[:, 512:640], stt[:, 3:4], stt[:, 0:1],
                                                          ALU.add, ALU.subtract),
                         reads=[stt, C.cm], writes=[lc])
                    P.tr(OT, OT[:], ob, ob[:], C.ident_b, C.ident_b[:])
                    P.mm(LB, LB[:], lc, lc[:], C.ident_f, C.ident_f[:])
                    yield
                    tok0 = r + d * 128 * n
                    dsto = oTg[g][:, tok0:tok0 + 127 * d + 1:d]
                    dstl = lsB[g][:, tok0:tok0 + 127 * d + 1:d]
                    evac(P, b, oTg[g], dsto, OT, OT[:])
                    evac(P, b + 1, lsB[g], dstl, LB, LB[:])
                    yield

                blist = [(r, n) for r in range(d) for n in range(nb)]
                for i in range(0, len(blist), 2):
                    gens = []
                    for (r, n) in blist[i:i + 2]:
                        gens.append(gen_block(r, n, blk[0]))
                        blk[0] += 1
                    interleave(*gens)
            for t in range(4):
                sl = slice(t * 512, (t + 1) * 512)
                m_, e0, e1, e2, acc = cw
                P.op("dve", lambda e: e.tensor_tensor(m_[:], lsB[0][:, sl], lsB[1][:, sl], ALU.max), reads=[lsB[0], lsB[1]], writes=[m_])
                P.op("dve", lambda e: e.tensor_tensor(m_[:], m_[:], lsB[2][:, sl], ALU.max), reads=[m_, lsB[2]], writes=[m_])
                for g, eg in enumerate((e0, e1, e2)):
                    P.op("pool", lambda e: e.tensor_tensor(eg[:], lsB[g][:, sl], m_[:], ALU.subtract), reads=[lsB[g], m_], writes=[eg])
                    P.op("act", lambda e: e.activation(eg[:], eg[:], AF.Exp), reads=[eg], writes=[eg])
                P.op("dve", lambda e: e.tensor_tensor(m_[:], e0[:], e1[:], ALU.add), reads=[e0, e1], writes=[m_])
                P.op("dve", lambda e: e.tensor_tensor(m_[:], m_[:], e2[:], ALU.add), reads=[m_, e2], writes=[m_])
                P.op("dve", lambda e: e.reciprocal(m_[:], m_[:]), reads=[m_], writes=[m_])
                P.op("dve", lambda e: e.tensor_tensor(acc[:], e0[:], oTg[0][:, sl], ALU.mult), reads=[e0, oTg[0]], writes=[acc])
                P.op("pool", lambda e: e.tensor_tensor(e1[:], e1[:], oTg[1][:, sl], ALU.mult), reads=[e1, oTg[1]], writes=[e1])
                P.op("pool", lambda e: e.tensor_tensor(e2[:], e2[:], oTg[2][:, sl], ALU.mult), reads=[e2, oTg[2]], writes=[e2])
                P.op("dve", lambda e: e.tensor_tensor(acc[:], acc[:], e1[:], ALU.add), reads=[acc, e1], writes=[acc])
                P.op("dve", lambda e: e.tensor_tensor(acc[:], acc[:], e2[:], ALU.add), reads=[acc, e2], writes=[acc])
                P.op("dve", lambda e: e.tensor_tensor(acc[:], acc[:], m_[:], ALU.mult), reads=[acc, m_], writes=[acc])
                if dbg is not None:
                    P.dma("sp", dbg["oat"][h * 128:(h + 1) * 128, sl], acc[:], reads=[acc])
                P.op("dve", lambda e: e.tensor_tensor(yaT[:, h, sl], acc[:], sz[:, sl], ALU.mult), reads=[acc, sz], writes=[yaT], partial=True)


def emit_out_phase(P, C, din, xT, xc0, x_rows_ap, yrT, yaT, ntok, y_out_ap, tag):
    with scope(P):
        wg = P.sbuf("wg" + tag, [128, 8, 2048], BF16)
        woa = P.sbuf("woa" + tag, [128, 4, 1024], BF16)
        wob = P.sbuf("wob" + tag, [128, 4, 1024], BF16)
        wout = P.sbuf("wout" + tag, [128, 8, 1024], BF16)
        lng = P.sbuf("lng" + tag, [128, 1024], F32)
        lnb = P.sbuf("lnb" + tag, [128, 1024], F32)
        TW = min(512, ntok)
        mixT = P.sbuf("mixT" + tag, [128, 8, TW], BF16)
        gr = [P.sbuf("gr%d%s" % (i, tag), [128, TW], F32) for i in range(2)]
        ga = [P.sbuf("ga%d%s" % (i, tag), [128, TW], F32) for i in range(2)]
        t1 = [P.sbuf("t1%d%s" % (i, tag), [128, TW], F32) for i in range(2)]
        xr = [P.sbuf("xr%d%s" % (i, tag), [128, 1024], F32) for i in range(2)]
        z = [P.sbuf("z%d%s" % (i, tag), [128, 1024], F32) for i in range(2)]
        bst = [P.sbuf("bst%d%s" % (i, tag), [128, 16], F32) for i in range(2)]
        ps_g = [carve(C, i, 0, 512, "psO_g%d%s" % (i, tag)) for i in range(4)]
        ps_y = [carve(C, 4 + i, 0, 512, "psO_y%d%s" % (i, tag)) for i in range(4)]
        P.dma("pool", wg[:], din["w_g"].rearrange("(k p) c -> p k c", p=128), writes=[wg])
        P.dma("pool", woa[:], din["w_oa"].rearrange("(k p) c -> p k c", p=128), writes=[woa])
        P.dma("pool", wob[:], din["w_ob"].rearrange("(k p) c -> p k c", p=128), writes=[wob])
        P.dma("pool", wout[:], din["w_out"].rearrange("(k p) c -> p k c", p=128), writes=[wout])
        P.dma("sp", lng[:], din["ln_gb"][0:1, :].to_broadcast([128, 1024]), writes=[lng])
        P.dma("sp", lnb[:], din["ln_gb"][1:2, :].to_broadcast([128, 1024]), writes=[lnb])
        it = 0
        for t0 in range(0, ntok, TW):
            for n in range(8):
                pgr, pga, pmr, pma = ps_g
                for k in range(8):
                    P.mm(pgr, pgr[:, 0:TW], wg, wg[:, k, n * 128:(n + 1) * 128], xT, xT[:, k, xc0 + t0:xc0 + t0 + TW], start=(k == 0), stop=(k == 7))
                for k in range(8):
                    P.mm(pga, pga[:, 0:TW], wg, wg[:, k, 1024 + n * 128:1024 + (n + 1) * 128], xT, xT[:, k, xc0 + t0:xc0 + t0 + TW], start=(k == 0), stop=(k == 7))
                for c in range(4):
                    P.mm(pmr, pmr[:, 0:TW], woa, woa[:, c, n * 128:(n + 1) * 128], yrT, yrT[:, c, t0:t0 + TW], start=(c == 0), stop=(c == 3))
                for c in range(4):
                    P.mm(pma, pma[:, 0:TW], wob, wob[:, c, n * 128:(n + 1) * 128], yaT, yaT[:, c, t0:t0 + TW], start=(c == 0), stop=(c == 3))
                a, b_, tt = gr[it % 2], ga[it % 2], t1[it % 2]
                it += 1
                P.op("act", lambda e: e.activation(a[:], pgr[:, 0:TW], AF.Sigmoid, bias=C.pv[:, PV_BG + n:PV_BG + n + 1], scale=1.0),
                     reads=[pgr, C.pv], writes=[a])
                P.op("act", lambda e: e.activation(b_[:], pga[:, 0:TW], AF.Sigmoid, bias=C.pv[:, PV_BG + 8 + n:PV_BG + 9 + n], scale=1.0),
                     reads=[pga, C.pv], writes=[b_])
                P.op("dve", lambda e: e.tensor_tensor(tt[:], a[:], pmr[:, 0:TW], ALU.mult), reads=[a, pmr], writes=[tt])
                P.op("dve", lambda e: e.tensor_tensor(b_[:], b_[:], pma[:, 0:TW], ALU.mult), reads=[b_, pma], writes=[b_])
                P.op("pool", lambda e: e.tensor_tensor(mixT[:, n, :], tt[:], b_[:], ALU.add), reads=[tt, b_], writes=[mixT], partial=True)
            for s0 in range(0, TW, 128):
                ns = min(128, ntok - t0 - s0)
                i2 = (t0 + s0) // 128
                xx, zz, bs = xr[i2 % 2], z[i2 % 2], bst[i2 % 2]
                py = ps_y[(i2 % 2) * 2:(i2 % 2) * 2 + 2]
                P.dma("sp", xx[0:ns, :], x_rows_ap[t0 + s0:t0 + s0 + ns, :], writes=[xx])
                for hf in range(2):
                    for m in range(8):
                        P.mm(py[hf], py[hf][0:ns, :], mixT, mixT[:, m, s0:s0 + ns], wout, wout[:, m, hf * 512:(hf + 1) * 512],
                             start=(m == 0), stop=(m == 7))
                    P.op("dve", lambda e: e.scalar_tensor_tensor(zz[0:ns, hf * 512:(hf + 1) * 512], xx[0:ns, hf * 512:(hf + 1) * 512],
                                                                 ALPHA, py[hf][0:ns, :], ALU.mult, ALU.add),
                         reads=[xx, py[hf]], writes=[zz], partial=True)
                    P.op("dve", lambda e: e.bn_stats(bs[0:ns, hf * 6:(hf + 1) * 6], zz[0:ns, hf * 512:(hf + 1) * 512]),
                         reads=[zz], writes=[bs], partial=True)
                P.op("dve", lambda e: e.bn_aggr(bs[0:ns, 12:14], bs[0:ns, 0:12]), reads=[bs], writes=[bs], partial=True)
                P.op("dve", lambda e: e.tensor_scalar(bs[0:ns, 14:15], bs[0:ns, 13:14], LN_EPS, None, ALU.add), reads=[bs], writes=[bs], partial=True)
                P.op("act", lambda e: e.activation(bs[0:ns, 14:15], bs[0:ns, 14:15], AF.Sqrt), reads=[bs], writes=[bs], partial=True)
                P.op("dve", lambda e: e.reciprocal(bs[0:ns, 14:15], bs[0:ns, 14:15]), reads=[bs], writes=[bs], partial=True)
                P.op("dve", lambda e: e.tensor_scalar(zz[0:ns, :], zz[0:ns, :], bs[0:ns, 12:13], bs[0:ns, 14:15], ALU.subtract, ALU.mult),
                     reads=[zz, bs], writes=[zz])
                P.op("pool", lambda e: e.tensor_tensor(zz[0:ns, :], zz[0:ns, :], lng[0:ns, :], ALU.mult), reads=[zz, lng], writes=[zz])
                P.op("pool", lambda e: e.tensor_tensor(zz[0:ns, :], zz[0:ns, :], lnb[0:ns, :], ALU.add), reads=[zz, lnb], writes=[zz])
                P.dma("sp", y_out_ap[t0 + s0:t0 + s0 + ns, :], zz[0:ns, :], reads=[zz])


def make_cmask():
    cm = np.zeros((128, 1408), np.float32)
    i = np.arange(128)[:, None]
    j = np.arange(256)[None, :]
    dist = 128 + i - j
    cm[:, 0:256] = np.where((dist >= 0) & (dist <= 128), 0.0, NEGM)
    p = np.arange(128)
    hs, s = p[:, None] // 64, p[:, None] % 64
    ht, t = p[None, :] // 64, p[None, :] % 64
    same = (hs == ht)
    cm[:, 640:768] = (same & (s < t)).astype(np.float32)
    cm[:, 768:896] = (same & (s <= t)).astype(np.float32)
    cm[:, 896:1024] = (same & (s > t)).astype(np.float32)
    cm[:, 1024:1152] = np.eye(128, dtype=np.float32)
    cm[:, 1152:1280] = same.astype(np.float32)
    cm[:, 1280:1282] = (p[:, None] // 64 == np.arange(2)[None, :]).astype(np.float32)
    sel = np.zeros((128, 64), np.float32)
    sel[p, p % 64] = 1.0
    cm[:, 1282:1346] = sel
    return cm


CM_STRICT, CM_INCL, CM_STRICT_T, CM_EYE, CM_BONES, CM_HM, CM_SEL = 640, 768, 896, 1024, 1152, 1280, 1282


def build(flags):
    nc = bass.Bass("TRN2", target_bir_lowering=False)
    din, dout = {}, {}

    def inp(name, shape, dt=F32):
        din[name] = nc.dram_tensor(name, list(shape), dt, kind="ExternalInput").ap()

    def outp(name, shape, dt=F32):
        dout[name] = nc.dram_tensor(name, list(shape), dt, kind="ExternalOutput").ap()

    inp("xe", [EXT, D])
    inp("w_rw", [D, RW_COLS])
    inp("w_att", [4, D, 1280])
    inp("w_g", [D, 2048])
    inp("w_oa", [512, D])
    inp("w_ob", [512, D])
    inp("w_out", [D, D])
    inp("w_l2", [128, 512])
    inp("pvec", [128, PV_OMM])
    inp("ln_gb", [2, D])
    inp("ident", [128, 128])
    inp("cmask", [128, 1408])
    inp("pbias", [128, 1])
    inp("xs", [64, D])
    inp("w_qkvz", [D, 5120])
    inp("cache1", [16, 128, 2, 4, 128])
    inp("cache2", [16, 512, 2, 4, 128])
    inp("cache3", [16, 2048, 2, 4, 128])
    inp("wkv_s", [16, 8, 64, 64])
    inp("shift_s", [16, SHIFT_COLS])
    inp("cmask_s", [128, NCS])
    inp("colmask", [128, 2048])
    outp("y_s", [64, D])
    for g in (1, 2, 3):
        outp("kvs%d" % g, [16, 4, 2, 4, 128])
    outp("wkv_so", [16, 8, 64, 64])
    outp("shift_so", [16, SHIFT_COLS])
    outp("y_p", [OWN, D])
    outp("kvp1", [128, 2, 4, 128])
    outp("kvp2", [512, 2, 4, 128])
    outp("kvp3", [2048, 2, 4, 128])
    outp("wkv_p", [8, 64, 64])
    outp("shift_p", [SHIFT_COLS])
    if flags.get("dbg"):
        inp("yr_dbg", [512, OWN])
        outp("oat", [512, OWN])
        outp("yat", [512, OWN])
        outp("yrt", [512, OWN])
    with contextlib.ExitStack() as st:
        P = Prog(nc, st)
        C = Ctx()
        C.bank = [P.psum("bank%d" % i, [128, 512], F32) for i in range(8)]
        emit_consts(P, C, din)
        barrier(P)
        yr_scr = P.dram("yr_scr", [128, 4 * OWN], BF16)
        ya_scr = P.dram("ya_scr", [128, 4 * OWN], BF16)
        if flags.get("attn", True):
          with scope(P):
              yaT = P.sbuf("yaT", [128, 4, OWN], BF16)
              xTA = P.sbuf("xTA", [128, 8, 4096], BF16)
              with scope(P):
                  xld = [P.sbuf("xldA%d" % i, [128, 1024], BF16) for i in range(3)]
                  psx = [carve(C, 6 + i, 0, 512, "psxA%d" % i, BF16) for i in range(2)]
                  emit_xT(P, C, din["xe"][4096:8192, :], 32, xTA, 0, xld, psx, [0])
              emit_attn_prompt(P, C, din, xTA, yaT, dbg=dout if flags.get("dbg") else None, dout=dout)
              if flags.get("dbg"):
                  with scope(P):
                      yaf = P.sbuf("yaf", [128, 4, OWN], F32)
                      P.op("dve", lambda e: e.tensor_copy(yaf[:], yaT[:]), reads=[yaT], writes=[yaf])
                      P.dma("sp", dout["yat"].rearrange("(h p) t -> p h t", p=128), yaf[:], reads=[yaf])
              P.dma("sp", ya_scr[:, :], yaT[:].rearrange("p h t -> p (h t)"), reads=[yaT], writes=[ya_scr])
        if flags.get("rwkv", True):
            emit_rwkv_prompt(P, C, din, dout, yr_scr, ntiles=flags.get("ntiles", 16), own_from=flags.get("own_from", 12),
                             budget=flags.get("budget"))
        if flags.get("outp", True):
          with scope(P):
              xTO = P.sbuf("xTO", [128, 8, OWN], BF16)
              yaT = P.sbuf("yaT2", [128, 4, OWN], BF16)
              P.dma("sp", yaT[:].rearrange("p h t -> p (h t)"), ya_scr[:, :], reads=[ya_scr], writes=[yaT])
              yrT = P.sbuf("yrT", [128, 4, OWN], BF16)
              if flags.get("rwkv", True):
                  P.dma("sp", yrT[:].rearrange("p h t -> p (h t)"), yr_scr[:, :], reads=[yr_scr], writes=[yrT])
              elif flags.get("dbg"):
                  P.dma("pool", yrT[:], din["yr_dbg"].rearrange("(h p) t -> p h t", p=128), writes=[yrT])
              if flags.get("dbg"):
                  with scope(P):
                      yrf = P.sbuf("yrf", [128, 4, OWN], F32)
                      P.op("dve", lambda e: e.tensor_copy(yrf[:], yrT[:]), reads=[yrT], writes=[yrf])
                      P.dma("sp", dout["yrt"].rearrange("(h p) t -> p h t", p=128), yrf[:], reads=[yrf])
              with scope(P):
                  xld = [P.sbuf("xldO%d" % i, [128, 1024], BF16) for i in range(3)]
                  psx = [carve(C, 6 + i, 0, 512, "psxO%d" % i, BF16) for i in range(2)]
                  emit_xT(P, C, din["xe"][6144:8192, :], 16, xTO, 0, xld, psx, [0])
              emit_out_phase(P, C, din, xTO, 0, din["xe"][6144:8192, :], yrT, yaT, OWN, dout["y_p"], "p")
        if flags.get("sample", True):
            with scope(P):
                emit_sample(P, C, din, dout, flags)
        P.finish()
    return nc


def _fm(vec, n):
    return np.ascontiguousarray(np.asarray(vec, np.float32).reshape(n, 128).T)


def prep_shared(inputs):
    w_in = np.asarray(inputs["w_in"][0], np.float32)
    sh = {}
    sh["w_rw"] = np.ascontiguousarray(w_in[:, 0:RW_COLS])
    w_att = np.empty((4, D, 1280), np.float32)
    for h in range(4):
        cols = []
        for base in (Q0, K0, V0):
            for g in range(3):
                cols.append(w_in[:, base + g * 512 + h * 128: base + g * 512 + (h + 1) * 128])
        cols.append(w_in[:, ZA0 + h * 128: ZA0 + (h + 1) * 128])
        w_att[h] = np.concatenate(cols, axis=1)
    sh["w_att"] = w_att
    sh["w_g"] = np.ascontiguousarray(w_in[:, GR0:GR0 + 2048])
    sh["w_oa"] = np.ascontiguousarray(inputs["w_oa"][0], np.float32)
    sh["w_ob"] = np.ascontiguousarray(inputs["w_ob"][0], np.float32)
    sh["w_out"] = np.ascontiguousarray(inputs["w_out"][0], np.float32)
    sh["w_l2"] = np.ascontiguousarray(np.concatenate([inputs["w_w2"][0], inputs["w_a2"][0]], axis=0), np.float32)
    pv = np.zeros((128, PV_OMM), np.float32)
    pv[:, PV_MU:PV_MU + 13] = _fm(inputs["mu_shift"][0], 13)
    pv[:, PV_W0:PV_W0 + 4] = _fm(inputs["w0"][0], 4)
    pv[:, PV_A0:PV_A0 + 4] = _fm(inputs["a0"][0], 4)
    pv[:, PV_KK:PV_KK + 4] = _fm(inputs["k_k"][0], 4)
    pv[:, PV_KA:PV_KA + 4] = _fm(inputs["k_a"][0], 4)
    pv[:, PV_RK:PV_RK + 4] = _fm(np.asarray(inputs["r_k"][0]).reshape(-1), 4)
    pv[:, PV_LG:PV_LG + 4] = _fm(inputs["lnx_g"][0], 4)
    pv[:, PV_LB:PV_LB + 4] = _fm(inputs["lnx_b"][0], 4)
    pv[:, PV_BG:PV_BG + 16] = _fm(inputs["b_gate"][0], 16)
    sh["pvec"] = pv
    sh["ln_gb"] = np.ascontiguousarray(np.stack([inputs["ln_g"][0], inputs["ln_b"][0]]), np.float32)
    sh["w_qkvz"] = np.ascontiguousarray(w_in[:, Q0:ZA0 + 512])
    sh["cmask_s"] = make_cmask_s()
    sh["colmask"] = make_colmask()
    sh["ident"] = np.eye(128, dtype=np.float32)
    sh["cmask"] = make_cmask()
    return sh


def prep_core(inputs, sh, c):
    b, q = c // 4, c % 4
    m = dict(sh)
    xe = np.zeros((EXT, D), np.float32)
    n = OWN * (q + 1)
    xe[EXT - n:] = np.asarray(inputs["x_prompt"][b, 0:n], np.float32)
    m["xe"] = xe
    m["pbias"] = np.full((128, 1), 0.0 if q > 0 else NEGM, np.float32)
    sl = slice(16 * c, 16 * c + 16)
    m["xs"] = np.ascontiguousarray(np.asarray(inputs["x_sample"][sl], np.float32).reshape(64, D))
    m["cache1"] = np.ascontiguousarray(inputs["cache_kv_g1"][0, sl], np.float32)
    m["cache2"] = np.ascontiguousarray(inputs["cache_kv_g2"][0, sl], np.float32)
    m["cache3"] = np.ascontiguousarray(inputs["cache_kv_g3"][0, sl], np.float32)
    m["wkv_s"] = np.ascontiguousarray(inputs["state_rwkv_wkv"][0, sl], np.float32)
    m["shift_s"] = np.ascontiguousarray(inputs["state_rwkv_shift"][0, sl], np.float32)
    return m


_NC_CACHE = {}


def kernel(**inputs):
    if "nc" not in _NC_CACHE:
        _NC_CACHE["nc"] = build({})
    nc = _NC_CACHE["nc"]
    sh = prep_shared(inputs)
    in_maps = [prep_core(inputs, sh, c) for c in range(NCORES)]
    res = run_bass_kernel_spmd(nc, in_maps, core_ids=list(range(NCORES)))
    R = res.results
    y_p = np.zeros((NB, SEQ, D), np.float32)
    for c in range(NCORES):
        y_p[c // 4, (c % 4) * OWN:(c % 4 + 1) * OWN] = R[c]["y_p"]
    y_s = np.concatenate([R[c]["y_s"].reshape(16, 4, D) for c in range(NCORES)], axis=0)
    outs = [y_p, y_s]
    for g in (1, 2, 3):
        outs.append(np.stack([R[4 * b + 3]["kvp%d" % g] for b in range(NB)])[None])
        outs.append(np.concatenate([R[c]["kvs%d" % g] for c in range(NCORES)], axis=0)[None])
    outs.append(np.stack([R[4 * b + 3]["wkv_p"] for b in range(NB)])[None])
    outs.append(np.concatenate([R[c]["wkv_so"] for c in range(NCORES)], axis=0)[None])
    outs.append(np.stack([R[4 * b + 3]["shift_p"] for b in range(NB)])[None])
    outs.append(np.concatenate([R[c]["shift_so"] for c in range(NCORES)], axis=0)[None])
    return tuple(np.ascontiguousarray(o, dtype=np.float32) for o in outs)


def interleave(*gens):
    live = list(gens)
    while live:
        for g in list(live):
            try:
                next(g)
            except StopIteration:
                live.remove(g)


def bc(ap, shape, axes):
    for a in axes:
        ap = ap.unsqueeze(a)
    return ap.to_broadcast(list(shape))


def emit_rwkv_prompt(P, C, din, dout, yrT, ntiles=16, own_from=12, dbg=None, budget=None):
    with scope(P):
        w_r = P.sbuf("w_r", [128, 8, RW_COLS], BF16)
        wl2 = P.sbuf("wl2", [128, 512], BF16)
        mask4 = P.sbuf("mask4", [128, 512], F32)
        bones_b = P.sbuf("bones_b", [128, 128], BF16)
        scanm = P.sbuf("scanm", [128, 512], F32)
        carry = P.sbuf("carry", [128, 16], F32)
        shout = P.sbuf("shout", [128, 16], F32)
        Sf = [P.sbuf("Sf%d" % i, [128, 128], F32) for i in range(4)]
        Sb = [P.sbuf("Sb%d" % i, [128, 128], BF16) for i in range(4)]
        xT = [P.sbuf("xTR%d" % i, [128, 8, 512], BF16) for i in range(1)]
        xld = [P.sbuf("xldR%d" % i, [128, 1024], BF16) for i in range(2)]
        bm = [P.sbuf("bm%d" % i, [128, 512], F32) for i in range(2)]
        lor = P.sbuf("lor", [128, 512], BF16)
        wdad = P.sbuf("wdad", [128, 512], F32)
        S1 = []
        for i in range(3):
            d_ = {}
            for nm in ("rt", "kt", "bt", "at"):
                d_[nm] = P.sbuf("%s%d" % (nm, i), [128, 512], BF16)
            for nm in ("vz", "eg", "bonus", "gate"):
                d_[nm] = P.sbuf("%s%d" % (nm, i), [128, 512], F32)
            S1.append(d_)
        tmp = {nm: P.sbuf("tp_" + nm, [128, 512], F32) for nm in ("r", "k", "sg", "a", "cs", "eng", "kk", "rn", "km", "bv")}
        sqb = P.sbuf("sqb", [128, 512], BF16)
        Lb = P.sbuf("Lb", [128, 8, 2, 64], BF16)
        Lk = P.sbuf("Lk", [128, 8, 2, 64], BF16)
        Ra = P.sbuf("Ra", [128, 8, 2, 64], BF16)
        KH = P.sbuf("KH", [128, 8, 2, 64], BF16)
        BH = P.sbuf("BH", [128, 8, 2, 64], BF16)
        VB = P.sbuf("VB", [128, 8, 2, 64], BF16)
        hmg = P.sbuf("hmg", [128, 8, 2], F32)
        NA = P.sbuf("NA", [128, 8, 2, 128], BF16)
        PT0 = P.sbuf("PT0", [128, 8, 128], BF16)
        Rat = P.sbuf("Rat", [128, 8, 128], BF16)
        Tt = [P.sbuf("Tt%d" % i, [128, 8, 128], BF16) for i in range(2)]
        Pp = [P.sbuf("Pp%d" % i, [128, 4, 128], BF16) for i in range(2)]
        PTp = [P.sbuf("PTp%d" % i, [128, 4, 128], BF16) for i in range(2)]
        Yb = P.sbuf("Yb", [128, 4, 128], BF16)
        SETS = []
        for i in range(2):
            d_ = {}
            for nm in ("KHt", "BHt", "VBt", "W1", "W2", "Rr"):
                d_[nm] = P.sbuf("%s_%d" % (nm, i), [128, 8, 128], BF16)
            d_["ABK"] = P.sbuf("ABK_%d" % i, [128, 8, 2, 128], BF16)
            d_["gC"] = P.sbuf("gC_%d" % i, [128, 8], F32)
            SETS.append(d_)
        yos = [P.sbuf("yos%d" % i, [128, 512], BF16) for i in range(2)]
        Ub = [P.sbuf("Ub%d" % i, [128, 128], BF16) for i in range(2)]
        Ob = [P.sbuf("Ob%d" % i, [128, 128], BF16) for i in range(2)]
        post = {nm: P.sbuf("po_" + nm, [128, 512], F32) for nm in ("yr", "cen", "sq", "rs")}
        pp = [carve(C, i, 0, 512, "psR_p%d" % i) for i in range(2)]
        pstr = carve(C, 3, 0, 512, "psR_tr", BF16)
        psx = pstr
        psYb = carve(C, 2, 0, 512, "psR_Y")
        psA = carve(C, 4, 0, 384, "psR_A")
        psA2 = carve(C, 4, 384, 512, "psR_A2")
        psP = carve(C, 6, 0, 512, "psR_P")
        psPT = carve(C, 7, 0, 512, "psR_PT")
        psU = carve(C, 5, 0, 128, "psR_U")
        psS = carve(C, 5, 128, 256, "psR_S")
        psO = carve(C, 5, 256, 384, "psR_O")
        ppi = [0]
        eci = [0]

        def nextpp():
            b = pp[ppi[0] % 2]
            ppi[0] += 1
            return b

        def pv(col):
            return C.pv[:, col:col + 1]

        P.dma("pool", w_r[:], din["w_rw"].rearrange("(k p) c -> p k c", p=128), writes=[w_r])
        P.dma("pool", wl2[:], din["w_l2"][:, :], writes=[wl2])
        for i in range(4):
            src = C.cm[:, CM_STRICT:CM_STRICT + 128] if i % 2 == 0 else C.cm[:, CM_INCL:CM_INCL + 128]
            if i in (0, 1):
                src = C.cm[:, CM_STRICT:CM_STRICT + 128]
            else:
                src = C.cm[:, CM_INCL:CM_INCL + 128]
            P.op("pool", lambda e: e.tensor_copy(mask4[:, i * 128:(i + 1) * 128], src), reads=[C.cm], writes=[mask4], partial=True)
        P.op("pool", lambda e: e.tensor_copy(bones_b[:], C.cm[:, CM_BONES:CM_BONES + 128]), reads=[C.cm], writes=[bones_b])
        P.op("pool", lambda e: e.memset(scanm[:], 1.0), writes=[scanm])
        P.op("pool", lambda e: e.memset(scanm[:, 0:512:64], 0.0), writes=[scanm])
        P.op("pool", lambda e: e.memset(carry[:], 0.0), writes=[carry])
        for i in range(4):
            P.op("pool", lambda e: e.memset(Sf[i][:], 0.0), writes=[Sf[i]])
            P.op("pool", lambda e: e.memset(Sb[i][:], 0.0), writes=[Sb[i]])

        def build_xT(tile):
            xt = xT[0]
            for s in range(4):
                xb = xld[s % 2]
                P.dma("pool", xb[:], din["xe"][tile * 512 + s * 128: tile * 512 + (s + 1) * 128, :], writes=[xb])
                for k in range(8):
                    P.tr(psx, psx[:, k * 128:(k + 1) * 128], xb, xb[:, k * 128:(k + 1) * 128], C.ident_b, C.ident_b[:])
                evac(P, eci[0], xt, xt[:, :, s * 128:(s + 1) * 128], psx, psx[:].rearrange("p (k t) -> p k t", k=8))
                eci[0] += 1

        def proj_shift(xt, c, dst, last_own):
            p_ = nextpp()
            for k in range(8):
                P.mm(p_, p_[:], w_r, w_r[:, k, c * 128:(c + 1) * 128], xt, xt[:, k, :], start=(k == 0), stop=(k == 7))
            b_ = bm[c % 2]
            P.op("act", lambda e: e.mul(b_[:], p_[:], pv(PV_MU + c)), reads=[p_, C.pv], writes=[b_])
            P.op("dve", lambda e: e.scalar_tensor_tensor(dst[:, 1:512], p_[:, 1:512], pv(PV_OMM + c), b_[:, 0:511], ALU.mult, ALU.add),
                 reads=[p_, b_, C.pv], writes=[dst], partial=True)
            P.op("dve", lambda e: e.scalar_tensor_tensor(dst[:, 0:1], p_[:, 0:1], pv(PV_OMM + c), carry[:, c:c + 1], ALU.mult, ALU.add),
                 reads=[p_, carry, C.pv], writes=[dst], partial=True)
            P.op("pool", lambda e: e.tensor_copy(carry[:, c:c + 1], b_[:, 511:512]), reads=[b_], writes=[carry], partial=True)
            if last_own:
                P.op("act", lambda e: e.activation(shout[:, c:c + 1], p_[:, 511:512], AF.Copy), reads=[p_], writes=[shout], partial=True)

        def stage1(tile, hp, own):
            xt = xT[0]
            s1 = S1[(tile * 4 + hp) % 3]
            last_own = (tile == ntiles - 1)
            if hp == 0:
                proj_shift(xt, 12, wdad, last_own)
                P.op("act", lambda e: e.activation(lor[0:64, :], wdad[0:64, :], AF.Tanh), reads=[wdad], writes=[lor], partial=True)
                P.op("dve", lambda e: e.tensor_copy(lor[64:128, :], wdad[64:128, :]), reads=[wdad], writes=[lor], partial=True)
                yield
            r, k, sg, a, cs, eng, kk, rn, km, bv = (tmp[n] for n in ("r", "k", "sg", "a", "cs", "eng", "kk", "rn", "km", "bv"))
            vz, eg = s1["vz"], s1["eg"]
            if own or tile == own_from - 1:
                proj_shift(xt, hp, r, last_own)
                yield
            proj_shift(xt, 4 + hp, k, last_own)
            yield
            proj_shift(xt, 8 + hp, vz, last_own)
            yield
            p_ = nextpp()
            P.mm(p_, p_[:], wl2, wl2[0:64, hp * 128:(hp + 1) * 128], lor, lor[0:64, :])
            P.op("act", lambda e: e.activation(sg[:], p_[:], AF.Sigmoid, bias=pv(PV_W0 + hp), scale=1.0), reads=[p_, C.pv], writes=[sg])
            p2 = nextpp()
            P.mm(p2, p2[:], wl2, wl2[64:128, hp * 128:(hp + 1) * 128], lor, lor[64:128, :])
            P.op("act", lambda e: e.activation(a[:], p2[:], AF.Sigmoid, bias=pv(PV_A0 + hp), scale=1.0), reads=[p2, C.pv], writes=[a])
            yield
            P.op("dve", lambda e: e.tensor_tensor_scan(cs[:], scanm[:], sg[:], 0.0, ALU.mult, ALU.add), reads=[scanm, sg], writes=[cs])
            P.op("act", lambda e: e.activation(eg[:], cs[:], AF.Exp, scale=-C0), reads=[cs], writes=[eg])
            P.op("act", lambda e: e.activation(eng[:], cs[:], AF.Exp, scale=C0), reads=[cs], writes=[eng])
            P.op("pool", lambda e: e.tensor_tensor(cs[:], cs[:], sg[:], ALU.subtract), reads=[cs, sg], writes=[cs])
            P.op("act", lambda e: e.activation(cs[:], cs[:], AF.Exp, scale=-C0), reads=[cs], writes=[cs])
            yield
            P.op("dve", lambda e: e.tensor_scalar(kk[:], k[:], pv(PV_KK + hp), None, ALU.mult), reads=[k, C.pv], writes=[kk])
            P.op("pool", lambda e: e.tensor_tensor(sqb[:], kk[:], kk[:], ALU.mult), reads=[kk], writes=[sqb])
            p3 = nextpp()
            P.mm(p3, p3[:], bones_b, bones_b[:], sqb, sqb[:])
            P.op("dve", lambda e: e.tensor_scalar(rn[:], p3[:], 1e-24, None, ALU.max), reads=[p3], writes=[rn])
            P.op("act", lambda e: e.activation(rn[:], rn[:], AF.Sqrt), reads=[rn], writes=[rn])
            P.op("dve", lambda e: e.reciprocal(rn[:], rn[:]), reads=[rn], writes=[rn])
            P.op("dve", lambda e: e.tensor_tensor(kk[:], kk[:], rn[:], ALU.mult), reads=[kk, rn], writes=[kk])
            yield
            P.op("dve", lambda e: e.tensor_scalar(km[:], a[:], -1.0, pv(PV_KA + hp), ALU.add, ALU.mult), reads=[a, C.pv], writes=[km])
            P.op("dve", lambda e: e.scalar_tensor_tensor(km[:], km[:], 1.0, k[:], ALU.add, ALU.mult), reads=[km, k], writes=[km])
            P.op("pool", lambda e: e.tensor_tensor(bv[:], kk[:], a[:], ALU.mult), reads=[kk, a], writes=[bv])
            yield
            if own:
                P.op("pool", lambda e: e.tensor_tensor(s1["rt"][:], r[:], eg[:], ALU.mult), reads=[r, eg], writes=[s1["rt"]])
            P.op("dve", lambda e: e.tensor_tensor(s1["kt"][:], km[:], eng[:], ALU.mult), reads=[km, eng], writes=[s1["kt"]])
            P.op("pool", lambda e: e.tensor_tensor(s1["bt"][:], bv[:], eng[:], ALU.mult), reads=[bv, eng], writes=[s1["bt"]])
            P.op("dve", lambda e: e.scalar_tensor_tensor(s1["at"][:], kk[:], -1.0, cs[:], ALU.mult, ALU.mult), reads=[kk, cs], writes=[s1["at"]])
            yield
            if own:
                P.op("dve", lambda e: e.scalar_tensor_tensor(sqb[:], r[:], pv(PV_RK + hp), km[:], ALU.mult, ALU.mult), reads=[r, km, C.pv], writes=[sqb])
                p4 = nextpp()
                P.mm(p4, p4[:], bones_b, bones_b[:], sqb, sqb[:])
                P.op("dve", lambda e: e.tensor_tensor(s1["bonus"][:], p4[:], vz[:], ALU.mult), reads=[p4, vz], writes=[s1["bonus"]])
                p5 = nextpp()
                for kq in range(8):
                    P.mm(p5, p5[:], w_r, w_r[:, kq, (13 + hp) * 128:(14 + hp) * 128], xt, xt[:, kq, :], start=(kq == 0), stop=(kq == 7))
                P.op("act", lambda e: e.activation(s1["gate"][:], p5[:], AF.Silu), reads=[p5], writes=[s1["gate"]])
                yield

        def stage2(tile, hp, own):
            u = tile * 4 + hp
            s1 = S1[u % 3]
            cs_ = SETS[u % 2]
            hm = C.cm[:, CM_HM:CM_HM + 2]
            eg = s1["eg"]
            gview = eg[:, 63:512:64]
            P.op("pool", lambda e: e.tensor_copy(cs_["gC"][:], gview), reads=[eg], writes=[cs_["gC"]])
            P.op("pool", lambda e: e.tensor_tensor(hmg[:], bc(hm, [128, 8, 2], [1]), bc(gview, [128, 8, 2], [2]), ALU.mult),
                 reads=[C.cm, eg], writes=[hmg])
            hm4 = bc(hm, [128, 8, 2, 64], [1, 3])
            hmg4 = bc(hmg[:], [128, 8, 2, 64], [3])

            def ex(x):
                return bc(x[:].rearrange("p (c s) -> p c s", s=64), [128, 8, 2, 64], [2])
            P.op("dve", lambda e: e.tensor_tensor(Lb[:], ex(s1["bt"]), hm4, ALU.mult), reads=[s1["bt"], C.cm], writes=[Lb])
            P.op("pool", lambda e: e.tensor_tensor(Ra[:], ex(s1["at"]), hm4, ALU.mult), reads=[s1["at"], C.cm], writes=[Ra])
            P.op("dve", lambda e: e.tensor_tensor(Lk[:], ex(s1["kt"]), hm4, ALU.mult), reads=[s1["kt"], C.cm], writes=[Lk])
            yield
            P.op("pool", lambda e: e.tensor_tensor(KH[:], ex(s1["kt"]), hmg4, ALU.mult), reads=[s1["kt"], hmg], writes=[KH])
            P.op("dve", lambda e: e.tensor_tensor(BH[:], ex(s1["bt"]), hmg4, ALU.mult), reads=[s1["bt"], hmg], writes=[BH])
            P.op("pool", lambda e: e.tensor_tensor(VB[:], ex(s1["vz"]), hm4, ALU.mult), reads=[s1["vz"], C.cm], writes=[VB])
            if own:
                Rr4 = cs_["Rr"][:].rearrange("p c (h s) -> p c h s", h=2)
                P.op("dve", lambda e: e.tensor_tensor(Rr4, ex(s1["rt"]), hm4, ALU.mult), reads=[s1["rt"], C.cm], writes=[cs_["Rr"]])
            yield

            def blk(t, c):
                return t[:, c].rearrange("p h s -> p (h s)")
            Tc = Tt[0]
            for c in range(8):
                P.mm(psA, psA[:, 0:128], Lb, blk(Lb, c), Ra, blk(Ra, c))
                P.mm(psA, psA[:, 128:256], Lk, blk(Lk, c), Ra, blk(Ra, c))
                P.mm(psA, psA[:, 256:384], Ra, blk(Ra, c), Lb, blk(Lb, c))
                P.op("dve", lambda e: e.tensor_tensor(NA[:, c].rearrange("p a t -> p (a t)"), psA[:, 0:256], mask4[:, 0:256], ALU.mult),
                     reads=[psA, mask4], writes=[NA], partial=True)
                P.op("dve", lambda e: e.tensor_tensor(PT0[:, c, :], psA[:, 256:384], C.cm[:, CM_STRICT_T:CM_STRICT_T + 128], ALU.mult),
                     reads=[psA, C.cm], writes=[PT0], partial=True)
                if own:
                    for a_, lx in enumerate((Lb, Lk)):
                        P.mm(psA2, psA2[:], lx, blk(lx, c), cs_["Rr"], cs_["Rr"][:, c, :])
                        P.op("dve", lambda e: e.tensor_tensor(cs_["ABK"][:, c, a_, :], psA2[:], mask4[:, 256:384], ALU.mult),
                             reads=[psA2, mask4], writes=[cs_["ABK"]], partial=True)
                P.op("pool", lambda e: e.tensor_tensor(Tc[:, c, :], NA[:, c, 0, :], C.cm[:, CM_EYE:CM_EYE + 128], ALU.add),
                     reads=[NA, C.cm], writes=[Tc], partial=True)
                if c % 2 == 1:
                    yield
            for qi, (src, dst_b) in enumerate(((KH, cs_["KHt"]), (BH, cs_["BHt"]), (Ra, Rat), (VB, cs_["VBt"]))):
                for c in range(8):
                    P.tr(pstr, pstr[:, c * 128:(c + 1) * 128], src, blk(src, c), C.ident_b, C.ident_b[:])
                evac(P, qi, dst_b, dst_b[:].rearrange("p c t -> p (c t)"), pstr, pstr[:])
                yield
            for cb in range(2):
                c0 = cb * 4
                Pc, PTc = None, None
                Tcur = Tt[0]
                for lvl in range(1, 6):
                    Pn, PTn = Pp[lvl % 2], PTp[lvl % 2]
                    for c in range(4):
                        lp = NA[:, c0 + c, 0, :] if lvl == 1 else Pc[:, c, :]
                        lpt = PT0[:, c0 + c, :] if lvl == 1 else PTc[:, c, :]
                        lpb = NA if lvl == 1 else Pc
                        lptb = PT0 if lvl == 1 else PTc
                        if lvl < 5:
                            P.mm(psP, psP[:, c * 128:(c + 1) * 128], lptb, lpt, lpb, lp)
                        P.mm(psPT, psPT[:, c * 128:(c + 1) * 128], lpb, lp, lptb, lpt)
                    if lvl < 5:
                        P.op("act", lambda e: e.activation(Pn[:].rearrange("p c t -> p (c t)"), psP[:], AF.Copy), reads=[psP], writes=[Pn])
                    P.op("act", lambda e: e.activation(PTn[:].rearrange("p c t -> p (c t)"), psPT[:], AF.Copy), reads=[psPT], writes=[PTn])
                    Tn = Tt[lvl % 2]
                    for c in range(4):
                        P.mm(psP, psP[:, c * 128:(c + 1) * 128], PTn, PTn[:, c, :], Tcur, Tcur[:, c0 + c, :])
                    P.op("dve", lambda e: e.tensor_tensor(Tn[:, c0:c0 + 4, :].rearrange("p c t -> p (c t)"), psP[:],
                                                          Tcur[:, c0:c0 + 4, :].rearrange("p c t -> p (c t)"), ALU.add),
                         reads=[psP, Tcur], writes=[Tn], partial=True)
                    Pc, PTc, Tcur = Pn, PTn, Tn
                    yield
                Tfin = Tcur
                p_ = nextpp()
                for c in range(4):
                    P.mm(p_, p_[:, c * 128:(c + 1) * 128], NA, NA[:, c0 + c, 1, :], cs_["VBt"], cs_["VBt"][:, c0 + c, :])
                evac(P, 0, Yb, Yb[:].rearrange("p c t -> p (c t)"), p_, p_[:])
                p_ = nextpp()
                for c in range(4):
                    P.mm(p_, p_[:, c * 128:(c + 1) * 128], Tfin, Tfin[:, c0 + c, :], Yb, Yb[:, c, :])
                evac(P, 1, cs_["W2"], cs_["W2"][:, c0:c0 + 4, :].rearrange("p c t -> p (c t)"), p_, p_[:])
                p_ = nextpp()
                for c in range(4):
                    P.mm(p_, p_[:, c * 128:(c + 1) * 128], Rat, Rat[:, c0 + c, :], Tfin, Tfin[:, c0 + c, :])
                evac(P, 0, cs_["W1"], cs_["W1"][:, c0:c0 + 4, :].rearrange("p c t -> p (c t)"), p_, p_[:])
                yield

        def stage3(tile, hp, own):
            u = tile * 4 + hp
            s1 = S1[u % 3]
            cs_ = SETS[u % 2]
            psY = psYb
            for c in range(8):
                ub, ob = Ub[c % 2], Ob[c % 2]
                P.mm(psU, psU[:], cs_["W1"], cs_["W1"][:, c, :], Sb[hp], Sb[hp][:])
                P.op("dve", lambda e: e.tensor_tensor(ub[:], psU[:], cs_["W2"][:, c, :], ALU.add), reads=[psU, cs_["W2"]], writes=[ub])
                P.mm(psS, psS[:], cs_["KHt"], cs_["KHt"][:, c, :], cs_["VBt"], cs_["VBt"][:, c, :], start=True, stop=False)
                P.mm(psS, psS[:], cs_["BHt"], cs_["BHt"][:, c, :], ub, ub[:], start=False, stop=True)
                if own:
                    P.mm(psO, psO[:], cs_["Rr"], cs_["Rr"][:, c, :], Sb[hp], Sb[hp][:], start=True, stop=False)
                    P.mm(psO, psO[:], cs_["ABK"], cs_["ABK"][:, c, 0, :], ub, ub[:], start=False, stop=False)
                    P.mm(psO, psO[:], cs_["ABK"], cs_["ABK"][:, c, 1, :], cs_["VBt"], cs_["VBt"][:, c, :], start=False, stop=True)
                gc = cs_["gC"][:, c:c + 1]
                P.op("dve", lambda e: e.scalar_tensor_tensor(Sb[hp][:], Sf[hp][:], gc, psS[:], ALU.mult, ALU.add),
                     reads=[Sf[hp], psS, cs_["gC"]], writes=[Sb[hp]])
                P.op("dve", lambda e: e.scalar_tensor_tensor(Sf[hp][:], Sf[hp][:], gc, psS[:], ALU.mult, ALU.add),
                     reads=[Sf[hp], psS, cs_["gC"]], writes=[Sf[hp]])
                if own:
                    P.op("act", lambda e: e.activation(ob[:], psO[:], AF.Copy), reads=[psO], writes=[ob])
                    P.mm(psY, psY[:, c * 64:(c + 1) * 64], ob, ob[:], C.selb, C.selb[:])
                yield
            if own and tile >= own_from:
                yr, cen, sq, rs = post["yr"], post["cen"], post["sq"], post["rs"]
                P.op("act", lambda e: e.activation(yr[:], psY[:], AF.Copy), reads=[psY], writes=[yr])
                pm = nextpp()
                P.mm(pm, pm[:], C.bones_f, C.bones_f[:], yr, yr[:])
                P.op("dve", lambda e: e.scalar_tensor_tensor(cen[:], pm[:], -1.0 / 64.0, yr[:], ALU.mult, ALU.add), reads=[pm, yr], writes=[cen])
                P.op("pool", lambda e: e.tensor_tensor(sq[:], cen[:], cen[:], ALU.mult), reads=[cen], writes=[sq])
                pv_ = nextpp()
                P.mm(pv_, pv_[:], C.bones_f, C.bones_f[:], sq, sq[:])
                P.op("dve", lambda e: e.tensor_scalar(rs[:], pv_[:], 1.0 / 64.0, GN_EPS, ALU.mult, ALU.add), reads=[pv_], writes=[rs])
                P.op("act", lambda e: e.activation(rs[:], rs[:], AF.Sqrt), reads=[rs], writes=[rs])
                P.op("dve", lambda e: e.reciprocal(rs[:], rs[:]), reads=[rs], writes=[rs])
                P.op("dve", lambda e: e.tensor_tensor(cen[:], cen[:], rs[:], ALU.mult), reads=[cen, rs], writes=[cen])
                P.op("dve", lambda e: e.tensor_scalar(cen[:], cen[:], pv(PV_LG + hp), pv(PV_LB + hp), ALU.mult, ALU.add), reads=[cen, C.pv], writes=[cen])
                P.op("pool", lambda e: e.tensor_tensor(cen[:], cen[:], s1["bonus"][:], ALU.add), reads=[cen, s1["bonus"]], writes=[cen])
                col = hp * OWN + (tile - own_from) * 512
                yo = yos[u % 2]
                P.op("pool", lambda e: e.tensor_tensor(yo[:], cen[:], s1["gate"][:], ALU.mult), reads=[cen, s1["gate"]], writes=[yo])
                P.dma("sp", yrT[:, col:col + 512], yo[:], reads=[yo], writes=[yrT])
                yield

        def pre1(tile, hp):
            own = tile >= own_from
            if hp == 0:
                build_xT(tile)
                yield
            yield from stage1(tile, hp, own)

        def pre2(tile, hp):
            yield from stage2(tile, hp, tile >= own_from)

        units = [(t, h) for t in range(ntiles) for h in range(4)]
        nu = len(units)
        for g_ in (pre1(*units[0]), pre2(*units[0])):
            for _ in g_:
                pass
        if nu > 1:
            for _ in pre1(*units[1]):
                pass
        for i, (t, h) in enumerate(units):
            gens = [stage3(t, h, t >= own_from)]
            if i + 1 < nu:
                gens.append(pre2(*units[i + 1]))
            if i + 2 < nu:
                gens.append(pre1(*units[i + 2]))
            interleave(*gens)
        for hp in range(4):
            p_ = nextpp()
            P.mm(p_, p_[:, 0:128], Sf[hp], Sf[hp][:], C.ident_f, C.ident_f[:])
            so = post["yr"]
            P.op("dve", lambda e: e.tensor_copy(so[:, 0:128], p_[:, 0:128]), reads=[p_], writes=[so])
            for hh in range(2):
                P.dma("sp", dout["wkv_p"][hp * 2 + hh, :, :], so[hh * 64:(hh + 1) * 64, hh * 64:(hh + 1) * 64], reads=[so])
        p_ = nextpp()
        P.mm(p_, p_[0:13, 0:128], shout, shout[:, 0:13], C.ident_f, C.ident_f[:])
        so = post["cen"]
        P.op("dve", lambda e: e.tensor_copy(so[0:13, 0:128], p_[0:13, 0:128]), reads=[p_], writes=[so])
        P.dma("sp", dout["shift_p"].rearrange("(c p) -> c p", p=128), so[0:13, 0:128], reads=[so])


CS_STRICT, CS_INCL, CS_STRICT_T, CS_ROW, CS_G1, CS_G23, NCS = 0, 128, 256, 384, 400, 532, 1048


def make_cmask_s():
    cs = np.zeros((128, NCS), np.float32)
    p = np.arange(128)
    h1, b1, t1 = p[:, None] // 64, (p[:, None] % 64) // 4, p[:, None] % 4
    h2, b2, t2 = p[None, :] // 64, (p[None, :] % 64) // 4, p[None, :] % 4
    same = (h1 == h2) & (b1 == b2)
    cs[:, CS_STRICT:CS_STRICT + 128] = (same & (t1 < t2))
    cs[:, CS_INCL:CS_INCL + 128] = (same & (t1 <= t2))
    cs[:, CS_STRICT_T:CS_STRICT_T + 128] = (same & (t1 > t2))
    cs[:, CS_ROW:CS_ROW + 16] = (((p[:, None] % 64) // 4) == np.arange(16)[None, :])
    t = p % 32
    g1 = np.zeros((128, 132), np.float32)
    r = np.arange(128)[None, :]
    g1[:, 0:128] = np.where(r >= t[:, None], 0.0, NEGM)
    u = np.arange(4)[None, :]
    g1[:, 128:132] = np.where(u <= t[:, None], 0.0, NEGM)
    g1[t >= 4] = 0.0
    cs[:, CS_G1:CS_G1 + 132] = g1
    g23 = np.zeros((128, 516), np.float32)
    c = (np.arange(512) // 128)[None, :]
    g23[:, 0:512] = np.where(c == t[:, None], 0.0, NEGM)
    g23[:, 512:516] = np.where(u == t[:, None], 0.0, NEGM)
    g23[t >= 4] = 0.0
    cs[:, CS_G23:CS_G23 + 516] = g23
    return cs


def make_colmask():
    col = np.arange(128)
    m = np.zeros((128, 16, 128), np.float32)
    for b in range(16):
        m[:, b, :] = (((col % 64) // 4) == b)[None, :]
    return m.reshape(128, 2048)


def emit_sample(P, C, din, dout, flags={}):
    yrS = P.sbuf("yrS", [128, 4, 64], BF16)
    yaS = P.sbuf("yaS", [128, 4, 64], BF16)
    xTs = P.sbuf("xTs", [128, 8, 64], BF16)
    cms = P.sbuf("cms", [128, NCS], F32)
    P.dma("sp", cms[:], din["cmask_s"][:, :], writes=[cms])
    with scope(P):
        xb = P.sbuf("xbS", [64, 1024], BF16)
        px = carve(C, 0, 0, 512, "psS_x", BF16)
        P.dma("pool", xb[:], din["xs"][:, :], writes=[xb])
        for k in range(8):
            P.tr(px, px[:, k * 64:(k + 1) * 64], xb, xb[:, k * 128:(k + 1) * 128], C.ident_b, C.ident_b[0:64, 0:64])
        P.op("dve", lambda e: e.tensor_copy(xTs[:].rearrange("p k t -> p (k t)"), px[:, 0:512]), reads=[px], writes=[xTs])
    if flags.get("s_rwkv", True):
        emit_rwkv_sample(P, C, din, dout, xTs, cms, yrS)
    if flags.get("s_attn", True):
        emit_attn_sample(P, C, din, dout, xTs, cms, yaS)
    if flags.get("s_out", True):
        emit_out_phase(P, C, din, xTs, 0, din["xs"], yrS, yaS, 64, dout["y_s"], "s")


def emit_rwkv_sample(P, C, din, dout, xTs, cms, yrS):
    W = 64
    with scope(P):
        w_r = P.sbuf("w_rS", [128, 8, RW_COLS], BF16)
        wl2 = P.sbuf("wl2S", [128, 512], BF16)
        bones_b = P.sbuf("bones_bS", [128, 128], BF16)
        scanm = P.sbuf("scanmS", [128, W], F32)
        colm = P.sbuf("colm", [128, 16, 128], BF16)
        shs = P.sbuf("shs", [16, SHIFT_COLS], F32)
        smu = P.sbuf("smu", [128, 13, 16], F32)
        shout = P.sbuf("shoutS", [128, 13, 16], F32)
        sho2 = P.sbuf("sho2", [16, SHIFT_COLS], F32)
        lor = P.sbuf("lorS", [128, W], BF16)
        wdad = P.sbuf("wdadS", [128, W], F32)
        bm = P.sbuf("bmS", [128, W], F32)
        tmp = {nm: P.sbuf("ts_" + nm, [128, W], F32) for nm in
               ("r", "k", "vz", "sg", "a", "cs", "eg", "eng", "kk", "rn", "km", "bv", "bonus", "gate", "gf", "yr", "cen", "sq", "rs")}
        tb = {nm: P.sbuf("tsb_" + nm, [128, W], BF16) for nm in ("rt", "kt", "bt", "at", "sqb", "ktg", "btg")}
        ex_ = {nm: P.sbuf("exs_" + nm, [128, 2, W], BF16) for nm in ("Lb", "Lk", "Ra", "Rr", "KH", "BH", "VB")}
        sq_ = {nm: P.sbuf("sqs_" + nm, [128, 128], BF16) for nm in
               ("N", "ak", "br", "kr", "NT", "T0", "P1T", "T", "KHt", "BHt", "Rat", "VBt", "Y", "W1", "W2", "Ub", "Ob")}
        W1b = P.sbuf("W1b", [128, 16, 128], BF16)
        Rrb = P.sbuf("Rrb", [128, 16, 128], BF16)
        KHtb = P.sbuf("KHtb", [128, 16, 128], BF16)
        BHtb = P.sbuf("BHtb", [128, 16, 128], BF16)
        Sv = P.sbuf("Sv", [128, 16, 64], F32)
        Svx = P.sbuf("Svx", [128, 16, 2, 64], F32)
        Sf = P.sbuf("SfS", [128, 16, 128], F32)
        Sb = P.sbuf("SbS", [128, 16, 128], BF16)
        So = P.sbuf("SoS", [128, 16, 128], F32)
        pp = [carve(C, i, 0, 512, "psS_p%d" % i) for i in range(2)]
        ptr = carve(C, 2, 0, 256, "psS_tr", BF16)
        pA = carve(C, 3, 0, 384, "psS_A")
        pB = carve(C, 2, 384, 512, "psS_B")
        pbig = [carve(C, 4 + i, 0, 512, "psS_big%d" % i) for i in range(4)]
        ppi = [0]

        def nextpp():
            b = pp[ppi[0] % 2]
            ppi[0] += 1
            return b

        def pv(col):
            return C.pv[:, col:col + 1]

        P.dma("pool", w_r[:], din["w_rw"].rearrange("(k p) c -> p k c", p=128), writes=[w_r])
        P.dma("pool", wl2[:], din["w_l2"][:, :], writes=[wl2])
        P.dma("pool", colm[:].rearrange("p b c -> p (b c)"), din["colmask"][:, :], writes=[colm])
        P.dma("sp", shs[:], din["shift_s"][:, :], writes=[shs])
        P.op("pool", lambda e: e.tensor_copy(bones_b[:], C.cm[:, CM_BONES:CM_BONES + 128]), reads=[C.cm], writes=[bones_b])
        P.op("pool", lambda e: e.memset(scanm[:], 1.0), writes=[scanm])
        P.op("pool", lambda e: e.memset(scanm[:, 0:W:4], 0.0), writes=[scanm])
        p_ = nextpp()
        for c in range(13):
            P.mm(p_, p_[:, c * 16:(c + 1) * 16], shs, shs[0:16, c * 128:(c + 1) * 128], C.ident_f, C.ident_f[0:16, 0:16])
        P.op("dve", lambda e: e.tensor_tensor(smu[:], p_[:, 0:208].rearrange("p (c b) -> p c b", b=16),
                                              bc(C.pv[:, PV_MU:PV_MU + 13], [128, 13, 16], [2]), ALU.mult),
             reads=[p_, C.pv], writes=[smu])

        def proj_shift(c, dst):
            q_ = nextpp()
            for k in range(8):
                P.mm(q_, q_[:, 0:W], w_r, w_r[:, k, c * 128:(c + 1) * 128], xTs, xTs[:, k, :], start=(k == 0), stop=(k == 7))
            P.op("act", lambda e: e.mul(bm[:], q_[:, 0:W], pv(PV_MU + c)), reads=[q_, C.pv], writes=[bm])
            q3 = q_[:, 0:W].rearrange("p (b t) -> p b t", t=4)
            d3 = dst[:].rearrange("p (b t) -> p b t", t=4)
            b3 = bm[:].rearrange("p (b t) -> p b t", t=4)
            P.op("dve", lambda e: e.scalar_tensor_tensor(d3[:, :, 1:4], q3[:, :, 1:4], pv(PV_OMM + c), b3[:, :, 0:3], ALU.mult, ALU.add),
                 reads=[q_, bm, C.pv], writes=[dst], partial=True)
            P.op("dve", lambda e: e.scalar_tensor_tensor(d3[:, :, 0:1], q3[:, :, 0:1], pv(PV_OMM + c), smu[:, c, :].unsqueeze(2), ALU.mult, ALU.add),
                 reads=[q_, smu, C.pv], writes=[dst], partial=True)
            P.op("act", lambda e: e.activation(shout[:, c, :].unsqueeze(2), q3[:, :, 3:4], AF.Copy), reads=[q_], writes=[shout], partial=True)

        proj_shift(12, wdad)
        P.op("act", lambda e: e.activation(lor[0:64, :], wdad[0:64, :], AF.Tanh), reads=[wdad], writes=[lor], partial=True)
        P.op("dve", lambda e: e.tensor_copy(lor[64:128, :], wdad[64:128, :]), reads=[wdad], writes=[lor], partial=True)
        hm = C.cm[:, CM_HM:CM_HM + 2]
        hm3 = bc(hm, [128, 2, W], [2])
        for hp in range(4):
            r, k, vz, sg, a, cs, eg, eng, kk, rn, km, bv = (tmp[n] for n in ("r", "k", "vz", "sg", "a", "cs", "eg", "eng", "kk", "rn", "km", "bv"))
            proj_shift(hp, r)
            proj_shift(4 + hp, k)
            proj_shift(8 + hp, vz)
            q_ = nextpp()
            P.mm(q_, q_[:, 0:W], wl2, wl2[0:64, hp * 128:(hp + 1) * 128], lor, lor[0:64, :])
            P.op("act", lambda e: e.activation(sg[:], q_[:, 0:W], AF.Sigmoid, bias=pv(PV_W0 + hp), scale=1.0), reads=[q_, C.pv], writes=[sg])
            q2 = nextpp()
            P.mm(q2, q2[:, 0:W], wl2, wl2[64:128, hp * 128:(hp + 1) * 128], lor, lor[64:128, :])
            P.op("act", lambda e: e.activation(a[:], q2[:, 0:W], AF.Sigmoid, bias=pv(PV_A0 + hp), scale=1.0), reads=[q2, C.pv], writes=[a])
            P.op("dve", lambda e: e.tensor_tensor_scan(cs[:], scanm[:], sg[:], 0.0, ALU.mult, ALU.add), reads=[scanm, sg], writes=[cs])
            P.op("act", lambda e: e.activation(eg[:], cs[:], AF.Exp, scale=-C0), reads=[cs], writes=[eg])
            P.op("act", lambda e: e.activation(eng[:], cs[:], AF.Exp, scale=C0), reads=[cs], writes=[eng])
            P.op("pool", lambda e: e.tensor_tensor(cs[:], cs[:], sg[:], ALU.subtract), reads=[cs, sg], writes=[cs])
            P.op("act", lambda e: e.activation(cs[:], cs[:], AF.Exp, scale=-C0), reads=[cs], writes=[cs])
            P.op("dve", lambda e: e.tensor_scalar(kk[:], k[:], pv(PV_KK + hp), None, ALU.mult), reads=[k, C.pv], writes=[kk])
            P.op("pool", lambda e: e.tensor_tensor(tb["sqb"][:], kk[:], kk[:], ALU.mult), reads=[kk], writes=[tb["sqb"]])
            q3_ = nextpp()
            P.mm(q3_, q3_[:, 0:W], bones_b, bones_b[:], tb["sqb"], tb["sqb"][:])
            P.op("dve", lambda e: e.tensor_scalar(rn[:], q3_[:, 0:W], 1e-24, None, ALU.max), reads=[q3_], writes=[rn])
            P.op("act", lambda e: e.activation(rn[:], rn[:], AF.Sqrt), reads=[rn], writes=[rn])
            P.op("dve", lambda e: e.reciprocal(rn[:], rn[:]), reads=[rn], writes=[rn])
            P.op("dve", lambda e: e.tensor_tensor(kk[:], kk[:], rn[:], ALU.mult), reads=[kk, rn], writes=[kk])
            P.op("dve", lambda e: e.tensor_scalar(km[:], a[:], -1.0, pv(PV_KA + hp), ALU.add, ALU.mult), reads=[a, C.pv], writes=[km])
            P.op("dve", lambda e: e.scalar_tensor_tensor(km[:], km[:], 1.0, k[:], ALU.add, ALU.mult), reads=[km, k], writes=[km])
            P.op("pool", lambda e: e.tensor_tensor(bv[:], kk[:], a[:], ALU.mult), reads=[kk, a], writes=[bv])
            P.op("pool", lambda e: e.tensor_tensor(tb["rt"][:], r[:], eg[:], ALU.mult), reads=[r, eg], writes=[tb["rt"]])
            P.op("dve", lambda e: e.tensor_tensor(tb["kt"][:], km[:], eng[:], ALU.mult), reads=[km, eng], writes=[tb["kt"]])
            P.op("pool", lambda e: e.tensor_tensor(tb["bt"][:], bv[:], eng[:], ALU.mult), reads=[bv, eng], writes=[tb["bt"]])
            P.op("dve", lambda e: e.scalar_tensor_tensor(tb["at"][:], kk[:], -1.0, cs[:], ALU.mult, ALU.mult), reads=[kk, cs], writes=[tb["at"]])
            P.op("dve", lambda e: e.scalar_tensor_tensor(tb["sqb"][:], r[:], pv(PV_RK + hp), km[:], ALU.mult, ALU.mult), reads=[r, km, C.pv], writes=[tb["sqb"]])
            q4 = nextpp()
            P.mm(q4, q4[:, 0:W], bones_b, bones_b[:], tb["sqb"], tb["sqb"][:])
            P.op("dve", lambda e: e.tensor_tensor(tmp["bonus"][:], q4[:, 0:W], vz[:], ALU.mult), reads=[q4, vz], writes=[tmp["bonus"]])
            q5 = nextpp()
            for kq in range(8):
                P.mm(q5, q5[:, 0:W], w_r, w_r[:, kq, (13 + hp) * 128:(14 + hp) * 128], xTs, xTs[:, kq, :], start=(kq == 0), stop=(kq == 7))
            P.op("act", lambda e: e.activation(tmp["gate"][:], q5[:, 0:W], AF.Silu), reads=[q5], writes=[tmp["gate"]])
            gcol = eg[:, 3:W:4]
            gf = tmp["gf"]
            P.op("pool", lambda e: e.tensor_copy(gf[:].rearrange("p (b t) -> p b t", t=4), bc(gcol, [128, 16, 4], [2])), reads=[eg], writes=[gf])
            P.op("dve", lambda e: e.tensor_tensor(tb["ktg"][:], tb["kt"][:], gf[:], ALU.mult), reads=[tb["kt"], gf], writes=[tb["ktg"]])
            P.op("pool", lambda e: e.tensor_tensor(tb["btg"][:], tb["bt"][:], gf[:], ALU.mult), reads=[tb["bt"], gf], writes=[tb["btg"]])

            def ex(x):
                return bc(x[:], [128, 2, W], [1])
            for i, (nm, src) in enumerate((("Lb", tb["bt"]), ("Lk", tb["kt"]), ("Ra", tb["at"]), ("Rr", tb["rt"]),
                                           ("KH", tb["ktg"]), ("BH", tb["btg"]), ("VB", vz))):
                P.op("dve" if i % 2 == 0 else "pool", lambda e: e.tensor_tensor(ex_[nm][:], ex(src), hm3, ALU.mult),
                     reads=[src, C.cm], writes=[ex_[nm]])

            def f2(nm):
                return ex_[nm][:].rearrange("p h s -> p (h s)")
            P.mm(pA, pA[:, 0:128], ex_["Lb"], f2("Lb"), ex_["Ra"], f2("Ra"))
            P.mm(pA, pA[:, 128:256], ex_["Lk"], f2("Lk"), ex_["Ra"], f2("Ra"))
            P.mm(pA, pA[:, 256:384], ex_["Ra"], f2("Ra"), ex_["Lb"], f2("Lb"))
            P.op("dve", lambda e: e.tensor_tensor(sq_["N"][:], pA[:, 0:128], cms[:, CS_STRICT:CS_STRICT + 128], ALU.mult), reads=[pA, cms], writes=[sq_["N"]])
            P.op("dve", lambda e: e.tensor_tensor(sq_["ak"][:], pA[:, 128:256], cms[:, CS_STRICT:CS_STRICT + 128], ALU.mult), reads=[pA, cms], writes=[sq_["ak"]])
            P.op("dve", lambda e: e.tensor_tensor(sq_["NT"][:], pA[:, 256:384], cms[:, CS_STRICT_T:CS_STRICT_T + 128], ALU.mult), reads=[pA, cms], writes=[sq_["NT"]])
            P.mm(pA, pA[:, 0:128], ex_["Lb"], f2("Lb"), ex_["Rr"], f2("Rr"))
            P.mm(pA, pA[:, 128:256], ex_["Lk"], f2("Lk"), ex_["Rr"], f2("Rr"))
            P.op("dve", lambda e: e.tensor_tensor(sq_["br"][:], pA[:, 0:128], cms[:, CS_INCL:CS_INCL + 128], ALU.mult), reads=[pA, cms], writes=[sq_["br"]])
            P.op("dve", lambda e: e.tensor_tensor(sq_["kr"][:], pA[:, 128:256], cms[:, CS_INCL:CS_INCL + 128], ALU.mult), reads=[pA, cms], writes=[sq_["kr"]])
            P.op("pool", lambda e: e.tensor_tensor(sq_["T0"][:], sq_["N"][:], C.cm[:, CM_EYE:CM_EYE + 128], ALU.add), reads=[sq_["N"], C.cm], writes=[sq_["T0"]])
            P.mm(pA, pA[:, 0:128], sq_["N"], sq_["N"][:], sq_["NT"], sq_["NT"][:])
            P.op("dve", lambda e: e.tensor_copy(sq_["P1T"][:], pA[:, 0:128]), reads=[pA], writes=[sq_["P1T"]])
            P.mm(pA, pA[:, 0:128], sq_["P1T"], sq_["P1T"][:], sq_["T0"], sq_["T0"][:])
            P.op("dve", lambda e: e.tensor_tensor(sq_["T"][:], pA[:, 0:128], sq_["T0"][:], ALU.add), reads=[pA, sq_["T0"]], writes=[sq_["T"]])
            for i, (src, dst) in enumerate((("KH", "KHt"), ("BH", "BHt"), ("Ra", "Rat"), ("VB", "VBt"))):
                P.tr(ptr, ptr[:, i * 128:(i + 1) * 128], ex_[src], f2(src), C.ident_b, C.ident_b[:])
            for i, dst in enumerate(("KHt", "BHt", "Rat", "VBt")):
                evac(P, 0, sq_[dst], sq_[dst][:], ptr, ptr[:, i * 128:(i + 1) * 128])
            P.mm(pA, pA[:, 0:128], sq_["ak"], sq_["ak"][:], sq_["VBt"], sq_["VBt"][:])
            P.op("act", lambda e: e.activation(sq_["Y"][:], pA[:, 0:128], AF.Copy), reads=[pA], writes=[sq_["Y"]])
            P.mm(pA, pA[:, 128:256], sq_["T"], sq_["T"][:], sq_["Y"], sq_["Y"][:])
            P.op("act", lambda e: e.activation(sq_["W2"][:], pA[:, 128:256], AF.Copy), reads=[pA], writes=[sq_["W2"]])
            P.mm(pA, pA[:, 256:384], sq_["Rat"], sq_["Rat"][:], sq_["T"], sq_["T"][:])
            P.op("dve", lambda e: e.tensor_copy(sq_["W1"][:], pA[:, 256:384]), reads=[pA], writes=[sq_["W1"]])
            P.dma("sp", Sv[:], din["wkv_s"][:, 2 * hp:2 * hp + 2, :, :].rearrange("b h v k -> (h v) b k"), writes=[Sv], partial=False)
            P.op("dve", lambda e: e.tensor_tensor(Svx[:], bc(Sv[:], [128, 16, 2, 64], [2]), bc(hm, [128, 16, 2, 64], [1, 3]), ALU.mult),
                 reads=[Sv, C.cm], writes=[Svx])
            for b in range(16):
                pb_ = pbig[b // 4]
                P.mm(pb_, pb_[:, (b % 4) * 128:(b % 4 + 1) * 128], Svx, Svx[:, b].rearrange("p h k -> p (h k)"), C.ident_f, C.ident_f[:])
            for i in range(4):
                P.op("dve", lambda e: e.tensor_copy(Sf[:, 4 * i:4 * i + 4, :].rearrange("p b c -> p (b c)"), pbig[i][:]), reads=[pbig[i]], writes=[Sf], partial=True)
                P.op("act", lambda e: e.activation(Sb[:, 4 * i:4 * i + 4, :].rearrange("p b c -> p (b c)"), pbig[i][:], AF.Copy), reads=[pbig[i]], writes=[Sb], partial=True)
            P.op("dve", lambda e: e.tensor_tensor(W1b[:], bc(sq_["W1"][:], [128, 16, 128], [1]), colm[:], ALU.mult), reads=[sq_["W1"], colm], writes=[W1b])
            P.op("pool", lambda e: e.tensor_tensor(Rrb[:], bc(f2("Rr"), [128, 16, 128], [1]), colm[:], ALU.mult), reads=[ex_["Rr"], colm], writes=[Rrb])
            rowm = bc(cms[:, CS_ROW:CS_ROW + 16], [128, 16, 128], [2])
            P.op("dve", lambda e: e.tensor_tensor(KHtb[:], bc(sq_["KHt"][:], [128, 16, 128], [1]), rowm, ALU.mult), reads=[sq_["KHt"], cms], writes=[KHtb])
            P.op("pool", lambda e: e.tensor_tensor(BHtb[:], bc(sq_["BHt"][:], [128, 16, 128], [1]), rowm, ALU.mult), reads=[sq_["BHt"], cms], writes=[BHtb])
            for b in range(16):
                P.mm(pB, pB[:, 0:128], W1b, W1b[:, b, :], Sb, Sb[:, b, :], start=(b == 0), stop=(b == 15))
            P.op("dve", lambda e: e.tensor_tensor(sq_["Ub"][:], pB[:, 0:128], sq_["W2"][:], ALU.add), reads=[pB, sq_["W2"]], writes=[sq_["Ub"]])
            for b in range(16):
                P.mm(pA, pA[:, 0:128], Rrb, Rrb[:, b, :], Sb, Sb[:, b, :], start=(b == 0), stop=False)
            P.mm(pA, pA[:, 0:128], sq_["br"], sq_["br"][:], sq_["Ub"], sq_["Ub"][:], start=False, stop=False)
            P.mm(pA, pA[:, 0:128], sq_["kr"], sq_["kr"][:], sq_["VBt"], sq_["VBt"][:], start=False, stop=True)
            P.op("act", lambda e: e.activation(sq_["Ob"][:], pA[:, 0:128], AF.Copy), reads=[pA], writes=[sq_["Ob"]])
            for b in range(16):
                pb_ = pbig[b // 4]
                sl = pb_[:, (b % 4) * 128:(b % 4 + 1) * 128]
                P.mm(pb_, sl, KHtb, KHtb[:, b, :], sq_["VBt"], sq_["VBt"][:], start=True, stop=False)
                P.mm(pb_, sl, BHtb, BHtb[:, b, :], sq_["Ub"], sq_["Ub"][:], start=False, stop=True)
            for i in range(4):
                sfv = Sf[:, 4 * i:4 * i + 4, :]
                P.op("dve", lambda e: e.tensor_tensor(sfv, sfv, bc(gcol[:, 4 * i:4 * i + 4], [128, 4, 128], [2]), ALU.mult), reads=[Sf, eg], writes=[Sf])
                P.op("dve", lambda e: e.tensor_tensor(sfv, sfv, pbig[i][:].rearrange("p (b c) -> p b c", b=4), ALU.add), reads=[Sf, pbig[i]], writes=[Sf])
            for b in range(16):
                pb_ = pbig[b // 4]
                P.mm(pb_, pb_[:, (b % 4) * 128:(b % 4 + 1) * 128], Sf, Sf[:, b, :], C.ident_f, C.ident_f[:])
            for i in range(4):
                evac(P, i, So, So[:, 4 * i:4 * i + 4, :].rearrange("p b c -> p (b c)"), pbig[i], pbig[i][:])
            for hh in range(2):
                P.dma("sp", dout["wkv_so"][:, 2 * hp + hh, :, :].rearrange("b v k -> v b k"),
                      So[hh * 64:(hh + 1) * 64, :, hh * 64:(hh + 1) * 64], reads=[So])
            py = nextpp()
            P.mm(py, py[:, 0:W], sq_["Ob"], sq_["Ob"][:], C.selb, C.selb[:])
            yr, cen, sq, rs = tmp["yr"], tmp["cen"], tmp["sq"], tmp["rs"]
            P.op("act", lambda e: e.activation(yr[:], py[:, 0:W], AF.Copy), reads=[py], writes=[yr])
            pm = nextpp()
            P.mm(pm, pm[:, 0:W], C.bones_f, C.bones_f[:], yr, yr[:])
            P.op("dve", lambda e: e.scalar_tensor_tensor(cen[:], pm[:, 0:W], -1.0 / 64.0, yr[:], ALU.mult, ALU.add), reads=[pm, yr], writes=[cen])
            P.op("pool", lambda e: e.tensor_tensor(sq[:], cen[:], cen[:], ALU.mult), reads=[cen], writes=[sq])
            pv_ = nextpp()
            P.mm(pv_, pv_[:, 0:W], C.bones_f, C.bones_f[:], sq, sq[:])
            P.op("dve", lambda e: e.tensor_scalar(rs[:], pv_[:, 0:W], 1.0 / 64.0, GN_EPS, ALU.mult, ALU.add), reads=[pv_], writes=[rs])
            P.op("act", lambda e: e.activation(rs[:], rs[:], AF.Sqrt), reads=[rs], writes=[rs])
            P.op("dve", lambda e: e.reciprocal(rs[:], rs[:]), reads=[rs], writes=[rs])
            P.op("dve", lambda e: e.tensor_tensor(cen[:], cen[:], rs[:], ALU.mult), reads=[cen, rs], writes=[cen])
            P.op("dve", lambda e: e.tensor_scalar(cen[:], cen[:], pv(PV_LG + hp), pv(PV_LB + hp), ALU.mult, ALU.add), reads=[cen, C.pv], writes=[cen])
            P.op("pool", lambda e: e.tensor_tensor(cen[:], cen[:], tmp["bonus"][:], ALU.add), reads=[cen, tmp["bonus"]], writes=[cen])
            P.op("pool", lambda e: e.tensor_tensor(yrS[:, hp, :], cen[:], tmp["gate"][:], ALU.mult), reads=[cen, tmp["gate"]], writes=[yrS], partial=True)
        for c0 in range(0, 13, 4):
            n = min(4, 13 - c0)
            q_ = nextpp()
            for c in range(n):
                P.mm(q_, q_[0:16, c * 128:(c + 1) * 128], shout, shout[:, c0 + c, :], C.ident_f, C.ident_f[:])
            P.op("dve", lambda e: e.tensor_copy(sho2[0:16, c0 * 128:(c0 + n) * 128], q_[0:16, 0:n * 128]), reads=[q_], writes=[sho2], partial=True)
        P.dma("sp", dout["shift_so"][:, :], sho2[:], reads=[sho2])


def emit_attn_sample(P, C, din, dout, xTs, cms, yaS):
    with scope(P):
        wq = P.sbuf("wqS", [128, 8, 5120], BF16)
        qTs = P.sbuf("qTs", [128, 12, 64], BF16)
        kvn = P.sbuf("kvn", [64, 3, 2, 512], F32)
        kvnb = P.sbuf("kvnb", [64, 3, 2, 512], BF16)
        szs = P.sbuf("szs", [128, 4, 64], F32)
        oaT = P.sbuf("oaT", [128, 4, 64], F32)
        kvt = [P.sbuf("kvtS%d" % i, [128, 4, 2, 512], BF16) for i in range(2)]
        kvnew = [P.sbuf("kvnw%d" % i, [4, 2, 512], BF16) for i in range(2)]
        KT = [P.sbuf("KTs%d" % i, [128, 4, 512], BF16) for i in range(2)]
        KTn = [P.sbuf("KTn%d" % i, [128, 4, 4], BF16) for i in range(2)]
        ss = [P.sbuf("ssS%d" % i, [128, 516], F32) for i in range(2)]
        pb = [P.sbuf("pbS%d" % i, [128, 516], BF16) for i in range(2)]
        PT = [P.sbuf("PTs%d" % i, [128, 5, 128], BF16) for i in range(2)]
        og = [P.sbuf("ogS%d" % i, [128, 3, 128], F32) for i in range(2)]
        stt = [P.sbuf("sttS%d" % i, [128, 24], F32) for i in range(2)]
        ob16 = [P.sbuf("ob16S%d" % i, [128, 128], BF16) for i in range(2)]
        pp = [carve(C, i, 0, 512, "psSA_p%d" % i) for i in range(2)]
        pkt = [carve(C, 2 + i, 0, 512, "psSA_kt%d" % i, BF16) for i in range(2)]
        pS = carve(C, 4, 0, 512, "psSA_S")
        pSn = carve(C, 5, 0, 16, "psSA_Sn")
        pKn = carve(C, 5, 16, 32, "psSA_Kn", BF16)
        pO = carve(C, 5, 128, 256, "psSA_O")
        pT = carve(C, 6, 0, 384, "psSA_T", BF16)
        pOT = carve(C, 7, 0, 128, "psSA_OT")
        P.dma("pool", wq[:], din["w_qkvz"].rearrange("(k p) c -> p k c", p=128), writes=[wq])
        P.op("dve", lambda e: e.memset(pS[:], 0.0), writes=[pS])
        P.op("dve", lambda e: e.memset(pSn[:], 0.0), writes=[pSn])
        P.op("dve", lambda e: e.memset(pO[:], 0.0), writes=[pO])
        ec = [0]
        for c in range(12):
            p_ = pp[c % 2]
            for k in range(8):
                P.mm(p_, p_[:, 0:64], wq, wq[:, k, c * 128:(c + 1) * 128], xTs, xTs[:, k, :], start=(k == 0), stop=(k == 7))
            evac(P, c, qTs, qTs[:, c, :], p_, p_[:, 0:64])
        for hh in range(4):
            p_ = pp[hh % 2]
            for k in range(8):
                P.mm(p_, p_[:, 0:64], wq, wq[:, k, 4608 + hh * 128:4608 + (hh + 1) * 128], xTs, xTs[:, k, :], start=(k == 0), stop=(k == 7))
            P.op("act", lambda e: e.activation(szs[:, hh, :], p_[:, 0:64], AF.Silu), reads=[p_], writes=[szs], partial=True)
        for g in range(3):
            for kv in range(2):
                p_ = pp[(g * 2 + kv) % 2]
                c0 = 1536 * (1 + kv) + g * 512
                for k in range(8):
                    P.mm(p_, p_[0:64, :], xTs, xTs[:, k, :], wq, wq[:, k, c0:c0 + 512], start=(k == 0), stop=(k == 7))
                evac(P, g * 2 + kv, kvn, kvn[:, g, kv, :], p_, p_[0:64, :])
        P.op("pool", lambda e: e.tensor_copy(kvnb[:], kvn[:]), reads=[kvn], writes=[kvnb])
        for g in range(3):
            P.dma("sp", dout["kvs%d" % (g + 1)].rearrange("b t kv h e -> (b t) kv (h e)"), kvn[:, g], reads=[kvn])
        it = 0
        for b in range(16):
            ogb, sb_ = og[b % 2], stt[b % 2]
            for g in range(3):
                i2 = it % 2
                it += 1
                ntile = 1 if g == 0 else 4
                nk = ntile * 128
                kt_, kn_, KT_, KTn_, ss_, pb_, PT_ = kvt[i2], kvnew[i2], KT[i2], KTn[i2], ss[i2], pb[i2], PT[i2]
                cache = din["cache%d" % (g + 1)]
                d = GROUPS[g][1]
                if g == 0:
                    P.dma("pool", kt_[:, 0].rearrange("p kv c -> p (kv c)"), cache[b].rearrange("r kv h e -> r (kv h e)"), writes=[kt_], partial=False)
                else:
                    for cl in range(4):
                        src = cache[b, cl:GROUPS[g][0]:d].rearrange("r kv h e -> r (kv h e)")
                        P.dma("pool", kt_[:, cl].rearrange("p kv c -> p (kv c)"), src, writes=[kt_], partial=(cl > 0))
                P.dma("sp", kn_[:], kvnb[4 * b:4 * b + 4, g], reads=[kvnb], writes=[kn_], partial=False)
                for h in range(4):
                    pk = pkt[h // 2]
                    for cl in range(ntile):
                        P.tr(pk, pk[:, (h % 2) * 512 + cl * 128:(h % 2) * 512 + (cl + 1) * 128], kt_, kt_[:, cl, 0, h * 128:(h + 1) * 128],
                             C.ident_b, C.ident_b[:])
                    P.tr(pKn, pKn[:, h * 4:(h + 1) * 4], kn_, kn_[0:4, 0, h * 128:(h + 1) * 128], C.ident_b, C.ident_b[0:4, 0:4])
                for j in range(2):
                    evac(P, ec[0], KT_, KT_[:, 2 * j:2 * j + 2, 0:nk], pkt[j], pkt[j][:].rearrange("p (h k) -> p h k", h=2)[:, :, 0:nk])
                    ec[0] += 1
                evac(P, ec[0], KTn_, KTn_[:].rearrange("p h k -> p (h k)"), pKn, pKn[:, 0:16])
                ec[0] += 1
                for h in range(4):
                    qv = qTs[:, g * 4 + h, 4 * b:4 * b + 4]
                    P.mm(pS, pS[32 * h:32 * h + 4, 0:nk], qTs, qv, KT_, KT_[:, h, 0:nk], tile_position=(0, 32 * h))
                    P.mm(pSn, pSn[32 * h:32 * h + 4, 0:4], qTs, qv, KTn_, KTn_[:, h, :], tile_position=(0, 32 * h))
                mk = cms[:, CS_G1:CS_G1 + 132] if g == 0 else cms[:, CS_G23:CS_G23 + 516]
                P.op("dve", lambda e: e.scalar_tensor_tensor(ss_[:, 0:nk], pS[:, 0:nk], SCALE, mk[:, 0:nk], ALU.mult, ALU.add),
                     reads=[pS, cms], writes=[ss_], partial=True)
                P.op("dve", lambda e: e.scalar_tensor_tensor(ss_[:, nk:nk + 4], pSn[:, 0:4], SCALE, mk[:, nk:nk + 4], ALU.mult, ALU.add),
                     reads=[pSn, cms], writes=[ss_], partial=True)
                c8 = g * 8
                P.op("dve", lambda e: e.tensor_reduce(sb_[:, c8:c8 + 1], ss_[:, 0:nk + 4], AX.X, ALU.max, negate=True), reads=[ss_], writes=[sb_], partial=True)
                P.op("act", lambda e: e.activation(pb_[:, 0:nk + 4], ss_[:, 0:nk + 4], AF.Exp, bias=sb_[:, c8:c8 + 1], scale=1.0,
                                                   accum_out=sb_[:, c8 + 1:c8 + 2]), reads=[ss_, sb_], writes=[pb_, sb_], partial=True)
                for cl in range(ntile):
                    P.tr(pT, pT[:, cl * 128:(cl + 1) * 128], pb_, pb_[:, cl * 128:(cl + 1) * 128], C.ident_b, C.ident_b[:])
                P.tr(pT, pT[0:4, 640:768], pb_, pb_[:, nk:nk + 4], C.ident_b, C.ident_b[:])
                evac(P, ec[0], PT_, PT_[:, 0:ntile, :].rearrange("p c q -> p (c q)"), pT, pT[:, 0:nk])
                ec[0] += 1
                evac(P, ec[0], PT_, PT_[0:4, 4, :], pT, pT[0:4, 640:768])
                ec[0] += 1
                for h in range(4):
                    for cl in range(ntile):
                        P.mm(pO, pO[32 * h:32 * h + 4, :], PT_, PT_[:, cl, 32 * h:32 * h + 4], kt_, kt_[:, cl, 1, h * 128:(h + 1) * 128],
                             start=(cl == 0), stop=False, tile_position=(0, 32 * h))
                    P.mm(pO, pO[32 * h:32 * h + 4, :], PT_, PT_[0:4, 4, 32 * h:32 * h + 4], kn_, kn_[0:4, 1, h * 128:(h + 1) * 128],
                         start=False, stop=True, tile_position=(0, 32 * h))
                P.op("dve", lambda e: e.reciprocal(sb_[:, c8 + 2:c8 + 3], sb_[:, c8 + 1:c8 + 2]), reads=[sb_], writes=[sb_], partial=True)
                P.op("dve", lambda e: e.tensor_scalar(ogb[:, g, :], pO[:], sb_[:, c8 + 2:c8 + 3], None, ALU.mult), reads=[pO, sb_], writes=[ogb], partial=True)
                P.op("act", lambda e: e.activation(sb_[:, c8 + 3:c8 + 4], sb_[:, c8 + 1:c8 + 2], AF.Ln), reads=[sb_], writes=[sb_], partial=True)
                P.op("dve", lambda e: e.tensor_tensor(sb_[:, c8 + 4:c8 + 5], sb_[:, c8 + 3:c8 + 4], sb_[:, c8:c8 + 1], ALU.subtract), reads=[sb_], writes=[sb_], partial=True)
            def col(i):
                return sb_[:, i:i + 1]
            P.op("dve", lambda e: e.tensor_tensor(col(5), col(4), col(12), ALU.max), reads=[sb_], writes=[sb_], partial=True)
            P.op("dve", lambda e: e.tensor_tensor(col(5), col(5), col(20), ALU.max), reads=[sb_], writes=[sb_], partial=True)
            for g in range(3):
                P.op("dve", lambda e: e.tensor_tensor(col(8 * g + 6), col(8 * g + 4), col(5), ALU.subtract), reads=[sb_], writes=[sb_], partial=True)
                P.op("act", lambda e: e.activation(col(8 * g + 6), col(8 * g + 6), AF.Exp), reads=[sb_], writes=[sb_], partial=True)
            P.op("dve", lambda e: e.tensor_tensor(col(7), col(6), col(14), ALU.add), reads=[sb_], writes=[sb_], partial=True)
            P.op("dve", lambda e: e.tensor_tensor(col(7), col(7), col(22), ALU.add), reads=[sb_], writes=[sb_], partial=True)
            P.op("dve", lambda e: e.reciprocal(col(7), col(7)), reads=[sb_], writes=[sb_], partial=True)
            for g in range(3):
                P.op("dve", lambda e: e.tensor_tensor(col(8 * g + 6), col(8 * g + 6), col(7), ALU.mult), reads=[sb_], writes=[sb_], partial=True)
            P.op("dve", lambda e: e.tensor_scalar(ogb[:, 0, :], ogb[:, 0, :], col(6), None, ALU.mult), reads=[ogb, sb_], writes=[ogb])
            P.op("dve", lambda e: e.scalar_tensor_tensor(ogb[:, 0, :], ogb[:, 1, :], col(14), ogb[:, 0, :], ALU.mult, ALU.add), reads=[ogb, sb_], writes=[ogb])
            P.op("dve", lambda e: e.scalar_tensor_tensor(ogb[:, 0, :], ogb[:, 2, :], col(22), ogb[:, 0, :], ALU.mult, ALU.add), reads=[ogb, sb_], writes=[ogb])
            P.mm(pOT, pOT[:], ogb, ogb[:, 0, :], C.ident_f, C.ident_f[:])
            evac(P, b, oaT, oaT[:, :, 4 * b:4 * b + 4], pOT, pOT[:].rearrange("p (h x) -> p h x", x=32)[:, :, 0:4])
        P.op("dve", lambda e: e.tensor_tensor(yaS[:], oaT[:], szs[:], ALU.mult), reads=[oaT, szs], writes=[yaS])
```

```python
import contextlib
import numpy as np
import concourse.bass as bass
import concourse.mybir as mybir
from concourse.bass_utils import run_bass_kernel_spmd

F32 = mybir.dt.float32
BF16 = mybir.dt.bfloat16
I32 = mybir.dt.int32
ALU = mybir.AluOpType
AF = mybir.ActivationFunctionType
AX = mybir.AxisListType

NCORES = 8
RING = 20
EAGER_SIGNAL = False


class Buf:
    __slots__ = ("name", "t", "writers", "readers", "prev_readers", "bank")

    def __init__(self, name, t, bank=None):
        self.name = name
        self.t = t
        self.bank = bank
        self.writers = {}
        self.readers = {}
        self.prev_readers = {}

    def __getitem__(self, idx):
        return self.t[idx]


def _merge(dst, src):
    for k, (s, v) in src.items():
        if k not in dst or dst[k][1] < v:
            dst[k] = (s, v)


class Prog:
    def __init__(self, nc, stack):
        self.nc = nc
        self.stack = stack
        self.eng = {"pe": nc.tensor, "act": nc.scalar, "dve": nc.vector, "pool": nc.gpsimd, "sp": nc.sync}
        self.esem = {}
        self.seq = {}
        self.known = {}
        self.sig = {}
        self.sig_idx = {}
        self.sigcount = {}
        self.last_inst = {}
        self.insts = {}
        for e in self.eng:
            self.esem[e] = stack.enter_context(nc.semaphore("es_" + e))
            self.seq[e] = 0
            self.known[e] = {}
            self.sig[e] = []
            self.sig_idx[e] = []
            self.sigcount[e] = 0
            self.last_inst[e] = None
            self.insts[e] = []
        self.ring = {}
        self.ring_val = {}
        self.dma_i = {}
        for q in ("sp", "pool", "act"):
            self.ring[q] = [stack.enter_context(nc.semaphore("dq_%s_%d" % (q, i))) for i in range(RING)]
            self.ring_val[q] = [0] * RING
            self.dma_i[q] = 0
        self.nbuf = 0
        self.bank_rd = {}

    def sbuf(self, name, shape, dtype):
        t = self.stack.enter_context(self.nc.sbuf_tensor("sb_" + name, list(shape), dtype))
        return Buf(name, t)

    def psum(self, name, shape, dtype):
        t = self.stack.enter_context(self.nc.psum_tensor(name, list(shape), dtype))
        return Buf(name, t)

    def dram(self, name, shape, dtype, kind="Internal"):
        t = self.nc.dram_tensor(name, list(shape), dtype, kind=kind)
        return Buf(name, t.ap())

    def _deps(self, reads, writes, partial, eng=None):
        deps = {}
        for b in list(reads) + list(writes):
            if b.bank is not None:
                for e2, (k2, ev2) in self.bank_rd.setdefault(b.bank, {}).items():
                    if e2 != eng:
                        _merge(deps, {k2: ev2})
        for b in reads:
            _merge(deps, b.writers)
        for b in writes:
            _merge(deps, b.prev_readers)
            _merge(deps, b.readers)
            if not partial:
                _merge(deps, b.writers)
        return deps

    def _resolve(self, e, idx):
        sig = self.sig[e]
        import bisect
        pos = bisect.bisect_left(self.sig_idx[e], idx)
        if pos < len(sig):
            return self.sig_idx[e][pos], sig[pos]
        self.sigcount[e] += 1
        self.insts[e][idx - 1].then_inc(self.esem[e], 1)
        self.sig_idx[e].append(idx)
        sig.append(self.sigcount[e])
        return idx, self.sigcount[e]

    def _wait(self, eng, deps):
        E = self.eng[eng]
        kn = self.known[eng]
        for k, (s, v) in deps.items():
            if eng == "pe" and k == "e_pe":
                continue
            if kn.get(k, 0) >= v:
                continue
            if s is None:
                e = k[2:]
                idx2, cnt = self._resolve(e, v)
                E.wait_ge(self.esem[e], cnt)
                kn[k] = idx2
            else:
                E.wait_ge(s, v)
                kn[k] = v

    def _record(self, ev_key, ev, reads, writes, partial):
        for b in reads:
            _merge(b.readers, {ev_key: ev})
        for b in writes:
            if b.readers or not partial:
                b.prev_readers = b.readers
                b.readers = {}
                b.writers = {ev_key: ev}
            else:
                _merge(b.writers, {ev_key: ev})

    opbudget = None
    opcount = 0

    def op(self, eng, fn, reads=(), writes=(), partial=False):
        Prog.opcount += 1
        if Prog.opbudget is not None and Prog.opcount > Prog.opbudget:
            return None
        deps = self._deps(reads, writes, partial, eng)
        self._wait(eng, deps)
        inst = fn(self.eng[eng])
        self.seq[eng] += 1
        self.last_inst[eng] = inst
        self.insts[eng].append(inst)
        if EAGER_SIGNAL:
            self.sigcount[eng] += 1
            inst.then_inc(self.esem[eng], 1)
            self.sig_idx[eng].append(self.seq[eng])
            self.sig[eng].append(self.sigcount[eng])
        self._record("e_" + eng, (None, self.seq[eng]), reads, writes, partial)
        for b in list(reads) + list(writes):
            if b.bank is not None:
                self.bank_rd.setdefault(b.bank, {})[eng] = ("e_" + eng, (None, self.seq[eng]))
        return inst

    def dma(self, q, out, in_, reads=(), writes=(), partial=True, **kw):
        Prog.opcount += 1
        if Prog.opbudget is not None and Prog.opcount > Prog.opbudget and not kw.pop("always", False):
            return None
        kw.pop("always", None)
        i = self.dma_i[q]
        self.dma_i[q] += 1
        slot = i % RING
        sem = self.ring[q][slot]
        prev = self.ring_val[q][slot]
        key = "d_%s_%d" % (q, slot)
        deps = self._deps(reads, writes, partial)
        if prev > 0:
            _merge(deps, {key: (sem, prev)})
        self._wait(q, deps)
        inst = self.eng[q].dma_start(out=out, in_=in_, **kw)
        inst.then_inc(sem, 16)
        self.ring_val[q][slot] = prev + 16
        self._record(key, (sem, prev + 16), reads, writes, partial)
        return inst

    def finish(self):
        deps = {}
        for q in self.ring:
            for slot in range(RING):
                if self.ring_val[q][slot] > 0:
                    deps["d_%s_%d" % (q, slot)] = (self.ring[q][slot], self.ring_val[q][slot])
        for e in self.eng:
            if self.seq[e] > 0:
                deps["e_" + e] = (None, self.seq[e])
        self._wait("sp", deps)

    def mm(self, out_b, out_ap, lhsT_b, lhsT_ap, rhs_b, rhs_ap, start=True, stop=True, **kw):
        rd = [b for b in (lhsT_b, rhs_b) if b is not None]
        return self.op("pe", lambda e: e.matmul(out_ap, lhsT_ap, rhs_ap, start=start, stop=stop, **kw),
                       reads=rd, writes=[out_b], partial=True)

    def tr(self, out_b, out_ap, in_b, in_ap, ident_b, ident_ap):
        return self.op("pe", lambda e: e.transpose(out_ap, in_ap, ident_ap),
                       reads=[in_b, ident_b], writes=[out_b], partial=True)


D = 1024
SEQ = 8192
NB = 2
R_HEADS = 8
SHIFT_COLS = 1664
RW_COLS = 2176
Q0, K0, V0, ZA0, GR0, GA0 = 2176, 3712, 5248, 6784, 7296, 8320
GROUPS = ((128, 1), (512, 4), (2048, 16))
ALPHA = 2.0 ** 0.25
LN_EPS = 1e-5
GN_EPS = 64e-5
C0 = float(np.exp(-0.5))
NEGM = -30000.0
OWN = 2048
EXT = 8192
SCALE = 1.0 / float(np.sqrt(128.0))

PV_MU = 0
PV_W0 = 13
PV_A0 = 17
PV_KK = 21
PV_KA = 25
PV_RK = 29
PV_LG = 33
PV_LB = 37
PV_BG = 41
PV_OMM = 57
NPV = 70


class Ctx:
    pass


def emit_consts(P, C, din):
    C.ident_f = P.sbuf("ident_f", [128, 128], F32)
    C.ident_b = P.sbuf("ident_b", [128, 128], BF16)
    C.cm = P.sbuf("cm", [128, din["cmask"].shape[1]], F32)
    C.pv = P.sbuf("pv", [128, NPV], F32)
    C.pbias = P.sbuf("pbias", [128, 1], F32)
    P.dma("sp", C.ident_f[:], din["ident"][:, :], writes=[C.ident_f])
    P.dma("pool", C.ident_b[:], din["ident"][:, :], writes=[C.ident_b])
    P.dma("sp", C.cm[:], din["cmask"][:, :], writes=[C.cm])
    P.dma("sp", C.pv[:, 0:PV_OMM], din["pvec"][:, :], writes=[C.pv])
    P.dma("sp", C.pbias[:], din["pbias"][:, :], writes=[C.pbias])
    C.selb = P.sbuf("selb", [128, 64], BF16)
    P.op("pool", lambda e: e.tensor_copy(C.selb[:], C.cm[:, CM_SEL:CM_SEL + 64]), reads=[C.cm], writes=[C.selb])
    C.bones_f = Buf("bones_f", C.cm.t[:, CM_BONES:CM_BONES + 128])
    P.op("dve", lambda e: e.tensor_scalar(C.pv[:, PV_OMM:PV_OMM + 13], C.pv[:, PV_MU:PV_MU + 13], -1.0, 1.0,
                                          ALU.mult, ALU.add), reads=[C.pv], writes=[C.pv])
    P.op("dve", lambda e: e.tensor_copy(C.cm[:, 256:512], C.cm[:, 0:256]), reads=[C.cm], writes=[C.cm])
    P.op("dve", lambda e: e.tensor_scalar(C.cm[:, 256:384], C.cm[:, 256:384], C.pbias[:, 0:1], None, ALU.add),
         reads=[C.cm, C.pbias], writes=[C.cm])


def emit_xT(P, C, x_rows_ap, ntiles, xT, col0, xld, ps_x, cnt):
    for t in range(ntiles):
        xb = xld[cnt[0] % len(xld)]
        px = ps_x[cnt[0] % len(ps_x)]
        P.dma("pool", xb[:], x_rows_ap[t * 128:(t + 1) * 128, :], writes=[xb])
        for k in range(8):
            P.tr(px, px[:, k * 128:(k + 1) * 128], xb, xb[:, k * 128:(k + 1) * 128], C.ident_b, C.ident_b[:])
        dst = xT[:, :, col0 + t * 128: col0 + (t + 1) * 128]
        src = px[:].rearrange("p (k t) -> p k t", k=8)
        if cnt[0] % 2 == 0:
            P.op("dve", lambda e: e.tensor_copy(dst, src), reads=[px], writes=[xT], partial=True)
        else:
            P.op("act", lambda e: e.activation(dst, src, AF.Copy), reads=[px], writes=[xT], partial=True)
        cnt[0] += 1


@contextlib.contextmanager
def scope(P):
    old = P.stack
    with contextlib.ExitStack() as st:
        P.stack = st
        try:
            yield
        finally:
            barrier(P)
            P.stack = old


def barrier(P):
    deps = {}
    for q in P.ring:
        for slot in range(RING):
            if P.ring_val[q][slot] > 0:
                deps["d_%s_%d" % (q, slot)] = (P.ring[q][slot], P.ring_val[q][slot])
    for e in P.eng:
        if P.seq[e] > 0:
            deps["e_" + e] = (None, P.seq[e])
    for e in P.eng:
        P._wait(e, dict(deps))


def carve(C, bank, c0, c1, name, dtype=F32):
    ap = C.bank[bank].t[:, c0:c1]
    if dtype != F32:
        ap = ap.bitcast(dtype)
    return Buf(name, ap, bank=bank)


def evac(P, i, dst_b, dst_ap, src_b, src_ap):
    if i % 2 == 0:
        P.op("dve", lambda e: e.tensor_copy(dst_ap, src_ap), reads=[src_b], writes=[dst_b], partial=True)
    else:
        P.op("act", lambda e: e.activation(dst_ap, src_ap, AF.Copy), reads=[src_b], writes=[dst_b], partial=True)


def emit_attn_prompt(P, C, din, xTA, yaT, dbg=None, dout=None):
    with scope(P):
        wh = P.sbuf("wh", [128, 8, 1280], BF16)
        qT = P.sbuf("qT", [128, 2048], BF16)
        kT = P.sbuf("kT", [128, 4096], BF16)
        vt = P.sbuf("vt", [128, 32, 128], BF16)
        oTg = [P.sbuf("oTg%d" % g, [128, 2048], BF16) for g in range(3)]
        lsB = [P.sbuf("lsB%d" % g, [128, 2048], F32) for g in range(3)]
        sz = P.sbuf("sz", [128, 2048], BF16)
        cw = [P.sbuf("cw%d" % i, [128, 512], F32) for i in range(5)]
        s_sb = [P.sbuf("s_sb%d" % i, [128, 256], F32) for i in range(3)]
        p_sb = [P.sbuf("p_sb%d" % i, [128, 256], BF16) for i in range(3)]
        pT = [P.sbuf("pT%d" % i, [128, 256], BF16) for i in range(3)]
        o_sb = [P.sbuf("o_sb%d" % i, [128, 128], BF16) for i in range(3)]
        st = [P.sbuf("st%d" % i, [128, 8], F32) for i in range(3)]
        kvts = [P.sbuf("kvt%d" % i, [128, 2, 384], F32) for i in range(2)]
        lcol = [P.sbuf("lcol%d" % i, [128, 128], F32) for i in range(3)]
        ps_p = [carve(C, i, 0, 512, "psA_p%d" % i) for i in range(2)]
        sbk = (2, 3, 6)
        obk = (4, 5, 7)
        ps_s = [carve(C, sbk[i], 0, 256, "psA_s%d" % i) for i in range(3)]
        ps_t = [carve(C, sbk[i], 256, 384, "psA_t%d" % i, BF16) for i in range(3)]
        ps_o = [carve(C, sbk[i], 384, 512, "psA_o%d" % i) for i in range(3)]
        ps_oT = [carve(C, obk[i], 0, 64, "psA_oT%d" % i, BF16) for i in range(3)]
        ps_l = [carve(C, obk[i], 128, 256, "psA_l%d" % i) for i in range(3)]
        ec = [0]
        pc = [0]
        blk = [0]

        def proj_fm(col, tok0, ntok, dst_b, dst_ap, src_view=None):
            pp = ps_p[pc[0] % 2]
            pc[0] += 1
            for k in range(8):
                P.mm(pp, pp[:, 0:ntok], wh, wh[:, k, col * 128:(col + 1) * 128], xTA, xTA[:, k, tok0:tok0 + ntok],
                     start=(k == 0), stop=(k == 7))
            src = pp[:, 0:ntok] if src_view is None else src_view(pp)
            evac(P, ec[0], dst_b, dst_ap, pp, src)
            ec[0] += 1

        for h in range(4):
            P.dma("pool", wh[:], din["w_att"][h].rearrange("(k p) c -> p k c", p=128), writes=[wh], partial=False)
            for t in range(4):
                pp = ps_p[pc[0] % 2]
                pc[0] += 1
                for k in range(8):
                    P.mm(pp, pp[:], wh, wh[:, k, 9 * 128:10 * 128], xTA, xTA[:, k, 2048 + t * 512:2048 + (t + 1) * 512],
                         start=(k == 0), stop=(k == 7))
                P.op("act", lambda e: e.activation(sz[:, t * 512:(t + 1) * 512], pp[:], AF.Silu),
                     reads=[pp], writes=[sz], partial=True)
            for tt in range(16):
                kvt = kvts[tt % 2]
                for kv in range(2):
                    pp = ps_p[pc[0] % 2]
                    pc[0] += 1
                    for k in range(8):
                        P.mm(pp, pp[:, 0:384], xTA, xTA[:, k, 2048 + tt * 128:2048 + (tt + 1) * 128], wh,
                             wh[:, k, (3 + 3 * kv) * 128:(6 + 3 * kv) * 128], start=(k == 0), stop=(k == 7))
                    evac(P, ec[0], kvt, kvt[:, kv, :], pp, pp[:, 0:384])
                    ec[0] += 1
                P.dma("sp", dout["kvp3"][tt * 128:(tt + 1) * 128, :, h, :], kvt[:, :, 256:384], reads=[kvt])
                if tt >= 12:
                    P.dma("sp", dout["kvp2"][(tt - 12) * 128:(tt - 11) * 128, :, h, :], kvt[:, :, 128:256], reads=[kvt])
                if tt == 15:
                    P.dma("sp", dout["kvp1"][0:128, :, h, :], kvt[:, :, 0:128], reads=[kvt])
            for g, (win, d) in enumerate(GROUPS):
                L = 2048 // d
                nb = L // 128
                KW = 128 + L
                for t in range(4):
                    mt = 512 // d
                    dst = qT[:].rearrange("p (r m) -> p r m", r=d)[:, :, t * mt:(t + 1) * mt]
                    proj_fm(g, 2048 + t * 512, 512, qT, dst,
                            src_view=lambda pp: pp[:].rearrange("p (m r) -> p r m", r=d))
                kv = kT[:, 0:d * KW].rearrange("p (r m) -> p r m", r=d)
                npt = 128 * d
                for t0 in range(0, npt, 512):
                    n = min(512, npt - t0)
                    dst = kv[:, :, t0 // d:(t0 + n) // d]
                    proj_fm(3 + g, 2048 - npt + t0, n, kT, dst,
                            src_view=lambda pp: pp[:, 0:n].rearrange("p (m r) -> p r m", r=d))
                for t in range(4):
                    mt = 512 // d
                    dst = kv[:, :, 128 + t * mt:128 + (t + 1) * mt]
                    proj_fm(3 + g, 2048 + t * 512, 512, kT, dst,
                            src_view=lambda pp: pp[:].rearrange("p (m r) -> p r m", r=d))
                for r in range(d):
                    for j0 in range(0, 1 + nb, 4):
                        nj = min(4, 1 + nb - j0)
                        pp = ps_p[pc[0] % 2]
                        pc[0] += 1
                        for jj in range(nj):
                            j = j0 + jj
                            tokbase = 2048 - 128 * d + r + j * 128 * d
                            for k in range(8):
                                lhs = xTA[:, k, tokbase:tokbase + 127 * d + 1:d]
                                P.mm(pp, pp[:, jj * 128:(jj + 1) * 128], xTA, lhs, wh, wh[:, k, (6 + g) * 128:(7 + g) * 128],
                                     start=(k == 0), stop=(k == 7))
                        bi = r * (1 + nb) + j0
                        evac(P, ec[0], vt, vt[:, bi:bi + nj, :], pp, pp[:, 0:nj * 128].rearrange("p (j e) -> p j e", j=nj))
                        ec[0] += 1
                def gen_block(r, n, b):
                    S, T_, O_, OT, LB = ps_s[b % 3], ps_t[b % 3], ps_o[b % 3], ps_oT[b % 3], ps_l[b % 3]
                    ss, pb, ptb, ob, stt, lc = s_sb[b % 3], p_sb[b % 3], pT[b % 3], o_sb[b % 3], st[b % 3], lcol[b % 3]
                    qblk = qT[:, r * L + n * 128: r * L + (n + 1) * 128]
                    kblk = kT[:, r * KW + n * 128: r * KW + n * 128 + 256]
                    P.mm(S, S[:], qT, qblk, kT, kblk)
                    mk = C.cm[:, 256:512] if n == 0 else C.cm[:, 0:256]
                    P.op("dve", lambda e: e.scalar_tensor_tensor(ss[:], S[:], SCALE, mk, ALU.mult, ALU.add),
                         reads=[S, C.cm], writes=[ss])
                    yield
                    P.op("dve", lambda e: e.tensor_reduce(stt[:, 0:1], ss[:], AX.X, ALU.max, negate=True),
                         reads=[ss], writes=[stt], partial=True)
                    P.op("act", lambda e: e.activation(pb[:], ss[:], AF.Exp, bias=stt[:, 0:1], scale=1.0,
                                                       accum_out=stt[:, 1:2]),
                         reads=[ss, stt], writes=[pb, stt], partial=True)
                    yield
                    for kb in range(2):
                        P.tr(T_, T_[:, kb * 128:(kb + 1) * 128], pb, pb[:, kb * 128:(kb + 1) * 128], C.ident_b, C.ident_b[:])
                    evac(P, b + 1, ptb, ptb[:], T_, T_[:])
                    yield
                    for kb in range(2):
                        P.mm(O_, O_[:], ptb, ptb[:, kb * 128:(kb + 1) * 128], vt, vt[:, r * (1 + nb) + n + kb, :],
                             start=(kb == 0), stop=(kb == 1))
                    P.op("dve", lambda e: e.reciprocal(stt[:, 2:3], stt[:, 1:2]), reads=[stt], writes=[stt], partial=True)
                    P.op("dve", lambda e: e.tensor_scalar(ob[:], O_[:], stt[:, 2:3], None, ALU.mult),
                         reads=[O_, stt], writes=[ob])
                    yield
                    P.op("act", lambda e: e.activation(stt[:, 3:4], stt[:, 1:2], AF.Ln), reads=[stt], writes=[stt], partial=True)
                    P.op("dve", lambda e: e.tensor_scalar(lc[:], C.cm[:, 512:640], stt[:, 3:4], stt[:, 0:1],
                                                          ALU.add, ALU.subtract),
                         reads=[stt, C.cm], writes=[lc])
                    P.tr(OT, OT[:], ob, ob[:], C.ident_b, C.ident_b[:])
                    P.mm(LB, LB[:], lc, lc[:], C.ident_f, C.ident_f[:])
                    yield
                    tok0 = r + d * 128 * n
                    dsto = oTg[g][:, tok0:tok0 + 127 * d + 1:d]
                    dstl = lsB[g][:, tok0:tok0 + 127 * d + 1:d]
                    evac(P, b, oTg[g], dsto, OT, OT[:])
                    evac(P, b + 1, lsB[g], dstl, LB, LB[:])
                    yield

                blist = [(r, n) for r in range(d) for n in range(nb)]
                for i in range(0, len(blist), 3):
                    gens = []
                    for (r, n) in blist[i:i + 3]:
                        gens.append(gen_block(r, n, blk[0]))
                        blk[0] += 1
                    interleave(*gens)
            for t in range(4):
                sl = slice(t * 512, (t + 1) * 512)
                m_, e0, e1, e2, acc = cw
                P.op("dve", lambda e: e.tensor_tensor(m_[:], lsB[0][:, sl], lsB[1][:, sl], ALU.max), reads=[lsB[0], lsB[1]], writes=[m_])
                P.op("dve", lambda e: e.tensor_tensor(m_[:], m_[:], lsB[2][:, sl], ALU.max), reads=[m_, lsB[2]], writes=[m_])
                for g, eg in enumerate((e0, e1, e2)):
                    P.op("pool", lambda e: e.tensor_tensor(eg[:], lsB[g][:, sl], m_[:], ALU.subtract), reads=[lsB[g], m_], writes=[eg])
                    P.op("act", lambda e: e.activation(eg[:], eg[:], AF.Exp), reads=[eg], writes=[eg])
                P.op("dve", lambda e: e.tensor_tensor(m_[:], e0[:], e1[:], ALU.add), reads=[e0, e1], writes=[m_])
                P.op("dve", lambda e: e.tensor_tensor(m_[:], m_[:], e2[:], ALU.add), reads=[m_, e2], writes=[m_])
                P.op("dve", lambda e: e.reciprocal(m_[:], m_[:]), reads=[m_], writes=[m_])
                P.op("dve", lambda e: e.tensor_tensor(acc[:], e0[:], oTg[0][:, sl], ALU.mult), reads=[e0, oTg[0]], writes=[acc])
                P.op("pool", lambda e: e.tensor_tensor(e1[:], e1[:], oTg[1][:, sl], ALU.mult), reads=[e1, oTg[1]], writes=[e1])
                P.op("pool", lambda e: e.tensor_tensor(e2[:], e2[:], oTg[2][:, sl], ALU.mult), reads=[e2, oTg[2]], writes=[e2])
                P.op("dve", lambda e: e.tensor_tensor(acc[:], acc[:], e1[:], ALU.add), reads=[acc, e1], writes=[acc])
                P.op("dve", lambda e: e.tensor_tensor(acc[:], acc[:], e2[:], ALU.add), reads=[acc, e2], writes=[acc])
                P.op("dve", lambda e: e.tensor_tensor(acc[:], acc[:], m_[:], ALU.mult), reads=[acc, m_], writes=[acc])
                if dbg is not None:
                    P.dma("sp", dbg["oat"][h * 128:(h + 1) * 128, sl], acc[:], reads=[acc])
                P.op("dve", lambda e: e.tensor_tensor(yaT[:, h, sl], acc[:], sz[:, sl], ALU.mult), reads=[acc, sz], writes=[yaT], partial=True)


def emit_out_phase(P, C, din, xT, xc0, x_rows_ap, yrT, yaT, ntok, y_out_ap, tag):
    with scope(P):
        wg = P.sbuf("wg" + tag, [128, 8, 2048], BF16)
        woa = P.sbuf("woa" + tag, [128, 4, 1024], BF16)
        wob = P.sbuf("wob" + tag, [128, 4, 1024], BF16)
        wout = P.sbuf("wout" + tag, [128, 8, 1024], BF16)
        lng = P.sbuf("lng" + tag, [128, 1024], F32)
        lnb = P.sbuf("lnb" + tag, [128, 1024], F32)
        TW = min(512, ntok)
        mixT = P.sbuf("mixT" + tag, [128, 8, TW], BF16)
        gr = [P.sbuf("gr%d%s" % (i, tag), [128, TW], F32) for i in range(2)]
        ga = [P.sbuf("ga%d%s" % (i, tag), [128, TW], F32) for i in range(2)]
        t1 = [P.sbuf("t1%d%s" % (i, tag), [128, TW], F32) for i in range(2)]
        xr = [P.sbuf("xr%d%s" % (i, tag), [128, 1024], F32) for i in range(2)]
        z = [P.sbuf("z%d%s" % (i, tag), [128, 1024], F32) for i in range(2)]
        bst = [P.sbuf("bst%d%s" % (i, tag), [128, 16], F32) for i in range(2)]
        ps_g = [carve(C, i, 0, 512, "psO_g%d%s" % (i, tag)) for i in range(4)]
        ps_y = [carve(C, 4 + i, 0, 512, "psO_y%d%s" % (i, tag)) for i in range(4)]
        P.dma("pool", wg[:], din["w_g"].rearrange("(k p) c -> p k c", p=128), writes=[wg])
        P.dma("pool", woa[:], din["w_oa"].rearrange("(k p) c -> p k c", p=128), writes=[woa])
        P.dma("pool", wob[:], din["w_ob"].rearrange("(k p) c -> p k c", p=128), writes=[wob])
        P.dma("pool", wout[:], din["w_out"].rearrange("(k p) c -> p k c", p=128), writes=[wout])
        P.dma("sp", lng[:], din["ln_gb"][0:1, :].to_broadcast([128, 1024]), writes=[lng])
        P.dma("sp", lnb[:], din["ln_gb"][1:2, :].to_broadcast([128, 1024]), writes=[lnb])
        it = 0
        for t0 in range(0, ntok, TW):
            for n in range(8):
                pgr, pga, pmr, pma = ps_g
                for k in range(8):
                    P.mm(pgr, pgr[:, 0:TW], wg, wg[:, k, n * 128:(n + 1) * 128], xT, xT[:, k, xc0 + t0:xc0 + t0 + TW], start=(k == 0), stop=(k == 7))
                for k in range(8):
                    P.mm(pga, pga[:, 0:TW], wg, wg[:, k, 1024 + n * 128:1024 + (n + 1) * 128], xT, xT[:, k, xc0 + t0:xc0 + t0 + TW], start=(k == 0), stop=(k == 7))
                for c in range(4):
                    P.mm(pmr, pmr[:, 0:TW], woa, woa[:, c, n * 128:(n + 1) * 128], yrT, yrT[:, c, t0:t0 + TW], start=(c == 0), stop=(c == 3))
                for c in range(4):
                    P.mm(pma, pma[:, 0:TW], wob, wob[:, c, n * 128:(n + 1) * 128], yaT, yaT[:, c, t0:t0 + TW], start=(c == 0), stop=(c == 3))
                a, b_, tt = gr[it % 2], ga[it % 2], t1[it % 2]
                it += 1
                P.op("act", lambda e: e.activation(a[:], pgr[:, 0:TW], AF.Sigmoid, bias=C.pv[:, PV_BG + n:PV_BG + n + 1], scale=1.0),
                     reads=[pgr, C.pv], writes=[a])
                P.op("act", lambda e: e.activation(b_[:], pga[:, 0:TW], AF.Sigmoid, bias=C.pv[:, PV_BG + 8 + n:PV_BG + 9 + n], scale=1.0),
                     reads=[pga, C.pv], writes=[b_])
                P.op("dve", lambda e: e.tensor_tensor(tt[:], a[:], pmr[:, 0:TW], ALU.mult), reads=[a, pmr], writes=[tt])
                P.op("dve", lambda e: e.tensor_tensor(b_[:], b_[:], pma[:, 0:TW], ALU.mult), reads=[b_, pma], writes=[b_])
                P.op("pool", lambda e: e.tensor_tensor(mixT[:, n, :], tt[:], b_[:], ALU.add), reads=[tt, b_], writes=[mixT], partial=True)
            for s0 in range(0, TW, 128):
                ns = min(128, ntok - t0 - s0)
                i2 = (t0 + s0) // 128
                xx, zz, bs = xr[i2 % 2], z[i2 % 2], bst[i2 % 2]
                py = ps_y[(i2 % 2) * 2:(i2 % 2) * 2 + 2]
                P.dma("sp", xx[0:ns, :], x_rows_ap[t0 + s0:t0 + s0 + ns, :], writes=[xx])
                for hf in range(2):
                    for m in range(8):
                        P.mm(py[hf], py[hf][0:ns, :], mixT, mixT[:, m, s0:s0 + ns], wout, wout[:, m, hf * 512:(hf + 1) * 512],
                             start=(m == 0), stop=(m == 7))
                    P.op("dve", lambda e: e.scalar_tensor_tensor(zz[0:ns, hf * 512:(hf + 1) * 512], xx[0:ns, hf * 512:(hf + 1) * 512],
                                                                 ALPHA, py[hf][0:ns, :], ALU.mult, ALU.add),
                         reads=[xx, py[hf]], writes=[zz], partial=True)
                    P.op("dve", lambda e: e.bn_stats(bs[0:ns, hf * 6:(hf + 1) * 6], zz[0:ns, hf * 512:(hf + 1) * 512]),
                         reads=[zz], writes=[bs], partial=True)
                P.op("dve", lambda e: e.bn_aggr(bs[0:ns, 12:14], bs[0:ns, 0:12]), reads=[bs], writes=[bs], partial=True)
                P.op("dve", lambda e: e.tensor_scalar(bs[0:ns, 14:15], bs[0:ns, 13:14], LN_EPS, None, ALU.add), reads=[bs], writes=[bs], partial=True)
                P.op("act", lambda e: e.activation(bs[0:ns, 14:15], bs[0:ns, 14:15], AF.Sqrt), reads=[bs], writes=[bs], partial=True)
                P.op("dve", lambda e: e.reciprocal(bs[0:ns, 14:15], bs[0:ns, 14:15]), reads=[bs], writes=[bs], partial=True)
                P.op("dve", lambda e: e.tensor_scalar(zz[0:ns, :], zz[0:ns, :], bs[0:ns, 12:13], bs[0:ns, 14:15], ALU.subtract, ALU.mult),
                     reads=[zz, bs], writes=[zz])
                P.op("pool", lambda e: e.tensor_tensor(zz[0:ns, :], zz[0:ns, :], lng[0:ns, :], ALU.mult), reads=[zz, lng], writes=[zz])
                P.op("pool", lambda e: e.tensor_tensor(zz[0:ns, :], zz[0:ns, :], lnb[0:ns, :], ALU.add), reads=[zz, lnb], writes=[zz])
                P.dma("sp", y_out_ap[t0 + s0:t0 + s0 + ns, :], zz[0:ns, :], reads=[zz])


def make_cmask():
    cm = np.zeros((128, 1408), np.float32)
    i = np.arange(128)[:, None]
    j = np.arange(256)[None, :]
    dist = 128 + i - j
    cm[:, 0:256] = np.where((dist >= 0) & (dist <= 128), 0.0, NEGM)
    p = np.arange(128)
    hs, s = p[:, None] // 64, p[:, None] % 64
    ht, t = p[None, :] // 64, p[None, :] % 64
    same = (hs == ht)
    cm[:, 640:768] = (same & (s < t)).astype(np.float32)
    cm[:, 768:896] = (same & (s <= t)).astype(np.float32)
    cm[:, 896:1024] = (same & (s > t)).astype(np.float32)
    cm[:, 1024:1152] = np.eye(128, dtype=np.float32)
    cm[:, 1152:1280] = same.astype(np.float32)
    cm[:, 1280:1282] = (p[:, None] // 64 == np.arange(2)[None, :]).astype(np.float32)
    sel = np.zeros((128, 64), np.float32)
    sel[p, p % 64] = 1.0
    cm[:, 1282:1346] = sel
    return cm


CM_STRICT, CM_INCL, CM_STRICT_T, CM_EYE, CM_BONES, CM_HM, CM_SEL = 640, 768, 896, 1024, 1152, 1280, 1282


def build(flags):
    nc = bass.Bass("TRN2", target_bir_lowering=False)
    din, dout = {}, {}

    def inp(name, shape, dt=F32):
        din[name] = nc.dram_tensor(name, list(shape), dt, kind="ExternalInput").ap()

    def outp(name, shape, dt=F32):
        dout[name] = nc.dram_tensor(name, list(shape), dt, kind="ExternalOutput").ap()

    inp("xe", [EXT, D])
    inp("w_rw", [D, RW_COLS])
    inp("w_att", [4, D, 1280])
    inp("w_g", [D, 2048])
    inp("w_oa", [512, D])
    inp("w_ob", [512, D])
    inp("w_out", [D, D])
    inp("w_l2", [128, 512])
    inp("pvec", [128, PV_OMM])
    inp("ln_gb", [2, D])
    inp("ident", [128, 128])
    inp("cmask", [128, 1408])
    inp("pbias", [128, 1])
    inp("xs", [64, D])
    inp("w_qkvz", [D, 5120])
    inp("cache1", [16, 128, 2, 4, 128])
    inp("cache2", [16, 512, 2, 4, 128])
    inp("cache3", [16, 2048, 2, 4, 128])
    inp("wkv_s", [16, 8, 64, 64])
    inp("shift_s", [16, SHIFT_COLS])
    inp("cmask_s", [128, NCS])
    inp("colmask", [128, 2048])
    outp("y_s", [64, D])
    for g in (1, 2, 3):
        outp("kvs%d" % g, [16, 4, 2, 4, 128])
    outp("wkv_so", [16, 8, 64, 64])
    outp("shift_so", [16, SHIFT_COLS])
    outp("y_p", [OWN, D])
    outp("kvp1", [128, 2, 4, 128])
    outp("kvp2", [512, 2, 4, 128])
    outp("kvp3", [2048, 2, 4, 128])
    outp("wkv_p", [8, 64, 64])
    outp("shift_p", [SHIFT_COLS])
    if flags.get("dbg"):
        inp("yr_dbg", [512, OWN])
        outp("oat", [512, OWN])
        outp("yat", [512, OWN])
        outp("yrt", [512, OWN])
    with contextlib.ExitStack() as st:
        P = Prog(nc, st)
        C = Ctx()
        C.bank = [P.psum("bank%d" % i, [128, 512], F32) for i in range(8)]
        emit_consts(P, C, din)
        barrier(P)
        yr_scr = P.dram("yr_scr", [128, 4 * OWN], BF16)
        ya_scr = P.dram("ya_scr", [128, 4 * OWN], BF16)
        if flags.get("attn", True):
          with scope(P):
              yaT = P.sbuf("yaT", [128, 4, OWN], BF16)
              xTA = P.sbuf("xTA", [128, 8, 4096], BF16)
              with scope(P):
                  xld = [P.sbuf("xldA%d" % i, [128, 1024], BF16) for i in range(3)]
                  psx = [carve(C, 6 + i, 0, 512, "psxA%d" % i, BF16) for i in range(2)]
                  emit_xT(P, C, din["xe"][4096:8192, :], 32, xTA, 0, xld, psx, [0])
              emit_attn_prompt(P, C, din, xTA, yaT, dbg=dout if flags.get("dbg") else None, dout=dout)
              if flags.get("dbg"):
                  with scope(P):
                      yaf = P.sbuf("yaf", [128, 4, OWN], F32)
                      P.op("dve", lambda e: e.tensor_copy(yaf[:], yaT[:]), reads=[yaT], writes=[yaf])
                      P.dma("sp", dout["yat"].rearrange("(h p) t -> p h t", p=128), yaf[:], reads=[yaf])
              P.dma("sp", ya_scr[:, :], yaT[:].rearrange("p h t -> p (h t)"), reads=[yaT], writes=[ya_scr])
        if flags.get("rwkv", True):
            emit_rwkv_prompt(P, C, din, dout, yr_scr, ntiles=flags.get("ntiles", 16), own_from=flags.get("own_from", 12),
                             budget=flags.get("budget"))
        if flags.get("outp", True):
          with scope(P):
              xTO = P.sbuf("xTO", [128, 8, OWN], BF16)
              yaT = P.sbuf("yaT2", [128, 4, OWN], BF16)
              P.dma("sp", yaT[:].rearrange("p h t -> p (h t)"), ya_scr[:, :], reads=[ya_scr], writes=[yaT])
              yrT = P.sbuf("yrT", [128, 4, OWN], BF16)
              if flags.get("rwkv", True):
                  P.dma("sp", yrT[:].rearrange("p h t -> p (h t)"), yr_scr[:, :], reads=[yr_scr], writes=[yrT])
              elif flags.get("dbg"):
                  P.dma("pool", yrT[:], din["yr_dbg"].rearrange("(h p) t -> p h t", p=128), writes=[yrT])
              if flags.get("dbg"):
                  with scope(P):
                      yrf = P.sbuf("yrf", [128, 4, OWN], F32)
                      P.op("dve", lambda e: e.tensor_copy(yrf[:], yrT[:]), reads=[yrT], writes=[yrf])
                      P.dma("sp", dout["yrt"].rearrange("(h p) t -> p h t", p=128), yrf[:], reads=[yrf])
              with scope(P):
                  xld = [P.sbuf("xldO%d" % i, [128, 1024], BF16) for i in range(3)]
                  psx = [carve(C, 6 + i, 0, 512, "psxO%d" % i, BF16) for i in range(2)]
                  emit_xT(P, C, din["xe"][6144:8192, :], 16, xTO, 0, xld, psx, [0])
              emit_out_phase(P, C, din, xTO, 0, din["xe"][6144:8192, :], yrT, yaT, OWN, dout["y_p"], "p")
        if flags.get("sample", True):
            with scope(P):
                emit_sample(P, C, din, dout, flags)
        P.finish()
    return nc


def _fm(vec, n):
    return np.ascontiguousarray(np.asarray(vec, np.float32).reshape(n, 128).T)


def prep_shared(inputs):
    w_in = np.asarray(inputs["w_in"][0], np.float32)
    sh = {}
    sh["w_rw"] = np.ascontiguousarray(w_in[:, 0:RW_COLS])
    w_att = np.empty((4, D, 1280), np.float32)
    for h in range(4):
        cols = []
        for base in (Q0, K0, V0):
            for g in range(3):
                cols.append(w_in[:, base + g * 512 + h * 128: base + g * 512 + (h + 1) * 128])
        cols.append(w_in[:, ZA0 + h * 128: ZA0 + (h + 1) * 128])
        w_att[h] = np.concatenate(cols, axis=1)
    sh["w_att"] = w_att
    sh["w_g"] = np.ascontiguousarray(w_in[:, GR0:GR0 + 2048])
    sh["w_oa"] = np.ascontiguousarray(inputs["w_oa"][0], np.float32)
    sh["w_ob"] = np.ascontiguousarray(inputs["w_ob"][0], np.float32)
    sh["w_out"] = np.ascontiguousarray(inputs["w_out"][0], np.float32)
    sh["w_l2"] = np.ascontiguousarray(np.concatenate([inputs["w_w2"][0], inputs["w_a2"][0]], axis=0), np.float32)
    pv = np.zeros((128, PV_OMM), np.float32)
    pv[:, PV_MU:PV_MU + 13] = _fm(inputs["mu_shift"][0], 13)
    pv[:, PV_W0:PV_W0 + 4] = _fm(inputs["w0"][0], 4)
    pv[:, PV_A0:PV_A0 + 4] = _fm(inputs["a0"][0], 4)
    pv[:, PV_KK:PV_KK + 4] = _fm(inputs["k_k"][0], 4)
    pv[:, PV_KA:PV_KA + 4] = _fm(inputs["k_a"][0], 4)
    pv[:, PV_RK:PV_RK + 4] = _fm(np.asarray(inputs["r_k"][0]).reshape(-1), 4)
    pv[:, PV_LG:PV_LG + 4] = _fm(inputs["lnx_g"][0], 4)
    pv[:, PV_LB:PV_LB + 4] = _fm(inputs["lnx_b"][0], 4)
    pv[:, PV_BG:PV_BG + 16] = _fm(inputs["b_gate"][0], 16)
    sh["pvec"] = pv
    sh["ln_gb"] = np.ascontiguousarray(np.stack([inputs["ln_g"][0], inputs["ln_b"][0]]), np.float32)
    sh["w_qkvz"] = np.ascontiguousarray(w_in[:, Q0:ZA0 + 512])
    sh["cmask_s"] = make_cmask_s()
    sh["colmask"] = make_colmask()
    sh["ident"] = np.eye(128, dtype=np.float32)
    sh["cmask"] = make_cmask()
    return sh


def prep_core(inputs, sh, c):
    b, q = c // 4, c % 4
    m = dict(sh)
    xe = np.zeros((EXT, D), np.float32)
    n = OWN * (q + 1)
    xe[EXT - n:] = np.asarray(inputs["x_prompt"][b, 0:n], np.float32)
    m["xe"] = xe
    m["pbias"] = np.full((128, 1), 0.0 if q > 0 else NEGM, np.float32)
    sl = slice(16 * c, 16 * c + 16)
    m["xs"] = np.ascontiguousarray(np.asarray(inputs["x_sample"][sl], np.float32).reshape(64, D))
    m["cache1"] = np.ascontiguousarray(inputs["cache_kv_g1"][0, sl], np.float32)
    m["cache2"] = np.ascontiguousarray(inputs["cache_kv_g2"][0, sl], np.float32)
    m["cache3"] = np.ascontiguousarray(inputs["cache_kv_g3"][0, sl], np.float32)
    m["wkv_s"] = np.ascontiguousarray(inputs["state_rwkv_wkv"][0, sl], np.float32)
    m["shift_s"] = np.ascontiguousarray(inputs["state_rwkv_shift"][0, sl], np.float32)
    return m


_NC_CACHE = {}


def kernel(**inputs):
    if "nc" not in _NC_CACHE:
        _NC_CACHE["nc"] = build({})
    nc = _NC_CACHE["nc"]
    sh = prep_shared(inputs)
    in_maps = [prep_core(inputs, sh, c) for c in range(NCORES)]
    res = run_bass_kernel_spmd(nc, in_maps, core_ids=list(range(NCORES)))
    R = res.results
    y_p = np.zeros((NB, SEQ, D), np.float32)
    for c in range(NCORES):
        y_p[c // 4, (c % 4) * OWN:(c % 4 + 1) * OWN] = R[c]["y_p"]
    y_s = np.concatenate([R[c]["y_s"].reshape(16, 4, D) for c in range(NCORES)], axis=0)
    outs = [y_p, y_s]
    for g in (1, 2, 3):
        outs.append(np.stack([R[4 * b + 3]["kvp%d" % g] for b in range(NB)])[None])
        outs.append(np.concatenate([R[c]["kvs%d" % g] for c in range(NCORES)], axis=0)[None])
    outs.append(np.stack([R[4 * b + 3]["wkv_p"] for b in range(NB)])[None])
    outs.append(np.concatenate([R[c]["wkv_so"] for c in range(NCORES)], axis=0)[None])
    outs.append(np.stack([R[4 * b + 3]["shift_p"] for b in range(NB)])[None])
    outs.append(np.concatenate([R[c]["shift_so"] for c in range(NCORES)], axis=0)[None])
    return tuple(np.ascontiguousarray(o, dtype=np.float32) for o in outs)


def interleave(*gens):
    live = list(gens)
    while live:
        for g in list(live):
            try:
                next(g)
            except StopIteration:
                live.remove(g)


def bc(ap, shape, axes):
    for a in axes:
        ap = ap.unsqueeze(a)
    return ap.to_broadcast(list(shape))


def emit_rwkv_prompt(P, C, din, dout, yrT, ntiles=16, own_from=12, dbg=None, budget=None):
    with scope(P):
        w_r = P.sbuf("w_r", [128, 8, RW_COLS], BF16)
        wl2 = P.sbuf("wl2", [128, 512], BF16)
        mask4 = P.sbuf("mask4", [128, 512], F32)
        bones_b = P.sbuf("bones_b", [128, 128], BF16)
        scanm = P.sbuf("scanm", [128, 512], F32)
        carry = P.sbuf("carry", [128, 16], F32)
        shout = P.sbuf("shout", [128, 16], F32)
        Sf = [P.sbuf("Sf%d" % i, [128, 128], F32) for i in range(4)]
        Sb = [P.sbuf("Sb%d" % i, [128, 128], BF16) for i in range(4)]
        xT = [P.sbuf("xTR%d" % i, [128, 8, 512], BF16) for i in range(1)]
        xld = [P.sbuf("xldR%d" % i, [128, 1024], BF16) for i in range(2)]
        bm = [P.sbuf("bm%d" % i, [128, 512], F32) for i in range(2)]
        lor = P.sbuf("lor", [128, 512], BF16)
        wdad = P.sbuf("wdad", [128, 512], F32)
        S1 = []
        for i in range(3):
            d_ = {}
            for nm in ("rt", "kt", "bt", "at"):
                d_[nm] = P.sbuf("%s%d" % (nm, i), [128, 512], BF16)
            for nm in ("vz", "eg", "bonus", "gate"):
                d_[nm] = P.sbuf("%s%d" % (nm, i), [128, 512], F32)
            S1.append(d_)
        tmp = {nm: P.sbuf("tp_" + nm, [128, 512], F32) for nm in ("r", "k", "sg", "a", "cs", "eng", "kk", "rn", "km", "bv")}
        sqb = P.sbuf("sqb", [128, 512], BF16)
        Lb = P.sbuf("Lb", [128, 8, 2, 64], BF16)
        Lk = P.sbuf("Lk", [128, 8, 2, 64], BF16)
        Ra = P.sbuf("Ra", [128, 8, 2, 64], BF16)
        KH = P.sbuf("KH", [128, 8, 2, 64], BF16)
        BH = P.sbuf("BH", [128, 8, 2, 64], BF16)
        VB = P.sbuf("VB", [128, 8, 2, 64], BF16)
        hmg = P.sbuf("hmg", [128, 8, 2], F32)
        NA = P.sbuf("NA", [128, 8, 2, 128], BF16)
        PT0 = P.sbuf("PT0", [128, 8, 128], BF16)
        Rat = P.sbuf("Rat", [128, 8, 128], BF16)
        Tt = [P.sbuf("Tt%d" % i, [128, 8, 128], BF16) for i in range(2)]
        Pp = [P.sbuf("Pp%d" % i, [128, 4, 128], BF16) for i in range(2)]
        PTp = [P.sbuf("PTp%d" % i, [128, 4, 128], BF16) for i in range(2)]
        Yb = P.sbuf("Yb", [128, 4, 128], BF16)
        SETS = []
        for i in range(2):
            d_ = {}
            for nm in ("KHt", "BHt", "VBt", "W1", "W2", "Rr"):
                d_[nm] = P.sbuf("%s_%d" % (nm, i), [128, 8, 128], BF16)
            d_["ABK"] = P.sbuf("ABK_%d" % i, [128, 8, 2, 128], BF16)
            d_["gC"] = P.sbuf("gC_%d" % i, [128, 8], F32)
            SETS.append(d_)
        yos = [P.sbuf("yos%d" % i, [128, 512], BF16) for i in range(2)]
        Ub = [P.sbuf("Ub%d" % i, [128, 128], BF16) for i in range(2)]
        Ob = [P.sbuf("Ob%d" % i, [128, 128], BF16) for i in range(2)]
        post = {nm: P.sbuf("po_" + nm, [128, 512], F32) for nm in ("yr", "cen", "sq", "rs")}
        pp = [carve(C, i, 0, 512, "psR_p%d" % i) for i in range(2)]
        pstr = carve(C, 3, 0, 512, "psR_tr", BF16)
        psx = pstr
        psYb = carve(C, 2, 0, 512, "psR_Y")
        psA = carve(C, 4, 0, 384, "psR_A")
        psA2 = carve(C, 4, 384, 512, "psR_A2")
        psP = carve(C, 6, 0, 512, "psR_P")
        psPT = carve(C, 7, 0, 512, "psR_PT")
        psU = carve(C, 5, 0, 128, "psR_U")
        psS = carve(C, 5, 128, 256, "psR_S")
        psO = carve(C, 5, 256, 384, "psR_O")
        ppi = [0]
        eci = [0]

        def nextpp():
            b = pp[ppi[0] % 2]
            ppi[0] += 1
            return b

        def pv(col):
            return C.pv[:, col:col + 1]

        P.dma("pool", w_r[:], din["w_rw"].rearrange("(k p) c -> p k c", p=128), writes=[w_r])
        P.dma("pool", wl2[:], din["w_l2"][:, :], writes=[wl2])
        for i in range(4):
            src = C.cm[:, CM_STRICT:CM_STRICT + 128] if i % 2 == 0 else C.cm[:, CM_INCL:CM_INCL + 128]
            if i in (0, 1):
                src = C.cm[:, CM_STRICT:CM_STRICT + 128]
            else:
                src = C.cm[:, CM_INCL:CM_INCL + 128]
            P.op("pool", lambda e: e.tensor_copy(mask4[:, i * 128:(i + 1) * 128], src), reads=[C.cm], writes=[mask4], partial=True)
        P.op("pool", lambda e: e.tensor_copy(bones_b[:], C.cm[:, CM_BONES:CM_BONES + 128]), reads=[C.cm], writes=[bones_b])
        P.op("pool", lambda e: e.memset(scanm[:], 1.0), writes=[scanm])
        P.op("pool", lambda e: e.memset(scanm[:, 0:512:64], 0.0), writes=[scanm])
        P.op("pool", lambda e: e.memset(carry[:], 0.0), writes=[carry])
        for i in range(4):
            P.op("pool", lambda e: e.memset(Sf[i][:], 0.0), writes=[Sf[i]])
            P.op("pool", lambda e: e.memset(Sb[i][:], 0.0), writes=[Sb[i]])

        def build_xT(tile):
            xt = xT[0]
            for s in range(4):
                xb = xld[s % 2]
                P.dma("pool", xb[:], din["xe"][tile * 512 + s * 128: tile * 512 + (s + 1) * 128, :], writes=[xb])
                for k in range(8):
                    P.tr(psx, psx[:, k * 128:(k + 1) * 128], xb, xb[:, k * 128:(k + 1) * 128], C.ident_b, C.ident_b[:])
                evac(P, eci[0], xt, xt[:, :, s * 128:(s + 1) * 128], psx, psx[:].rearrange("p (k t) -> p k t", k=8))
                eci[0] += 1

        def proj_shift(xt, c, dst, last_own):
            p_ = nextpp()
            for k in range(8):
                P.mm(p_, p_[:], w_r, w_r[:, k, c * 128:(c + 1) * 128], xt, xt[:, k, :], start=(k == 0), stop=(k == 7))
            b_ = bm[c % 2]
            P.op("act", lambda e: e.mul(b_[:], p_[:], pv(PV_MU + c)), reads=[p_, C.pv], writes=[b_])
            P.op("dve", lambda e: e.scalar_tensor_tensor(dst[:, 1:512], p_[:, 1:512], pv(PV_OMM + c), b_[:, 0:511], ALU.mult, ALU.add),
                 reads=[p_, b_, C.pv], writes=[dst], partial=True)
            P.op("dve", lambda e: e.scalar_tensor_tensor(dst[:, 0:1], p_[:, 0:1], pv(PV_OMM + c), carry[:, c:c + 1], ALU.mult, ALU.add),
                 reads=[p_, carry, C.pv], writes=[dst], partial=True)
            P.op("pool", lambda e: e.tensor_copy(carry[:, c:c + 1], b_[:, 511:512]), reads=[b_], writes=[carry], partial=True)
            if last_own:
                P.op("act", lambda e: e.activation(shout[:, c:c + 1], p_[:, 511:512], AF.Copy), reads=[p_], writes=[shout], partial=True)

        def stage1(tile, hp, own):
            xt = xT[0]
            s1 = S1[(tile * 4 + hp) % 3]
            last_own = (tile == ntiles - 1)
            if hp == 0:
                proj_shift(xt, 12, wdad, last_own)
                P.op("act", lambda e: e.activation(lor[0:64, :], wdad[0:64, :], AF.Tanh), reads=[wdad], writes=[lor], partial=True)
                P.op("dve", lambda e: e.tensor_copy(lor[64:128, :], wdad[64:128, :]), reads=[wdad], writes=[lor], partial=True)
                yield
            r, k, sg, a, cs, eng, kk, rn, km, bv = (tmp[n] for n in ("r", "k", "sg", "a", "cs", "eng", "kk", "rn", "km", "bv"))
            vz, eg = s1["vz"], s1["eg"]
            if own or tile == own_from - 1:
                proj_shift(xt, hp, r, last_own)
                yield
            proj_shift(xt, 4 + hp, k, last_own)
            yield
            proj_shift(xt, 8 + hp, vz, last_own)
            yield
            p_ = nextpp()
            P.mm(p_, p_[:], wl2, wl2[0:64, hp * 128:(hp + 1) * 128], lor, lor[0:64, :])
            P.op("act", lambda e: e.activation(sg[:], p_[:], AF.Sigmoid, bias=pv(PV_W0 + hp), scale=1.0), reads=[p_, C.pv], writes=[sg])
            p2 = nextpp()
            P.mm(p2, p2[:], wl2, wl2[64:128, hp * 128:(hp + 1) * 128], lor, lor[64:128, :])
            P.op("act", lambda e: e.activation(a[:], p2[:], AF.Sigmoid, bias=pv(PV_A0 + hp), scale=1.0), reads=[p2, C.pv], writes=[a])
            yield
            P.op("dve", lambda e: e.tensor_tensor_scan(cs[:], scanm[:], sg[:], 0.0, ALU.mult, ALU.add), reads=[scanm, sg], writes=[cs])
            P.op("act", lambda e: e.activation(eg[:], cs[:], AF.Exp, scale=-C0), reads=[cs], writes=[eg])
            P.op("act", lambda e: e.activation(eng[:], cs[:], AF.Exp, scale=C0), reads=[cs], writes=[eng])
            P.op("pool", lambda e: e.tensor_tensor(cs[:], cs[:], sg[:], ALU.subtract), reads=[cs, sg], writes=[cs])
            P.op("act", lambda e: e.activation(cs[:], cs[:], AF.Exp, scale=-C0), reads=[cs], writes=[cs])
            yield
            P.op("dve", lambda e: e.tensor_scalar(kk[:], k[:], pv(PV_KK + hp), None, ALU.mult), reads=[k, C.pv], writes=[kk])
            P.op("pool", lambda e: e.tensor_tensor(sqb[:], kk[:], kk[:], ALU.mult), reads=[kk], writes=[sqb])
            p3 = nextpp()
            P.mm(p3, p3[:], bones_b, bones_b[:], sqb, sqb[:])
            P.op("dve", lambda e: e.tensor_scalar(rn[:], p3[:], 1e-24, None, ALU.max), reads=[p3], writes=[rn])
            P.op("act", lambda e: e.activation(rn[:], rn[:], AF.Sqrt), reads=[rn], writes=[rn])
            P.op("dve", lambda e: e.reciprocal(rn[:], rn[:]), reads=[rn], writes=[rn])
            P.op("dve", lambda e: e.tensor_tensor(kk[:], kk[:], rn[:], ALU.mult), reads=[kk, rn], writes=[kk])
            yield
            P.op("dve", lambda e: e.tensor_scalar(km[:], a[:], -1.0, pv(PV_KA + hp), ALU.add, ALU.mult), reads=[a, C.pv], writes=[km])
            P.op("dve", lambda e: e.scalar_tensor_tensor(km[:], km[:], 1.0, k[:], ALU.add, ALU.mult), reads=[km, k], writes=[km])
            P.op("pool", lambda e: e.tensor_tensor(bv[:], kk[:], a[:], ALU.mult), reads=[kk, a], writes=[bv])
            yield
            if own:
                P.op("pool", lambda e: e.tensor_tensor(s1["rt"][:], r[:], eg[:], ALU.mult), reads=[r, eg], writes=[s1["rt"]])
            P.op("dve", lambda e: e.tensor_tensor(s1["kt"][:], km[:], eng[:], ALU.mult), reads=[km, eng], writes=[s1["kt"]])
            P.op("pool", lambda e: e.tensor_tensor(s1["bt"][:], bv[:], eng[:], ALU.mult), reads=[bv, eng], writes=[s1["bt"]])
            P.op("dve", lambda e: e.scalar_tensor_tensor(s1["at"][:], kk[:], -1.0, cs[:], ALU.mult, ALU.mult), reads=[kk, cs], writes=[s1["at"]])
            yield
            if own:
                P.op("dve", lambda e: e.scalar_tensor_tensor(sqb[:], r[:], pv(PV_RK + hp), km[:], ALU.mult, ALU.mult), reads=[r, km, C.pv], writes=[sqb])
                p4 = nextpp()
                P.mm(p4, p4[:], bones_b, bones_b[:], sqb, sqb[:])
                P.op("dve", lambda e: e.tensor_tensor(s1["bonus"][:], p4[:], vz[:], ALU.mult), reads=[p4, vz], writes=[s1["bonus"]])
                p5 = nextpp()
                for kq in range(8):
                    P.mm(p5, p5[:], w_r, w_r[:, kq, (13 + hp) * 128:(14 + hp) * 128], xt, xt[:, kq, :], start=(kq == 0), stop=(kq == 7))
                P.op("act", lambda e: e.activation(s1["gate"][:], p5[:], AF.Silu), reads=[p5], writes=[s1["gate"]])
                yield

        def stage2(tile, hp, own):
            u = tile * 4 + hp
            s1 = S1[u % 3]
            cs_ = SETS[u % 2]
            hm = C.cm[:, CM_HM:CM_HM + 2]
            eg = s1["eg"]
            gview = eg[:, 63:512:64]
            P.op("pool", lambda e: e.tensor_copy(cs_["gC"][:], gview), reads=[eg], writes=[cs_["gC"]])
            P.op("pool", lambda e: e.tensor_tensor(hmg[:], bc(hm, [128, 8, 2], [1]), bc(gview, [128, 8, 2], [2]), ALU.mult),
                 reads=[C.cm, eg], writes=[hmg])
            hm4 = bc(hm, [128, 8, 2, 64], [1, 3])
            hmg4 = bc(hmg[:], [128, 8, 2, 64], [3])

            def ex(x):
                return bc(x[:].rearrange("p (c s) -> p c s", s=64), [128, 8, 2, 64], [2])
            P.op("dve", lambda e: e.tensor_tensor(Lb[:], ex(s1["bt"]), hm4, ALU.mult), reads=[s1["bt"], C.cm], writes=[Lb])
            P.op("pool", lambda e: e.tensor_tensor(Ra[:], ex(s1["at"]), hm4, ALU.mult), reads=[s1["at"], C.cm], writes=[Ra])
            P.op("dve", lambda e: e.tensor_tensor(Lk[:], ex(s1["kt"]), hm4, ALU.mult), reads=[s1["kt"], C.cm], writes=[Lk])
            yield
            P.op("pool", lambda e: e.tensor_tensor(KH[:], ex(s1["kt"]), hmg4, ALU.mult), reads=[s1["kt"], hmg], writes=[KH])
            P.op("dve", lambda e: e.tensor_tensor(BH[:], ex(s1["bt"]), hmg4, ALU.mult), reads=[s1["bt"], hmg], writes=[BH])
            P.op("pool", lambda e: e.tensor_tensor(VB[:], ex(s1["vz"]), hm4, ALU.mult), reads=[s1["vz"], C.cm], writes=[VB])
            if own:
                Rr4 = cs_["Rr"][:].rearrange("p c (h s) -> p c h s", h=2)
                P.op("dve", lambda e: e.tensor_tensor(Rr4, ex(s1["rt"]), hm4, ALU.mult), reads=[s1["rt"], C.cm], writes=[cs_["Rr"]])
            yield

            def blk(t, c):
                return t[:, c].rearrange("p h s -> p (h s)")
            Tc = Tt[0]
            for c in range(8):
                P.mm(psA, psA[:, 0:128], Lb, blk(Lb, c), Ra, blk(Ra, c))
                P.mm(psA, psA[:, 128:256], Lk, blk(Lk, c), Ra, blk(Ra, c))
                P.mm(psA, psA[:, 256:384], Ra, blk(Ra, c), Lb, blk(Lb, c))
                P.op("dve", lambda e: e.tensor_tensor(NA[:, c].rearrange("p a t -> p (a t)"), psA[:, 0:256], mask4[:, 0:256], ALU.mult),
                     reads=[psA, mask4], writes=[NA], partial=True)
                P.op("dve", lambda e: e.tensor_tensor(PT0[:, c, :], psA[:, 256:384], C.cm[:, CM_STRICT_T:CM_STRICT_T + 128], ALU.mult),
                     reads=[psA, C.cm], writes=[PT0], partial=True)
                if own:
                    for a_, lx in enumerate((Lb, Lk)):
                        P.mm(psA2, psA2[:], lx, blk(lx, c), cs_["Rr"], cs_["Rr"][:, c, :])
                        P.op("dve", lambda e: e.tensor_tensor(cs_["ABK"][:, c, a_, :], psA2[:], mask4[:, 256:384], ALU.mult),
                             reads=[psA2, mask4], writes=[cs_["ABK"]], partial=True)
                P.op("pool", lambda e: e.tensor_tensor(Tc[:, c, :], NA[:, c, 0, :], C.cm[:, CM_EYE:CM_EYE + 128], ALU.add),
                     reads=[NA, C.cm], writes=[Tc], partial=True)
                if c % 2 == 1:
                    yield
            for qi, (src, dst_b) in enumerate(((KH, cs_["KHt"]), (BH, cs_["BHt"]), (Ra, Rat), (VB, cs_["VBt"]))):
                for c in range(8):
                    P.tr(pstr, pstr[:, c * 128:(c + 1) * 128], src, blk(src, c), C.ident_b, C.ident_b[:])
                evac(P, qi, dst_b, dst_b[:].rearrange("p c t -> p (c t)"), pstr, pstr[:])
                yield
            for cb in range(2):
                c0 = cb * 4
                Pc, PTc = None, None
                Tcur = Tt[0]
                for lvl in range(1, 6):
                    Pn, PTn = Pp[lvl % 2], PTp[lvl % 2]
                    for c in range(4):
                        lp = NA[:, c0 + c, 0, :] if lvl == 1 else Pc[:, c, :]
                        lpt = PT0[:, c0 + c, :] if lvl == 1 else PTc[:, c, :]
                        lpb = NA if lvl == 1 else Pc
                        lptb = PT0 if lvl == 1 else PTc
                        if lvl < 5:
                            P.mm(psP, psP[:, c * 128:(c + 1) * 128], lptb, lpt, lpb, lp)
                        P.mm(psPT, psPT[:, c * 128:(c + 1) * 128], lpb, lp, lptb, lpt)
                    if lvl < 5:
                        P.op("act", lambda e: e.activation(Pn[:].rearrange("p c t -> p (c t)"), psP[:], AF.Copy), reads=[psP], writes=[Pn])
                    P.op("act", lambda e: e.activation(PTn[:].rearrange("p c t -> p (c t)"), psPT[:], AF.Copy), reads=[psPT], writes=[PTn])
                    Tn = Tt[lvl % 2]
                    for c in range(4):
                        P.mm(psP, psP[:, c * 128:(c + 1) * 128], PTn, PTn[:, c, :], Tcur, Tcur[:, c0 + c, :])
                    P.op("dve", lambda e: e.tensor_tensor(Tn[:, c0:c0 + 4, :].rearrange("p c t -> p (c t)"), psP[:],
                                                          Tcur[:, c0:c0 + 4, :].rearrange("p c t -> p (c t)"), ALU.add),
                         reads=[psP, Tcur], writes=[Tn], partial=True)
                    Pc, PTc, Tcur = Pn, PTn, Tn
                    yield
                Tfin = Tcur
                p_ = nextpp()
                for c in range(4):
                    P.mm(p_, p_[:, c * 128:(c + 1) * 128], NA, NA[:, c0 + c, 1, :], cs_["VBt"], cs_["VBt"][:, c0 + c, :])
                evac(P, 0, Yb, Yb[:].rearrange("p c t -> p (c t)"), p_, p_[:])
                p_ = nextpp()
                for c in range(4):
                    P.mm(p_, p_[:, c * 128:(c + 1) * 128], Tfin, Tfin[:, c0 + c, :], Yb, Yb[:, c, :])
                evac(P, 1, cs_["W2"], cs_["W2"][:, c0:c0 + 4, :].rearrange("p c t -> p (c t)"), p_, p_[:])
                p_ = nextpp()
                for c in range(4):
                    P.mm(p_, p_[:, c * 128:(c + 1) * 128], Rat, Rat[:, c0 + c, :], Tfin, Tfin[:, c0 + c, :])
                evac(P, 0, cs_["W1"], cs_["W1"][:, c0:c0 + 4, :].rearrange("p c t -> p (c t)"), p_, p_[:])
                yield

        def stage3(tile, hp, own):
            u = tile * 4 + hp
            s1 = S1[u % 3]
            cs_ = SETS[u % 2]
            psY = psYb
            for c in range(8):
                ub, ob = Ub[c % 2], Ob[c % 2]
                P.mm(psU, psU[:], cs_["W1"], cs_["W1"][:, c, :], Sb[hp], Sb[hp][:])
                P.op("dve", lambda e: e.tensor_tensor(ub[:], psU[:], cs_["W2"][:, c, :], ALU.add), reads=[psU, cs_["W2"]], writes=[ub])
                P.mm(psS, psS[:], cs_["KHt"], cs_["KHt"][:, c, :], cs_["VBt"], cs_["VBt"][:, c, :], start=True, stop=False)
                P.mm(psS, psS[:], cs_["BHt"], cs_["BHt"][:, c, :], ub, ub[:], start=False, stop=True)
                if own:
                    P.mm(psO, psO[:], cs_["Rr"], cs_["Rr"][:, c, :], Sb[hp], Sb[hp][:], start=True, stop=False)
                    P.mm(psO, psO[:], cs_["ABK"], cs_["ABK"][:, c, 0, :], ub, ub[:], start=False, stop=False)
                    P.mm(psO, psO[:], cs_["ABK"], cs_["ABK"][:, c, 1, :], cs_["VBt"], cs_["VBt"][:, c, :], start=False, stop=True)
                gc = cs_["gC"][:, c:c + 1]
                P.op("dve", lambda e: e.scalar_tensor_tensor(Sb[hp][:], Sf[hp][:], gc, psS[:], ALU.mult, ALU.add),
                     reads=[Sf[hp], psS, cs_["gC"]], writes=[Sb[hp]])
                P.op("dve", lambda e: e.scalar_tensor_tensor(Sf[hp][:], Sf[hp][:], gc, psS[:], ALU.mult, ALU.add),
                     reads=[Sf[hp], psS, cs_["gC"]], writes=[Sf[hp]])
                if own:
                    P.op("act", lambda e: e.activation(ob[:], psO[:], AF.Copy), reads=[psO], writes=[ob])
                    P.mm(psY, psY[:, c * 64:(c + 1) * 64], ob, ob[:], C.selb, C.selb[:])
                yield
            if own and tile >= own_from:
                yr, cen, sq, rs = post["yr"], post["cen"], post["sq"], post["rs"]
                P.op("act", lambda e: e.activation(yr[:], psY[:], AF.Copy), reads=[psY], writes=[yr])
                pm = nextpp()
                P.mm(pm, pm[:], C.bones_f, C.bones_f[:], yr, yr[:])
                P.op("dve", lambda e: e.scalar_tensor_tensor(cen[:], pm[:], -1.0 / 64.0, yr[:], ALU.mult, ALU.add), reads=[pm, yr], writes=[cen])
                P.op("pool", lambda e: e.tensor_tensor(sq[:], cen[:], cen[:], ALU.mult), reads=[cen], writes=[sq])
                pv_ = nextpp()
                P.mm(pv_, pv_[:], C.bones_f, C.bones_f[:], sq, sq[:])
                P.op("dve", lambda e: e.tensor_scalar(rs[:], pv_[:], 1.0 / 64.0, GN_EPS, ALU.mult, ALU.add), reads=[pv_], writes=[rs])
                P.op("act", lambda e: e.activation(rs[:], rs[:], AF.Sqrt), reads=[rs], writes=[rs])
                P.op("dve", lambda e: e.reciprocal(rs[:], rs[:]), reads=[rs], writes=[rs])
                P.op("dve", lambda e: e.tensor_tensor(cen[:], cen[:], rs[:], ALU.mult), reads=[cen, rs], writes=[cen])
                P.op("dve", lambda e: e.tensor_scalar(cen[:], cen[:], pv(PV_LG + hp), pv(PV_LB + hp), ALU.mult, ALU.add), reads=[cen, C.pv], writes=[cen])
                P.op("pool", lambda e: e.tensor_tensor(cen[:], cen[:], s1["bonus"][:], ALU.add), reads=[cen, s1["bonus"]], writes=[cen])
                col = hp * OWN + (tile - own_from) * 512
                yo = yos[u % 2]
                P.op("pool", lambda e: e.tensor_tensor(yo[:], cen[:], s1["gate"][:], ALU.mult), reads=[cen, s1["gate"]], writes=[yo])
                P.dma("sp", yrT[:, col:col + 512], yo[:], reads=[yo], writes=[yrT])
                yield

        def pre1(tile, hp):
            own = tile >= own_from
            if hp == 0:
                build_xT(tile)
                yield
            yield from stage1(tile, hp, own)

        def pre2(tile, hp):
            yield from stage2(tile, hp, tile >= own_from)

        units = [(t, h) for t in range(ntiles) for h in range(4)]
        nu = len(units)
        for g_ in (pre1(*units[0]), pre2(*units[0])):
            for _ in g_:
                pass
        if nu > 1:
            for _ in pre1(*units[1]):
                pass
        for i, (t, h) in enumerate(units):
            gens = [stage3(t, h, t >= own_from)]
            if i + 1 < nu:
                gens.append(pre2(*units[i + 1]))
            if i + 2 < nu:
                gens.append(pre1(*units[i + 2]))
            interleave(*gens)
        for hp in range(4):
            p_ = nextpp()
            P.mm(p_, p_[:, 0:128], Sf[hp], Sf[hp][:], C.ident_f, C.ident_f[:])
            so = post["yr"]
            P.op("dve", lambda e: e.tensor_copy(so[:, 0:128], p_[:, 0:128]), reads=[p_], writes=[so])
            for hh in range(2):
                P.dma("sp", dout["wkv_p"][hp * 2 + hh, :, :], so[hh * 64:(hh + 1) * 64, hh * 64:(hh + 1) * 64], reads=[so])
        p_ = nextpp()
        P.mm(p_, p_[0:13, 0:128], shout, shout[:, 0:13], C.ident_f, C.ident_f[:])
        so = post["cen"]
        P.op("dve", lambda e: e.tensor_copy(so[0:13, 0:128], p_[0:13, 0:128]), reads=[p_], writes=[so])
        P.dma("sp", dout["shift_p"].rearrange("(c p) -> c p", p=128), so[0:13, 0:128], reads=[so])


CS_STRICT, CS_INCL, CS_STRICT_T, CS_ROW, CS_G1, CS_G23, NCS = 0, 128, 256, 384, 400, 532, 1048


def make_cmask_s():
    cs = np.zeros((128, NCS), np.float32)
    p = np.arange(128)
    h1, b1, t1 = p[:, None] // 64, (p[:, None] % 64) // 4, p[:, None] % 4
    h2, b2, t2 = p[None, :] // 64, (p[None, :] % 64) // 4, p[None, :] % 4
    same = (h1 == h2) & (b1 == b2)
    cs[:, CS_STRICT:CS_STRICT + 128] = (same & (t1 < t2))
    cs[:, CS_INCL:CS_INCL + 128] = (same & (t1 <= t2))
    cs[:, CS_STRICT_T:CS_STRICT_T + 128] = (same & (t1 > t2))
    cs[:, CS_ROW:CS_ROW + 16] = (((p[:, None] % 64) // 4) == np.arange(16)[None, :])
    t = p % 32
    g1 = np.zeros((128, 132), np.float32)
    r = np.arange(128)[None, :]
    g1[:, 0:128] = np.where(r >= t[:, None], 0.0, NEGM)
    u = np.arange(4)[None, :]
    g1[:, 128:132] = np.where(u <= t[:, None], 0.0, NEGM)
    g1[t >= 4] = 0.0
    cs[:, CS_G1:CS_G1 + 132] = g1
    g23 = np.zeros((128, 516), np.float32)
    c = (np.arange(512) // 128)[None, :]
    g23[:, 0:512] = np.where(c == t[:, None], 0.0, NEGM)
    g23[:, 512:516] = np.where(u == t[:, None], 0.0, NEGM)
    g23[t >= 4] = 0.0
    cs[:, CS_G23:CS_G23 + 516] = g23
    return cs


def make_colmask():
    col = np.arange(128)
    m = np.zeros((128, 16, 128), np.float32)
    for b in range(16):
        m[:, b, :] = (((col % 64) // 4) == b)[None, :]
    return m.reshape(128, 2048)


def emit_sample(P, C, din, dout, flags={}):
    yrS = P.sbuf("yrS", [128, 4, 64], BF16)
    yaS = P.sbuf("yaS", [128, 4, 64], BF16)
    xTs = P.sbuf("xTs", [128, 8, 64], BF16)
    cms = P.sbuf("cms", [128, NCS], F32)
    P.dma("sp", cms[:], din["cmask_s"][:, :], writes=[cms])
    with scope(P):
        xb = P.sbuf("xbS", [64, 1024], BF16)
        px = carve(C, 0, 0, 512, "psS_x", BF16)
        P.dma("pool", xb[:], din["xs"][:, :], writes=[xb])
        for k in range(8):
            P.tr(px, px[:, k * 64:(k + 1) * 64], xb, xb[:, k * 128:(k + 1) * 128], C.ident_b, C.ident_b[0:64, 0:64])
        P.op("dve", lambda e: e.tensor_copy(xTs[:].rearrange("p k t -> p (k t)"), px[:, 0:512]), reads=[px], writes=[xTs])
    if flags.get("s_rwkv", True):
        emit_rwkv_sample(P, C, din, dout, xTs, cms, yrS)
    if flags.get("s_attn", True):
        emit_attn_sample(P, C, din, dout, xTs, cms, yaS)
    if flags.get("s_out", True):
        emit_out_phase(P, C, din, xTs, 0, din["xs"], yrS, yaS, 64, dout["y_s"], "s")


def emit_rwkv_sample(P, C, din, dout, xTs, cms, yrS):
    W = 64
    with scope(P):
        w_r = P.sbuf("w_rS", [128, 8, RW_COLS], BF16)
        wl2 = P.sbuf("wl2S", [128, 512], BF16)
        bones_b = P.sbuf("bones_bS", [128, 128], BF16)
        scanm = P.sbuf("scanmS", [128, W], F32)
        colm = P.sbuf("colm", [128, 16, 128], BF16)
        shs = P.sbuf("shs", [16, SHIFT_COLS], F32)
        smu = P.sbuf("smu", [128, 13, 16], F32)
        shout = P.sbuf("shoutS", [128, 13, 16], F32)
        sho2 = P.sbuf("sho2", [16, SHIFT_COLS], F32)
        lor = P.sbuf("lorS", [128, W], BF16)
        wdad = P.sbuf("wdadS", [128, W], F32)
        bm = P.sbuf("bmS", [128, W], F32)
        tmp = {nm: P.sbuf("ts_" + nm, [128, W], F32) for nm in
               ("r", "k", "vz", "sg", "a", "cs", "eg", "eng", "kk", "rn", "km", "bv", "bonus", "gate", "gf", "yr", "cen", "sq", "rs")}
        tb = {nm: P.sbuf("tsb_" + nm, [128, W], BF16) for nm in ("rt", "kt", "bt", "at", "sqb", "ktg", "btg")}
        ex_ = {nm: P.sbuf("exs_" + nm, [128, 2, W], BF16) for nm in ("Lb", "Lk", "Ra", "Rr", "KH", "BH", "VB")}
        sq_ = {nm: P.sbuf("sqs_" + nm, [128, 128], BF16) for nm in
               ("N", "ak", "br", "kr", "NT", "T0", "P1T", "T", "KHt", "BHt", "Rat", "VBt", "Y", "W1", "W2", "Ub", "Ob")}
        W1b = P.sbuf("W1b", [128, 16, 128], BF16)
        Rrb = P.sbuf("Rrb", [128, 16, 128], BF16)
        KHtb = P.sbuf("KHtb", [128, 16, 128], BF16)
        BHtb = P.sbuf("BHtb", [128, 16, 128], BF16)
        Sv = P.sbuf("Sv", [128, 16, 64], F32)
        Svx = P.sbuf("Svx", [128, 16, 2, 64], F32)
        Sf = P.sbuf("SfS", [128, 16, 128], F32)
        Sb = P.sbuf("SbS", [128, 16, 128], BF16)
        So = P.sbuf("SoS", [128, 16, 128], F32)
        pp = [carve(C, i, 0, 512, "psS_p%d" % i) for i in range(2)]
        ptr = carve(C, 2, 0, 256, "psS_tr", BF16)
        pA = carve(C, 3, 0, 384, "psS_A")
        pB = carve(C, 2, 384, 512, "psS_B")
        pbig = [carve(C, 4 + i, 0, 512, "psS_big%d" % i) for i in range(4)]
        ppi = [0]

        def nextpp():
            b = pp[ppi[0] % 2]
            ppi[0] += 1
            return b

        def pv(col):
            return C.pv[:, col:col + 1]

        P.dma("pool", w_r[:], din["w_rw"].rearrange("(k p) c -> p k c", p=128), writes=[w_r])
        P.dma("pool", wl2[:], din["w_l2"][:, :], writes=[wl2])
        P.dma("pool", colm[:].rearrange("p b c -> p (b c)"), din["colmask"][:, :], writes=[colm])
        P.dma("sp", shs[:], din["shift_s"][:, :], writes=[shs])
        P.op("pool", lambda e: e.tensor_copy(bones_b[:], C.cm[:, CM_BONES:CM_BONES + 128]), reads=[C.cm], writes=[bones_b])
        P.op("pool", lambda e: e.memset(scanm[:], 1.0), writes=[scanm])
        P.op("pool", lambda e: e.memset(scanm[:, 0:W:4], 0.0), writes=[scanm])
        p_ = nextpp()
        for c in range(13):
            P.mm(p_, p_[:, c * 16:(c + 1) * 16], shs, shs[0:16, c * 128:(c + 1) * 128], C.ident_f, C.ident_f[0:16, 0:16])
        P.op("dve", lambda e: e.tensor_tensor(smu[:], p_[:, 0:208].rearrange("p (c b) -> p c b", b=16),
                                              bc(C.pv[:, PV_MU:PV_MU + 13], [128, 13, 16], [2]), ALU.mult),
             reads=[p_, C.pv], writes=[smu])

        def proj_shift(c, dst):
            q_ = nextpp()
            for k in range(8):
                P.mm(q_, q_[:, 0:W], w_r, w_r[:, k, c * 128:(c + 1) * 128], xTs, xTs[:, k, :], start=(k == 0), stop=(k == 7))
            P.op("act", lambda e: e.mul(bm[:], q_[:, 0:W], pv(PV_MU + c)), reads=[q_, C.pv], writes=[bm])
            q3 = q_[:, 0:W].rearrange("p (b t) -> p b t", t=4)
            d3 = dst[:].rearrange("p (b t) -> p b t", t=4)
            b3 = bm[:].rearrange("p (b t) -> p b t", t=4)
            P.op("dve", lambda e: e.scalar_tensor_tensor(d3[:, :, 1:4], q3[:, :, 1:4], pv(PV_OMM + c), b3[:, :, 0:3], ALU.mult, ALU.add),
                 reads=[q_, bm, C.pv], writes=[dst], partial=True)
            P.op("dve", lambda e: e.scalar_tensor_tensor(d3[:, :, 0:1], q3[:, :, 0:1], pv(PV_OMM + c), smu[:, c, :].unsqueeze(2), ALU.mult, ALU.add),
                 reads=[q_, smu, C.pv], writes=[dst], partial=True)
            P.op("act", lambda e: e.activation(shout[:, c, :].unsqueeze(2), q3[:, :, 3:4], AF.Copy), reads=[q_], writes=[shout], partial=True)

        proj_shift(12, wdad)
        P.op("act", lambda e: e.activation(lor[0:64, :], wdad[0:64, :], AF.Tanh), reads=[wdad], writes=[lor], partial=True)
        P.op("dve", lambda e: e.tensor_copy(lor[64:128, :], wdad[64:128, :]), reads=[wdad], writes=[lor], partial=True)
        hm = C.cm[:, CM_HM:CM_HM + 2]
        hm3 = bc(hm, [128, 2, W], [2])
        for hp in range(4):
            r, k, vz, sg, a, cs, eg, eng, kk, rn, km, bv = (tmp[n] for n in ("r", "k", "vz", "sg", "a", "cs", "eg", "eng", "kk", "rn", "km", "bv"))
            proj_shift(hp, r)
            proj_shift(4 + hp, k)
            proj_shift(8 + hp, vz)
            q_ = nextpp()
            P.mm(q_, q_[:, 0:W], wl2, wl2[0:64, hp * 128:(hp + 1) * 128], lor, lor[0:64, :])
            P.op("act", lambda e: e.activation(sg[:], q_[:, 0:W], AF.Sigmoid, bias=pv(PV_W0 + hp), scale=1.0), reads=[q_, C.pv], writes=[sg])
            q2 = nextpp()
            P.mm(q2, q2[:, 0:W], wl2, wl2[64:128, hp * 128:(hp + 1) * 128], lor, lor[64:128, :])
            P.op("act", lambda e: e.activation(a[:], q2[:, 0:W], AF.Sigmoid, bias=pv(PV_A0 + hp), scale=1.0), reads=[q2, C.pv], writes=[a])
            P.op("dve", lambda e: e.tensor_tensor_scan(cs[:], scanm[:], sg[:], 0.0, ALU.mult, ALU.add), reads=[scanm, sg], writes=[cs])
            P.op("act", lambda e: e.activation(eg[:], cs[:], AF.Exp, scale=-C0), reads=[cs], writes=[eg])
            P.op("act", lambda e: e.activation(eng[:], cs[:], AF.Exp, scale=C0), reads=[cs], writes=[eng])
            P.op("pool", lambda e: e.tensor_tensor(cs[:], cs[:], sg[:], ALU.subtract), reads=[cs, sg], writes=[cs])
            P.op("act", lambda e: e.activation(cs[:], cs[:], AF.Exp, scale=-C0), reads=[cs], writes=[cs])
            P.op("dve", lambda e: e.tensor_scalar(kk[:], k[:], pv(PV_KK + hp), None, ALU.mult), reads=[k, C.pv], writes=[kk])
            P.op("pool", lambda e: e.tensor_tensor(tb["sqb"][:], kk[:], kk[:], ALU.mult), reads=[kk], writes=[tb["sqb"]])
            q3_ = nextpp()
            P.mm(q3_, q3_[:, 0:W], bones_b, bones_b[:], tb["sqb"], tb["sqb"][:])
            P.op("dve", lambda e: e.tensor_scalar(rn[:], q3_[:, 0:W], 1e-24, None, ALU.max), reads=[q3_], writes=[rn])
            P.op("act", lambda e: e.activation(rn[:], rn[:], AF.Sqrt), reads=[rn], writes=[rn])
            P.op("dve", lambda e: e.reciprocal(rn[:], rn[:]), reads=[rn], writes=[rn])
            P.op("dve", lambda e: e.tensor_tensor(kk[:], kk[:], rn[:], ALU.mult), reads=[kk, rn], writes=[kk])
            P.op("dve", lambda e: e.tensor_scalar(km[:], a[:], -1.0, pv(PV_KA + hp), ALU.add, ALU.mult), reads=[a, C.pv], writes=[km])
            P.op("dve", lambda e: e.scalar_tensor_tensor(km[:], km[:], 1.0, k[:], ALU.add, ALU.mult), reads=[km, k], writes=[km])
            P.op("pool", lambda e: e.tensor_tensor(bv[:], kk[:], a[:], ALU.mult), reads=[kk, a], writes=[bv])
            P.op("pool", lambda e: e.tensor_tensor(tb["rt"][:], r[:], eg[:], ALU.mult), reads=[r, eg], writes=[tb["rt"]])
            P.op("dve", lambda e: e.tensor_tensor(tb["kt"][:], km[:], eng[:], ALU.mult), reads=[km, eng], writes=[tb["kt"]])
            P.op("pool", lambda e: e.tensor_tensor(tb["bt"][:], bv[:], eng[:], ALU.mult), reads=[bv, eng], writes=[tb["bt"]])
            P.op("dve", lambda e: e.scalar_tensor_tensor(tb["at"][:], kk[:], -1.0, cs[:], ALU.mult, ALU.mult), reads=[kk, cs], writes=[tb["at"]])
            P.op("dve", lambda e: e.scalar_tensor_tensor(tb["sqb"][:], r[:], pv(PV_RK + hp), km[:], ALU.mult, ALU.mult), reads=[r, km, C.pv], writes=[tb["sqb"]])
            q4 = nextpp()
            P.mm(q4, q4[:, 0:W], bones_b, bones_b[:], tb["sqb"], tb["sqb"][:])
            P.op("dve", lambda e: e.tensor_tensor(tmp["bonus"][:], q4[:, 0:W], vz[:], ALU.mult), reads=[q4, vz], writes=[tmp["bonus"]])
            q5 = nextpp()
            for kq in range(8):
                P.mm(q5, q5[:, 0:W], w_r, w_r[:, kq, (13 + hp) * 128:(14 + hp) * 128], xTs, xTs[:, kq, :], start=(kq == 0), stop=(kq == 7))
            P.op("act", lambda e: e.activation(tmp["gate"][:], q5[:, 0:W], AF.Silu), reads=[q5], writes=[tmp["gate"]])
            gcol = eg[:, 3:W:4]
            gf = tmp["gf"]
            P.op("pool", lambda e: e.tensor_copy(gf[:].rearrange("p (b t) -> p b t", t=4), bc(gcol, [128, 16, 4], [2])), reads=[eg], writes=[gf])
            P.op("dve", lambda e: e.tensor_tensor(tb["ktg"][:], tb["kt"][:], gf[:], ALU.mult), reads=[tb["kt"], gf], writes=[tb["ktg"]])
            P.op("pool", lambda e: e.tensor_tensor(tb["btg"][:], tb["bt"][:], gf[:], ALU.mult), reads=[tb["bt"], gf], writes=[tb["btg"]])

            def ex(x):
                return bc(x[:], [128, 2, W], [1])
            for i, (nm, src) in enumerate((("Lb", tb["bt"]), ("Lk", tb["kt"]), ("Ra", tb["at"]), ("Rr", tb["rt"]),
                                           ("KH", tb["ktg"]), ("BH", tb["btg"]), ("VB", vz))):
                P.op("dve" if i % 2 == 0 else "pool", lambda e: e.tensor_tensor(ex_[nm][:], ex(src), hm3, ALU.mult),
                     reads=[src, C.cm], writes=[ex_[nm]])

            def f2(nm):
                return ex_[nm][:].rearrange("p h s -> p (h s)")
            P.mm(pA, pA[:, 0:128], ex_["Lb"], f2("Lb"), ex_["Ra"], f2("Ra"))
            P.mm(pA, pA[:, 128:256], ex_["Lk"], f2("Lk"), ex_["Ra"], f2("Ra"))
            P.mm(pA, pA[:, 256:384], ex_["Ra"], f2("Ra"), ex_["Lb"], f2("Lb"))
            P.op("dve", lambda e: e.tensor_tensor(sq_["N"][:], pA[:, 0:128], cms[:, CS_STRICT:CS_STRICT + 128], ALU.mult), reads=[pA, cms], writes=[sq_["N"]])
            P.op("dve", lambda e: e.tensor_tensor(sq_["ak"][:], pA[:, 128:256], cms[:, CS_STRICT:CS_STRICT + 128], ALU.mult), reads=[pA, cms], writes=[sq_["ak"]])
            P.op("dve", lambda e: e.tensor_tensor(sq_["NT"][:], pA[:, 256:384], cms[:, CS_STRICT_T:CS_STRICT_T + 128], ALU.mult), reads=[pA, cms], writes=[sq_["NT"]])
            P.mm(pA, pA[:, 0:128], ex_["Lb"], f2("Lb"), ex_["Rr"], f2("Rr"))
            P.mm(pA, pA[:, 128:256], ex_["Lk"], f2("Lk"), ex_["Rr"], f2("Rr"))
            P.op("dve", lambda e: e.tensor_tensor(sq_["br"][:], pA[:, 0:128], cms[:, CS_INCL:CS_INCL + 128], ALU.mult), reads=[pA, cms], writes=[sq_["br"]])
            P.op("dve", lambda e: e.tensor_tensor(sq_["kr"][:], pA[:, 128:256], cms[:, CS_INCL:CS_INCL + 128], ALU.mult), reads=[pA, cms], writes=[sq_["kr"]])
            P.op("pool", lambda e: e.tensor_tensor(sq_["T0"][:], sq_["N"][:], C.cm[:, CM_EYE:CM_EYE + 128], ALU.add), reads=[sq_["N"], C.cm], writes=[sq_["T0"]])
            P.mm(pA, pA[:, 0:128], sq_["N"], sq_["N"][:], sq_["NT"], sq_["NT"][:])
            P.op("dve", lambda e: e.tensor_copy(sq_["P1T"][:], pA[:, 0:128]), reads=[pA], writes=[sq_["P1T"]])
            P.mm(pA, pA[:, 0:128], sq_["P1T"], sq_["P1T"][:], sq_["T0"], sq_["T0"][:])
            P.op("dve", lambda e: e.tensor_tensor(sq_["T"][:], pA[:, 0:128], sq_["T0"][:], ALU.add), reads=[pA, sq_["T0"]], writes=[sq_["T"]])
            for i, (src, dst) in enumerate((("KH", "KHt"), ("BH", "BHt"), ("Ra", "Rat"), ("VB", "VBt"))):
                P.tr(ptr, ptr[:, i * 128:(i + 1) * 128], ex_[src], f2(src), C.ident_b, C.ident_b[:])
            for i, dst in enumerate(("KHt", "BHt", "Rat", "VBt")):
                evac(P, 0, sq_[dst], sq_[dst][:], ptr, ptr[:, i * 128:(i + 1) * 128])
            P.mm(pA, pA[:, 0:128], sq_["ak"], sq_["ak"][:], sq_["VBt"], sq_["VBt"][:])
            P.op("act", lambda e: e.activation(sq_["Y"][:], pA[:, 0:128], AF.Copy), reads=[pA], writes=[sq_["Y"]])
            P.mm(pA, pA[:, 128:256], sq_["T"], sq_["T"][:], sq_["Y"], sq_["Y"][:])
            P.op("act", lambda e: e.activation(sq_["W2"][:], pA[:, 128:256], AF.Copy), reads=[pA], writes=[sq_["W2"]])
            P.mm(pA, pA[:, 256:384], sq_["Rat"], sq_["Rat"][:], sq_["T"], sq_["T"][:])
            P.op("dve", lambda e: e.tensor_copy(sq_["W1"][:], pA[:, 256:384]), reads=[pA], writes=[sq_["W1"]])
            P.dma("sp", Sv[:], din["wkv_s"][:, 2 * hp:2 * hp + 2, :, :].rearrange("b h v k -> (h v) b k"), writes=[Sv], partial=False)
            P.op("dve", lambda e: e.tensor_tensor(Svx[:], bc(Sv[:], [128, 16, 2, 64], [2]), bc(hm, [128, 16, 2, 64], [1, 3]), ALU.mult),
                 reads=[Sv, C.cm], writes=[Svx])
            for b in range(16):
                pb_ = pbig[b // 4]
                P.mm(pb_, pb_[:, (b % 4) * 128:(b % 4 + 1) * 128], Svx, Svx[:, b].rearrange("p h k -> p (h k)"), C.ident_f, C.ident_f[:])
            for i in range(4):
                P.op("dve", lambda e: e.tensor_copy(Sf[:, 4 * i:4 * i + 4, :].rearrange("p b c -> p (b c)"), pbig[i][:]), reads=[pbig[i]], writes=[Sf], partial=True)
                P.op("act", lambda e: e.activation(Sb[:, 4 * i:4 * i + 4, :].rearrange("p b c -> p (b c)"), pbig[i][:], AF.Copy), reads=[pbig[i]], writes=[Sb], partial=True)
            P.op("dve", lambda e: e.tensor_tensor(W1b[:], bc(sq_["W1"][:], [128, 16, 128], [1]), colm[:], ALU.mult), reads=[sq_["W1"], colm], writes=[W1b])
            P.op("pool", lambda e: e.tensor_tensor(Rrb[:], bc(f2("Rr"), [128, 16, 128], [1]), colm[:], ALU.mult), reads=[ex_["Rr"], colm], writes=[Rrb])
            rowm = bc(cms[:, CS_ROW:CS_ROW + 16], [128, 16, 128], [2])
            P.op("dve", lambda e: e.tensor_tensor(KHtb[:], bc(sq_["KHt"][:], [128, 16, 128], [1]), rowm, ALU.mult), reads=[sq_["KHt"], cms], writes=[KHtb])
            P.op("pool", lambda e: e.tensor_tensor(BHtb[:], bc(sq_["BHt"][:], [128, 16, 128], [1]), rowm, ALU.mult), reads=[sq_["BHt"], cms], writes=[BHtb])
            for b in range(16):
                P.mm(pB, pB[:, 0:128], W1b, W1b[:, b, :], Sb, Sb[:, b, :], start=(b == 0), stop=(b == 15))
            P.op("dve", lambda e: e.tensor_tensor(sq_["Ub"][:], pB[:, 0:128], sq_["W2"][:], ALU.add), reads=[pB, sq_["W2"]], writes=[sq_["Ub"]])
            for b in range(16):
                P.mm(pA, pA[:, 0:128], Rrb, Rrb[:, b, :], Sb, Sb[:, b, :], start=(b == 0), stop=False)
            P.mm(pA, pA[:, 0:128], sq_["br"], sq_["br"][:], sq_["Ub"], sq_["Ub"][:], start=False, stop=False)
            P.mm(pA, pA[:, 0:128], sq_["kr"], sq_["kr"][:], sq_["VBt"], sq_["VBt"][:], start=False, stop=True)
            P.op("act", lambda e: e.activation(sq_["Ob"][:], pA[:, 0:128], AF.Copy), reads=[pA], writes=[sq_["Ob"]])
            for b in range(16):
                pb_ = pbig[b // 4]
                sl = pb_[:, (b % 4) * 128:(b % 4 + 1) * 128]
                P.mm(pb_, sl, KHtb, KHtb[:, b, :], sq_["VBt"], sq_["VBt"][:], start=True, stop=False)
                P.mm(pb_, sl, BHtb, BHtb[:, b, :], sq_["Ub"], sq_["Ub"][:], start=False, stop=True)
            for i in range(4):
                sfv = Sf[:, 4 * i:4 * i + 4, :]
                P.op("dve", lambda e: e.tensor_tensor(sfv, sfv, bc(gcol[:, 4 * i:4 * i + 4], [128, 4, 128], [2]), ALU.mult), reads=[Sf, eg], writes=[Sf])
                P.op("dve", lambda e: e.tensor_tensor(sfv, sfv, pbig[i][:].rearrange("p (b c) -> p b c", b=4), ALU.add), reads=[Sf, pbig[i]], writes=[Sf])
            for b in range(16):
                pb_ = pbig[b // 4]
                P.mm(pb_, pb_[:, (b % 4) * 128:(b % 4 + 1) * 128], Sf, Sf[:, b, :], C.ident_f, C.ident_f[:])
            for i in range(4):
                evac(P, i, So, So[:, 4 * i:4 * i + 4, :].rearrange("p b c -> p (b c)"), pbig[i], pbig[i][:])
            for hh in range(2):
                P.dma("sp", dout["wkv_so"][:, 2 * hp + hh, :, :].rearrange("b v k -> v b k"),
                      So[hh * 64:(hh + 1) * 64, :, hh * 64:(hh + 1) * 64], reads=[So])
            py = nextpp()
            P.mm(py, py[:, 0:W], sq_["Ob"], sq_["Ob"][:], C.selb, C.selb[:])
            yr, cen, sq, rs = tmp["yr"], tmp["cen"], tmp["sq"], tmp["rs"]
            P.op("act", lambda e: e.activation(yr[:], py[:, 0:W], AF.Copy), reads=[py], writes=[yr])
            pm = nextpp()
            P.mm(pm, pm[:, 0:W], C.bones_f, C.bones_f[:], yr, yr[:])
            P.op("dve", lambda e: e.scalar_tensor_tensor(cen[:], pm[:, 0:W], -1.0 / 64.0, yr[:], ALU.mult, ALU.add), reads=[pm, yr], writes=[cen])
            P.op("pool", lambda e: e.tensor_tensor(sq[:], cen[:], cen[:], ALU.mult), reads=[cen], writes=[sq])
            pv_ = nextpp()
            P.mm(pv_, pv_[:, 0:W], C.bones_f, C.bones_f[:], sq, sq[:])
            P.op("dve", lambda e: e.tensor_scalar(rs[:], pv_[:, 0:W], 1.0 / 64.0, GN_EPS, ALU.mult, ALU.add), reads=[pv_], writes=[rs])
            P.op("act", lambda e: e.activation(rs[:], rs[:], AF.Sqrt), reads=[rs], writes=[rs])
            P.op("dve", lambda e: e.reciprocal(rs[:], rs[:]), reads=[rs], writes=[rs])
            P.op("dve", lambda e: e.tensor_tensor(cen[:], cen[:], rs[:], ALU.mult), reads=[cen, rs], writes=[cen])
            P.op("dve", lambda e: e.tensor_scalar(cen[:], cen[:], pv(PV_LG + hp), pv(PV_LB + hp), ALU.mult, ALU.add), reads=[cen, C.pv], writes=[cen])
            P.op("pool", lambda e: e.tensor_tensor(cen[:], cen[:], tmp["bonus"][:], ALU.add), reads=[cen, tmp["bonus"]], writes=[cen])
            P.op("pool", lambda e: e.tensor_tensor(yrS[:, hp, :], cen[:], tmp["gate"][:], ALU.mult), reads=[cen, tmp["gate"]], writes=[yrS], partial=True)
        for c0 in range(0, 13, 4):
            n = min(4, 13 - c0)
            q_ = nextpp()
            for c in range(n):
                P.mm(q_, q_[0:16, c * 128:(c + 1) * 128], shout, shout[:, c0 + c, :], C.ident_f, C.ident_f[:])
            P.op("dve", lambda e: e.tensor_copy(sho2[0:16, c0 * 128:(c0 + n) * 128], q_[0:16, 0:n * 128]), reads=[q_], writes=[sho2], partial=True)
        P.dma("sp", dout["shift_so"][:, :], sho2[:], reads=[sho2])


def emit_attn_sample(P, C, din, dout, xTs, cms, yaS):
    with scope(P):
        wq = P.sbuf("wqS", [128, 8, 5120], BF16)
        qTs = P.sbuf("qTs", [128, 12, 64], BF16)
        kvn = P.sbuf("kvn", [64, 3, 2, 512], F32)
        kvnb = P.sbuf("kvnb", [64, 3, 2, 512], BF16)
        szs = P.sbuf("szs", [128, 4, 64], F32)
        oaT = P.sbuf("oaT", [128, 4, 64], F32)
        kvt = [P.sbuf("kvtS%d" % i, [128, 4, 2, 512], BF16) for i in range(2)]
        kvnew = [P.sbuf("kvnw%d" % i, [4, 2, 512], BF16) for i in range(2)]
        KT = [P.sbuf("KTs%d" % i, [128, 4, 512], BF16) for i in range(2)]
        KTn = [P.sbuf("KTn%d" % i, [128, 4, 4], BF16) for i in range(2)]
        ss = [P.sbuf("ssS%d" % i, [128, 516], F32) for i in range(2)]
        pb = [P.sbuf("pbS%d" % i, [128, 516], BF16) for i in range(2)]
        PT = [P.sbuf("PTs%d" % i, [128, 5, 128], BF16) for i in range(2)]
        og = [P.sbuf("ogS%d" % i, [128, 3, 128], F32) for i in range(2)]
        stt = [P.sbuf("sttS%d" % i, [128, 24], F32) for i in range(2)]
        ob16 = [P.sbuf("ob16S%d" % i, [128, 128], BF16) for i in range(2)]
        pp = [carve(C, i, 0, 512, "psSA_p%d" % i) for i in range(2)]
        pkt = [carve(C, 2 + i, 0, 512, "psSA_kt%d" % i, BF16) for i in range(2)]
        pS = carve(C, 4, 0, 512, "psSA_S")
        pSn = carve(C, 5, 0, 16, "psSA_Sn")
        pKn = carve(C, 5, 16, 32, "psSA_Kn", BF16)
        pO = carve(C, 5, 128, 256, "psSA_O")
        pT = carve(C, 6, 0, 384, "psSA_T", BF16)
        pOT = carve(C, 7, 0, 128, "psSA_OT")
        P.dma("pool", wq[:], din["w_qkvz"].rearrange("(k p) c -> p k c", p=128), writes=[wq])
        P.op("dve", lambda e: e.memset(pS[:], 0.0), writes=[pS])
        P.op("dve", lambda e: e.memset(pSn[:], 0.0), writes=[pSn])
        P.op("dve", lambda e: e.memset(pO[:], 0.0), writes=[pO])
        ec = [0]
        for c in range(12):
            p_ = pp[c % 2]
            for k in range(8):
                P.mm(p_, p_[:, 0:64], wq, wq[:, k, c * 128:(c + 1) * 128], xTs, xTs[:, k, :], start=(k == 0), stop=(k == 7))
            evac(P, c, qTs, qTs[:, c, :], p_, p_[:, 0:64])
        for hh in range(4):
            p_ = pp[hh % 2]
            for k in range(8):
                P.mm(p_, p_[:, 0:64], wq, wq[:, k, 4608 + hh * 128:4608 + (hh + 1) * 128], xTs, xTs[:, k, :], start=(k == 0), stop=(k == 7))
            P.op("act", lambda e: e.activation(szs[:, hh, :], p_[:, 0:64], AF.Silu), reads=[p_], writes=[szs], partial=True)
        for g in range(3):
            for kv in range(2):
                p_ = pp[(g * 2 + kv) % 2]
                c0 = 1536 * (1 + kv) + g * 512
                for k in range(8):
                    P.mm(p_, p_[0:64, :], xTs, xTs[:, k, :], wq, wq[:, k, c0:c0 + 512], start=(k == 0), stop=(k == 7))
                evac(P, g * 2 + kv, kvn, kvn[:, g, kv, :], p_, p_[0:64, :])
        P.op("pool", lambda e: e.tensor_copy(kvnb[:], kvn[:]), reads=[kvn], writes=[kvnb])
        for g in range(3):
            P.dma("sp", dout["kvs%d" % (g + 1)].rearrange("b t kv h e -> (b t) kv (h e)"), kvn[:, g], reads=[kvn])
        it = 0
        for b in range(16):
            ogb, sb_ = og[b % 2], stt[b % 2]
            for g in range(3):
                i2 = it % 2
                it += 1
                ntile = 1 if g == 0 else 4
                nk = ntile * 128
                kt_, kn_, KT_, KTn_, ss_, pb_, PT_ = kvt[i2], kvnew[i2], KT[i2], KTn[i2], ss[i2], pb[i2], PT[i2]
                cache = din["cache%d" % (g + 1)]
                d = GROUPS[g][1]
                if g == 0:
                    P.dma("pool", kt_[:, 0].rearrange("p kv c -> p (kv c)"), cache[b].rearrange("r kv h e -> r (kv h e)"), writes=[kt_], partial=False)
                else:
                    for cl in range(4):
                        src = cache[b, cl:GROUPS[g][0]:d].rearrange("r kv h e -> r (kv h e)")
                        P.dma("pool", kt_[:, cl].rearrange("p kv c -> p (kv c)"), src, writes=[kt_], partial=(cl > 0))
                P.dma("sp", kn_[:], kvnb[4 * b:4 * b + 4, g], reads=[kvnb], writes=[kn_], partial=False)
                for h in range(4):
                    pk = pkt[h // 2]
                    for cl in range(ntile):
                        P.tr(pk, pk[:, (h % 2) * 512 + cl * 128:(h % 2) * 512 + (cl + 1) * 128], kt_, kt_[:, cl, 0, h * 128:(h + 1) * 128],
                             C.ident_b, C.ident_b[:])
                    P.tr(pKn, pKn[:, h * 4:(h + 1) * 4], kn_, kn_[0:4, 0, h * 128:(h + 1) * 128], C.ident_b, C.ident_b[0:4, 0:4])
                for j in range(2):
                    evac(P, ec[0], KT_, KT_[:, 2 * j:2 * j + 2, 0:nk], pkt[j], pkt[j][:].rearrange("p (h k) -> p h k", h=2)[:, :, 0:nk])
                    ec[0] += 1
                evac(P, ec[0], KTn_, KTn_[:].rearrange("p h k -> p (h k)"), pKn, pKn[:, 0:16])
                ec[0] += 1
                for h in range(4):
                    qv = qTs[:, g * 4 + h, 4 * b:4 * b + 4]
                    P.mm(pS, pS[32 * h:32 * h + 4, 0:nk], qTs, qv, KT_, KT_[:, h, 0:nk], tile_position=(0, 32 * h))
                    P.mm(pSn, pSn[32 * h:32 * h + 4, 0:4], qTs, qv, KTn_, KTn_[:, h, :], tile_position=(0, 32 * h))
                mk = cms[:, CS_G1:CS_G1 + 132] if g == 0 else cms[:, CS_G23:CS_G23 + 516]
                P.op("dve", lambda e: e.scalar_tensor_tensor(ss_[:, 0:nk], pS[:, 0:nk], SCALE, mk[:, 0:nk], ALU.mult, ALU.add),
                     reads=[pS, cms], writes=[ss_], partial=True)
                P.op("dve", lambda e: e.scalar_tensor_tensor(ss_[:, nk:nk + 4], pSn[:, 0:4], SCALE, mk[:, nk:nk + 4], ALU.mult, ALU.add),
                     reads=[pSn, cms], writes=[ss_], partial=True)
                c8 = g * 8
                P.op("dve", lambda e: e.tensor_reduce(sb_[:, c8:c8 + 1], ss_[:, 0:nk + 4], AX.X, ALU.max, negate=True), reads=[ss_], writes=[sb_], partial=True)
                P.op("act", lambda e: e.activation(pb_[:, 0:nk + 4], ss_[:, 0:nk + 4], AF.Exp, bias=sb_[:, c8:c8 + 1], scale=1.0,
                                                   accum_out=sb_[:, c8 + 1:c8 + 2]), reads=[ss_, sb_], writes=[pb_, sb_], partial=True)
                for cl in range(ntile):
                    P.tr(pT, pT[:, cl * 128:(cl + 1) * 128], pb_, pb_[:, cl * 128:(cl + 1) * 128], C.ident_b, C.ident_b[:])
                P.tr(pT, pT[0:4, 640:768], pb_, pb_[:, nk:nk + 4], C.ident_b, C.ident_b[:])
                evac(P, ec[0], PT_, PT_[:, 0:ntile, :].rearrange("p c q -> p (c q)"), pT, pT[:, 0:nk])
                ec[0] += 1
                evac(P, ec[0], PT_, PT_[0:4, 4, :], pT, pT[0:4, 640:768])
                ec[0] += 1
                for h in range(4):
                    for cl in range(ntile):
                        P.mm(pO, pO[32 * h:32 * h + 4, :], PT_, PT_[:, cl, 32 * h:32 * h + 4], kt_, kt_[:, cl, 1, h * 128:(h + 1) * 128],
                             start=(cl == 0), stop=False, tile_position=(0, 32 * h))
                    P.mm(pO, pO[32 * h:32 * h + 4, :], PT_, PT_[0:4, 4, 32 * h:32 * h + 4], kn_, kn_[0:4, 1, h * 128:(h + 1) * 128],
                         start=False, stop=True, tile_position=(0, 32 * h))
                P.op("dve", lambda e: e.reciprocal(sb_[:, c8 + 2:c8 + 3], sb_[:, c8 + 1:c8 + 2]), reads=[sb_], writes=[sb_], partial=True)
                P.op("dve", lambda e: e.tensor_scalar(ogb[:, g, :], pO[:], sb_[:, c8 + 2:c8 + 3], None, ALU.mult), reads=[pO, sb_], writes=[ogb], partial=True)
                P.op("act", lambda e: e.activation(sb_[:, c8 + 3:c8 + 4], sb_[:, c8 + 1:c8 + 2], AF.Ln), reads=[sb_], writes=[sb_], partial=True)
                P.op("dve", lambda e: e.tensor_tensor(sb_[:, c8 + 4:c8 + 5], sb_[:, c8 + 3:c8 + 4], sb_[:, c8:c8 + 1], ALU.subtract), reads=[sb_], writes=[sb_], partial=True)
            def col(i):
                return sb_[:, i:i + 1]
            P.op("dve", lambda e: e.tensor_tensor(col(5), col(4), col(12), ALU.max), reads=[sb_], writes=[sb_], partial=True)
            P.op("dve", lambda e: e.tensor_tensor(col(5), col(5), col(20), ALU.max), reads=[sb_], writes=[sb_], partial=True)
            for g in range(3):
                P.op("dve", lambda e: e.tensor_tensor(col(8 * g + 6), col(8 * g + 4), col(5), ALU.subtract), reads=[sb_], writes=[sb_], partial=True)
                P.op("act", lambda e: e.activation(col(8 * g + 6), col(8 * g + 6), AF.Exp), reads=[sb_], writes=[sb_], partial=True)
            P.op("dve", lambda e: e.tensor_tensor(col(7), col(6), col(14), ALU.add), reads=[sb_], writes=[sb_], partial=True)
            P.op("dve", lambda e: e.tensor_tensor(col(7), col(7), col(22), ALU.add), reads=[sb_], writes=[sb_], partial=True)
            P.op("dve", lambda e: e.reciprocal(col(7), col(7)), reads=[sb_], writes=[sb_], partial=True)
            for g in range(3):
                P.op("dve", lambda e: e.tensor_tensor(col(8 * g + 6), col(8 * g + 6), col(7), ALU.mult), reads=[sb_], writes=[sb_], partial=True)
            P.op("dve", lambda e: e.tensor_scalar(ogb[:, 0, :], ogb[:, 0, :], col(6), None, ALU.mult), reads=[ogb, sb_], writes=[ogb])
            P.op("dve", lambda e: e.scalar_tensor_tensor(ogb[:, 0, :], ogb[:, 1, :], col(14), ogb[:, 0, :], ALU.mult, ALU.add), reads=[ogb, sb_], writes=[ogb])
            P.op("dve", lambda e: e.scalar_tensor_tensor(ogb[:, 0, :], ogb[:, 2, :], col(22), ogb[:, 0, :], ALU.mult, ALU.add), reads=[ogb, sb_], writes=[ogb])
            P.mm(pOT, pOT[:], ogb, ogb[:, 0, :], C.ident_f, C.ident_f[:])
            evac(P, b, oaT, oaT[:, :, 4 * b:4 * b + 4], pOT, pOT[:].rearrange("p (h x) -> p h x", x=32)[:, :, 0:4])
        P.op("dve", lambda e: e.tensor_tensor(yaS[:], oaT[:], szs[:], ALU.mult), reads=[oaT, szs], writes=[yaS])
```

```python
import contextlib
import numpy as np
import concourse.bass as bass
import concourse.mybir as mybir
from concourse.bass_utils import run_bass_kernel_spmd

F32 = mybir.dt.float32
BF16 = mybir.dt.bfloat16
I32 = mybir.dt.int32
ALU = mybir.AluOpType
AF = mybir.ActivationFunctionType
AX = mybir.AxisListType

NCORES = 8
RING = 20
EAGER_SIGNAL = False


class Buf:
    __slots__ = ("name", "t", "writers", "readers", "prev_readers", "bank")

    def __init__(self, name, t, bank=None):
        self.name = name
        self.t = t
        self.bank = bank
        self.writers = {}
        self.readers = {}
        self.prev_readers = {}

    def __getitem__(self, idx):
        return self.t[idx]


def _merge(dst, src):
    for k, (s, v) in src.items():
        if k not in dst or dst[k][1] < v:
            dst[k] = (s, v)


class Prog:
    def __init__(self, nc, stack):
        self.nc = nc
        self.stack = stack
        self.eng = {"pe": nc.tensor, "act": nc.scalar, "dve": nc.vector, "pool": nc.gpsimd, "sp": nc.sync}
        self.esem = {}
        self.seq = {}
        self.known = {}
        self.sig = {}
        self.sig_idx = {}
        self.sigcount = {}
        self.last_inst = {}
        self.insts = {}
        for e in self.eng:
            self.esem[e] = stack.enter_context(nc.semaphore("es_" + e))
            self.seq[e] = 0
            self.known[e] = {}
            self.sig[e] = []
            self.sig_idx[e] = []
            self.sigcount[e] = 0
            self.last_inst[e] = None
            self.insts[e] = []
        self.ring = {}
        self.ring_val = {}
        self.dma_i = {}
        for q in ("sp", "pool", "act"):
            self.ring[q] = [stack.enter_context(nc.semaphore("dq_%s_%d" % (q, i))) for i in range(RING)]
            self.ring_val[q] = [0] * RING
            self.dma_i[q] = 0
        self.nbuf = 0
        self.bank_rd = {}

    def sbuf(self, name, shape, dtype):
        t = self.stack.enter_context(self.nc.sbuf_tensor("sb_" + name, list(shape), dtype))
        return Buf(name, t)

    def psum(self, name, shape, dtype):
        t = self.stack.enter_context(self.nc.psum_tensor(name, list(shape), dtype))
        return Buf(name, t)

    def dram(self, name, shape, dtype, kind="Internal"):
        t = self.nc.dram_tensor(name, list(shape), dtype, kind=kind)
        return Buf(name, t.ap())

    def _deps(self, reads, writes, partial, eng=None):
        deps = {}
        for b in list(reads) + list(writes):
            if b.bank is not None:
                for e2, (k2, ev2) in self.bank_rd.setdefault(b.bank, {}).items():
                    if e2 != eng:
                        _merge(deps, {k2: ev2})
        for b in reads:
            _merge(deps, b.writers)
        for b in writes:
            _merge(deps, b.prev_readers)
            _merge(deps, b.readers)
            if not partial:
                _merge(deps, b.writers)
        return deps

    def _resolve(self, e, idx):
        sig = self.sig[e]
        import bisect
        pos = bisect.bisect_left(self.sig_idx[e], idx)
        if pos < len(sig):
            return self.sig_idx[e][pos], sig[pos]
        self.sigcount[e] += 1
        self.insts[e][idx - 1].then_inc(self.esem[e], 1)
        self.sig_idx[e].append(idx)
        sig.append(self.sigcount[e])
        return idx, self.sigcount[e]

    def _wait(self, eng, deps):
        E = self.eng[eng]
        kn = self.known[eng]
        for k, (s, v) in deps.items():
            if eng == "pe" and k == "e_pe":
                continue
            if kn.get(k, 0) >= v:
                continue
            if s is None:
                e = k[2:]
                idx2, cnt = self._resolve(e, v)
                E.wait_ge(self.esem[e], cnt)
                kn[k] = idx2
            else:
                E.wait_ge(s, v)
                kn[k] = v

    def _record(self, ev_key, ev, reads, writes, partial):
        for b in reads:
            _merge(b.readers, {ev_key: ev})
        for b in writes:
            if b.readers or not partial:
                b.prev_readers = b.readers
                b.readers = {}
                b.writers = {ev_key: ev}
            else:
                _merge(b.writers, {ev_key: ev})

    opbudget = None
    opcount = 0

    def op(self, eng, fn, reads=(), writes=(), partial=False):
        Prog.opcount += 1
        if Prog.opbudget is not None and Prog.opcount > Prog.opbudget:
            return None
        deps = self._deps(reads, writes, partial, eng)
        self._wait(eng, deps)
        inst = fn(self.eng[eng])
        self.seq[eng] += 1
        self.last_inst[eng] = inst
        self.insts[eng].append(inst)
        if EAGER_SIGNAL:
            self.sigcount[eng] += 1
            inst.then_inc(self.esem[eng], 1)
            self.sig_idx[eng].append(self.seq[eng])
            self.sig[eng].append(self.sigcount[eng])
        self._record("e_" + eng, (None, self.seq[eng]), reads, writes, partial)
        for b in list(reads) + list(writes):
            if b.bank is not None:
                self.bank_rd.setdefault(b.bank, {})[eng] = ("e_" + eng, (None, self.seq[eng]))
        return inst

    def dma(self, q, out, in_, reads=(), writes=(), partial=True, **kw):
        Prog.opcount += 1
        if Prog.opbudget is not None and Prog.opcount > Prog.opbudget and not kw.pop("always", False):
            return None
        kw.pop("always", None)
        i = self.dma_i[q]
        self.dma_i[q] += 1
        slot = i % RING
        sem = self.ring[q][slot]
        prev = self.ring_val[q][slot]
        key = "d_%s_%d" % (q, slot)
        deps = self._deps(reads, writes, partial)
        if prev > 0:
            _merge(deps, {key: (sem, prev)})
        self._wait(q, deps)
        inst = self.eng[q].dma_start(out=out, in_=in_, **kw)
        inst.then_inc(sem, 16)
        self.ring_val[q][slot] = prev + 16
        self._record(key, (sem, prev + 16), reads, writes, partial)
        return inst

    def finish(self):
        deps = {}
        for q in self.ring:
            for slot in range(RING):
                if self.ring_val[q][slot] > 0:
                    deps["d_%s_%d" % (q, slot)] = (self.ring[q][slot], self.ring_val[q][slot])
        for e in self.eng:
            if self.seq[e] > 0:
                deps["e_" + e] = (None, self.seq[e])
        self._wait("sp", deps)

    def mm(self, out_b, out_ap, lhsT_b, lhsT_ap, rhs_b, rhs_ap, start=True, stop=True, **kw):
        rd = [b for b in (lhsT_b, rhs_b) if b is not None]
        return self.op("pe", lambda e: e.matmul(out_ap, lhsT_ap, rhs_ap, start=start, stop=stop, **kw),
                       reads=rd, writes=[out_b], partial=True)

    def tr(self, out_b, out_ap, in_b, in_ap, ident_b, ident_ap):
        return self.op("pe", lambda e: e.transpose(out_ap, in_ap, ident_ap),
                       reads=[in_b, ident_b], writes=[out_b], partial=True)


D = 1024
SEQ = 8192
NB = 2
R_HEADS = 8
SHIFT_COLS = 1664
RW_COLS = 2176
Q0, K0, V0, ZA0, GR0, GA0 = 2176, 3712, 5248, 6784, 7296, 8320
GROUPS = ((128, 1), (512, 4), (2048, 16))
ALPHA = 2.0 ** 0.25
LN_EPS = 1e-5
GN_EPS = 64e-5
C0 = float(np.exp(-0.5))
NEGM = -30000.0
OWN = 2048
EXT = 8192
SCALE = 1.0 / float(np.sqrt(128.0))

PV_MU = 0
PV_W0 = 13
PV_A0 = 17
PV_KK = 21
PV_KA = 25
PV_RK = 29
PV_LG = 33
PV_LB = 37
PV_BG = 41
PV_OMM = 57
NPV = 70


class Ctx:
    pass


def emit_consts(P, C, din):
    C.ident_f = P.sbuf("ident_f", [128, 128], F32)
    C.ident_b = P.sbuf("ident_b", [128, 128], BF16)
    C.cm = P.sbuf("cm", [128, din["cmask"].shape[1]], F32)
    C.pv = P.sbuf("pv", [128, NPV], F32)
    C.pbias = P.sbuf("pbias", [128, 1], F32)
    P.dma("sp", C.ident_f[:], din["ident"][:, :], writes=[C.ident_f])
    P.dma("pool", C.ident_b[:], din["ident"][:, :], writes=[C.ident_b])
    P.dma("sp", C.cm[:], din["cmask"][:, :], writes=[C.cm])
    P.dma("sp", C.pv[:, 0:PV_OMM], din["pvec"][:, :], writes=[C.pv])
    P.dma("sp", C.pbias[:], din["pbias"][:, :], writes=[C.pbias])
    C.selb = P.sbuf("selb", [128, 64], BF16)
    P.op("pool", lambda e: e.tensor_copy(C.selb[:], C.cm[:, CM_SEL:CM_SEL + 64]), reads=[C.cm], writes=[C.selb])
    C.bones_f = Buf("bones_f", C.cm.t[:, CM_BONES:CM_BONES + 128])
    P.op("dve", lambda e: e.tensor_scalar(C.pv[:, PV_OMM:PV_OMM + 13], C.pv[:, PV_MU:PV_MU + 13], -1.0, 1.0,
                                          ALU.mult, ALU.add), reads=[C.pv], writes=[C.pv])
    P.op("dve", lambda e: e.tensor_copy(C.cm[:, 256:512], C.cm[:, 0:256]), reads=[C.cm], writes=[C.cm])
    P.op("dve", lambda e: e.tensor_scalar(C.cm[:, 256:384], C.cm[:, 256:384], C.pbias[:, 0:1], None, ALU.add),
         reads=[C.cm, C.pbias], writes=[C.cm])


def emit_xT(P, C, x_rows_ap, ntiles, xT, col0, xld, ps_x, cnt):
    for t in range(ntiles):
        xb = xld[cnt[0] % len(xld)]
        px = ps_x[cnt[0] % len(ps_x)]
        P.dma("pool", xb[:], x_rows_ap[t * 128:(t + 1) * 128, :], writes=[xb])
        for k in range(8):
            P.tr(px, px[:, k * 128:(k + 1) * 128], xb, xb[:, k * 128:(k + 1) * 128], C.ident_b, C.ident_b[:])
        dst = xT[:, :, col0 + t * 128: col0 + (t + 1) * 128]
        src = px[:].rearrange("p (k t) -> p k t", k=8)
        if cnt[0] % 2 == 0:
            P.op("dve", lambda e: e.tensor_copy(dst, src), reads=[px], writes=[xT], partial=True)
        else:
            P.op("act", lambda e: e.activation(dst, src, AF.Copy), reads=[px], writes=[xT], partial=True)
        cnt[0] += 1


@contextlib.contextmanager
def scope(P):
    old = P.stack
    with contextlib.ExitStack() as st:
        P.stack = st
        try:
            yield
        finally:
            barrier(P)
            P.stack = old


def barrier(P):
    deps = {}
    for q in P.ring:
        for slot in range(RING):
            if P.ring_val[q][slot] > 0:
                deps["d_%s_%d" % (q, slot)] = (P.ring[q][slot], P.ring_val[q][slot])
    for e in P.eng:
        if P.seq[e] > 0:
            deps["e_" + e] = (None, P.seq[e])
    for e in P.eng:
        P._wait(e, dict(deps))


def carve(C, bank, c0, c1, name, dtype=F32):
    ap = C.bank[bank].t[:, c0:c1]
    if dtype != F32:
        ap = ap.bitcast(dtype)
    return Buf(name, ap, bank=bank)


def evac(P, i, dst_b, dst_ap, src_b, src_ap):
    if i % 2 == 0:
        P.op("dve", lambda e: e.tensor_copy(dst_ap, src_ap), reads=[src_b], writes=[dst_b], partial=True)
    else:
        P.op("act", lambda e: e.activation(dst_ap, src_ap, AF.Copy), reads=[src_b], writes=[dst_b], partial=True)


def emit_attn_prompt(P, C, din, xTA, yaT, dbg=None, dout=None):
    with scope(P):
        wh = P.sbuf("wh", [128, 8, 1280], BF16)
        qT = P.sbuf("qT", [128, 2048], BF16)
        kT = P.sbuf("kT", [128, 4096], BF16)
        vt = P.sbuf("vt", [128, 32, 128], BF16)
        oTg = [P.sbuf("oTg%d" % g, [128, 2048], BF16) for g in range(3)]
        lsB = [P.sbuf("lsB%d" % g, [128, 2048], F32) for g in range(3)]
        sz = P.sbuf("sz", [128, 2048], BF16)
        cw = [P.sbuf("cw%d" % i, [128, 512], F32) for i in range(5)]
        s_sb = [P.sbuf("s_sb%d" % i, [128, 256], F32) for i in range(3)]
        p_sb = [P.sbuf("p_sb%d" % i, [128, 256], BF16) for i in range(3)]
        pT = [P.sbuf("pT%d" % i, [128, 256], BF16) for i in range(3)]
        o_sb = [P.sbuf("o_sb%d" % i, [128, 128], BF16) for i in range(3)]
        st = [P.sbuf("st%d" % i, [128, 8], F32) for i in range(3)]
        kvts = [P.sbuf("kvt%d" % i, [128, 2, 384], F32) for i in range(2)]
        lcol = [P.sbuf("lcol%d" % i, [128, 128], F32) for i in range(3)]
        ps_p = [carve(C, i, 0, 512, "psA_p%d" % i) for i in range(2)]
        sbk = (2, 3, 6)
        obk = (4, 5, 7)
        ps_s = [carve(C, sbk[i], 0, 256, "psA_s%d" % i) for i in range(3)]
        ps_t = [carve(C, sbk[i], 256, 384, "psA_t%d" % i, BF16) for i in range(3)]
        ps_o = [carve(C, sbk[i], 384, 512, "psA_o%d" % i) for i in range(3)]
        ps_oT = [carve(C, obk[i], 0, 64, "psA_oT%d" % i, BF16) for i in range(3)]
        ps_l = [carve(C, obk[i], 128, 256, "psA_l%d" % i) for i in range(3)]
        ec = [0]
        pc = [0]
        blk = [0]

        def proj_fm(col, tok0, ntok, dst_b, dst_ap, src_view=None):
            pp = ps_p[pc[0] % 2]
            pc[0] += 1
            for k in range(8):
                P.mm(pp, pp[:, 0:ntok], wh, wh[:, k, col * 128:(col + 1) * 128], xTA, xTA[:, k, tok0:tok0 + ntok],
                     start=(k == 0), stop=(k == 7))
            src = pp[:, 0:ntok] if src_view is None else src_view(pp)
            evac(P, ec[0], dst_b, dst_ap, pp, src)
            ec[0] += 1

        for h in range(4):
            P.dma("pool", wh[:], din["w_att"][h].rearrange("(k p) c -> p k c", p=128), writes=[wh], partial=False)
            for t in range(4):
                pp = ps_p[pc[0] % 2]
                pc[0] += 1
                for k in range(8):
                    P.mm(pp, pp[:], wh, wh[:, k, 9 * 128:10 * 128], xTA, xTA[:, k, 2048 + t * 512:2048 + (t + 1) * 512],
                         start=(k == 0), stop=(k == 7))
                P.op("act", lambda e: e.activation(sz[:, t * 512:(t + 1) * 512], pp[:], AF.Silu),
                     reads=[pp], writes=[sz], partial=True)
            for tt in range(16):
                kvt = kvts[tt % 2]
                for kv in range(2):
                    pp = ps_p[pc[0] % 2]
                    pc[0] += 1
                    for k in range(8):
                        P.mm(pp, pp[:, 0:384], xTA, xTA[:, k, 2048 + tt * 128:2048 + (tt + 1) * 128], wh,
                             wh[:, k, (3 + 3 * kv) * 128:(6 + 3 * kv) * 128], start=(k == 0), stop=(k == 7))
                    evac(P, ec[0], kvt, kvt[:, kv, :], pp, pp[:, 0:384])
                    ec[0] += 1
                P.dma("sp", dout["kvp3"][tt * 128:(tt + 1) * 128, :, h, :], kvt[:, :, 256:384], reads=[kvt])
                if tt >= 12:
                    P.dma("sp", dout["kvp2"][(tt - 12) * 128:(tt - 11) * 128, :, h, :], kvt[:, :, 128:256], reads=[kvt])
                if tt == 15:
                    P.dma("sp", dout["kvp1"][0:128, :, h, :], kvt[:, :, 0:128], reads=[kvt])
            for g, (win, d) in enumerate(GROUPS):
                L = 2048 // d
                nb = L // 128
                KW = 128 + L
                for t in range(4):
                    mt = 512 // d
                    dst = qT[:].rearrange("p (r m) -> p r m", r=d)[:, :, t * mt:(t + 1) * mt]
                    proj_fm(g, 2048 + t * 512, 512, qT, dst,
                            src_view=lambda pp: pp[:].rearrange("p (m r) -> p r m", r=d))
                kv = kT[:, 0:d * KW].rearrange("p (r m) -> p r m", r=d)
                npt = 128 * d
                for t0 in range(0, npt, 512):
                    n = min(512, npt - t0)
                    dst = kv[:, :, t0 // d:(t0 + n) // d]
                    proj_fm(3 + g, 2048 - npt + t0, n, kT, dst,
                            src_view=lambda pp: pp[:, 0:n].rearrange("p (m r) -> p r m", r=d))
                for t in range(4):
                    mt = 512 // d
                    dst = kv[:, :, 128 + t * mt:128 + (t + 1) * mt]
                    proj_fm(3 + g, 2048 + t * 512, 512, kT, dst,
                            src_view=lambda pp: pp[:].rearrange("p (m r) -> p r m", r=d))
                for r in range(d):
                    for j0 in range(0, 1 + nb, 4):
                        nj = min(4, 1 + nb - j0)
                        pp = ps_p[pc[0] % 2]
                        pc[0] += 1
                        for jj in range(nj):
                            j = j0 + jj
                            tokbase = 2048 - 128 * d + r + j * 128 * d
                            for k in range(8):
                                lhs = xTA[:, k, tokbase:tokbase + 127 * d + 1:d]
                                P.mm(pp, pp[:, jj * 128:(jj + 1) * 128], xTA, lhs, wh, wh[:, k, (6 + g) * 128:(7 + g) * 128],
                                     start=(k == 0), stop=(k == 7))
                        bi = r * (1 + nb) + j0
                        evac(P, ec[0], vt, vt[:, bi:bi + nj, :], pp, pp[:, 0:nj * 128].rearrange("p (j e) -> p j e", j=nj))
                        ec[0] += 1
                def gen_block(r, n, b):
                    S, T_, O_, OT, LB = ps_s[b % 3], ps_t[b % 3], ps_o[b % 3], ps_oT[b % 3], ps_l[b % 3]
                    ss, pb, ptb, ob, stt, lc = s_sb[b % 3], p_sb[b % 3], pT[b % 3], o_sb[b % 3], st[b % 3], lcol[b % 3]
                    qblk = qT[:, r * L + n * 128: r * L + (n + 1) * 128]
                    kblk = kT[:, r * KW + n * 128: r * KW + n * 128 + 256]
                    P.mm(S, S[:], qT, qblk, kT, kblk)
                    mk = C.cm[:, 256:512] if n == 0 else C.cm[:, 0:256]
                    P.op("dve", lambda e: e.scalar_tensor_tensor(ss[:], S[:], SCALE, mk, ALU.mult, ALU.add),
                         reads=[S, C.cm], writes=[ss])
                    yield
                    P.op("dve", lambda e: e.tensor_reduce(stt[:, 0:1], ss[:], AX.X, ALU.max, negate=True),
                         reads=[ss], writes=[stt], partial=True)
                    P.op("act", lambda e: e.activation(pb[:], ss[:], AF.Exp, bias=stt[:, 0:1], scale=1.0,
                                                       accum_out=stt[:, 1:2]),
                         reads=[ss, stt], writes=[pb, stt], partial=True)
                    yield
                    for kb in range(2):
                        P.tr(T_, T_[:, kb * 128:(kb + 1) * 128], pb, pb[:, kb * 128:(kb + 1) * 128], C.ident_b, C.ident_b[:])
                    evac(P, b + 1, ptb, ptb[:], T_, T_[:])
                    yield
                    for kb in range(2):
                        P.mm(O_, O_[:], ptb, ptb[:, kb * 128:(kb + 1) * 128], vt, vt[:, r * (1 + nb) + n + kb, :],
                             start=(kb == 0), stop=(kb == 1))
                    P.op("dve", lambda e: e.reciprocal(stt[:, 2:3], stt[:, 1:2]), reads=[stt], writes=[stt], partial=True)
                    P.op("dve", lambda e: e.tensor_scalar(ob[:], O_[:], stt[:, 2:3], None, ALU.mult),
                         reads=[O_, stt], writes=[ob])
                    yield
                    P.op("act", lambda e: e.activation(stt[:, 3:4], stt[:, 1:2], AF.Ln), reads=[stt], writes=[stt], partial=True)
                    P.op("dve", lambda e: e.tensor_scalar(lc[:], C.cm[:, 512:640], stt[:, 3:4], stt[:, 0:1],
                                                          ALU.add, ALU.subtract),
                         reads=[stt, C.cm], writes=[lc])
                    P.tr(OT, OT[:], ob, ob[:], C.ident_b, C.ident_b[:])
                    P.mm(LB, LB[:], lc, lc[:], C.ident_f, C.ident_f[:])
                    yield
                    tok0 = r + d * 128 * n
                    dsto = oTg[g][:, tok0:tok0 + 127 * d + 1:d]
                    dstl = lsB[g][:, tok0:tok0 + 127 * d + 1:d]
                    evac(P, b, oTg[g], dsto, OT, OT[:])
                    evac(P, b + 1, lsB[g], dstl, LB, LB[:])
                    yield

                blist = [(r, n) for r in range(d) for n in range(nb)]
                for i in range(0, len(blist), 3):
                    gens = []
                    for (r, n) in blist[i:i + 3]:
                        gens.append(gen_block(r, n, blk[0]))
                        blk[0] += 1
                    interleave(*gens)
            for t in range(4):
                sl = slice(t * 512, (t + 1) * 512)
                m_, e0, e1, e2, acc = cw
                P.op("dve", lambda e: e.tensor_tensor(m_[:], lsB[0][:, sl], lsB[1][:, sl], ALU.max), reads=[lsB[0], lsB[1]], writes=[m_])
                P.op("dve", lambda e: e.tensor_tensor(m_[:], m_[:], lsB[2][:, sl], ALU.max), reads=[m_, lsB[2]], writes=[m_])
                for g, eg in enumerate((e0, e1, e2)):
                    P.op("pool", lambda e: e.tensor_tensor(eg[:], lsB[g][:, sl], m_[:], ALU.subtract), reads=[lsB[g], m_], writes=[eg])
                    P.op("act", lambda e: e.activation(eg[:], eg[:], AF.Exp), reads=[eg], writes=[eg])
                P.op("dve", lambda e: e.tensor_tensor(m_[:], e0[:], e1[:], ALU.add), reads=[e0, e1], writes=[m_])
                P.op("dve", lambda e: e.tensor_tensor(m_[:], m_[:], e2[:], ALU.add), reads=[m_, e2], writes=[m_])
                P.op("dve", lambda e: e.reciprocal(m_[:], m_[:]), reads=[m_], writes=[m_])
                P.op("dve", lambda e: e.tensor_tensor(acc[:], e0[:], oTg[0][:, sl], ALU.mult), reads=[e0, oTg[0]], writes=[acc])
                P.op("pool", lambda e: e.tensor_tensor(e1[:], e1[:], oTg[1][:, sl], ALU.mult), reads=[e1, oTg[1]], writes=[e1])
                P.op("pool", lambda e: e.tensor_tensor(e2[:], e2[:], oTg[2][:, sl], ALU.mult), reads=[e2, oTg[2]], writes=[e2])
                P.op("dve", lambda e: e.tensor_tensor(acc[:], acc[:], e1[:], ALU.add), reads=[acc, e1], writes=[acc])
                P.op("dve", lambda e: e.tensor_tensor(acc[:], acc[:], e2[:], ALU.add), reads=[acc, e2], writes=[acc])
                P.op("dve", lambda e: e.tensor_tensor(acc[:], acc[:], m_[:], ALU.mult), reads=[acc, m_], writes=[acc])
                if dbg is not None:
                    P.dma("sp", dbg["oat"][h * 128:(h + 1) * 128, sl], acc[:], reads=[acc])
                P.op("dve", lambda e: e.tensor_tensor(yaT[:, h, sl], acc[:], sz[:, sl], ALU.mult), reads=[acc, sz], writes=[yaT], partial=True)


def emit_out_phase(P, C, din, xT, xc0, x_rows_ap, yrT, yaT, ntok, y_out_ap, tag):
    with scope(P):
        wg = P.sbuf("wg" + tag, [128, 8, 2048], BF16)
        woa = P.sbuf("woa" + tag, [128, 4, 1024], BF16)
        wob = P.sbuf("wob" + tag, [128, 4, 1024], BF16)
        wout = P.sbuf("wout" + tag, [128, 8, 1024], BF16)
        lng = P.sbuf("lng" + tag, [128, 1024], F32)
        lnb = P.sbuf("lnb" + tag, [128, 1024], F32)
        TW = min(512, ntok)
        mixT = P.sbuf("mixT" + tag, [128, 8, TW], BF16)
        gr = [P.sbuf("gr%d%s" % (i, tag), [128, TW], F32) for i in range(2)]
        ga = [P.sbuf("ga%d%s" % (i, tag), [128, TW], F32) for i in range(2)]
        t1 = [P.sbuf("t1%d%s" % (i, tag), [128, TW], F32) for i in range(2)]
        xr = [P.sbuf("xr%d%s" % (i, tag), [128, 1024], F32) for i in range(2)]
        z = [P.sbuf("z%d%s" % (i, tag), [128, 1024], F32) for i in range(2)]
        bst = [P.sbuf("bst%d%s" % (i, tag), [128, 16], F32) for i in range(2)]
        psb = [carve(C, i, 0, 512, "psO_b%d%s" % (i, tag)) for i in range(8)]
        ps_y = psb[4:8]
        P.dma("pool", wg[:], din["w_g"].rearrange("(k p) c -> p k c", p=128), writes=[wg])
        P.dma("pool", woa[:], din["w_oa"].rearrange("(k p) c -> p k c", p=128), writes=[woa])
        P.dma("pool", wob[:], din["w_ob"].rearrange("(k p) c -> p k c", p=128), writes=[wob])
        P.dma("pool", wout[:], din["w_out"].rearrange("(k p) c -> p k c", p=128), writes=[wout])
        P.dma("sp", lng[:], din["ln_gb"][0:1, :].to_broadcast([128, 1024]), writes=[lng])
        P.dma("sp", lnb[:], din["ln_gb"][1:2, :].to_broadcast([128, 1024]), writes=[lnb])
        it = 0
        for t0 in range(0, ntok, TW):
            for n in range(8):
                pgr, pga, pmr, pma = psb[(n % 2) * 4:(n % 2) * 4 + 4]
                for k in range(8):
                    P.mm(pgr, pgr[:, 0:TW], wg, wg[:, k, n * 128:(n + 1) * 128], xT, xT[:, k, xc0 + t0:xc0 + t0 + TW], start=(k == 0), stop=(k == 7))
                for k in range(8):
                    P.mm(pga, pga[:, 0:TW], wg, wg[:, k, 1024 + n * 128:1024 + (n + 1) * 128], xT, xT[:, k, xc0 + t0:xc0 + t0 + TW], start=(k == 0), stop=(k == 7))
                for c in range(4):
                    P.mm(pmr, pmr[:, 0:TW], woa, woa[:, c, n * 128:(n + 1) * 128], yrT, yrT[:, c, t0:t0 + TW], start=(c == 0), stop=(c == 3))
                for c in range(4):
                    P.mm(pma, pma[:, 0:TW], wob, wob[:, c, n * 128:(n + 1) * 128], yaT, yaT[:, c, t0:t0 + TW], start=(c == 0), stop=(c == 3))
                a, b_, tt = gr[it % 2], ga[it % 2], t1[it % 2]
                it += 1
                P.op("act", lambda e: e.activation(a[:], pgr[:, 0:TW], AF.Sigmoid, bias=C.pv[:, PV_BG + n:PV_BG + n + 1], scale=1.0),
                     reads=[pgr, C.pv], writes=[a])
                P.op("act", lambda e: e.activation(b_[:], pga[:, 0:TW], AF.Sigmoid, bias=C.pv[:, PV_BG + 8 + n:PV_BG + 9 + n], scale=1.0),
                     reads=[pga, C.pv], writes=[b_])
                P.op("dve", lambda e: e.tensor_tensor(tt[:], a[:], pmr[:, 0:TW], ALU.mult), reads=[a, pmr], writes=[tt])
                P.op("dve", lambda e: e.tensor_tensor(b_[:], b_[:], pma[:, 0:TW], ALU.mult), reads=[b_, pma], writes=[b_])
                P.op("pool", lambda e: e.tensor_tensor(mixT[:, n, :], tt[:], b_[:], ALU.add), reads=[tt, b_], writes=[mixT], partial=True)
            for s0 in range(0, TW, 128):
                ns = min(128, ntok - t0 - s0)
                i2 = (t0 + s0) // 128
                xx, zz, bs = xr[i2 % 2], z[i2 % 2], bst[i2 % 2]
                py = ps_y[(i2 % 2) * 2:(i2 % 2) * 2 + 2]
                P.dma("sp", xx[0:ns, :], x_rows_ap[t0 + s0:t0 + s0 + ns, :], writes=[xx])
                for hf in range(2):
                    for m in range(8):
                        P.mm(py[hf], py[hf][0:ns, :], mixT, mixT[:, m, s0:s0 + ns], wout, wout[:, m, hf * 512:(hf + 1) * 512],
                             start=(m == 0), stop=(m == 7))
                    P.op("dve", lambda e: e.scalar_tensor_tensor(zz[0:ns, hf * 512:(hf + 1) * 512], xx[0:ns, hf * 512:(hf + 1) * 512],
                                                                 ALPHA, py[hf][0:ns, :], ALU.mult, ALU.add),
                         reads=[xx, py[hf]], writes=[zz], partial=True)
                    P.op("dve", lambda e: e.bn_stats(bs[0:ns, hf * 6:(hf + 1) * 6], zz[0:ns, hf * 512:(hf + 1) * 512]),
                         reads=[zz], writes=[bs], partial=True)
                P.op("dve", lambda e: e.bn_aggr(bs[0:ns, 12:14], bs[0:ns, 0:12]), reads=[bs], writes=[bs], partial=True)
                P.op("dve", lambda e: e.tensor_scalar(bs[0:ns, 14:15], bs[0:ns, 13:14], LN_EPS, None, ALU.add), reads=[bs], writes=[bs], partial=True)
                P.op("act", lambda e: e.activation(bs[0:ns, 14:15], bs[0:ns, 14:15], AF.Sqrt), reads=[bs], writes=[bs], partial=True)
                P.op("dve", lambda e: e.reciprocal(bs[0:ns, 14:15], bs[0:ns, 14:15]), reads=[bs], writes=[bs], partial=True)
                P.op("dve", lambda e: e.tensor_scalar(zz[0:ns, :], zz[0:ns, :], bs[0:ns, 12:13], bs[0:ns, 14:15], ALU.subtract, ALU.mult),
                     reads=[zz, bs], writes=[zz])
                P.op("pool", lambda e: e.tensor_tensor(zz[0:ns, :], zz[0:ns, :], lng[0:ns, :], ALU.mult), reads=[zz, lng], writes=[zz])
                P.op("pool", lambda e: e.tensor_tensor(zz[0:ns, :], zz[0:ns, :], lnb[0:ns, :], ALU.add), reads=[zz, lnb], writes=[zz])
                P.dma("sp", y_out_ap[t0 + s0:t0 + s0 + ns, :], zz[0:ns, :], reads=[zz])


def make_cmask():
    cm = np.zeros((128, 1408), np.float32)
    i = np.arange(128)[:, None]
    j = np.arange(256)[None, :]
    dist = 128 + i - j
    cm[:, 0:256] = np.where((dist >= 0) & (dist <= 128), 0.0, NEGM)
    p = np.arange(128)
    hs, s = p[:, None] // 64, p[:, None] % 64
    ht, t = p[None, :] // 64, p[None, :] % 64
    same = (hs == ht)
    cm[:, 640:768] = (same & (s < t)).astype(np.float32)
    cm[:, 768:896] = (same & (s <= t)).astype(np.float32)
    cm[:, 896:1024] = (same & (s > t)).astype(np.float32)
    cm[:, 1024:1152] = np.eye(128, dtype=np.float32)
    cm[:, 1152:1280] = same.astype(np.float32)
    cm[:, 1280:1282] = (p[:, None] // 64 == np.arange(2)[None, :]).astype(np.float32)
    sel = np.zeros((128, 64), np.float32)
    sel[p, p % 64] = 1.0
    cm[:, 1282:1346] = sel
    return cm


CM_STRICT, CM_INCL, CM_STRICT_T, CM_EYE, CM_BONES, CM_HM, CM_SEL = 640, 768, 896, 1024, 1152, 1280, 1282


def build(flags):
    nc = bass.Bass("TRN2", target_bir_lowering=False)
    din, dout = {}, {}

    def inp(name, shape, dt=F32):
        din[name] = nc.dram_tensor(name, list(shape), dt, kind="ExternalInput").ap()

    def outp(name, shape, dt=F32):
        dout[name] = nc.dram_tensor(name, list(shape), dt, kind="ExternalOutput").ap()

    inp("xe", [EXT, D])
    inp("w_rw", [D, RW_COLS])
    inp("w_att", [4, D, 1280])
    inp("w_g", [D, 2048])
    inp("w_oa", [512, D])
    inp("w_ob", [512, D])
    inp("w_out", [D, D])
    inp("w_l2", [128, 512])
    inp("pvec", [128, PV_OMM])
    inp("ln_gb", [2, D])
    inp("ident", [128, 128])
    inp("cmask", [128, 1408])
    inp("pbias", [128, 1])
    inp("xs", [64, D])
    inp("w_qkvz", [D, 5120])
    inp("cache1", [16, 128, 2, 4, 128])
    inp("cache2", [16, 512, 2, 4, 128])
    inp("cache3", [16, 2048, 2, 4, 128])
    inp("wkv_s", [16, 8, 64, 64])
    inp("shift_s", [16, SHIFT_COLS])
    inp("cmask_s", [128, NCS])
    inp("colmask", [128, 2048])
    outp("y_s", [64, D])
    for g in (1, 2, 3):
        outp("kvs%d" % g, [16, 4, 2, 4, 128])
    outp("wkv_so", [16, 8, 64, 64])
    outp("shift_so", [16, SHIFT_COLS])
    outp("y_p", [OWN, D])
    outp("kvp1", [128, 2, 4, 128])
    outp("kvp2", [512, 2, 4, 128])
    outp("kvp3", [2048, 2, 4, 128])
    outp("wkv_p", [8, 64, 64])
    outp("shift_p", [SHIFT_COLS])
    if flags.get("dbg"):
        inp("yr_dbg", [512, OWN])
        outp("oat", [512, OWN])
        outp("yat", [512, OWN])
        outp("yrt", [512, OWN])
    with contextlib.ExitStack() as st:
        P = Prog(nc, st)
        C = Ctx()
        C.bank = [P.psum("bank%d" % i, [128, 512], F32) for i in range(8)]
        emit_consts(P, C, din)
        barrier(P)
        yr_scr = P.dram("yr_scr", [128, 4 * OWN], BF16)
        ya_scr = P.dram("ya_scr", [128, 4 * OWN], BF16)
        if flags.get("attn", True):
          with scope(P):
              yaT = P.sbuf("yaT", [128, 4, OWN], BF16)
              xTA = P.sbuf("xTA", [128, 8, 4096], BF16)
              with scope(P):
                  xld = [P.sbuf("xldA%d" % i, [128, 1024], BF16) for i in range(3)]
                  psx = [carve(C, 6 + i, 0, 512, "psxA%d" % i, BF16) for i in range(2)]
                  emit_xT(P, C, din["xe"][4096:8192, :], 32, xTA, 0, xld, psx, [0])
              emit_attn_prompt(P, C, din, xTA, yaT, dbg=dout if flags.get("dbg") else None, dout=dout)
              if flags.get("dbg"):
                  with scope(P):
                      yaf = P.sbuf("yaf", [128, 4, OWN], F32)
                      P.op("dve", lambda e: e.tensor_copy(yaf[:], yaT[:]), reads=[yaT], writes=[yaf])
                      P.dma("sp", dout["yat"].rearrange("(h p) t -> p h t", p=128), yaf[:], reads=[yaf])
              P.dma("sp", ya_scr[:, :], yaT[:].rearrange("p h t -> p (h t)"), reads=[yaT], writes=[ya_scr])
        if flags.get("rwkv", True):
            emit_rwkv_prompt(P, C, din, dout, yr_scr, ntiles=flags.get("ntiles", 16), own_from=flags.get("own_from", 12),
                             budget=flags.get("budget"))
        if flags.get("outp", True):
          with scope(P):
              xTO = P.sbuf("xTO", [128, 8, OWN], BF16)
              yaT = P.sbuf("yaT2", [128, 4, OWN], BF16)
              P.dma("sp", yaT[:].rearrange("p h t -> p (h t)"), ya_scr[:, :], reads=[ya_scr], writes=[yaT])
              yrT = P.sbuf("yrT", [128, 4, OWN], BF16)
              if flags.get("rwkv", True):
                  P.dma("sp", yrT[:].rearrange("p h t -> p (h t)"), yr_scr[:, :], reads=[yr_scr], writes=[yrT])
              elif flags.get("dbg"):
                  P.dma("pool", yrT[:], din["yr_dbg"].rearrange("(h p) t -> p h t", p=128), writes=[yrT])
              if flags.get("dbg"):
                  with scope(P):
                      yrf = P.sbuf("yrf", [128, 4, OWN], F32)
                      P.op("dve", lambda e: e.tensor_copy(yrf[:], yrT[:]), reads=[yrT], writes=[yrf])
                      P.dma("sp", dout["yrt"].rearrange("(h p) t -> p h t", p=128), yrf[:], reads=[yrf])
              with scope(P):
                  xld = [P.sbuf("xldO%d" % i, [128, 1024], BF16) for i in range(3)]
                  psx = [carve(C, 6 + i, 0, 512, "psxO%d" % i, BF16) for i in range(2)]
                  emit_xT(P, C, din["xe"][6144:8192, :], 16, xTO, 0, xld, psx, [0])
              emit_out_phase(P, C, din, xTO, 0, din["xe"][6144:8192, :], yrT, yaT, OWN, dout["y_p"], "p")
        if flags.get("sample", True):
            with scope(P):
                emit_sample(P, C, din, dout, flags)
        P.finish()
    return nc


def _fm(vec, n):
    return np.ascontiguousarray(np.asarray(vec, np.float32).reshape(n, 128).T)


def prep_shared(inputs):
    w_in = np.asarray(inputs["w_in"][0], np.float32)
    sh = {}
    sh["w_rw"] = np.ascontiguousarray(w_in[:, 0:RW_COLS])
    w_att = np.empty((4, D, 1280), np.float32)
    for h in range(4):
        cols = []
        for base in (Q0, K0, V0):
            for g in range(3):
                cols.append(w_in[:, base + g * 512 + h * 128: base + g * 512 + (h + 1) * 128])
        cols.append(w_in[:, ZA0 + h * 128: ZA0 + (h + 1) * 128])
        w_att[h] = np.concatenate(cols, axis=1)
    sh["w_att"] = w_att
    sh["w_g"] = np.ascontiguousarray(w_in[:, GR0:GR0 + 2048])
    sh["w_oa"] = np.ascontiguousarray(inputs["w_oa"][0], np.float32)
    sh["w_ob"] = np.ascontiguousarray(inputs["w_ob"][0], np.float32)
    sh["w_out"] = np.ascontiguousarray(inputs["w_out"][0], np.float32)
    sh["w_l2"] = np.ascontiguousarray(np.concatenate([inputs["w_w2"][0], inputs["w_a2"][0]], axis=0), np.float32)
    pv = np.zeros((128, PV_OMM), np.float32)
    pv[:, PV_MU:PV_MU + 13] = _fm(inputs["mu_shift"][0], 13)
    pv[:, PV_W0:PV_W0 + 4] = _fm(inputs["w0"][0], 4)
    pv[:, PV_A0:PV_A0 + 4] = _fm(inputs["a0"][0], 4)
    pv[:, PV_KK:PV_KK + 4] = _fm(inputs["k_k"][0], 4)
    pv[:, PV_KA:PV_KA + 4] = _fm(inputs["k_a"][0], 4)
    pv[:, PV_RK:PV_RK + 4] = _fm(np.asarray(inputs["r_k"][0]).reshape(-1), 4)
    pv[:, PV_LG:PV_LG + 4] = _fm(inputs["lnx_g"][0], 4)
    pv[:, PV_LB:PV_LB + 4] = _fm(inputs["lnx_b"][0], 4)
    pv[:, PV_BG:PV_BG + 16] = _fm(inputs["b_gate"][0], 16)
    sh["pvec"] = pv
    sh["ln_gb"] = np.ascontiguousarray(np.stack([inputs["ln_g"][0], inputs["ln_b"][0]]), np.float32)
    sh["w_qkvz"] = np.ascontiguousarray(w_in[:, Q0:ZA0 + 512])
    sh["cmask_s"] = make_cmask_s()
    sh["colmask"] = make_colmask()
    sh["ident"] = np.eye(128, dtype=np.float32)
    sh["cmask"] = make_cmask()
    return sh


def prep_core(inputs, sh, c):
    b, q = c // 4, c % 4
    m = dict(sh)
    xe = np.zeros((EXT, D), np.float32)
    n = OWN * (q + 1)
    xe[EXT - n:] = np.asarray(inputs["x_prompt"][b, 0:n], np.float32)
    m["xe"] = xe
    m["pbias"] = np.full((128, 1), 0.0 if q > 0 else NEGM, np.float32)
    sl = slice(16 * c, 16 * c + 16)
    m["xs"] = np.ascontiguousarray(np.asarray(inputs["x_sample"][sl], np.float32).reshape(64, D))
    m["cache1"] = np.ascontiguousarray(inputs["cache_kv_g1"][0, sl], np.float32)
    m["cache2"] = np.ascontiguousarray(inputs["cache_kv_g2"][0, sl], np.float32)
    m["cache3"] = np.ascontiguousarray(inputs["cache_kv_g3"][0, sl], np.float32)
    m["wkv_s"] = np.ascontiguousarray(inputs["state_rwkv_wkv"][0, sl], np.float32)
    m["shift_s"] = np.ascontiguousarray(inputs["state_rwkv_shift"][0, sl], np.float32)
    return m


_NC_CACHE = {}


def kernel(**inputs):
    if "nc" not in _NC_CACHE:
        _NC_CACHE["nc"] = build({})
    nc = _NC_CACHE["nc"]
    sh = prep_shared(inputs)
    in_maps = [prep_core(inputs, sh, c) for c in range(NCORES)]
    res = run_bass_kernel_spmd(nc, in_maps, core_ids=list(range(NCORES)))
    R = res.results
    y_p = np.zeros((NB, SEQ, D), np.float32)
    for c in range(NCORES):
        y_p[c // 4, (c % 4) * OWN:(c % 4 + 1) * OWN] = R[c]["y_p"]
    y_s = np.concatenate([R[c]["y_s"].reshape(16, 4, D) for c in range(NCORES)], axis=0)
    outs = [y_p, y_s]
    for g in (1, 2, 3):
        outs.append(np.stack([R[4 * b + 3]["kvp%d" % g] for b in range(NB)])[None])
        outs.append(np.concatenate([R[c]["kvs%d" % g] for c in range(NCORES)], axis=0)[None])
    outs.append(np.stack([R[4 * b + 3]["wkv_p"] for b in range(NB)])[None])
    outs.append(np.concatenate([R[c]["wkv_so"] for c in range(NCORES)], axis=0)[None])
    outs.append(np.stack([R[4 * b + 3]["shift_p"] for b in range(NB)])[None])
    outs.append(np.concatenate([R[c]["shift_so"] for c in range(NCORES)], axis=0)[None])
    return tuple(np.ascontiguousarray(o, dtype=np.float32) for o in outs)


def interleave(*gens):
    live = list(gens)
    while live:
        for g in list(live):
            try:
                next(g)
            except StopIteration:
                live.remove(g)


def bc(ap, shape, axes):
    for a in axes:
        ap = ap.unsqueeze(a)
    return ap.to_broadcast(list(shape))


def emit_rwkv_prompt(P, C, din, dout, yrT, ntiles=16, own_from=12, dbg=None, budget=None):
    with scope(P):
        w_r = P.sbuf("w_r", [128, 8, RW_COLS], BF16)
        wl2 = P.sbuf("wl2", [128, 512], BF16)
        mask4 = P.sbuf("mask4", [128, 512], F32)
        bones_b = P.sbuf("bones_b", [128, 128], BF16)
        scanm = P.sbuf("scanm", [128, 512], F32)
        carry = P.sbuf("carry", [128, 16], F32)
        shout = P.sbuf("shout", [128, 16], F32)
        Sf = [P.sbuf("Sf%d" % i, [128, 128], F32) for i in range(4)]
        Sb = [P.sbuf("Sb%d" % i, [128, 128], BF16) for i in range(4)]
        xT = [P.sbuf("xTR%d" % i, [128, 8, 512], BF16) for i in range(1)]
        xld = [P.sbuf("xldR%d" % i, [128, 1024], BF16) for i in range(2)]
        bm = [P.sbuf("bm%d" % i, [128, 512], F32) for i in range(2)]
        lor = P.sbuf("lor", [128, 512], BF16)
        wdad = P.sbuf("wdad", [128, 512], F32)
        S1 = []
        for i in range(3):
            d_ = {}
            for nm in ("rt", "kt", "bt", "at"):
                d_[nm] = P.sbuf("%s%d" % (nm, i), [128, 512], BF16)
            for nm in ("vz", "eg", "bonus", "gate"):
                d_[nm] = P.sbuf("%s%d" % (nm, i), [128, 512], F32)
            S1.append(d_)
        tmp = {nm: P.sbuf("tp_" + nm, [128, 512], F32) for nm in ("r", "k", "sg", "a", "cs", "eng", "kk", "rn", "km", "bv")}
        sqb = P.sbuf("sqb", [128, 512], BF16)
        Lb = P.sbuf("Lb", [128, 8, 2, 64], BF16)
        Lk = P.sbuf("Lk", [128, 8, 2, 64], BF16)
        Ra = P.sbuf("Ra", [128, 8, 2, 64], BF16)
        KH = P.sbuf("KH", [128, 8, 2, 64], BF16)
        BH = P.sbuf("BH", [128, 8, 2, 64], BF16)
        VB = P.sbuf("VB", [128, 8, 2, 64], BF16)
        hmg = P.sbuf("hmg", [128, 8, 2], F32)
        NA = P.sbuf("NA", [128, 8, 2, 128], BF16)
        PT0 = P.sbuf("PT0", [128, 8, 128], BF16)
        Rat = P.sbuf("Rat", [128, 8, 128], BF16)
        Tt = [P.sbuf("Tt%d" % i, [128, 8, 128], BF16) for i in range(2)]
        Pp = [P.sbuf("Pp%d" % i, [128, 4, 128], BF16) for i in range(2)]
        PTp = [P.sbuf("PTp%d" % i, [128, 4, 128], BF16) for i in range(2)]
        Yb = P.sbuf("Yb", [128, 4, 128], BF16)
        SETS = []
        for i in range(2):
            d_ = {}
            for nm in ("KHt", "BHt", "VBt", "W1", "W2", "Rr"):
                d_[nm] = P.sbuf("%s_%d" % (nm, i), [128, 8, 128], BF16)
            d_["ABK"] = P.sbuf("ABK_%d" % i, [128, 8, 2, 128], BF16)
            d_["gC"] = P.sbuf("gC_%d" % i, [128, 8], F32)
            SETS.append(d_)
        yos = [P.sbuf("yos%d" % i, [128, 512], BF16) for i in range(2)]
        Ub = [P.sbuf("Ub%d" % i, [128, 128], BF16) for i in range(2)]
        Ob = [P.sbuf("Ob%d" % i, [128, 128], BF16) for i in range(2)]
        post = {nm: P.sbuf("po_" + nm, [128, 512], F32) for nm in ("yr", "cen", "sq", "rs")}
        pp = [carve(C, i, 0, 512, "psR_p%d" % i) for i in range(2)]
        pstr = carve(C, 3, 0, 512, "psR_tr", BF16)
        psx = pstr
        psYb = carve(C, 2, 0, 512, "psR_Y")
        psA_ = [carve(C, 4, 0, 384, "psR_A"), carve(C, 3, 0, 384, "psR_Ab")]
        psA2_ = [carve(C, 4, 384, 512, "psR_A2"), carve(C, 3, 384, 512, "psR_A2b")]
        psP = carve(C, 6, 0, 512, "psR_P")
        psPT = carve(C, 7, 0, 512, "psR_PT")
        psU = carve(C, 5, 0, 128, "psR_U")
        psS = carve(C, 5, 128, 256, "psR_S")
        psO = carve(C, 5, 256, 384, "psR_O")
        ppi = [0]
        eci = [0]

        def nextpp():
            b = pp[ppi[0] % 2]
            ppi[0] += 1
            return b

        def pv(col):
            return C.pv[:, col:col + 1]

        P.dma("pool", w_r[:], din["w_rw"].rearrange("(k p) c -> p k c", p=128), writes=[w_r])
        P.dma("pool", wl2[:], din["w_l2"][:, :], writes=[wl2])
        for i in range(4):
            src = C.cm[:, CM_STRICT:CM_STRICT + 128] if i % 2 == 0 else C.cm[:, CM_INCL:CM_INCL + 128]
            if i in (0, 1):
                src = C.cm[:, CM_STRICT:CM_STRICT + 128]
            else:
                src = C.cm[:, CM_INCL:CM_INCL + 128]
            P.op("pool", lambda e: e.tensor_copy(mask4[:, i * 128:(i + 1) * 128], src), reads=[C.cm], writes=[mask4], partial=True)
        P.op("pool", lambda e: e.tensor_copy(bones_b[:], C.cm[:, CM_BONES:CM_BONES + 128]), reads=[C.cm], writes=[bones_b])
        P.op("pool", lambda e: e.memset(scanm[:], 1.0), writes=[scanm])
        P.op("pool", lambda e: e.memset(scanm[:, 0:512:64], 0.0), writes=[scanm])
        P.op("pool", lambda e: e.memset(carry[:], 0.0), writes=[carry])
        for i in range(4):
            P.op("pool", lambda e: e.memset(Sf[i][:], 0.0), writes=[Sf[i]])
            P.op("pool", lambda e: e.memset(Sb[i][:], 0.0), writes=[Sb[i]])

        def build_xT(tile):
            xt = xT[0]
            for s in range(4):
                xb = xld[s % 2]
                P.dma("pool", xb[:], din["xe"][tile * 512 + s * 128: tile * 512 + (s + 1) * 128, :], writes=[xb])
                for k in range(8):
                    P.tr(psx, psx[:, k * 128:(k + 1) * 128], xb, xb[:, k * 128:(k + 1) * 128], C.ident_b, C.ident_b[:])
                evac(P, eci[0], xt, xt[:, :, s * 128:(s + 1) * 128], psx, psx[:].rearrange("p (k t) -> p k t", k=8))
                eci[0] += 1

        def proj_shift(xt, c, dst, last_own):
            p_ = nextpp()
            for k in range(8):
                P.mm(p_, p_[:], w_r, w_r[:, k, c * 128:(c + 1) * 128], xt, xt[:, k, :], start=(k == 0), stop=(k == 7))
            b_ = bm[c % 2]
            P.op("act", lambda e: e.mul(b_[:], p_[:], pv(PV_MU + c)), reads=[p_, C.pv], writes=[b_])
            P.op("dve", lambda e: e.scalar_tensor_tensor(dst[:, 1:512], p_[:, 1:512], pv(PV_OMM + c), b_[:, 0:511], ALU.mult, ALU.add),
                 reads=[p_, b_, C.pv], writes=[dst], partial=True)
            P.op("dve", lambda e: e.scalar_tensor_tensor(dst[:, 0:1], p_[:, 0:1], pv(PV_OMM + c), carry[:, c:c + 1], ALU.mult, ALU.add),
                 reads=[p_, carry, C.pv], writes=[dst], partial=True)
            P.op("pool", lambda e: e.tensor_copy(carry[:, c:c + 1], b_[:, 511:512]), reads=[b_], writes=[carry], partial=True)
            if last_own:
                P.op("act", lambda e: e.activation(shout[:, c:c + 1], p_[:, 511:512], AF.Copy), reads=[p_], writes=[shout], partial=True)

        def stage1(tile, hp, own):
            xt = xT[0]
            s1 = S1[(tile * 4 + hp) % 3]
            last_own = (tile == ntiles - 1)
            if hp == 0:
                proj_shift(xt, 12, wdad, last_own)
                P.op("act", lambda e: e.activation(lor[0:64, :], wdad[0:64, :], AF.Tanh), reads=[wdad], writes=[lor], partial=True)
                P.op("dve", lambda e: e.tensor_copy(lor[64:128, :], wdad[64:128, :]), reads=[wdad], writes=[lor], partial=True)
                yield
            r, k, sg, a, cs, eng, kk, rn, km, bv = (tmp[n] for n in ("r", "k", "sg", "a", "cs", "eng", "kk", "rn", "km", "bv"))
            vz, eg = s1["vz"], s1["eg"]
            if own or tile == own_from - 1:
                proj_shift(xt, hp, r, last_own)
                yield
            proj_shift(xt, 4 + hp, k, last_own)
            yield
            proj_shift(xt, 8 + hp, vz, last_own)
            yield
            p_ = nextpp()
            P.mm(p_, p_[:], wl2, wl2[0:64, hp * 128:(hp + 1) * 128], lor, lor[0:64, :])
            P.op("act", lambda e: e.activation(sg[:], p_[:], AF.Sigmoid, bias=pv(PV_W0 + hp), scale=1.0), reads=[p_, C.pv], writes=[sg])
            p2 = nextpp()
            P.mm(p2, p2[:], wl2, wl2[64:128, hp * 128:(hp + 1) * 128], lor, lor[64:128, :])
            P.op("act", lambda e: e.activation(a[:], p2[:], AF.Sigmoid, bias=pv(PV_A0 + hp), scale=1.0), reads=[p2, C.pv], writes=[a])
            yield
            P.op("dve", lambda e: e.tensor_tensor_scan(cs[:], scanm[:], sg[:], 0.0, ALU.mult, ALU.add), reads=[scanm, sg], writes=[cs])
            P.op("act", lambda e: e.activation(eg[:], cs[:], AF.Exp, scale=-C0), reads=[cs], writes=[eg])
            P.op("act", lambda e: e.activation(eng[:], cs[:], AF.Exp, scale=C0), reads=[cs], writes=[eng])
            P.op("pool", lambda e: e.tensor_tensor(cs[:], cs[:], sg[:], ALU.subtract), reads=[cs, sg], writes=[cs])
            P.op("act", lambda e: e.activation(cs[:], cs[:], AF.Exp, scale=-C0), reads=[cs], writes=[cs])
            yield
            P.op("dve", lambda e: e.tensor_scalar(kk[:], k[:], pv(PV_KK + hp), None, ALU.mult), reads=[k, C.pv], writes=[kk])
            P.op("pool", lambda e: e.tensor_tensor(sqb[:], kk[:], kk[:], ALU.mult), reads=[kk], writes=[sqb])
            p3 = nextpp()
            P.mm(p3, p3[:], bones_b, bones_b[:], sqb, sqb[:])
            P.op("dve", lambda e: e.tensor_scalar(rn[:], p3[:], 1e-24, None, ALU.max), reads=[p3], writes=[rn])
            P.op("act", lambda e: e.activation(rn[:], rn[:], AF.Sqrt), reads=[rn], writes=[rn])
            P.op("dve", lambda e: e.reciprocal(rn[:], rn[:]), reads=[rn], writes=[rn])
            P.op("dve", lambda e: e.tensor_tensor(kk[:], kk[:], rn[:], ALU.mult), reads=[kk, rn], writes=[kk])
            yield
            P.op("dve", lambda e: e.tensor_scalar(km[:], a[:], -1.0, pv(PV_KA + hp), ALU.add, ALU.mult), reads=[a, C.pv], writes=[km])
            P.op("dve", lambda e: e.scalar_tensor_tensor(km[:], km[:], 1.0, k[:], ALU.add, ALU.mult), reads=[km, k], writes=[km])
            P.op("pool", lambda e: e.tensor_tensor(bv[:], kk[:], a[:], ALU.mult), reads=[kk, a], writes=[bv])
            yield
            if own:
                P.op("pool", lambda e: e.tensor_tensor(s1["rt"][:], r[:], eg[:], ALU.mult), reads=[r, eg], writes=[s1["rt"]])
            P.op("dve", lambda e: e.tensor_tensor(s1["kt"][:], km[:], eng[:], ALU.mult), reads=[km, eng], writes=[s1["kt"]])
            P.op("pool", lambda e: e.tensor_tensor(s1["bt"][:], bv[:], eng[:], ALU.mult), reads=[bv, eng], writes=[s1["bt"]])
            P.op("dve", lambda e: e.scalar_tensor_tensor(s1["at"][:], kk[:], -1.0, cs[:], ALU.mult, ALU.mult), reads=[kk, cs], writes=[s1["at"]])
            yield
            if own:
                P.op("dve", lambda e: e.scalar_tensor_tensor(sqb[:], r[:], pv(PV_RK + hp), km[:], ALU.mult, ALU.mult), reads=[r, km, C.pv], writes=[sqb])
                p4 = nextpp()
                P.mm(p4, p4[:], bones_b, bones_b[:], sqb, sqb[:])
                P.op("dve", lambda e: e.tensor_tensor(s1["bonus"][:], p4[:], vz[:], ALU.mult), reads=[p4, vz], writes=[s1["bonus"]])
                p5 = nextpp()
                for kq in range(8):
                    P.mm(p5, p5[:], w_r, w_r[:, kq, (13 + hp) * 128:(14 + hp) * 128], xt, xt[:, kq, :], start=(kq == 0), stop=(kq == 7))
                P.op("act", lambda e: e.activation(s1["gate"][:], p5[:], AF.Silu), reads=[p5], writes=[s1["gate"]])
                yield

        def stage2(tile, hp, own):
            u = tile * 4 + hp
            s1 = S1[u % 3]
            cs_ = SETS[u % 2]
            hm = C.cm[:, CM_HM:CM_HM + 2]
            eg = s1["eg"]
            gview = eg[:, 63:512:64]
            P.op("pool", lambda e: e.tensor_copy(cs_["gC"][:], gview), reads=[eg], writes=[cs_["gC"]])
            P.op("pool", lambda e: e.tensor_tensor(hmg[:], bc(hm, [128, 8, 2], [1]), bc(gview, [128, 8, 2], [2]), ALU.mult),
                 reads=[C.cm, eg], writes=[hmg])
            hm4 = bc(hm, [128, 8, 2, 64], [1, 3])
            hmg4 = bc(hmg[:], [128, 8, 2, 64], [3])

            def ex(x):
                return bc(x[:].rearrange("p (c s) -> p c s", s=64), [128, 8, 2, 64], [2])
            P.op("dve", lambda e: e.tensor_tensor(Lb[:], ex(s1["bt"]), hm4, ALU.mult), reads=[s1["bt"], C.cm], writes=[Lb])
            P.op("pool", lambda e: e.tensor_tensor(Ra[:], ex(s1["at"]), hm4, ALU.mult), reads=[s1["at"], C.cm], writes=[Ra])
            P.op("dve", lambda e: e.tensor_tensor(Lk[:], ex(s1["kt"]), hm4, ALU.mult), reads=[s1["kt"], C.cm], writes=[Lk])
            yield
            P.op("pool", lambda e: e.tensor_tensor(KH[:], ex(s1["kt"]), hmg4, ALU.mult), reads=[s1["kt"], hmg], writes=[KH])
            P.op("dve", lambda e: e.tensor_tensor(BH[:], ex(s1["bt"]), hmg4, ALU.mult), reads=[s1["bt"], hmg], writes=[BH])
            P.op("pool", lambda e: e.tensor_tensor(VB[:], ex(s1["vz"]), hm4, ALU.mult), reads=[s1["vz"], C.cm], writes=[VB])
            if own:
                Rr4 = cs_["Rr"][:].rearrange("p c (h s) -> p c h s", h=2)
                P.op("dve", lambda e: e.tensor_tensor(Rr4, ex(s1["rt"]), hm4, ALU.mult), reads=[s1["rt"], C.cm], writes=[cs_["Rr"]])
            yield

            def blk(t, c):
                return t[:, c].rearrange("p h s -> p (h s)")
            Tc = Tt[0]
            for c in range(8):
                psA, psA2 = psA_[c % 2], psA2_[c % 2]
                P.mm(psA, psA[:, 0:128], Lb, blk(Lb, c), Ra, blk(Ra, c))
                P.mm(psA, psA[:, 128:256], Lk, blk(Lk, c), Ra, blk(Ra, c))
                P.mm(psA, psA[:, 256:384], Ra, blk(Ra, c), Lb, blk(Lb, c))
                P.op("dve", lambda e: e.tensor_tensor(NA[:, c].rearrange("p a t -> p (a t)"), psA[:, 0:256], mask4[:, 0:256], ALU.mult),
                     reads=[psA, mask4], writes=[NA], partial=True)
                P.op("dve", lambda e: e.tensor_tensor(PT0[:, c, :], psA[:, 256:384], C.cm[:, CM_STRICT_T:CM_STRICT_T + 128], ALU.mult),
                     reads=[psA, C.cm], writes=[PT0], partial=True)
                if own:
                    for a_, lx in enumerate((Lb, Lk)):
                        P.mm(psA2, psA2[:], lx, blk(lx, c), cs_["Rr"], cs_["Rr"][:, c, :])
                        P.op("dve", lambda e: e.tensor_tensor(cs_["ABK"][:, c, a_, :], psA2[:], mask4[:, 256:384], ALU.mult),
                             reads=[psA2, mask4], writes=[cs_["ABK"]], partial=True)
                P.op("pool", lambda e: e.tensor_tensor(Tc[:, c, :], NA[:, c, 0, :], C.cm[:, CM_EYE:CM_EYE + 128], ALU.add),
                     reads=[NA, C.cm], writes=[Tc], partial=True)
                if c % 2 == 1:
                    yield
            for qi, (src, dst_b) in enumerate(((KH, cs_["KHt"]), (BH, cs_["BHt"]), (Ra, Rat), (VB, cs_["VBt"]))):
                for c in range(8):
                    P.tr(pstr, pstr[:, c * 128:(c + 1) * 128], src, blk(src, c), C.ident_b, C.ident_b[:])
                evac(P, qi, dst_b, dst_b[:].rearrange("p c t -> p (c t)"), pstr, pstr[:])
                yield
            for cb in range(2):
                c0 = cb * 4
                Pc, PTc = None, None
                Tcur = Tt[0]
                for lvl in range(1, 6):
                    Pn, PTn = Pp[lvl % 2], PTp[lvl % 2]
                    for c in range(4):
                        lp = NA[:, c0 + c, 0, :] if lvl == 1 else Pc[:, c, :]
                        lpt = PT0[:, c0 + c, :] if lvl == 1 else PTc[:, c, :]
                        lpb = NA if lvl == 1 else Pc
                        lptb = PT0 if lvl == 1 else PTc
                        if lvl < 5:
                            P.mm(psP, psP[:, c * 128:(c + 1) * 128], lptb, lpt, lpb, lp)
                        P.mm(psPT, psPT[:, c * 128:(c + 1) * 128], lpb, lp, lptb, lpt)
                    if lvl < 5:
                        P.op("act", lambda e: e.activation(Pn[:].rearrange("p c t -> p (c t)"), psP[:], AF.Copy), reads=[psP], writes=[Pn])
                    P.op("act", lambda e: e.activation(PTn[:].rearrange("p c t -> p (c t)"), psPT[:], AF.Copy), reads=[psPT], writes=[PTn])
                    Tn = Tt[lvl % 2]
                    for c in range(4):
                        P.mm(psP, psP[:, c * 128:(c + 1) * 128], PTn, PTn[:, c, :], Tcur, Tcur[:, c0 + c, :])
                    P.op("dve", lambda e: e.tensor_tensor(Tn[:, c0:c0 + 4, :].rearrange("p c t -> p (c t)"), psP[:],
                                                          Tcur[:, c0:c0 + 4, :].rearrange("p c t -> p (c t)"), ALU.add),
                         reads=[psP, Tcur], writes=[Tn], partial=True)
                    Pc, PTc, Tcur = Pn, PTn, Tn
                    yield
                Tfin = Tcur
                p_ = nextpp()
                for c in range(4):
                    P.mm(p_, p_[:, c * 128:(c + 1) * 128], NA, NA[:, c0 + c, 1, :], cs_["VBt"], cs_["VBt"][:, c0 + c, :])
                evac(P, 0, Yb, Yb[:].rearrange("p c t -> p (c t)"), p_, p_[:])
                p_ = nextpp()
                for c in range(4):
                    P.mm(p_, p_[:, c * 128:(c + 1) * 128], Tfin, Tfin[:, c0 + c, :], Yb, Yb[:, c, :])
                evac(P, 1, cs_["W2"], cs_["W2"][:, c0:c0 + 4, :].rearrange("p c t -> p (c t)"), p_, p_[:])
                p_ = nextpp()
                for c in range(4):
                    P.mm(p_, p_[:, c * 128:(c + 1) * 128], Rat, Rat[:, c0 + c, :], Tfin, Tfin[:, c0 + c, :])
                evac(P, 0, cs_["W1"], cs_["W1"][:, c0:c0 + 4, :].rearrange("p c t -> p (c t)"), p_, p_[:])
                yield

        def stage3(tile, hp, own):
            u = tile * 4 + hp
            s1 = S1[u % 3]
            cs_ = SETS[u % 2]
            psY = psYb
            for c in range(8):
                ub, ob = Ub[c % 2], Ob[c % 2]
                P.mm(psU, psU[:], cs_["W1"], cs_["W1"][:, c, :], Sb[hp], Sb[hp][:])
                P.op("dve", lambda e: e.tensor_tensor(ub[:], psU[:], cs_["W2"][:, c, :], ALU.add), reads=[psU, cs_["W2"]], writes=[ub])
                P.mm(psS, psS[:], cs_["KHt"], cs_["KHt"][:, c, :], cs_["VBt"], cs_["VBt"][:, c, :], start=True, stop=False)
                P.mm(psS, psS[:], cs_["BHt"], cs_["BHt"][:, c, :], ub, ub[:], start=False, stop=True)
                if own:
                    P.mm(psO, psO[:], cs_["Rr"], cs_["Rr"][:, c, :], Sb[hp], Sb[hp][:], start=True, stop=False)
                    P.mm(psO, psO[:], cs_["ABK"], cs_["ABK"][:, c, 0, :], ub, ub[:], start=False, stop=False)
                    P.mm(psO, psO[:], cs_["ABK"], cs_["ABK"][:, c, 1, :], cs_["VBt"], cs_["VBt"][:, c, :], start=False, stop=True)
                gc = cs_["gC"][:, c:c + 1]
                P.op("dve", lambda e: e.scalar_tensor_tensor(Sb[hp][:], Sf[hp][:], gc, psS[:], ALU.mult, ALU.add),
                     reads=[Sf[hp], psS, cs_["gC"]], writes=[Sb[hp]])
                P.op("dve", lambda e: e.scalar_tensor_tensor(Sf[hp][:], Sf[hp][:], gc, psS[:], ALU.mult, ALU.add),
                     reads=[Sf[hp], psS, cs_["gC"]], writes=[Sf[hp]])
                if own:
                    P.op("act", lambda e: e.activation(ob[:], psO[:], AF.Copy), reads=[psO], writes=[ob])
                    P.mm(psY, psY[:, c * 64:(c + 1) * 64], ob, ob[:], C.selb, C.selb[:])
                yield
            if own and tile >= own_from:
                yr, cen, sq, rs = post["yr"], post["cen"], post["sq"], post["rs"]
                P.op("act", lambda e: e.activation(yr[:], psY[:], AF.Copy), reads=[psY], writes=[yr])
                pm = nextpp()
                P.mm(pm, pm[:], C.bones_f, C.bones_f[:], yr, yr[:])
                P.op("dve", lambda e: e.scalar_tensor_tensor(cen[:], pm[:], -1.0 / 64.0, yr[:], ALU.mult, ALU.add), reads=[pm, yr], writes=[cen])
                P.op("pool", lambda e: e.tensor_tensor(sq[:], cen[:], cen[:], ALU.mult), reads=[cen], writes=[sq])
                pv_ = nextpp()
                P.mm(pv_, pv_[:], C.bones_f, C.bones_f[:], sq, sq[:])
                P.op("dve", lambda e: e.tensor_scalar(rs[:], pv_[:], 1.0 / 64.0, GN_EPS, ALU.mult, ALU.add), reads=[pv_], writes=[rs])
                P.op("act", lambda e: e.activation(rs[:], rs[:], AF.Sqrt), reads=[rs], writes=[rs])
                P.op("dve", lambda e: e.reciprocal(rs[:], rs[:]), reads=[rs], writes=[rs])
                P.op("dve", lambda e: e.tensor_tensor(cen[:], cen[:], rs[:], ALU.mult), reads=[cen, rs], writes=[cen])
                P.op("dve", lambda e: e.tensor_scalar(cen[:], cen[:], pv(PV_LG + hp), pv(PV_LB + hp), ALU.mult, ALU.add), reads=[cen, C.pv], writes=[cen])
                P.op("pool", lambda e: e.tensor_tensor(cen[:], cen[:], s1["bonus"][:], ALU.add), reads=[cen, s1["bonus"]], writes=[cen])
                col = hp * OWN + (tile - own_from) * 512
                yo = yos[u % 2]
                P.op("pool", lambda e: e.tensor_tensor(yo[:], cen[:], s1["gate"][:], ALU.mult), reads=[cen, s1["gate"]], writes=[yo])
                P.dma("sp", yrT[:, col:col + 512], yo[:], reads=[yo], writes=[yrT])
                yield

        def pre1(tile, hp):
            own = tile >= own_from
            if hp == 0:
                build_xT(tile)
                yield
            yield from stage1(tile, hp, own)

        def pre2(tile, hp):
            yield from stage2(tile, hp, tile >= own_from)

        units = [(t, h) for t in range(ntiles) for h in range(4)]
        nu = len(units)
        for g_ in (pre1(*units[0]), pre2(*units[0])):
            for _ in g_:
                pass
        if nu > 1:
            for _ in pre1(*units[1]):
                pass
        for i, (t, h) in enumerate(units):
            gens = [stage3(t, h, t >= own_from)]
            if i + 1 < nu:
                gens.append(pre2(*units[i + 1]))
            if i + 2 < nu:
                gens.append(pre1(*units[i + 2]))
            interleave(*gens)
        for hp in range(4):
            p_ = nextpp()
            P.mm(p_, p_[:, 0:128], Sf[hp], Sf[hp][:], C.ident_f, C.ident_f[:])
            so = post["yr"]
            P.op("dve", lambda e: e.tensor_copy(so[:, 0:128], p_[:, 0:128]), reads=[p_], writes=[so])
            for hh in range(2):
                P.dma("sp", dout["wkv_p"][hp * 2 + hh, :, :], so[hh * 64:(hh + 1) * 64, hh * 64:(hh + 1) * 64], reads=[so])
        p_ = nextpp()
        P.mm(p_, p_[0:13, 0:128], shout, shout[:, 0:13], C.ident_f, C.ident_f[:])
        so = post["cen"]
        P.op("dve", lambda e: e.tensor_copy(so[0:13, 0:128], p_[0:13, 0:128]), reads=[p_], writes=[so])
        P.dma("sp", dout["shift_p"].rearrange("(c p) -> c p", p=128), so[0:13, 0:128], reads=[so])


CS_STRICT, CS_INCL, CS_STRICT_T, CS_ROW, CS_G1, CS_G23, NCS = 0, 128, 256, 384, 400, 532, 1048


def make_cmask_s():
    cs = np.zeros((128, NCS), np.float32)
    p = np.arange(128)
    h1, b1, t1 = p[:, None] // 64, (p[:, None] % 64) // 4, p[:, None] % 4
    h2, b2, t2 = p[None, :] // 64, (p[None, :] % 64) // 4, p[None, :] % 4
    same = (h1 == h2) & (b1 == b2)
    cs[:, CS_STRICT:CS_STRICT + 128] = (same & (t1 < t2))
    cs[:, CS_INCL:CS_INCL + 128] = (same & (t1 <= t2))
    cs[:, CS_STRICT_T:CS_STRICT_T + 128] = (same & (t1 > t2))
    cs[:, CS_ROW:CS_ROW + 16] = (((p[:, None] % 64) // 4) == np.arange(16)[None, :])
    t = p % 32
    g1 = np.zeros((128, 132), np.float32)
    r = np.arange(128)[None, :]
    g1[:, 0:128] = np.where(r >= t[:, None], 0.0, NEGM)
    u = np.arange(4)[None, :]
    g1[:, 128:132] = np.where(u <= t[:, None], 0.0, NEGM)
    g1[t >= 4] = 0.0
    cs[:, CS_G1:CS_G1 + 132] = g1
    g23 = np.zeros((128, 516), np.float32)
    c = (np.arange(512) // 128)[None, :]
    g23[:, 0:512] = np.where(c == t[:, None], 0.0, NEGM)
    g23[:, 512:516] = np.where(u == t[:, None], 0.0, NEGM)
    g23[t >= 4] = 0.0
    cs[:, CS_G23:CS_G23 + 516] = g23
    return cs


def make_colmask():
    col = np.arange(128)
    m = np.zeros((128, 16, 128), np.float32)
    for b in range(16):
        m[:, b, :] = (((col % 64) // 4) == b)[None, :]
    return m.reshape(128, 2048)


def emit_sample(P, C, din, dout, flags={}):
    yrS = P.sbuf("yrS", [128, 4, 64], BF16)
    yaS = P.sbuf("yaS", [128, 4, 64], BF16)
    xTs = P.sbuf("xTs", [128, 8, 64], BF16)
    cms = P.sbuf("cms", [128, NCS], F32)
    P.dma("sp", cms[:], din["cmask_s"][:, :], writes=[cms])
    with scope(P):
        xb = P.sbuf("xbS", [64, 1024], BF16)
        px = carve(C, 0, 0, 512, "psS_x", BF16)
        P.dma("pool", xb[:], din["xs"][:, :], writes=[xb])
        for k in range(8):
            P.tr(px, px[:, k * 64:(k + 1) * 64], xb, xb[:, k * 128:(k + 1) * 128], C.ident_b, C.ident_b[0:64, 0:64])
        P.op("dve", lambda e: e.tensor_copy(xTs[:].rearrange("p k t -> p (k t)"), px[:, 0:512]), reads=[px], writes=[xTs])
    if flags.get("s_rwkv", True):
        emit_rwkv_sample(P, C, din, dout, xTs, cms, yrS)
    if flags.get("s_attn", True):
        emit_attn_sample(P, C, din, dout, xTs, cms, yaS)
    if flags.get("s_out", True):
        emit_out_phase(P, C, din, xTs, 0, din["xs"], yrS, yaS, 64, dout["y_s"], "s")


def emit_rwkv_sample(P, C, din, dout, xTs, cms, yrS):
    W = 64
    with scope(P):
        w_r = P.sbuf("w_rS", [128, 8, RW_COLS], BF16)
        wl2 = P.sbuf("wl2S", [128, 512], BF16)
        bones_b = P.sbuf("bones_bS", [128, 128], BF16)
        scanm = P.sbuf("scanmS", [128, W], F32)
        colm = P.sbuf("colm", [128, 16, 128], BF16)
        shs = P.sbuf("shs", [16, SHIFT_COLS], F32)
        smu = P.sbuf("smu", [128, 13, 16], F32)
        shout = P.sbuf("shoutS", [128, 13, 16], F32)
        sho2 = P.sbuf("sho2", [16, SHIFT_COLS], F32)
        lor = P.sbuf("lorS", [128, W], BF16)
        wdad = P.sbuf("wdadS", [128, W], F32)
        bm = P.sbuf("bmS", [128, W], F32)
        tmp = {nm: P.sbuf("ts_" + nm, [128, W], F32) for nm in
               ("r", "k", "vz", "sg", "a", "cs", "eg", "eng", "kk", "rn", "km", "bv", "bonus", "gate", "gf", "yr", "cen", "sq", "rs")}
        tb = {nm: P.sbuf("tsb_" + nm, [128, W], BF16) for nm in ("rt", "kt", "bt", "at", "sqb", "ktg", "btg")}
        ex_ = {nm: P.sbuf("exs_" + nm, [128, 2, W], BF16) for nm in ("Lb", "Lk", "Ra", "Rr", "KH", "BH", "VB")}
        sq_ = {nm: P.sbuf("sqs_" + nm, [128, 128], BF16) for nm in
               ("N", "ak", "br", "kr", "NT", "T0", "P1T", "T", "KHt", "BHt", "Rat", "VBt", "Y", "W1", "W2", "Ub", "Ob")}
        W1b = P.sbuf("W1b", [128, 16, 128], BF16)
        Rrb = P.sbuf("Rrb", [128, 16, 128], BF16)
        KHtb = P.sbuf("KHtb", [128, 16, 128], BF16)
        BHtb = P.sbuf("BHtb", [128, 16, 128], BF16)
        Sv = P.sbuf("Sv", [128, 16, 64], F32)
        Svx = P.sbuf("Svx", [128, 16, 2, 64], F32)
        Sf = P.sbuf("SfS", [128, 16, 128], F32)
        Sb = P.sbuf("SbS", [128, 16, 128], BF16)
        So = P.sbuf("SoS", [128, 16, 128], F32)
        pp = [carve(C, i, 0, 512, "psS_p%d" % i) for i in range(2)]
        ptr = carve(C, 2, 0, 256, "psS_tr", BF16)
        pA = carve(C, 3, 0, 384, "psS_A")
        pB = carve(C, 2, 384, 512, "psS_B")
        pbig = [carve(C, 4 + i, 0, 512, "psS_big%d" % i) for i in range(4)]
        ppi = [0]

        def nextpp():
            b = pp[ppi[0] % 2]
            ppi[0] += 1
            return b

        def pv(col):
            return C.pv[:, col:col + 1]

        P.dma("pool", w_r[:], din["w_rw"].rearrange("(k p) c -> p k c", p=128), writes=[w_r])
        P.dma("pool", wl2[:], din["w_l2"][:, :], writes=[wl2])
        P.dma("pool", colm[:].rearrange("p b c -> p (b c)"), din["colmask"][:, :], writes=[colm])
        P.dma("sp", shs[:], din["shift_s"][:, :], writes=[shs])
        P.op("pool", lambda e: e.tensor_copy(bones_b[:], C.cm[:, CM_BONES:CM_BONES + 128]), reads=[C.cm], writes=[bones_b])
        P.op("pool", lambda e: e.memset(scanm[:], 1.0), writes=[scanm])
        P.op("pool", lambda e: e.memset(scanm[:, 0:W:4], 0.0), writes=[scanm])
        p_ = nextpp()
        for c in range(13):
            P.mm(p_, p_[:, c * 16:(c + 1) * 16], shs, shs[0:16, c * 128:(c + 1) * 128], C.ident_f, C.ident_f[0:16, 0:16])
        P.op("dve", lambda e: e.tensor_tensor(smu[:], p_[:, 0:208].rearrange("p (c b) -> p c b", b=16),
                                              bc(C.pv[:, PV_MU:PV_MU + 13], [128, 13, 16], [2]), ALU.mult),
             reads=[p_, C.pv], writes=[smu])

        def proj_shift(c, dst):
            q_ = nextpp()
            for k in range(8):
                P.mm(q_, q_[:, 0:W], w_r, w_r[:, k, c * 128:(c + 1) * 128], xTs, xTs[:, k, :], start=(k == 0), stop=(k == 7))
            P.op("act", lambda e: e.mul(bm[:], q_[:, 0:W], pv(PV_MU + c)), reads=[q_, C.pv], writes=[bm])
            q3 = q_[:, 0:W].rearrange("p (b t) -> p b t", t=4)
            d3 = dst[:].rearrange("p (b t) -> p b t", t=4)
            b3 = bm[:].rearrange("p (b t) -> p b t", t=4)
            P.op("dve", lambda e: e.scalar_tensor_tensor(d3[:, :, 1:4], q3[:, :, 1:4], pv(PV_OMM + c), b3[:, :, 0:3], ALU.mult, ALU.add),
                 reads=[q_, bm, C.pv], writes=[dst], partial=True)
            P.op("dve", lambda e: e.scalar_tensor_tensor(d3[:, :, 0:1], q3[:, :, 0:1], pv(PV_OMM + c), smu[:, c, :].unsqueeze(2), ALU.mult, ALU.add),
                 reads=[q_, smu, C.pv], writes=[dst], partial=True)
            P.op("act", lambda e: e.activation(shout[:, c, :].unsqueeze(2), q3[:, :, 3:4], AF.Copy), reads=[q_], writes=[shout], partial=True)

        proj_shift(12, wdad)
        P.op("act", lambda e: e.activation(lor[0:64, :], wdad[0:64, :], AF.Tanh), reads=[wdad], writes=[lor], partial=True)
        P.op("dve", lambda e: e.tensor_copy(lor[64:128, :], wdad[64:128, :]), reads=[wdad], writes=[lor], partial=True)
        hm = C.cm[:, CM_HM:CM_HM + 2]
        hm3 = bc(hm, [128, 2, W], [2])
        for hp in range(4):
            r, k, vz, sg, a, cs, eg, eng, kk, rn, km, bv = (tmp[n] for n in ("r", "k", "vz", "sg", "a", "cs", "eg", "eng", "kk", "rn", "km", "bv"))
            proj_shift(hp, r)
            proj_shift(4 + hp, k)
            proj_shift(8 + hp, vz)
            q_ = nextpp()
            P.mm(q_, q_[:, 0:W], wl2, wl2[0:64, hp * 128:(hp + 1) * 128], lor, lor[0:64, :])
            P.op("act", lambda e: e.activation(sg[:], q_[:, 0:W], AF.Sigmoid, bias=pv(PV_W0 + hp), scale=1.0), reads=[q_, C.pv], writes=[sg])
            q2 = nextpp()
            P.mm(q2, q2[:, 0:W], wl2, wl2[64:128, hp * 128:(hp + 1) * 128], lor, lor[64:128, :])
            P.op("act", lambda e: e.activation(a[:], q2[:, 0:W], AF.Sigmoid, bias=pv(PV_A0 + hp), scale=1.0), reads=[q2, C.pv], writes=[a])
            P.op("dve", lambda e: e.tensor_tensor_scan(cs[:], scanm[:], sg[:], 0.0, ALU.mult, ALU.add), reads=[scanm, sg], writes=[cs])
            P.op("act", lambda e: e.activation(eg[:], cs[:], AF.Exp, scale=-C0), reads=[cs], writes=[eg])
            P.op("act", lambda e: e.activation(eng[:], cs[:], AF.Exp, scale=C0), reads=[cs], writes=[eng])
            P.op("pool", lambda e: e.tensor_tensor(cs[:], cs[:], sg[:], ALU.subtract), reads=[cs, sg], writes=[cs])
            P.op("act", lambda e: e.activation(cs[:], cs[:], AF.Exp, scale=-C0), reads=[cs], writes=[cs])
            P.op("dve", lambda e: e.tensor_scalar(kk[:], k[:], pv(PV_KK + hp), None, ALU.mult), reads=[k, C.pv], writes=[kk])
            P.op("pool", lambda e: e.tensor_tensor(tb["sqb"][:], kk[:], kk[:], ALU.mult), reads=[kk], writes=[tb["sqb"]])
            q3_ = nextpp()
            P.mm(q3_, q3_[:, 0:W], bones_b, bones_b[:], tb["sqb"], tb["sqb"][:])
            P.op("dve", lambda e: e.tensor_scalar(rn[:], q3_[:, 0:W], 1e-24, None, ALU.max), reads=[q3_], writes=[rn])
            P.op("act", lambda e: e.activation(rn[:], rn[:], AF.Sqrt), reads=[rn], writes=[rn])
            P.op("dve", lambda e: e.reciprocal(rn[:], rn[:]), reads=[rn], writes=[rn])
            P.op("dve", lambda e: e.tensor_tensor(kk[:], kk[:], rn[:], ALU.mult), reads=[kk, rn], writes=[kk])
            P.op("dve", lambda e: e.tensor_scalar(km[:], a[:], -1.0, pv(PV_KA + hp), ALU.add, ALU.mult), reads=[a, C.pv], writes=[km])
            P.op("dve", lambda e: e.scalar_tensor_tensor(km[:], km[:], 1.0, k[:], ALU.add, ALU.mult), reads=[km, k], writes=[km])
            P.op("pool", lambda e: e.tensor_tensor(bv[:], kk[:], a[:], ALU.mult), reads=[kk, a], writes=[bv])
            P.op("pool", lambda e: e.tensor_tensor(tb["rt"][:], r[:], eg[:], ALU.mult), reads=[r, eg], writes=[tb["rt"]])
            P.op("dve", lambda e: e.tensor_tensor(tb["kt"][:], km[:], eng[:], ALU.mult), reads=[km, eng], writes=[tb["kt"]])
            P.op("pool", lambda e: e.tensor_tensor(tb["bt"][:], bv[:], eng[:], ALU.mult), reads=[bv, eng], writes=[tb["bt"]])
            P.op("dve", lambda e: e.scalar_tensor_tensor(tb["at"][:], kk[:], -1.0, cs[:], ALU.mult, ALU.mult), reads=[kk, cs], writes=[tb["at"]])
            P.op("dve", lambda e: e.scalar_tensor_tensor(tb["sqb"][:], r[:], pv(PV_RK + hp), km[:], ALU.mult, ALU.mult), reads=[r, km, C.pv], writes=[tb["sqb"]])
            q4 = nextpp()
            P.mm(q4, q4[:, 0:W], bones_b, bones_b[:], tb["sqb"], tb["sqb"][:])
            P.op("dve", lambda e: e.tensor_tensor(tmp["bonus"][:], q4[:, 0:W], vz[:], ALU.mult), reads=[q4, vz], writes=[tmp["bonus"]])
            q5 = nextpp()
            for kq in range(8):
                P.mm(q5, q5[:, 0:W], w_r, w_r[:, kq, (13 + hp) * 128:(14 + hp) * 128], xTs, xTs[:, kq, :], start=(kq == 0), stop=(kq == 7))
            P.op("act", lambda e: e.activation(tmp["gate"][:], q5[:, 0:W], AF.Silu), reads=[q5], writes=[tmp["gate"]])
            gcol = eg[:, 3:W:4]
            gf = tmp["gf"]
            P.op("pool", lambda e: e.tensor_copy(gf[:].rearrange("p (b t) -> p b t", t=4), bc(gcol, [128, 16, 4], [2])), reads=[eg], writes=[gf])
            P.op("dve", lambda e: e.tensor_tensor(tb["ktg"][:], tb["kt"][:], gf[:], ALU.mult), reads=[tb["kt"], gf], writes=[tb["ktg"]])
            P.op("pool", lambda e: e.tensor_tensor(tb["btg"][:], tb["bt"][:], gf[:], ALU.mult), reads=[tb["bt"], gf], writes=[tb["btg"]])

            def ex(x):
                return bc(x[:], [128, 2, W], [1])
            for i, (nm, src) in enumerate((("Lb", tb["bt"]), ("Lk", tb["kt"]), ("Ra", tb["at"]), ("Rr", tb["rt"]),
                                           ("KH", tb["ktg"]), ("BH", tb["btg"]), ("VB", vz))):
                P.op("dve" if i % 2 == 0 else "pool", lambda e: e.tensor_tensor(ex_[nm][:], ex(src), hm3, ALU.mult),
                     reads=[src, C.cm], writes=[ex_[nm]])

            def f2(nm):
                return ex_[nm][:].rearrange("p h s -> p (h s)")
            P.mm(pA, pA[:, 0:128], ex_["Lb"], f2("Lb"), ex_["Ra"], f2("Ra"))
            P.mm(pA, pA[:, 128:256], ex_["Lk"], f2("Lk"), ex_["Ra"], f2("Ra"))
            P.mm(pA, pA[:, 256:384], ex_["Ra"], f2("Ra"), ex_["Lb"], f2("Lb"))
            P.op("dve", lambda e: e.tensor_tensor(sq_["N"][:], pA[:, 0:128], cms[:, CS_STRICT:CS_STRICT + 128], ALU.mult), reads=[pA, cms], writes=[sq_["N"]])
            P.op("dve", lambda e: e.tensor_tensor(sq_["ak"][:], pA[:, 128:256], cms[:, CS_STRICT:CS_STRICT + 128], ALU.mult), reads=[pA, cms], writes=[sq_["ak"]])
            P.op("dve", lambda e: e.tensor_tensor(sq_["NT"][:], pA[:, 256:384], cms[:, CS_STRICT_T:CS_STRICT_T + 128], ALU.mult), reads=[pA, cms], writes=[sq_["NT"]])
            P.mm(pA, pA[:, 0:128], ex_["Lb"], f2("Lb"), ex_["Rr"], f2("Rr"))
            P.mm(pA, pA[:, 128:256], ex_["Lk"], f2("Lk"), ex_["Rr"], f2("Rr"))
            P.op("dve", lambda e: e.tensor_tensor(sq_["br"][:], pA[:, 0:128], cms[:, CS_INCL:CS_INCL + 128], ALU.mult), reads=[pA, cms], writes=[sq_["br"]])
            P.op("dve", lambda e: e.tensor_tensor(sq_["kr"][:], pA[:, 128:256], cms[:, CS_INCL:CS_INCL + 128], ALU.mult), reads=[pA, cms], writes=[sq_["kr"]])
            P.op("pool", lambda e: e.tensor_tensor(sq_["T0"][:], sq_["N"][:], C.cm[:, CM_EYE:CM_EYE + 128], ALU.add), reads=[sq_["N"], C.cm], writes=[sq_["T0"]])
            P.mm(pA, pA[:, 0:128], sq_["N"], sq_["N"][:], sq_["NT"], sq_["NT"][:])
            P.op("dve", lambda e: e.tensor_copy(sq_["P1T"][:], pA[:, 0:128]), reads=[pA], writes=[sq_["P1T"]])
            P.mm(pA, pA[:, 0:128], sq_["P1T"], sq_["P1T"][:], sq_["T0"], sq_["T0"][:])
            P.op("dve", lambda e: e.tensor_tensor(sq_["T"][:], pA[:, 0:128], sq_["T0"][:], ALU.add), reads=[pA, sq_["T0"]], writes=[sq_["T"]])
            for i, (src, dst) in enumerate((("KH", "KHt"), ("BH", "BHt"), ("Ra", "Rat"), ("VB", "VBt"))):
                P.tr(ptr, ptr[:, i * 128:(i + 1) * 128], ex_[src], f2(src), C.ident_b, C.ident_b[:])
            for i, dst in enumerate(("KHt", "BHt", "Rat", "VBt")):
                evac(P, 0, sq_[dst], sq_[dst][:], ptr, ptr[:, i * 128:(i + 1) * 128])
            P.mm(pA, pA[:, 0:128], sq_["ak"], sq_["ak"][:], sq_["VBt"], sq_["VBt"][:])
            P.op("act", lambda e: e.activation(sq_["Y"][:], pA[:, 0:128], AF.Copy), reads=[pA], writes=[sq_["Y"]])
            P.mm(pA, pA[:, 128:256], sq_["T"], sq_["T"][:], sq_["Y"], sq_["Y"][:])
            P.op("act", lambda e: e.activation(sq_["W2"][:], pA[:, 128:256], AF.Copy), reads=[pA], writes=[sq_["W2"]])
            P.mm(pA, pA[:, 256:384], sq_["Rat"], sq_["Rat"][:], sq_["T"], sq_["T"][:])
            P.op("dve", lambda e: e.tensor_copy(sq_["W1"][:], pA[:, 256:384]), reads=[pA], writes=[sq_["W1"]])
            P.dma("sp", Sv[:], din["wkv_s"][:, 2 * hp:2 * hp + 2, :, :].rearrange("b h v k -> (h v) b k"), writes=[Sv], partial=False)
            P.op("dve", lambda e: e.tensor_tensor(Svx[:], bc(Sv[:], [128, 16, 2, 64], [2]), bc(hm, [128, 16, 2, 64], [1, 3]), ALU.mult),
                 reads=[Sv, C.cm], writes=[Svx])
            for b in range(16):
                pb_ = pbig[b // 4]
                P.mm(pb_, pb_[:, (b % 4) * 128:(b % 4 + 1) * 128], Svx, Svx[:, b].rearrange("p h k -> p (h k)"), C.ident_f, C.ident_f[:])
            for i in range(4):
                P.op("dve", lambda e: e.tensor_copy(Sf[:, 4 * i:4 * i + 4, :].rearrange("p b c -> p (b c)"), pbig[i][:]), reads=[pbig[i]], writes=[Sf], partial=True)
                P.op("act", lambda e: e.activation(Sb[:, 4 * i:4 * i + 4, :].rearrange("p b c -> p (b c)"), pbig[i][:], AF.Copy), reads=[pbig[i]], writes=[Sb], partial=True)
            P.op("dve", lambda e: e.tensor_tensor(W1b[:], bc(sq_["W1"][:], [128, 16, 128], [1]), colm[:], ALU.mult), reads=[sq_["W1"], colm], writes=[W1b])
            P.op("pool", lambda e: e.tensor_tensor(Rrb[:], bc(f2("Rr"), [128, 16, 128], [1]), colm[:], ALU.mult), reads=[ex_["Rr"], colm], writes=[Rrb])
            rowm = bc(cms[:, CS_ROW:CS_ROW + 16], [128, 16, 128], [2])
            P.op("dve", lambda e: e.tensor_tensor(KHtb[:], bc(sq_["KHt"][:], [128, 16, 128], [1]), rowm, ALU.mult), reads=[sq_["KHt"], cms], writes=[KHtb])
            P.op("pool", lambda e: e.tensor_tensor(BHtb[:], bc(sq_["BHt"][:], [128, 16, 128], [1]), rowm, ALU.mult), reads=[sq_["BHt"], cms], writes=[BHtb])
            for b in range(16):
                P.mm(pB, pB[:, 0:128], W1b, W1b[:, b, :], Sb, Sb[:, b, :], start=(b == 0), stop=(b == 15))
            P.op("dve", lambda e: e.tensor_tensor(sq_["Ub"][:], pB[:, 0:128], sq_["W2"][:], ALU.add), reads=[pB, sq_["W2"]], writes=[sq_["Ub"]])
            for b in range(16):
                P.mm(pA, pA[:, 0:128], Rrb, Rrb[:, b, :], Sb, Sb[:, b, :], start=(b == 0), stop=False)
            P.mm(pA, pA[:, 0:128], sq_["br"], sq_["br"][:], sq_["Ub"], sq_["Ub"][:], start=False, stop=False)
            P.mm(pA, pA[:, 0:128], sq_["kr"], sq_["kr"][:], sq_["VBt"], sq_["VBt"][:], start=False, stop=True)
            P.op("act", lambda e: e.activation(sq_["Ob"][:], pA[:, 0:128], AF.Copy), reads=[pA], writes=[sq_["Ob"]])
            for b in range(16):
                pb_ = pbig[b // 4]
                sl = pb_[:, (b % 4) * 128:(b % 4 + 1) * 128]
                P.mm(pb_, sl, KHtb, KHtb[:, b, :], sq_["VBt"], sq_["VBt"][:], start=True, stop=False)
                P.mm(pb_, sl, BHtb, BHtb[:, b, :], sq_["Ub"], sq_["Ub"][:], start=False, stop=True)
            for i in range(4):
                sfv = Sf[:, 4 * i:4 * i + 4, :]
                P.op("dve", lambda e: e.tensor_tensor(sfv, sfv, bc(gcol[:, 4 * i:4 * i + 4], [128, 4, 128], [2]), ALU.mult), reads=[Sf, eg], writes=[Sf])
                P.op("dve", lambda e: e.tensor_tensor(sfv, sfv, pbig[i][:].rearrange("p (b c) -> p b c", b=4), ALU.add), reads=[Sf, pbig[i]], writes=[Sf])
            for b in range(16):
                pb_ = pbig[b // 4]
                P.mm(pb_, pb_[:, (b % 4) * 128:(b % 4 + 1) * 128], Sf, Sf[:, b, :], C.ident_f, C.ident_f[:])
            for i in range(4):
                evac(P, i, So, So[:, 4 * i:4 * i + 4, :].rearrange("p b c -> p (b c)"), pbig[i], pbig[i][:])
            for hh in range(2):
                P.dma("sp", dout["wkv_so"][:, 2 * hp + hh, :, :].rearrange("b v k -> v b k"),
                      So[hh * 64:(hh + 1) * 64, :, hh * 64:(hh + 1) * 64], reads=[So])
            py = nextpp()
            P.mm(py, py[:, 0:W], sq_["Ob"], sq_["Ob"][:], C.selb, C.selb[:])
            yr, cen, sq, rs = tmp["yr"], tmp["cen"], tmp["sq"], tmp["rs"]
            P.op("act", lambda e: e.activation(yr[:], py[:, 0:W], AF.Copy), reads=[py], writes=[yr])
            pm = nextpp()
            P.mm(pm, pm[:, 0:W], C.bones_f, C.bones_f[:], yr, yr[:])
            P.op("dve", lambda e: e.scalar_tensor_tensor(cen[:], pm[:, 0:W], -1.0 / 64.0, yr[:], ALU.mult, ALU.add), reads=[pm, yr], writes=[cen])
            P.op("pool", lambda e: e.tensor_tensor(sq[:], cen[:], cen[:], ALU.mult), reads=[cen], writes=[sq])
            pv_ = nextpp()
            P.mm(pv_, pv_[:, 0:W], C.bones_f, C.bones_f[:], sq, sq[:])
            P.op("dve", lambda e: e.tensor_scalar(rs[:], pv_[:, 0:W], 1.0 / 64.0, GN_EPS, ALU.mult, ALU.add), reads=[pv_], writes=[rs])
            P.op("act", lambda e: e.activation(rs[:], rs[:], AF.Sqrt), reads=[rs], writes=[rs])
            P.op("dve", lambda e: e.reciprocal(rs[:], rs[:]), reads=[rs], writes=[rs])
            P.op("dve", lambda e: e.tensor_tensor(cen[:], cen[:], rs[:], ALU.mult), reads=[cen, rs], writes=[cen])
            P.op("dve", lambda e: e.tensor_scalar(cen[:], cen[:], pv(PV_LG + hp), pv(PV_LB + hp), ALU.mult, ALU.add), reads=[cen, C.pv], writes=[cen])
            P.op("pool", lambda e: e.tensor_tensor(cen[:], cen[:], tmp["bonus"][:], ALU.add), reads=[cen, tmp["bonus"]], writes=[cen])
            P.op("pool", lambda e: e.tensor_tensor(yrS[:, hp, :], cen[:], tmp["gate"][:], ALU.mult), reads=[cen, tmp["gate"]], writes=[yrS], partial=True)
        for c0 in range(0, 13, 4):
            n = min(4, 13 - c0)
            q_ = nextpp()
            for c in range(n):
                P.mm(q_, q_[0:16, c * 128:(c + 1) * 128], shout, shout[:, c0 + c, :], C.ident_f, C.ident_f[:])
            P.op("dve", lambda e: e.tensor_copy(sho2[0:16, c0 * 128:(c0 + n) * 128], q_[0:16, 0:n * 128]), reads=[q_], writes=[sho2], partial=True)
        P.dma("sp", dout["shift_so"][:, :], sho2[:], reads=[sho2])


def emit_attn_sample(P, C, din, dout, xTs, cms, yaS):
    with scope(P):
        wq = P.sbuf("wqS", [128, 8, 5120], BF16)
        qTs = P.sbuf("qTs", [128, 12, 64], BF16)
        kvn = P.sbuf("kvn", [64, 3, 2, 512], F32)
        kvnb = P.sbuf("kvnb", [64, 3, 2, 512], BF16)
        szs = P.sbuf("szs", [128, 4, 64], F32)
        oaT = P.sbuf("oaT", [128, 4, 64], F32)
        kvt = [P.sbuf("kvtS%d" % i, [128, 4, 2, 512], BF16) for i in range(2)]
        kvnew = [P.sbuf("kvnw%d" % i, [4, 2, 512], BF16) for i in range(2)]
        KT = [P.sbuf("KTs%d" % i, [128, 4, 512], BF16) for i in range(2)]
        KTn = [P.sbuf("KTn%d" % i, [128, 4, 4], BF16) for i in range(2)]
        ss = [P.sbuf("ssS%d" % i, [128, 516], F32) for i in range(2)]
        pb = [P.sbuf("pbS%d" % i, [128, 516], BF16) for i in range(2)]
        PT = [P.sbuf("PTs%d" % i, [128, 5, 128], BF16) for i in range(2)]
        og = [P.sbuf("ogS%d" % i, [128, 3, 128], F32) for i in range(2)]
        stt = [P.sbuf("sttS%d" % i, [128, 24], F32) for i in range(2)]
        ob16 = [P.sbuf("ob16S%d" % i, [128, 128], BF16) for i in range(2)]
        pp = [carve(C, i, 0, 512, "psSA_p%d" % i) for i in range(2)]
        pkt = [carve(C, 2 + i, 0, 512, "psSA_kt%d" % i, BF16) for i in range(2)]
        pS = carve(C, 4, 0, 512, "psSA_S")
        pSn = carve(C, 5, 0, 16, "psSA_Sn")
        pKn = carve(C, 5, 16, 32, "psSA_Kn", BF16)
        pO = carve(C, 5, 128, 256, "psSA_O")
        pT = carve(C, 6, 0, 384, "psSA_T", BF16)
        pOT = carve(C, 7, 0, 128, "psSA_OT")
        P.dma("pool", wq[:], din["w_qkvz"].rearrange("(k p) c -> p k c", p=128), writes=[wq])
        P.op("dve", lambda e: e.memset(pS[:], 0.0), writes=[pS])
        P.op("dve", lambda e: e.memset(pSn[:], 0.0), writes=[pSn])
        P.op("dve", lambda e: e.memset(pO[:], 0.0), writes=[pO])
        ec = [0]
        for c in range(12):
            p_ = pp[c % 2]
            for k in range(8):
                P.mm(p_, p_[:, 0:64], wq, wq[:, k, c * 128:(c + 1) * 128], xTs, xTs[:, k, :], start=(k == 0), stop=(k == 7))
            evac(P, c, qTs, qTs[:, c, :], p_, p_[:, 0:64])
        for hh in range(4):
            p_ = pp[hh % 2]
            for k in range(8):
                P.mm(p_, p_[:, 0:64], wq, wq[:, k, 4608 + hh * 128:4608 + (hh + 1) * 128], xTs, xTs[:, k, :], start=(k == 0), stop=(k == 7))
            P.op("act", lambda e: e.activation(szs[:, hh, :], p_[:, 0:64], AF.Silu), reads=[p_], writes=[szs], partial=True)
        for g in range(3):
            for kv in range(2):
                p_ = pp[(g * 2 + kv) % 2]
                c0 = 1536 * (1 + kv) + g * 512
                for k in range(8):
                    P.mm(p_, p_[0:64, :], xTs, xTs[:, k, :], wq, wq[:, k, c0:c0 + 512], start=(k == 0), stop=(k == 7))
                evac(P, g * 2 + kv, kvn, kvn[:, g, kv, :], p_, p_[0:64, :])
        P.op("pool", lambda e: e.tensor_copy(kvnb[:], kvn[:]), reads=[kvn], writes=[kvnb])
        for g in range(3):
            P.dma("sp", dout["kvs%d" % (g + 1)].rearrange("b t kv h e -> (b t) kv (h e)"), kvn[:, g], reads=[kvn])
        it = 0
        for b in range(16):
            ogb, sb_ = og[b % 2], stt[b % 2]
            for g in range(3):
                i2 = it % 2
                it += 1
                ntile = 1 if g == 0 else 4
                nk = ntile * 128
                kt_, kn_, KT_, KTn_, ss_, pb_, PT_ = kvt[i2], kvnew[i2], KT[i2], KTn[i2], ss[i2], pb[i2], PT[i2]
                cache = din["cache%d" % (g + 1)]
                d = GROUPS[g][1]
                if g == 0:
                    P.dma("pool", kt_[:, 0].rearrange("p kv c -> p (kv c)"), cache[b].rearrange("r kv h e -> r (kv h e)"), writes=[kt_], partial=False)
                else:
                    for cl in range(4):
                        src = cache[b, cl:GROUPS[g][0]:d].rearrange("r kv h e -> r (kv h e)")
                        P.dma("pool", kt_[:, cl].rearrange("p kv c -> p (kv c)"), src, writes=[kt_], partial=(cl > 0))
                P.dma("sp", kn_[:], kvnb[4 * b:4 * b + 4, g], reads=[kvnb], writes=[kn_], partial=False)
                for h in range(4):
                    pk = pkt[h // 2]
                    for cl in range(ntile):
                        P.tr(pk, pk[:, (h % 2) * 512 + cl * 128:(h % 2) * 512 + (cl + 1) * 128], kt_, kt_[:, cl, 0, h * 128:(h + 1) * 128],
                             C.ident_b, C.ident_b[:])
                    P.tr(pKn, pKn[:, h * 4:(h + 1) * 4], kn_, kn_[0:4, 0, h * 128:(h + 1) * 128], C.ident_b, C.ident_b[0:4, 0:4])
                for j in range(2):
                    evac(P, ec[0], KT_, KT_[:, 2 * j:2 * j + 2, 0:nk], pkt[j], pkt[j][:].rearrange("p (h k) -> p h k", h=2)[:, :, 0:nk])
                    ec[0] += 1
                evac(P, ec[0], KTn_, KTn_[:].rearrange("p h k -> p (h k)"), pKn, pKn[:, 0:16])
                ec[0] += 1
                for h in range(4):
                    qv = qTs[:, g * 4 + h, 4 * b:4 * b + 4]
                    P.mm(pS, pS[32 * h:32 * h + 4, 0:nk], qTs, qv, KT_, KT_[:, h, 0:nk], tile_position=(0, 32 * h))
                    P.mm(pSn, pSn[32 * h:32 * h + 4, 0:4], qTs, qv, KTn_, KTn_[:, h, :], tile_position=(0, 32 * h))
                mk = cms[:, CS_G1:CS_G1 + 132] if g == 0 else cms[:, CS_G23:CS_G23 + 516]
                P.op("dve", lambda e: e.scalar_tensor_tensor(ss_[:, 0:nk], pS[:, 0:nk], SCALE, mk[:, 0:nk], ALU.mult, ALU.add),
                     reads=[pS, cms], writes=[ss_], partial=True)
                P.op("dve", lambda e: e.scalar_tensor_tensor(ss_[:, nk:nk + 4], pSn[:, 0:4], SCALE, mk[:, nk:nk + 4], ALU.mult, ALU.add),
                     reads=[pSn, cms], writes=[ss_], partial=True)
                c8 = g * 8
                P.op("dve", lambda e: e.tensor_reduce(sb_[:, c8:c8 + 1], ss_[:, 0:nk + 4], AX.X, ALU.max, negate=True), reads=[ss_], writes=[sb_], partial=True)
                P.op("act", lambda e: e.activation(pb_[:, 0:nk + 4], ss_[:, 0:nk + 4], AF.Exp, bias=sb_[:, c8:c8 + 1], scale=1.0,
                                                   accum_out=sb_[:, c8 + 1:c8 + 2]), reads=[ss_, sb_], writes=[pb_, sb_], partial=True)
                for cl in range(ntile):
                    P.tr(pT, pT[:, cl * 128:(cl + 1) * 128], pb_, pb_[:, cl * 128:(cl + 1) * 128], C.ident_b, C.ident_b[:])
                P.tr(pT, pT[0:4, 640:768], pb_, pb_[:, nk:nk + 4], C.ident_b, C.ident_b[:])
                evac(P, ec[0], PT_, PT_[:, 0:ntile, :].rearrange("p c q -> p (c q)"), pT, pT[:, 0:nk])
                ec[0] += 1
                evac(P, ec[0], PT_, PT_[0:4, 4, :], pT, pT[0:4, 640:768])
                ec[0] += 1
                for h in range(4):
                    for cl in range(ntile):
                        P.mm(pO, pO[32 * h:32 * h + 4, :], PT_, PT_[:, cl, 32 * h:32 * h + 4], kt_, kt_[:, cl, 1, h * 128:(h + 1) * 128],
                             start=(cl == 0), stop=False, tile_position=(0, 32 * h))
                    P.mm(pO, pO[32 * h:32 * h + 4, :], PT_, PT_[0:4, 4, 32 * h:32 * h + 4], kn_, kn_[0:4, 1, h * 128:(h + 1) * 128],
                         start=False, stop=True, tile_position=(0, 32 * h))
                P.op("dve", lambda e: e.reciprocal(sb_[:, c8 + 2:c8 + 3], sb_[:, c8 + 1:c8 + 2]), reads=[sb_], writes=[sb_], partial=True)
                P.op("dve", lambda e: e.tensor_scalar(ogb[:, g, :], pO[:], sb_[:, c8 + 2:c8 + 3], None, ALU.mult), reads=[pO, sb_], writes=[ogb], partial=True)
                P.op("act", lambda e: e.activation(sb_[:, c8 + 3:c8 + 4], sb_[:, c8 + 1:c8 + 2], AF.Ln), reads=[sb_], writes=[sb_], partial=True)
                P.op("dve", lambda e: e.tensor_tensor(sb_[:, c8 + 4:c8 + 5], sb_[:, c8 + 3:c8 + 4], sb_[:, c8:c8 + 1], ALU.subtract), reads=[sb_], writes=[sb_], partial=True)
            def col(i):
                return sb_[:, i:i + 1]
            P.op("dve", lambda e: e.tensor_tensor(col(5), col(4), col(12), ALU.max), reads=[sb_], writes=[sb_], partial=True)
            P.op("dve", lambda e: e.tensor_tensor(col(5), col(5), col(20), ALU.max), reads=[sb_], writes=[sb_], partial=True)
            for g in range(3):
                P.op("dve", lambda e: e.tensor_tensor(col(8 * g + 6), col(8 * g + 4), col(5), ALU.subtract), reads=[sb_], writes=[sb_], partial=True)
                P.op("act", lambda e: e.activation(col(8 * g + 6), col(8 * g + 6), AF.Exp), reads=[sb_], writes=[sb_], partial=True)
            P.op("dve", lambda e: e.tensor_tensor(col(7), col(6), col(14), ALU.add), reads=[sb_], writes=[sb_], partial=True)
            P.op("dve", lambda e: e.tensor_tensor(col(7), col(7), col(22), ALU.add), reads=[sb_], writes=[sb_], partial=True)
            P.op("dve", lambda e: e.reciprocal(col(7), col(7)), reads=[sb_], writes=[sb_], partial=True)
            for g in range(3):
                P.op("dve", lambda e: e.tensor_tensor(col(8 * g + 6), col(8 * g + 6), col(7), ALU.mult), reads=[sb_], writes=[sb_], partial=True)
            P.op("dve", lambda e: e.tensor_scalar(ogb[:, 0, :], ogb[:, 0, :], col(6), None, ALU.mult), reads=[ogb, sb_], writes=[ogb])
            P.op("dve", lambda e: e.scalar_tensor_tensor(ogb[:, 0, :], ogb[:, 1, :], col(14), ogb[:, 0, :], ALU.mult, ALU.add), reads=[ogb, sb_], writes=[ogb])
            P.op("dve", lambda e: e.scalar_tensor_tensor(ogb[:, 0, :], ogb[:, 2, :], col(22), ogb[:, 0, :], ALU.mult, ALU.add), reads=[ogb, sb_], writes=[ogb])
            P.mm(pOT, pOT[:], ogb, ogb[:, 0, :], C.ident_f, C.ident_f[:])
            evac(P, b, oaT, oaT[:, :, 4 * b:4 * b + 4], pOT, pOT[:].rearrange("p (h x) -> p h x", x=32)[:, :, 0:4])
        P.op("dve", lambda e: e.tensor_tensor(yaS[:], oaT[:], szs[:], ALU.mult), reads=[oaT, szs], writes=[yaS])
```

```python
import contextlib
import numpy as np
import concourse.bass as bass
import concourse.mybir as mybir
from concourse.bass_utils import run_bass_kernel_spmd

F32 = mybir.dt.float32
BF16 = mybir.dt.bfloat16
I32 = mybir.dt.int32
ALU = mybir.AluOpType
AF = mybir.ActivationFunctionType
AX = mybir.AxisListType

NCORES = 8
RING = 20
EAGER_SIGNAL = False


class Buf:
    __slots__ = ("name", "t", "writers", "readers", "prev_readers", "bank")

    def __init__(self, name, t, bank=None):
        self.name = name
        self.t = t
        self.bank = bank
        self.writers = {}
        self.readers = {}
        self.prev_readers = {}

    def __getitem__(self, idx):
        return self.t[idx]


def _merge(dst, src):
    for k, (s, v) in src.items():
        if k not in dst or dst[k][1] < v:
            dst[k] = (s, v)


class Prog:
    def __init__(self, nc, stack):
        self.nc = nc
        self.stack = stack
        self.eng = {"pe": nc.tensor, "act": nc.scalar, "dve": nc.vector, "pool": nc.gpsimd, "sp": nc.sync}
        self.esem = {}
        self.seq = {}
        self.known = {}
        self.sig = {}
        self.sig_idx = {}
        self.sigcount = {}
        self.last_inst = {}
        self.insts = {}
        for e in self.eng:
            self.esem[e] = stack.enter_context(nc.semaphore("es_" + e))
            self.seq[e] = 0
            self.known[e] = {}
            self.sig[e] = []
            self.sig_idx[e] = []
            self.sigcount[e] = 0
            self.last_inst[e] = None
            self.insts[e] = []
        self.ring = {}
        self.ring_val = {}
        self.dma_i = {}
        for q in ("sp", "pool", "act"):
            self.ring[q] = [stack.enter_context(nc.semaphore("dq_%s_%d" % (q, i))) for i in range(RING)]
            self.ring_val[q] = [0] * RING
            self.dma_i[q] = 0
        self.nbuf = 0
        self.bank_rd = {}

    def sbuf(self, name, shape, dtype):
        t = self.stack.enter_context(self.nc.sbuf_tensor("sb_" + name, list(shape), dtype))
        return Buf(name, t)

    def psum(self, name, shape, dtype):
        t = self.stack.enter_context(self.nc.psum_tensor(name, list(shape), dtype))
        return Buf(name, t)

    def dram(self, name, shape, dtype, kind="Internal"):
        t = self.nc.dram_tensor(name, list(shape), dtype, kind=kind)
        return Buf(name, t.ap())

    def _deps(self, reads, writes, partial, eng=None):
        deps = {}
        for b in list(reads) + list(writes):
            if b.bank is not None:
                for e2, (k2, ev2) in self.bank_rd.setdefault(b.bank, {}).items():
                    if e2 != eng:
                        _merge(deps, {k2: ev2})
        for b in reads:
            _merge(deps, b.writers)
        for b in writes:
            _merge(deps, b.prev_readers)
            _merge(deps, b.readers)
            if not partial:
                _merge(deps, b.writers)
        return deps

    def _resolve(self, e, idx):
        sig = self.sig[e]
        import bisect
        pos = bisect.bisect_left(self.sig_idx[e], idx)
        if pos < len(sig):
            return self.sig_idx[e][pos], sig[pos]
        self.sigcount[e] += 1
        self.insts[e][idx - 1].then_inc(self.esem[e], 1)
        self.sig_idx[e].append(idx)
        sig.append(self.sigcount[e])
        return idx, self.sigcount[e]

    def _wait(self, eng, deps):
        E = self.eng[eng]
        kn = self.known[eng]
        for k, (s, v) in deps.items():
            if eng == "pe" and k == "e_pe":
                continue
            if kn.get(k, 0) >= v:
                continue
            if s is None:
                e = k[2:]
                idx2, cnt = self._resolve(e, v)
                E.wait_ge(self.esem[e], cnt)
                kn[k] = idx2
            else:
                E.wait_ge(s, v)
                kn[k] = v

    def _record(self, ev_key, ev, reads, writes, partial):
        for b in reads:
            _merge(b.readers, {ev_key: ev})
        for b in writes:
            if b.readers or not partial:
                b.prev_readers = b.readers
                b.readers = {}
                b.writers = {ev_key: ev}
            else:
                _merge(b.writers, {ev_key: ev})

    opbudget = None
    opcount = 0

    def op(self, eng, fn, reads=(), writes=(), partial=False):
        Prog.opcount += 1
        if Prog.opbudget is not None and Prog.opcount > Prog.opbudget:
            return None
        deps = self._deps(reads, writes, partial, eng)
        self._wait(eng, deps)
        inst = fn(self.eng[eng])
        self.seq[eng] += 1
        self.last_inst[eng] = inst
        self.insts[eng].append(inst)
        if EAGER_SIGNAL:
            self.sigcount[eng] += 1
            inst.then_inc(self.esem[eng], 1)
            self.sig_idx[eng].append(self.seq[eng])
            self.sig[eng].append(self.sigcount[eng])
        self._record("e_" + eng, (None, self.seq[eng]), reads, writes, partial)
        for b in list(reads) + list(writes):
            if b.bank is not None:
                self.bank_rd.setdefault(b.bank, {})[eng] = ("e_" + eng, (None, self.seq[eng]))
        return inst

    def dma(self, q, out, in_, reads=(), writes=(), partial=True, **kw):
        Prog.opcount += 1
        if Prog.opbudget is not None and Prog.opcount > Prog.opbudget and not kw.pop("always", False):
            return None
        kw.pop("always", None)
        i = self.dma_i[q]
        self.dma_i[q] += 1
        slot = i % RING
        sem = self.ring[q][slot]
        prev = self.ring_val[q][slot]
        key = "d_%s_%d" % (q, slot)
        deps = self._deps(reads, writes, partial)
        if prev > 0:
            _merge(deps, {key: (sem, prev)})
        self._wait(q, deps)
        inst = self.eng[q].dma_start(out=out, in_=in_, **kw)
        inst.then_inc(sem, 16)
        self.ring_val[q][slot] = prev + 16
        self._record(key, (sem, prev + 16), reads, writes, partial)
        return inst

    def finish(self):
        deps = {}
        for q in self.ring:
            for slot in range(RING):
                if self.ring_val[q][slot] > 0:
                    deps["d_%s_%d" % (q, slot)] = (self.ring[q][slot], self.ring_val[q][slot])
        for e in self.eng:
            if self.seq[e] > 0:
                deps["e_" + e] = (None, self.seq[e])
        self._wait("sp", deps)

    def mm(self, out_b, out_ap, lhsT_b, lhsT_ap, rhs_b, rhs_ap, start=True, stop=True, **kw):
        rd = [b for b in (lhsT_b, rhs_b) if b is not None]
        return self.op("pe", lambda e: e.matmul(out_ap, lhsT_ap, rhs_ap, start=start, stop=stop, **kw),
                       reads=rd, writes=[out_b], partial=True)

    def tr(self, out_b, out_ap, in_b, in_ap, ident_b, ident_ap):
        return self.op("pe", lambda e: e.transpose(out_ap, in_ap, ident_ap),
                       reads=[in_b, ident_b], writes=[out_b], partial=True)


D = 1024
SEQ = 8192
NB = 2
R_HEADS = 8
SHIFT_COLS = 1664
RW_COLS = 2176
Q0, K0, V0, ZA0, GR0, GA0 = 2176, 3712, 5248, 6784, 7296, 8320
GROUPS = ((128, 1), (512, 4), (2048, 16))
ALPHA = 2.0 ** 0.25
LN_EPS = 1e-5
GN_EPS = 64e-5
C0 = float(np.exp(-0.5))
NEGM = -30000.0
OWN = 2048
EXT = 8192
SCALE = 1.0 / float(np.sqrt(128.0))

PV_MU = 0
PV_W0 = 13
PV_A0 = 17
PV_KK = 21
PV_KA = 25
PV_RK = 29
PV_LG = 33
PV_LB = 37
PV_BG = 41
PV_OMM = 57
NPV = 70


class Ctx:
    pass


def emit_consts(P, C, din):
    C.ident_f = P.sbuf("ident_f", [128, 128], F32)
    C.ident_b = P.sbuf("ident_b", [128, 128], BF16)
    C.cm = P.sbuf("cm", [128, din["cmask"].shape[1]], F32)
    C.pv = P.sbuf("pv", [128, NPV], F32)
    C.pbias = P.sbuf("pbias", [128, 1], F32)
    P.dma("sp", C.ident_f[:], din["ident"][:, :], writes=[C.ident_f])
    P.dma("pool", C.ident_b[:], din["ident"][:, :], writes=[C.ident_b])
    P.dma("sp", C.cm[:], din["cmask"][:, :], writes=[C.cm])
    P.dma("sp", C.pv[:, 0:PV_OMM], din["pvec"][:, :], writes=[C.pv])
    P.dma("sp", C.pbias[:], din["pbias"][:, :], writes=[C.pbias])
    C.selb = P.sbuf("selb", [128, 64], BF16)
    P.op("pool", lambda e: e.tensor_copy(C.selb[:], C.cm[:, CM_SEL:CM_SEL + 64]), reads=[C.cm], writes=[C.selb])
    C.bones_f = Buf("bones_f", C.cm.t[:, CM_BONES:CM_BONES + 128])
    P.op("dve", lambda e: e.tensor_scalar(C.pv[:, PV_OMM:PV_OMM + 13], C.pv[:, PV_MU:PV_MU + 13], -1.0, 1.0,
                                          ALU.mult, ALU.add), reads=[C.pv], writes=[C.pv])
    P.op("dve", lambda e: e.tensor_copy(C.cm[:, 256:512], C.cm[:, 0:256]), reads=[C.cm], writes=[C.cm])
    P.op("dve", lambda e: e.tensor_scalar(C.cm[:, 256:384], C.cm[:, 256:384], C.pbias[:, 0:1], None, ALU.add),
         reads=[C.cm, C.pbias], writes=[C.cm])


def emit_xT(P, C, x_rows_ap, ntiles, xT, col0, xld, ps_x, cnt):
    for t in range(ntiles):
        xb = xld[cnt[0] % len(xld)]
        px = ps_x[cnt[0] % len(ps_x)]
        P.dma("pool", xb[:], x_rows_ap[t * 128:(t + 1) * 128, :], writes=[xb])
        for k in range(8):
            P.tr(px, px[:, k * 128:(k + 1) * 128], xb, xb[:, k * 128:(k + 1) * 128], C.ident_b, C.ident_b[:])
        dst = xT[:, :, col0 + t * 128: col0 + (t + 1) * 128]
        src = px[:].rearrange("p (k t) -> p k t", k=8)
        if cnt[0] % 2 == 0:
            P.op("dve", lambda e: e.tensor_copy(dst, src), reads=[px], writes=[xT], partial=True)
        else:
            P.op("act", lambda e: e.activation(dst, src, AF.Copy), reads=[px], writes=[xT], partial=True)
        cnt[0] += 1


@contextlib.contextmanager
def scope(P):
    old = P.stack
    with contextlib.ExitStack() as st:
        P.stack = st
        try:
            yield
        finally:
            barrier(P)
            P.stack = old


def barrier(P):
    deps = {}
    for q in P.ring:
        for slot in range(RING):
            if P.ring_val[q][slot] > 0:
                deps["d_%s_%d" % (q, slot)] = (P.ring[q][slot], P.ring_val[q][slot])
    for e in P.eng:
        if P.seq[e] > 0:
            deps["e_" + e] = (None, P.seq[e])
    for e in P.eng:
        P._wait(e, dict(deps))


def carve(C, bank, c0, c1, name, dtype=F32):
    ap = C.bank[bank].t[:, c0:c1]
    if dtype != F32:
        ap = ap.bitcast(dtype)
    return Buf(name, ap, bank=bank)


def evac(P, i, dst_b, dst_ap, src_b, src_ap):
    if i % 2 == 0:
        P.op("dve", lambda e: e.tensor_copy(dst_ap, src_ap), reads=[src_b], writes=[dst_b], partial=True)
    else:
        P.op("act", lambda e: e.activation(dst_ap, src_ap, AF.Copy), reads=[src_b], writes=[dst_b], partial=True)


def emit_attn_prompt(P, C, din, xTA, yaT, dbg=None, dout=None):
    with scope(P):
        wh = P.sbuf("wh", [128, 8, 1280], BF16)
        qT = P.sbuf("qT", [128, 2048], BF16)
        kT = P.sbuf("kT", [128, 4096], BF16)
        vt = P.sbuf("vt", [128, 32, 128], BF16)
        oTg = [P.sbuf("oTg%d" % g, [128, 2048], BF16) for g in range(3)]
        lsB = [P.sbuf("lsB%d" % g, [128, 2048], F32) for g in range(3)]
        sz = P.sbuf("sz", [128, 2048], BF16)
        cw = [P.sbuf("cw%d" % i, [128, 512], F32) for i in range(5)]
        s_sb = [P.sbuf("s_sb%d" % i, [128, 256], F32) for i in range(3)]
        p_sb = [P.sbuf("p_sb%d" % i, [128, 256], BF16) for i in range(3)]
        pT = [P.sbuf("pT%d" % i, [128, 256], BF16) for i in range(3)]
        o_sb = [P.sbuf("o_sb%d" % i, [128, 128], BF16) for i in range(3)]
        st = [P.sbuf("st%d" % i, [128, 8], F32) for i in range(3)]
        kvts = [P.sbuf("kvt%d" % i, [128, 2, 384], F32) for i in range(2)]
        lcol = [P.sbuf("lcol%d" % i, [128, 128], F32) for i in range(3)]
        ps_p = [carve(C, i, 0, 512, "psA_p%d" % i) for i in range(2)]
        sbk = (2, 3, 6)
        obk = (4, 5, 7)
        ps_s = [carve(C, sbk[i], 0, 256, "psA_s%d" % i) for i in range(3)]
        ps_t = [carve(C, sbk[i], 256, 384, "psA_t%d" % i, BF16) for i in range(3)]
        ps_o = [carve(C, sbk[i], 384, 512, "psA_o%d" % i) for i in range(3)]
        ps_oT = [carve(C, obk[i], 0, 64, "psA_oT%d" % i, BF16) for i in range(3)]
        ps_l = [carve(C, obk[i], 128, 256, "psA_l%d" % i) for i in range(3)]
        ec = [0]
        pc = [0]
        blk = [0]

        def proj_fm(col, tok0, ntok, dst_b, dst_ap, src_view=None):
            pp = ps_p[pc[0] % 2]
            pc[0] += 1
            for k in range(8):
                P.mm(pp, pp[:, 0:ntok], wh, wh[:, k, col * 128:(col + 1) * 128], xTA, xTA[:, k, tok0:tok0 + ntok],
                     start=(k == 0), stop=(k == 7))
            src = pp[:, 0:ntok] if src_view is None else src_view(pp)
            evac(P, ec[0], dst_b, dst_ap, pp, src)
            ec[0] += 1

        for h in range(4):
            P.dma("pool", wh[:], din["w_att"][h].rearrange("(k p) c -> p k c", p=128), writes=[wh], partial=False)
            for t in range(4):
                pp = ps_p[pc[0] % 2]
                pc[0] += 1
                for k in range(8):
                    P.mm(pp, pp[:], wh, wh[:, k, 9 * 128:10 * 128], xTA, xTA[:, k, 2048 + t * 512:2048 + (t + 1) * 512],
                         start=(k == 0), stop=(k == 7))
                P.op("act", lambda e: e.activation(sz[:, t * 512:(t + 1) * 512], pp[:], AF.Silu),
                     reads=[pp], writes=[sz], partial=True)
            for tt in range(16):
                kvt = kvts[tt % 2]
                for kv in range(2):
                    pp = ps_p[pc[0] % 2]
                    pc[0] += 1
                    for k in range(8):
                        P.mm(pp, pp[:, 0:384], xTA, xTA[:, k, 2048 + tt * 128:2048 + (tt + 1) * 128], wh,
                             wh[:, k, (3 + 3 * kv) * 128:(6 + 3 * kv) * 128], start=(k == 0), stop=(k == 7))
                    evac(P, ec[0], kvt, kvt[:, kv, :], pp, pp[:, 0:384])
                    ec[0] += 1
                P.dma("sp", dout["kvp3"][tt * 128:(tt + 1) * 128, :, h, :], kvt[:, :, 256:384], reads=[kvt])
                if tt >= 12:
                    P.dma("sp", dout["kvp2"][(tt - 12) * 128:(tt - 11) * 128, :, h, :], kvt[:, :, 128:256], reads=[kvt])
                if tt == 15:
                    P.dma("sp", dout["kvp1"][0:128, :, h, :], kvt[:, :, 0:128], reads=[kvt])
            for g, (win, d) in enumerate(GROUPS):
                L = 2048 // d
                nb = L // 128
                KW = 128 + L
                for t in range(4):
                    mt = 512 // d
                    dst = qT[:].rearrange("p (r m) -> p r m", r=d)[:, :, t * mt:(t + 1) * mt]
                    proj_fm(g, 2048 + t * 512, 512, qT, dst,
                            src_view=lambda pp: pp[:].rearrange("p (m r) -> p r m", r=d))
                kv = kT[:, 0:d * KW].rearrange("p (r m) -> p r m", r=d)
                npt = 128 * d
                for t0 in range(0, npt, 512):
                    n = min(512, npt - t0)
                    dst = kv[:, :, t0 // d:(t0 + n) // d]
                    proj_fm(3 + g, 2048 - npt + t0, n, kT, dst,
                            src_view=lambda pp: pp[:, 0:n].rearrange("p (m r) -> p r m", r=d))
                for t in range(4):
                    mt = 512 // d
                    dst = kv[:, :, 128 + t * mt:128 + (t + 1) * mt]
                    proj_fm(3 + g, 2048 + t * 512, 512, kT, dst,
                            src_view=lambda pp: pp[:].rearrange("p (m r) -> p r m", r=d))
                for r in range(d):
                    for j0 in range(0, 1 + nb, 4):
                        nj = min(4, 1 + nb - j0)
                        pp = ps_p[pc[0] % 2]
                        pc[0] += 1
                        for jj in range(nj):
                            j = j0 + jj
                            tokbase = 2048 - 128 * d + r + j * 128 * d
                            for k in range(8):
                                lhs = xTA[:, k, tokbase:tokbase + 127 * d + 1:d]
                                P.mm(pp, pp[:, jj * 128:(jj + 1) * 128], xTA, lhs, wh, wh[:, k, (6 + g) * 128:(7 + g) * 128],
                                     start=(k == 0), stop=(k == 7))
                        bi = r * (1 + nb) + j0
                        evac(P, ec[0], vt, vt[:, bi:bi + nj, :], pp, pp[:, 0:nj * 128].rearrange("p (j e) -> p j e", j=nj))
                        ec[0] += 1
                def gen_block(r, n, b):
                    S, T_, O_, OT, LB = ps_s[b % 3], ps_t[b % 3], ps_o[b % 3], ps_oT[b % 3], ps_l[b % 3]
                    ss, pb, ptb, ob, stt, lc = s_sb[b % 3], p_sb[b % 3], pT[b % 3], o_sb[b % 3], st[b % 3], lcol[b % 3]
                    qblk = qT[:, r * L + n * 128: r * L + (n + 1) * 128]
                    kblk = kT[:, r * KW + n * 128: r * KW + n * 128 + 256]
                    P.mm(S, S[:], qT, qblk, kT, kblk)
                    mk = C.cm[:, 256:512] if n == 0 else C.cm[:, 0:256]
                    P.op("dve", lambda e: e.scalar_tensor_tensor(ss[:], S[:], SCALE, mk, ALU.mult, ALU.add),
                         reads=[S, C.cm], writes=[ss])
                    yield
                    P.op("dve", lambda e: e.tensor_reduce(stt[:, 0:1], ss[:], AX.X, ALU.max, negate=True),
                         reads=[ss], writes=[stt], partial=True)
                    P.op("act", lambda e: e.activation(pb[:], ss[:], AF.Exp, bias=stt[:, 0:1], scale=1.0,
                                                       accum_out=stt[:, 1:2]),
                         reads=[ss, stt], writes=[pb, stt], partial=True)
                    yield
                    for kb in range(2):
                        P.tr(T_, T_[:, kb * 128:(kb + 1) * 128], pb, pb[:, kb * 128:(kb + 1) * 128], C.ident_b, C.ident_b[:])
                    evac(P, b + 1, ptb, ptb[:], T_, T_[:])
                    yield
                    for kb in range(2):
                        P.mm(O_, O_[:], ptb, ptb[:, kb * 128:(kb + 1) * 128], vt, vt[:, r * (1 + nb) + n + kb, :],
                             start=(kb == 0), stop=(kb == 1))
                    P.op("dve", lambda e: e.reciprocal(stt[:, 2:3], stt[:, 1:2]), reads=[stt], writes=[stt], partial=True)
                    P.op("dve", lambda e: e.tensor_scalar(ob[:], O_[:], stt[:, 2:3], None, ALU.mult),
                         reads=[O_, stt], writes=[ob])
                    yield
                    P.op("act", lambda e: e.activation(stt[:, 3:4], stt[:, 1:2], AF.Ln), reads=[stt], writes=[stt], partial=True)
                    P.op("dve", lambda e: e.tensor_scalar(lc[:], C.cm[:, 512:640], stt[:, 3:4], stt[:, 0:1],
                                                          ALU.add, ALU.subtract),
                         reads=[stt, C.cm], writes=[lc])
                    P.tr(OT, OT[:], ob, ob[:], C.ident_b, C.ident_b[:])
                    P.mm(LB, LB[:], lc, lc[:], C.ident_f, C.ident_f[:])
                    yield
                    tok0 = r + d * 128 * n
                    dsto = oTg[g][:, tok0:tok0 + 127 * d + 1:d]
                    dstl = lsB[g][:, tok0:tok0 + 127 * d + 1:d]
                    evac(P, b, oTg[g], dsto, OT, OT[:])
                    evac(P, b + 1, lsB[g], dstl, LB, LB[:])
                    yield

                blist = [(r, n) for r in range(d) for n in range(nb)]
                for i in range(0, len(blist), 3):
                    gens = []
                    for (r, n) in blist[i:i + 3]:
                        gens.append(gen_block(r, n, blk[0]))
                        blk[0] += 1
                    interleave(*gens)
            for t in range(4):
                sl = slice(t * 512, (t + 1) * 512)
                m_, e0, e1, e2, acc = cw
                P.op("dve", lambda e: e.tensor_tensor(m_[:], lsB[0][:, sl], lsB[1][:, sl], ALU.max), reads=[lsB[0], lsB[1]], writes=[m_])
                P.op("dve", lambda e: e.tensor_tensor(m_[:], m_[:], lsB[2][:, sl], ALU.max), reads=[m_, lsB[2]], writes=[m_])
                for g, eg in enumerate((e0, e1, e2)):
                    P.op("pool", lambda e: e.tensor_tensor(eg[:], lsB[g][:, sl], m_[:], ALU.subtract), reads=[lsB[g], m_], writes=[eg])
                    P.op("act", lambda e: e.activation(eg[:], eg[:], AF.Exp), reads=[eg], writes=[eg])
                P.op("dve", lambda e: e.tensor_tensor(m_[:], e0[:], e1[:], ALU.add), reads=[e0, e1], writes=[m_])
                P.op("dve", lambda e: e.tensor_tensor(m_[:], m_[:], e2[:], ALU.add), reads=[m_, e2], writes=[m_])
                P.op("dve", lambda e: e.reciprocal(m_[:], m_[:]), reads=[m_], writes=[m_])
                P.op("dve", lambda e: e.tensor_tensor(acc[:], e0[:], oTg[0][:, sl], ALU.mult), reads=[e0, oTg[0]], writes=[acc])
                P.op("pool", lambda e: e.tensor_tensor(e1[:], e1[:], oTg[1][:, sl], ALU.mult), reads=[e1, oTg[1]], writes=[e1])
                P.op("pool", lambda e: e.tensor_tensor(e2[:], e2[:], oTg[2][:, sl], ALU.mult), reads=[e2, oTg[2]], writes=[e2])
                P.op("dve", lambda e: e.tensor_tensor(acc[:], acc[:], e1[:], ALU.add), reads=[acc, e1], writes=[acc])
                P.op("dve", lambda e: e.tensor_tensor(acc[:], acc[:], e2[:], ALU.add), reads=[acc, e2], writes=[acc])
                P.op("dve", lambda e: e.tensor_tensor(acc[:], acc[:], m_[:], ALU.mult), reads=[acc, m_], writes=[acc])
                if dbg is not None:
                    P.dma("sp", dbg["oat"][h * 128:(h + 1) * 128, sl], acc[:], reads=[acc])
                P.op("dve", lambda e: e.tensor_tensor(yaT[:, h, sl], acc[:], sz[:, sl], ALU.mult), reads=[acc, sz], writes=[yaT], partial=True)


def emit_out_phase(P, C, din, xT, xc0, x_rows_ap, yrT, yaT, ntok, y_out_ap, tag):
    with scope(P):
        wg = P.sbuf("wg" + tag, [128, 8, 2048], BF16)
        woa = P.sbuf("woa" + tag, [128, 4, 1024], BF16)
        wob = P.sbuf("wob" + tag, [128, 4, 1024], BF16)
        wout = P.sbuf("wout" + tag, [128, 8, 1024], BF16)
        lng = P.sbuf("lng" + tag, [128, 1024], F32)
        lnb = P.sbuf("lnb" + tag, [128, 1024], F32)
        TW = min(512, ntok)
        mixT = P.sbuf("mixT" + tag, [128, 8, TW], BF16)
        gr = [P.sbuf("gr%d%s" % (i, tag), [128, TW], F32) for i in range(2)]
        ga = [P.sbuf("ga%d%s" % (i, tag), [128, TW], F32) for i in range(2)]
        t1 = [P.sbuf("t1%d%s" % (i, tag), [128, TW], F32) for i in range(2)]
        xr = [P.sbuf("xr%d%s" % (i, tag), [128, 1024], F32) for i in range(2)]
        z = [P.sbuf("z%d%s" % (i, tag), [128, 1024], F32) for i in range(2)]
        bst = [P.sbuf("bst%d%s" % (i, tag), [128, 16], F32) for i in range(2)]
        psb = [carve(C, i, 0, 512, "psO_b%d%s" % (i, tag)) for i in range(8)]
        ps_y = psb[4:8]
        P.dma("pool", wg[:], din["w_g"].rearrange("(k p) c -> p k c", p=128), writes=[wg])
        P.dma("pool", woa[:], din["w_oa"].rearrange("(k p) c -> p k c", p=128), writes=[woa])
        P.dma("pool", wob[:], din["w_ob"].rearrange("(k p) c -> p k c", p=128), writes=[wob])
        P.dma("pool", wout[:], din["w_out"].rearrange("(k p) c -> p k c", p=128), writes=[wout])
        P.dma("sp", lng[:], din["ln_gb"][0:1, :].to_broadcast([128, 1024]), writes=[lng])
        P.dma("sp", lnb[:], din["ln_gb"][1:2, :].to_broadcast([128, 1024]), writes=[lnb])
        it = 0
        for t0 in range(0, ntok, TW):
            for n in range(8):
                pgr, pga, pmr, pma = psb[(n % 2) * 4:(n % 2) * 4 + 4]
                for k in range(8):
                    P.mm(pgr, pgr[:, 0:TW], wg, wg[:, k, n * 128:(n + 1) * 128], xT, xT[:, k, xc0 + t0:xc0 + t0 + TW], start=(k == 0), stop=(k == 7))
                for k in range(8):
                    P.mm(pga, pga[:, 0:TW], wg, wg[:, k, 1024 + n * 128:1024 + (n + 1) * 128], xT, xT[:, k, xc0 + t0:xc0 + t0 + TW], start=(k == 0), stop=(k == 7))
                for c in range(4):
                    P.mm(pmr, pmr[:, 0:TW], woa, woa[:, c, n * 128:(n + 1) * 128], yrT, yrT[:, c, t0:t0 + TW], start=(c == 0), stop=(c == 3))
                for c in range(4):
                    P.mm(pma, pma[:, 0:TW], wob, wob[:, c, n * 128:(n + 1) * 128], yaT, yaT[:, c, t0:t0 + TW], start=(c == 0), stop=(c == 3))
                a, b_, tt = gr[it % 2], ga[it % 2], t1[it % 2]
                it += 1
                P.op("act", lambda e: e.activation(a[:], pgr[:, 0:TW], AF.Sigmoid, bias=C.pv[:, PV_BG + n:PV_BG + n + 1], scale=1.0),
                     reads=[pgr, C.pv], writes=[a])
                P.op("act", lambda e: e.activation(b_[:], pga[:, 0:TW], AF.Sigmoid, bias=C.pv[:, PV_BG + 8 + n:PV_BG + 9 + n], scale=1.0),
                     reads=[pga, C.pv], writes=[b_])
                P.op("dve", lambda e: e.tensor_tensor(tt[:], a[:], pmr[:, 0:TW], ALU.mult), reads=[a, pmr], writes=[tt])
                P.op("dve", lambda e: e.tensor_tensor(b_[:], b_[:], pma[:, 0:TW], ALU.mult), reads=[b_, pma], writes=[b_])
                P.op("pool", lambda e: e.tensor_tensor(mixT[:, n, :], tt[:], b_[:], ALU.add), reads=[tt, b_], writes=[mixT], partial=True)
            for s0 in range(0, TW, 128):
                ns = min(128, ntok - t0 - s0)
                i2 = (t0 + s0) // 128
                xx, zz, bs = xr[i2 % 2], z[i2 % 2], bst[i2 % 2]
                py = ps_y[(i2 % 2) * 2:(i2 % 2) * 2 + 2]
                P.dma("sp", xx[0:ns, :], x_rows_ap[t0 + s0:t0 + s0 + ns, :], writes=[xx])
                for hf in range(2):
                    for m in range(8):
                        P.mm(py[hf], py[hf][0:ns, :], mixT, mixT[:, m, s0:s0 + ns], wout, wout[:, m, hf * 512:(hf + 1) * 512],
                             start=(m == 0), stop=(m == 7))
                    P.op("dve", lambda e: e.scalar_tensor_tensor(zz[0:ns, hf * 512:(hf + 1) * 512], xx[0:ns, hf * 512:(hf + 1) * 512],
                                                                 ALPHA, py[hf][0:ns, :], ALU.mult, ALU.add),
                         reads=[xx, py[hf]], writes=[zz], partial=True)
                    P.op("dve", lambda e: e.bn_stats(bs[0:ns, hf * 6:(hf + 1) * 6], zz[0:ns, hf * 512:(hf + 1) * 512]),
                         reads=[zz], writes=[bs], partial=True)
                P.op("dve", lambda e: e.bn_aggr(bs[0:ns, 12:14], bs[0:ns, 0:12]), reads=[bs], writes=[bs], partial=True)
                P.op("dve", lambda e: e.tensor_scalar(bs[0:ns, 14:15], bs[0:ns, 13:14], LN_EPS, None, ALU.add), reads=[bs], writes=[bs], partial=True)
                P.op("act", lambda e: e.activation(bs[0:ns, 14:15], bs[0:ns, 14:15], AF.Sqrt), reads=[bs], writes=[bs], partial=True)
                P.op("dve", lambda e: e.reciprocal(bs[0:ns, 14:15], bs[0:ns, 14:15]), reads=[bs], writes=[bs], partial=True)
                P.op("dve", lambda e: e.tensor_scalar(zz[0:ns, :], zz[0:ns, :], bs[0:ns, 12:13], bs[0:ns, 14:15], ALU.subtract, ALU.mult),
                     reads=[zz, bs], writes=[zz])
                P.op("pool", lambda e: e.tensor_tensor(zz[0:ns, :], zz[0:ns, :], lng[0:ns, :], ALU.mult), reads=[zz, lng], writes=[zz])
                P.op("pool", lambda e: e.tensor_tensor(zz[0:ns, :], zz[0:ns, :], lnb[0:ns, :], ALU.add), reads=[zz, lnb], writes=[zz])
                P.dma("sp", y_out_ap[t0 + s0:t0 + s0 + ns, :], zz[0:ns, :], reads=[zz])


def make_cmask():
    cm = np.zeros((128, 1408), np.float32)
    i = np.arange(128)[:, None]
    j = np.arange(256)[None, :]
    dist = 128 + i - j
    cm[:, 0:256] = np.where((dist >= 0) & (dist <= 128), 0.0, NEGM)
    p = np.arange(128)
    hs, s = p[:, None] // 64, p[:, None] % 64
    ht, t = p[None, :] // 64, p[None, :] % 64
    same = (hs == ht)
    cm[:, 640:768] = (same & (s < t)).astype(np.float32)
    cm[:, 768:896] = (same & (s <= t)).astype(np.float32)
    cm[:, 896:1024] = (same & (s > t)).astype(np.float32)
    cm[:, 1024:1152] = np.eye(128, dtype=np.float32)
    cm[:, 1152:1280] = same.astype(np.float32)
    cm[:, 1280:1282] = (p[:, None] // 64 == np.arange(2)[None, :]).astype(np.float32)
    sel = np.zeros((128, 64), np.float32)
    sel[p, p % 64] = 1.0
    cm[:, 1282:1346] = sel
    return cm


CM_STRICT, CM_INCL, CM_STRICT_T, CM_EYE, CM_BONES, CM_HM, CM_SEL = 640, 768, 896, 1024, 1152, 1280, 1282


def build(flags):
    nc = bass.Bass("TRN2", target_bir_lowering=False)
    din, dout = {}, {}

    def inp(name, shape, dt=F32):
        din[name] = nc.dram_tensor(name, list(shape), dt, kind="ExternalInput").ap()

    def outp(name, shape, dt=F32):
        dout[name] = nc.dram_tensor(name, list(shape), dt, kind="ExternalOutput").ap()

    inp("xe", [EXT, D])
    inp("w_rw", [D, RW_COLS])
    inp("w_att", [4, D, 1280])
    inp("w_g", [D, 2048])
    inp("w_oa", [512, D])
    inp("w_ob", [512, D])
    inp("w_out", [D, D])
    inp("w_l2", [128, 512])
    inp("pvec", [128, PV_OMM])
    inp("ln_gb", [2, D])
    inp("ident", [128, 128])
    inp("cmask", [128, 1408])
    inp("pbias", [128, 1])
    inp("xs", [64, D])
    inp("w_qkvz", [D, 5120])
    inp("cache1", [16, 128, 2, 4, 128])
    inp("cache2", [16, 512, 2, 4, 128])
    inp("cache3", [16, 2048, 2, 4, 128])
    inp("wkv_s", [16, 8, 64, 64])
    inp("shift_s", [16, SHIFT_COLS])
    inp("cmask_s", [128, NCS])
    inp("colmask", [128, 2048])
    outp("y_s", [64, D])
    for g in (1, 2, 3):
        outp("kvs%d" % g, [16, 4, 2, 4, 128])
    outp("wkv_so", [16, 8, 64, 64])
    outp("shift_so", [16, SHIFT_COLS])
    outp("y_p", [OWN, D])
    outp("kvp1", [128, 2, 4, 128])
    outp("kvp2", [512, 2, 4, 128])
    outp("kvp3", [2048, 2, 4, 128])
    outp("wkv_p", [8, 64, 64])
    outp("shift_p", [SHIFT_COLS])
    if flags.get("dbg"):
        inp("yr_dbg", [512, OWN])
        outp("oat", [512, OWN])
        outp("yat", [512, OWN])
        outp("yrt", [512, OWN])
    with contextlib.ExitStack() as st:
        P = Prog(nc, st)
        C = Ctx()
        C.bank = [P.psum("bank%d" % i, [128, 512], F32) for i in range(8)]
        emit_consts(P, C, din)
        barrier(P)
        yr_scr = P.dram("yr_scr", [128, 4 * OWN], BF16)
        ya_scr = P.dram("ya_scr", [128, 4 * OWN], BF16)
        if flags.get("attn", True):
          with scope(P):
              yaT = P.sbuf("yaT", [128, 4, OWN], BF16)
              xTA = P.sbuf("xTA", [128, 8, 4096], BF16)
              with scope(P):
                  xld = [P.sbuf("xldA%d" % i, [128, 1024], BF16) for i in range(3)]
                  psx = [carve(C, 6 + i, 0, 512, "psxA%d" % i, BF16) for i in range(2)]
                  emit_xT(P, C, din["xe"][4096:8192, :], 32, xTA, 0, xld, psx, [0])
              emit_attn_prompt(P, C, din, xTA, yaT, dbg=dout if flags.get("dbg") else None, dout=dout)
              if flags.get("dbg"):
                  with scope(P):
                      yaf = P.sbuf("yaf", [128, 4, OWN], F32)
                      P.op("dve", lambda e: e.tensor_copy(yaf[:], yaT[:]), reads=[yaT], writes=[yaf])
                      P.dma("sp", dout["yat"].rearrange("(h p) t -> p h t", p=128), yaf[:], reads=[yaf])
              P.dma("sp", ya_scr[:, :], yaT[:].rearrange("p h t -> p (h t)"), reads=[yaT], writes=[ya_scr])
        if flags.get("rwkv", True):
            emit_rwkv_prompt(P, C, din, dout, yr_scr, ntiles=flags.get("ntiles", 16), own_from=flags.get("own_from", 12),
                             budget=flags.get("budget"))
        if flags.get("outp", True):
          with scope(P):
              xTO = P.sbuf("xTO", [128, 8, OWN], BF16)
              yaT = P.sbuf("yaT2", [128, 4, OWN], BF16)
              P.dma("sp", yaT[:].rearrange("p h t -> p (h t)"), ya_scr[:, :], reads=[ya_scr], writes=[yaT])
              yrT = P.sbuf("yrT", [128, 4, OWN], BF16)
              if flags.get("rwkv", True):
                  P.dma("sp", yrT[:].rearrange("p h t -> p (h t)"), yr_scr[:, :], reads=[yr_scr], writes=[yrT])
              elif flags.get("dbg"):
                  P.dma("pool", yrT[:], din["yr_dbg"].rearrange("(h p) t -> p h t", p=128), writes=[yrT])
              if flags.get("dbg"):
                  with scope(P):
                      yrf = P.sbuf("yrf", [128, 4, OWN], F32)
                      P.op("dve", lambda e: e.tensor_copy(yrf[:], yrT[:]), reads=[yrT], writes=[yrf])
                      P.dma("sp", dout["yrt"].rearrange("(h p) t -> p h t", p=128), yrf[:], reads=[yrf])
              with scope(P):
                  xld = [P.sbuf("xldO%d" % i, [128, 1024], BF16) for i in range(3)]
                  psx = [carve(C, 6 + i, 0, 512, "psxO%d" % i, BF16) for i in range(2)]
                  emit_xT(P, C, din["xe"][6144:8192, :], 16, xTO, 0, xld, psx, [0])
              emit_out_phase(P, C, din, xTO, 0, din["xe"][6144:8192, :], yrT, yaT, OWN, dout["y_p"], "p")
        if flags.get("sample", True):
            with scope(P):
                emit_sample(P, C, din, dout, flags)
        P.finish()
    return nc


def _fm(vec, n):
    return np.ascontiguousarray(np.asarray(vec, np.float32).reshape(n, 128).T)


def prep_shared(inputs):
    w_in = np.asarray(inputs["w_in"][0], np.float32)
    sh = {}
    sh["w_rw"] = np.ascontiguousarray(w_in[:, 0:RW_COLS])
    w_att = np.empty((4, D, 1280), np.float32)
    for h in range(4):
        cols = []
        for base in (Q0, K0, V0):
            for g in range(3):
                cols.append(w_in[:, base + g * 512 + h * 128: base + g * 512 + (h + 1) * 128])
        cols.append(w_in[:, ZA0 + h * 128: ZA0 + (h + 1) * 128])
        w_att[h] = np.concatenate(cols, axis=1)
    sh["w_att"] = w_att
    sh["w_g"] = np.ascontiguousarray(w_in[:, GR0:GR0 + 2048])
    sh["w_oa"] = np.ascontiguousarray(inputs["w_oa"][0], np.float32)
    sh["w_ob"] = np.ascontiguousarray(inputs["w_ob"][0], np.float32)
    sh["w_out"] = np.ascontiguousarray(inputs["w_out"][0], np.float32)
    sh["w_l2"] = np.ascontiguousarray(np.concatenate([inputs["w_w2"][0], inputs["w_a2"][0]], axis=0), np.float32)
    pv = np.zeros((128, PV_OMM), np.float32)
    pv[:, PV_MU:PV_MU + 13] = _fm(inputs["mu_shift"][0], 13)
    pv[:, PV_W0:PV_W0 + 4] = _fm(inputs["w0"][0], 4)
    pv[:, PV_A0:PV_A0 + 4] = _fm(inputs["a0"][0], 4)
    pv[:, PV_KK:PV_KK + 4] = _fm(inputs["k_k"][0], 4)
    pv[:, PV_KA:PV_KA + 4] = _fm(inputs["k_a"][0], 4)
    pv[:, PV_RK:PV_RK + 4] = _fm(np.asarray(inputs["r_k"][0]).reshape(-1), 4)
    pv[:, PV_LG:PV_LG + 4] = _fm(inputs["lnx_g"][0], 4)
    pv[:, PV_LB:PV_LB + 4] = _fm(inputs["lnx_b"][0], 4)
    pv[:, PV_BG:PV_BG + 16] = _fm(inputs["b_gate"][0], 16)
    sh["pvec"] = pv
    sh["ln_gb"] = np.ascontiguousarray(np.stack([inputs["ln_g"][0], inputs["ln_b"][0]]), np.float32)
    sh["w_qkvz"] = np.ascontiguousarray(w_in[:, Q0:ZA0 + 512])
    sh["cmask_s"] = make_cmask_s()
    sh["colmask"] = make_colmask()
    sh["ident"] = np.eye(128, dtype=np.float32)
    sh["cmask"] = make_cmask()
    return sh


def prep_core(inputs, sh, c):
    b, q = c // 4, c % 4
    m = dict(sh)
    xe = np.zeros((EXT, D), np.float32)
    n = OWN * (q + 1)
    xe[EXT - n:] = np.asarray(inputs["x_prompt"][b, 0:n], np.float32)
    m["xe"] = xe
    m["pbias"] = np.full((128, 1), 0.0 if q > 0 else NEGM, np.float32)
    sl = slice(16 * c, 16 * c + 16)
    m["xs"] = np.ascontiguousarray(np.asarray(inputs["x_sample"][sl], np.float32).reshape(64, D))
    m["cache1"] = np.ascontiguousarray(inputs["cache_kv_g1"][0, sl], np.float32)
    m["cache2"] = np.ascontiguousarray(inputs["cache_kv_g2"][0, sl], np.float32)
    m["cache3"] = np.ascontiguousarray(inputs["cache_kv_g3"][0, sl], np.float32)
    m["wkv_s"] = np.ascontiguousarray(inputs["state_rwkv_wkv"][0, sl], np.float32)
    m["shift_s"] = np.ascontiguousarray(inputs["state_rwkv_shift"][0, sl], np.float32)
    return m


_NC_CACHE = {}


def kernel(**inputs):
    if "nc" not in _NC_CACHE:
        _NC_CACHE["nc"] = build({})
    nc = _NC_CACHE["nc"]
    sh = prep_shared(inputs)
    in_maps = [prep_core(inputs, sh, c) for c in range(NCORES)]
    res = run_bass_kernel_spmd(nc, in_maps, core_ids=list(range(NCORES)))
    R = res.results
    y_p = np.zeros((NB, SEQ, D), np.float32)
    for c in range(NCORES):
        y_p[c // 4, (c % 4) * OWN:(c % 4 + 1) * OWN] = R[c]["y_p"]
    y_s = np.concatenate([R[c]["y_s"].reshape(16, 4, D) for c in range(NCORES)], axis=0)
    outs = [y_p, y_s]
    for g in (1, 2, 3):
        outs.append(np.stack([R[4 * b + 3]["kvp%d" % g] for b in range(NB)])[None])
        outs.append(np.concatenate([R[c]["kvs%d" % g] for c in range(NCORES)], axis=0)[None])
    outs.append(np.stack([R[4 * b + 3]["wkv_p"] for b in range(NB)])[None])
    outs.append(np.concatenate([R[c]["wkv_so"] for c in range(NCORES)], axis=0)[None])
    outs.append(np.stack([R[4 * b + 3]["shift_p"] for b in range(NB)])[None])
    outs.append(np.concatenate([R[c]["shift_so"] for c in range(NCORES)], axis=0)[None])
    return tuple(np.ascontiguousarray(o, dtype=np.float32) for o in outs)


def interleave(*gens):
    live = list(gens)
    while live:
        for g in list(live):
            try:
                next(g)
            except StopIteration:
                live.remove(g)


def bc(ap, shape, axes):
    for a in axes:
        ap = ap.unsqueeze(a)
    return ap.to_broadcast(list(shape))


def emit_rwkv_prompt(P, C, din, dout, yrT, ntiles=16, own_from=12, dbg=None, budget=None):
    with scope(P):
        w_r = P.sbuf("w_r", [128, 8, RW_COLS], BF16)
        wl2 = P.sbuf("wl2", [128, 512], BF16)
        mask4 = P.sbuf("mask4", [128, 512], F32)
        bones_b = P.sbuf("bones_b", [128, 128], BF16)
        scanm = P.sbuf("scanm", [128, 512], F32)
        carry = P.sbuf("carry", [128, 16], F32)
        shout = P.sbuf("shout", [128, 16], F32)
        Sf = [P.sbuf("Sf%d" % i, [128, 128], F32) for i in range(4)]
        Sb = [P.sbuf("Sb%d" % i, [128, 128], BF16) for i in range(4)]
        xT = [P.sbuf("xTR%d" % i, [128, 8, 512], BF16) for i in range(1)]
        xld = [P.sbuf("xldR%d" % i, [128, 1024], BF16) for i in range(2)]
        bm = [P.sbuf("bm%d" % i, [128, 512], F32) for i in range(2)]
        lor = P.sbuf("lor", [128, 512], BF16)
        wdad = P.sbuf("wdad", [128, 512], F32)
        S1 = []
        for i in range(3):
            d_ = {}
            for nm in ("rt", "kt", "bt", "at"):
                d_[nm] = P.sbuf("%s%d" % (nm, i), [128, 512], BF16)
            for nm in ("vz", "eg", "bonus", "gate"):
                d_[nm] = P.sbuf("%s%d" % (nm, i), [128, 512], F32)
            S1.append(d_)
        tmp = {nm: P.sbuf("tp_" + nm, [128, 512], F32) for nm in ("r", "k", "sg", "a", "cs", "eng", "kk", "rn", "km", "bv")}
        sqb = P.sbuf("sqb", [128, 512], BF16)
        Lb = P.sbuf("Lb", [128, 8, 2, 64], BF16)
        Lk = P.sbuf("Lk", [128, 8, 2, 64], BF16)
        Ra = P.sbuf("Ra", [128, 8, 2, 64], BF16)
        KH = P.sbuf("KH", [128, 8, 2, 64], BF16)
        BH = P.sbuf("BH", [128, 8, 2, 64], BF16)
        VB = P.sbuf("VB", [128, 8, 2, 64], BF16)
        hmg = P.sbuf("hmg", [128, 8, 2], F32)
        NA = P.sbuf("NA", [128, 8, 2, 128], BF16)
        PT0 = P.sbuf("PT0", [128, 8, 128], BF16)
        Rat = P.sbuf("Rat", [128, 8, 128], BF16)
        Tt = [P.sbuf("Tt%d" % i, [128, 8, 128], BF16) for i in range(2)]
        Pp = [P.sbuf("Pp%d" % i, [128, 4, 128], BF16) for i in range(2)]
        PTp = [P.sbuf("PTp%d" % i, [128, 4, 128], BF16) for i in range(2)]
        Yb = P.sbuf("Yb", [128, 4, 128], BF16)
        SETS = []
        for i in range(2):
            d_ = {}
            for nm in ("KHt", "BHt", "VBt", "W1", "W2", "Rr"):
                d_[nm] = P.sbuf("%s_%d" % (nm, i), [128, 8, 128], BF16)
            d_["ABK"] = P.sbuf("ABK_%d" % i, [128, 8, 2, 128], BF16)
            d_["gC"] = P.sbuf("gC_%d" % i, [128, 8], F32)
            SETS.append(d_)
        yos = [P.sbuf("yos%d" % i, [128, 512], BF16) for i in range(2)]
        Ub = [P.sbuf("Ub%d" % i, [128, 128], BF16) for i in range(2)]
        Ob = [P.sbuf("Ob%d" % i, [128, 128], BF16) for i in range(2)]
        post = {nm: P.sbuf("po_" + nm, [128, 512], F32) for nm in ("yr", "cen", "sq", "rs")}
        pp = [carve(C, i, 0, 512, "psR_p%d" % i) for i in range(2)]
        pstr = carve(C, 3, 0, 512, "psR_tr", BF16)
        psx = pstr
        psYb = carve(C, 2, 0, 512, "psR_Y")
        psA_ = [carve(C, 4, 0, 384, "psR_A"), carve(C, 3, 0, 384, "psR_Ab")]
        psA2_ = [carve(C, 4, 384, 512, "psR_A2"), carve(C, 3, 384, 512, "psR_A2b")]
        psP = carve(C, 6, 0, 512, "psR_P")
        psPT = carve(C, 7, 0, 512, "psR_PT")
        psU = carve(C, 5, 0, 128, "psR_U")
        psS = carve(C, 5, 128, 256, "psR_S")
        psO = carve(C, 5, 256, 384, "psR_O")
        ppi = [0]
        eci = [0]

        def nextpp():
            b = pp[ppi[0] % 2]
            ppi[0] += 1
            return b

        def pv(col):
            return C.pv[:, col:col + 1]

        P.dma("pool", w_r[:], din["w_rw"].rearrange("(k p) c -> p k c", p=128), writes=[w_r])
        P.dma("pool", wl2[:], din["w_l2"][:, :], writes=[wl2])
        for i in range(4):
            src = C.cm[:, CM_STRICT:CM_STRICT + 128] if i % 2 == 0 else C.cm[:, CM_INCL:CM_INCL + 128]
            if i in (0, 1):
                src = C.cm[:, CM_STRICT:CM_STRICT + 128]
            else:
                src = C.cm[:, CM_INCL:CM_INCL + 128]
            P.op("pool", lambda e: e.tensor_copy(mask4[:, i * 128:(i + 1) * 128], src), reads=[C.cm], writes=[mask4], partial=True)
        P.op("pool", lambda e: e.tensor_copy(bones_b[:], C.cm[:, CM_BONES:CM_BONES + 128]), reads=[C.cm], writes=[bones_b])
        P.op("pool", lambda e: e.memset(scanm[:], 1.0), writes=[scanm])
        P.op("pool", lambda e: e.memset(scanm[:, 0:512:64], 0.0), writes=[scanm])
        P.op("pool", lambda e: e.memset(carry[:], 0.0), writes=[carry])
        for i in range(4):
            P.op("pool", lambda e: e.memset(Sf[i][:], 0.0), writes=[Sf[i]])
            P.op("pool", lambda e: e.memset(Sb[i][:], 0.0), writes=[Sb[i]])

        def build_xT(tile):
            xt = xT[0]
            for s in range(4):
                xb = xld[s % 2]
                P.dma("pool", xb[:], din["xe"][tile * 512 + s * 128: tile * 512 + (s + 1) * 128, :], writes=[xb])
                for k in range(8):
                    P.tr(psx, psx[:, k * 128:(k + 1) * 128], xb, xb[:, k * 128:(k + 1) * 128], C.ident_b, C.ident_b[:])
                evac(P, eci[0], xt, xt[:, :, s * 128:(s + 1) * 128], psx, psx[:].rearrange("p (k t) -> p k t", k=8))
                eci[0] += 1

        def proj_shift(xt, c, dst, last_own):
            p_ = nextpp()
            for k in range(8):
                P.mm(p_, p_[:], w_r, w_r[:, k, c * 128:(c + 1) * 128], xt, xt[:, k, :], start=(k == 0), stop=(k == 7))
            b_ = bm[c % 2]
            P.op("act", lambda e: e.mul(b_[:], p_[:], pv(PV_MU + c)), reads=[p_, C.pv], writes=[b_])
            P.op("dve", lambda e: e.scalar_tensor_tensor(dst[:, 1:512], p_[:, 1:512], pv(PV_OMM + c), b_[:, 0:511], ALU.mult, ALU.add),
                 reads=[p_, b_, C.pv], writes=[dst], partial=True)
            P.op("dve", lambda e: e.scalar_tensor_tensor(dst[:, 0:1], p_[:, 0:1], pv(PV_OMM + c), carry[:, c:c + 1], ALU.mult, ALU.add),
                 reads=[p_, carry, C.pv], writes=[dst], partial=True)
            P.op("pool", lambda e: e.tensor_copy(carry[:, c:c + 1], b_[:, 511:512]), reads=[b_], writes=[carry], partial=True)
            if last_own:
                P.op("act", lambda e: e.activation(shout[:, c:c + 1], p_[:, 511:512], AF.Copy), reads=[p_], writes=[shout], partial=True)

        def stage1(tile, hp, own):
            xt = xT[0]
            s1 = S1[(tile * 4 + hp) % 3]
            last_own = (tile == ntiles - 1)
            if hp == 0:
                proj_shift(xt, 12, wdad, last_own)
                P.op("act", lambda e: e.activation(lor[0:64, :], wdad[0:64, :], AF.Tanh), reads=[wdad], writes=[lor], partial=True)
                P.op("dve", lambda e: e.tensor_copy(lor[64:128, :], wdad[64:128, :]), reads=[wdad], writes=[lor], partial=True)
                yield
            r, k, sg, a, cs, eng, kk, rn, km, bv = (tmp[n] for n in ("r", "k", "sg", "a", "cs", "eng", "kk", "rn", "km", "bv"))
            vz, eg = s1["vz"], s1["eg"]
            if own or tile == own_from - 1:
                proj_shift(xt, hp, r, last_own)
                yield
            proj_shift(xt, 4 + hp, k, last_own)
            yield
            proj_shift(xt, 8 + hp, vz, last_own)
            yield
            p_ = nextpp()
            P.mm(p_, p_[:], wl2, wl2[0:64, hp * 128:(hp + 1) * 128], lor, lor[0:64, :])
            P.op("act", lambda e: e.activation(sg[:], p_[:], AF.Sigmoid, bias=pv(PV_W0 + hp), scale=1.0), reads=[p_, C.pv], writes=[sg])
            p2 = nextpp()
            P.mm(p2, p2[:], wl2, wl2[64:128, hp * 128:(hp + 1) * 128], lor, lor[64:128, :])
            P.op("act", lambda e: e.activation(a[:], p2[:], AF.Sigmoid, bias=pv(PV_A0 + hp), scale=1.0), reads=[p2, C.pv], writes=[a])
            yield
            P.op("dve", lambda e: e.tensor_tensor_scan(cs[:], scanm[:], sg[:], 0.0, ALU.mult, ALU.add), reads=[scanm, sg], writes=[cs])
            P.op("act", lambda e: e.activation(eg[:], cs[:], AF.Exp, scale=-C0), reads=[cs], writes=[eg])
            P.op("act", lambda e: e.activation(eng[:], cs[:], AF.Exp, scale=C0), reads=[cs], writes=[eng])
            P.op("pool", lambda e: e.tensor_tensor(cs[:], cs[:], sg[:], ALU.subtract), reads=[cs, sg], writes=[cs])
            P.op("act", lambda e: e.activation(cs[:], cs[:], AF.Exp, scale=-C0), reads=[cs], writes=[cs])
            yield
            P.op("dve", lambda e: e.tensor_scalar(kk[:], k[:], pv(PV_KK + hp), None, ALU.mult), reads=[k, C.pv], writes=[kk])
            P.op("pool", lambda e: e.tensor_tensor(sqb[:], kk[:], kk[:], ALU.mult), reads=[kk], writes=[sqb])
            p3 = nextpp()
            P.mm(p3, p3[:], bones_b, bones_b[:], sqb, sqb[:])
            P.op("dve", lambda e: e.tensor_scalar(rn[:], p3[:], 1e-24, None, ALU.max), reads=[p3], writes=[rn])
            P.op("act", lambda e: e.activation(rn[:], rn[:], AF.Sqrt), reads=[rn], writes=[rn])
            P.op("dve", lambda e: e.reciprocal(rn[:], rn[:]), reads=[rn], writes=[rn])
            P.op("dve", lambda e: e.tensor_tensor(kk[:], kk[:], rn[:], ALU.mult), reads=[kk, rn], writes=[kk])
            yield
            P.op("dve", lambda e: e.tensor_scalar(km[:], a[:], -1.0, pv(PV_KA + hp), ALU.add, ALU.mult), reads=[a, C.pv], writes=[km])
            P.op("dve", lambda e: e.scalar_tensor_tensor(km[:], km[:], 1.0, k[:], ALU.add, ALU.mult), reads=[km, k], writes=[km])
            P.op("pool", lambda e: e.tensor_tensor(bv[:], kk[:], a[:], ALU.mult), reads=[kk, a], writes=[bv])
            yield
            if own:
                P.op("pool", lambda e: e.tensor_tensor(s1["rt"][:], r[:], eg[:], ALU.mult), reads=[r, eg], writes=[s1["rt"]])
            P.op("dve", lambda e: e.tensor_tensor(s1["kt"][:], km[:], eng[:], ALU.mult), reads=[km, eng], writes=[s1["kt"]])
            P.op("pool", lambda e: e.tensor_tensor(s1["bt"][:], bv[:], eng[:], ALU.mult), reads=[bv, eng], writes=[s1["bt"]])
            P.op("dve", lambda e: e.scalar_tensor_tensor(s1["at"][:], kk[:], -1.0, cs[:], ALU.mult, ALU.mult), reads=[kk, cs], writes=[s1["at"]])
            yield
            if own:
                P.op("dve", lambda e: e.scalar_tensor_tensor(sqb[:], r[:], pv(PV_RK + hp), km[:], ALU.mult, ALU.mult), reads=[r, km, C.pv], writes=[sqb])
                p4 = nextpp()
                P.mm(p4, p4[:], bones_b, bones_b[:], sqb, sqb[:])
                P.op("dve", lambda e: e.tensor_tensor(s1["bonus"][:], p4[:], vz[:], ALU.mult), reads=[p4, vz], writes=[s1["bonus"]])
                p5 = nextpp()
                for kq in range(8):
                    P.mm(p5, p5[:], w_r, w_r[:, kq, (13 + hp) * 128:(14 + hp) * 128], xt, xt[:, kq, :], start=(kq == 0), stop=(kq == 7))
                P.op("act", lambda e: e.activation(s1["gate"][:], p5[:], AF.Silu), reads=[p5], writes=[s1["gate"]])
                yield

        def stage2(tile, hp, own):
            u = tile * 4 + hp
            s1 = S1[u % 3]
            cs_ = SETS[u % 2]
            hm = C.cm[:, CM_HM:CM_HM + 2]
            eg = s1["eg"]
            gview = eg[:, 63:512:64]
            P.op("pool", lambda e: e.tensor_copy(cs_["gC"][:], gview), reads=[eg], writes=[cs_["gC"]])
            P.op("pool", lambda e: e.tensor_tensor(hmg[:], bc(hm, [128, 8, 2], [1]), bc(gview, [128, 8, 2], [2]), ALU.mult),
                 reads=[C.cm, eg], writes=[hmg])
            hm4 = bc(hm, [128, 8, 2, 64], [1, 3])
            hmg4 = bc(hmg[:], [128, 8, 2, 64], [3])

            def ex(x):
                return bc(x[:].rearrange("p (c s) -> p c s", s=64), [128, 8, 2, 64], [2])
            P.op("dve", lambda e: e.tensor_tensor(Lb[:], ex(s1["bt"]), hm4, ALU.mult), reads=[s1["bt"], C.cm], writes=[Lb])
            P.op("pool", lambda e: e.tensor_tensor(Ra[:], ex(s1["at"]), hm4, ALU.mult), reads=[s1["at"], C.cm], writes=[Ra])
            P.op("dve", lambda e: e.tensor_tensor(Lk[:], ex(s1["kt"]), hm4, ALU.mult), reads=[s1["kt"], C.cm], writes=[Lk])
            yield
            P.op("pool", lambda e: e.tensor_tensor(KH[:], ex(s1["kt"]), hmg4, ALU.mult), reads=[s1["kt"], hmg], writes=[KH])
            P.op("dve", lambda e: e.tensor_tensor(BH[:], ex(s1["bt"]), hmg4, ALU.mult), reads=[s1["bt"], hmg], writes=[BH])
            P.op("pool", lambda e: e.tensor_tensor(VB[:], ex(s1["vz"]), hm4, ALU.mult), reads=[s1["vz"], C.cm], writes=[VB])
            if own:
                Rr4 = cs_["Rr"][:].rearrange("p c (h s) -> p c h s", h=2)
                P.op("dve", lambda e: e.tensor_tensor(Rr4, ex(s1["rt"]), hm4, ALU.mult), reads=[s1["rt"], C.cm], writes=[cs_["Rr"]])
            yield

            def blk(t, c):
                return t[:, c].rearrange("p h s -> p (h s)")
            Tc = Tt[0]
            for c in range(8):
                psA, psA2 = psA_[c % 2], psA2_[c % 2]
                P.mm(psA, psA[:, 0:128], Lb, blk(Lb, c), Ra, blk(Ra, c))
                P.mm(psA, psA[:, 128:256], Lk, blk(Lk, c), Ra, blk(Ra, c))
                P.mm(psA, psA[:, 256:384], Ra, blk(Ra, c), Lb, blk(Lb, c))
                P.op("dve", lambda e: e.tensor_tensor(NA[:, c].rearrange("p a t -> p (a t)"), psA[:, 0:256], mask4[:, 0:256], ALU.mult),
                     reads=[psA, mask4], writes=[NA], partial=True)
                P.op("dve", lambda e: e.tensor_tensor(PT0[:, c, :], psA[:, 256:384], C.cm[:, CM_STRICT_T:CM_STRICT_T + 128], ALU.mult),
                     reads=[psA, C.cm], writes=[PT0], partial=True)
                if own:
                    for a_, lx in enumerate((Lb, Lk)):
                        P.mm(psA2, psA2[:], lx, blk(lx, c), cs_["Rr"], cs_["Rr"][:, c, :])
                        P.op("dve", lambda e: e.tensor_tensor(cs_["ABK"][:, c, a_, :], psA2[:], mask4[:, 256:384], ALU.mult),
                             reads=[psA2, mask4], writes=[cs_["ABK"]], partial=True)
                P.op("pool", lambda e: e.tensor_tensor(Tc[:, c, :], NA[:, c, 0, :], C.cm[:, CM_EYE:CM_EYE + 128], ALU.add),
                     reads=[NA, C.cm], writes=[Tc], partial=True)
                if c % 2 == 1:
                    yield
            for qi, (src, dst_b) in enumerate(((KH, cs_["KHt"]), (BH, cs_["BHt"]), (Ra, Rat), (VB, cs_["VBt"]))):
                for c in range(8):
                    P.tr(pstr, pstr[:, c * 128:(c + 1) * 128], src, blk(src, c), C.ident_b, C.ident_b[:])
                evac(P, qi, dst_b, dst_b[:].rearrange("p c t -> p (c t)"), pstr, pstr[:])
                yield
            for cb in range(2):
                c0 = cb * 4
                Pc, PTc = None, None
                Tcur = Tt[0]
                for lvl in range(1, 6):
                    Pn, PTn = Pp[lvl % 2], PTp[lvl % 2]
                    for c in range(4):
                        lp = NA[:, c0 + c, 0, :] if lvl == 1 else Pc[:, c, :]
                        lpt = PT0[:, c0 + c, :] if lvl == 1 else PTc[:, c, :]
                        lpb = NA if lvl == 1 else Pc
                        lptb = PT0 if lvl == 1 else PTc
                        if lvl < 5:
                            P.mm(psP, psP[:, c * 128:(c + 1) * 128], lptb, lpt, lpb, lp)
                        P.mm(psPT, psPT[:, c * 128:(c + 1) * 128], lpb, lp, lptb, lpt)
                    if lvl < 5:
                        P.op("act", lambda e: e.activation(Pn[:].rearrange("p c t -> p (c t)"), psP[:], AF.Copy), reads=[psP], writes=[Pn])
                    P.op("dve", lambda e: e.tensor_copy(PTn[:].rearrange("p c t -> p (c t)"), psPT[:]), reads=[psPT], writes=[PTn])
                    Tn = Tt[lvl % 2]
                    for c in range(4):
                        P.mm(psP, psP[:, c * 128:(c + 1) * 128], PTn, PTn[:, c, :], Tcur, Tcur[:, c0 + c, :])
                    P.op("dve", lambda e: e.tensor_tensor(Tn[:, c0:c0 + 4, :].rearrange("p c t -> p (c t)"), psP[:],
                                                          Tcur[:, c0:c0 + 4, :].rearrange("p c t -> p (c t)"), ALU.add),
                         reads=[psP, Tcur], writes=[Tn], partial=True)
                    Pc, PTc, Tcur = Pn, PTn, Tn
                    yield
                Tfin = Tcur
                p_ = nextpp()
                for c in range(4):
                    P.mm(p_, p_[:, c * 128:(c + 1) * 128], NA, NA[:, c0 + c, 1, :], cs_["VBt"], cs_["VBt"][:, c0 + c, :])
                evac(P, 0, Yb, Yb[:].rearrange("p c t -> p (c t)"), p_, p_[:])
                p_ = nextpp()
                for c in range(4):
                    P.mm(p_, p_[:, c * 128:(c + 1) * 128], Tfin, Tfin[:, c0 + c, :], Yb, Yb[:, c, :])
                evac(P, 1, cs_["W2"], cs_["W2"][:, c0:c0 + 4, :].rearrange("p c t -> p (c t)"), p_, p_[:])
                p_ = nextpp()
                for c in range(4):
                    P.mm(p_, p_[:, c * 128:(c + 1) * 128], Rat, Rat[:, c0 + c, :], Tfin, Tfin[:, c0 + c, :])
                evac(P, 0, cs_["W1"], cs_["W1"][:, c0:c0 + 4, :].rearrange("p c t -> p (c t)"), p_, p_[:])
                yield

        def stage3(tile, hp, own):
            u = tile * 4 + hp
            s1 = S1[u % 3]
            cs_ = SETS[u % 2]
            psY = psYb
            for c in range(8):
                ub, ob = Ub[c % 2], Ob[c % 2]
                P.mm(psU, psU[:], cs_["W1"], cs_["W1"][:, c, :], Sb[hp], Sb[hp][:])
                P.op("dve", lambda e: e.tensor_tensor(ub[:], psU[:], cs_["W2"][:, c, :], ALU.add), reads=[psU, cs_["W2"]], writes=[ub])
                P.mm(psS, psS[:], cs_["KHt"], cs_["KHt"][:, c, :], cs_["VBt"], cs_["VBt"][:, c, :], start=True, stop=False)
                P.mm(psS, psS[:], cs_["BHt"], cs_["BHt"][:, c, :], ub, ub[:], start=False, stop=True)
                if own:
                    P.mm(psO, psO[:], cs_["Rr"], cs_["Rr"][:, c, :], Sb[hp], Sb[hp][:], start=True, stop=False)
                    P.mm(psO, psO[:], cs_["ABK"], cs_["ABK"][:, c, 0, :], ub, ub[:], start=False, stop=False)
                    P.mm(psO, psO[:], cs_["ABK"], cs_["ABK"][:, c, 1, :], cs_["VBt"], cs_["VBt"][:, c, :], start=False, stop=True)
                gc = cs_["gC"][:, c:c + 1]
                P.op("dve", lambda e: e.scalar_tensor_tensor(Sb[hp][:], Sf[hp][:], gc, psS[:], ALU.mult, ALU.add),
                     reads=[Sf[hp], psS, cs_["gC"]], writes=[Sb[hp]])
                P.op("dve", lambda e: e.scalar_tensor_tensor(Sf[hp][:], Sf[hp][:], gc, psS[:], ALU.mult, ALU.add),
                     reads=[Sf[hp], psS, cs_["gC"]], writes=[Sf[hp]])
                if own:
                    P.op("act", lambda e: e.activation(ob[:], psO[:], AF.Copy), reads=[psO], writes=[ob])
                    P.mm(psY, psY[:, c * 64:(c + 1) * 64], ob, ob[:], C.selb, C.selb[:])
                yield
            if own and tile >= own_from:
                yr, cen, sq, rs = post["yr"], post["cen"], post["sq"], post["rs"]
                P.op("act", lambda e: e.activation(yr[:], psY[:], AF.Copy), reads=[psY], writes=[yr])
                pm = nextpp()
                P.mm(pm, pm[:], C.bones_f, C.bones_f[:], yr, yr[:])
                P.op("dve", lambda e: e.scalar_tensor_tensor(cen[:], pm[:], -1.0 / 64.0, yr[:], ALU.mult, ALU.add), reads=[pm, yr], writes=[cen])
                P.op("pool", lambda e: e.tensor_tensor(sq[:], cen[:], cen[:], ALU.mult), reads=[cen], writes=[sq])
                pv_ = nextpp()
                P.mm(pv_, pv_[:], C.bones_f, C.bones_f[:], sq, sq[:])
                P.op("dve", lambda e: e.tensor_scalar(rs[:], pv_[:], 1.0 / 64.0, GN_EPS, ALU.mult, ALU.add), reads=[pv_], writes=[rs])
                P.op("act", lambda e: e.activation(rs[:], rs[:], AF.Sqrt), reads=[rs], writes=[rs])
                P.op("dve", lambda e: e.reciprocal(rs[:], rs[:]), reads=[rs], writes=[rs])
                P.op("dve", lambda e: e.tensor_tensor(cen[:], cen[:], rs[:], ALU.mult), reads=[cen, rs], writes=[cen])
                P.op("dve", lambda e: e.tensor_scalar(cen[:], cen[:], pv(PV_LG + hp), pv(PV_LB + hp), ALU.mult, ALU.add), reads=[cen, C.pv], writes=[cen])
                P.op("pool", lambda e: e.tensor_tensor(cen[:], cen[:], s1["bonus"][:], ALU.add), reads=[cen, s1["bonus"]], writes=[cen])
                col = hp * OWN + (tile - own_from) * 512
                yo = yos[u % 2]
                P.op("pool", lambda e: e.tensor_tensor(yo[:], cen[:], s1["gate"][:], ALU.mult), reads=[cen, s1["gate"]], writes=[yo])
                P.dma("sp", yrT[:, col:col + 512], yo[:], reads=[yo], writes=[yrT])
                yield

        def pre1(tile, hp):
            own = tile >= own_from
            if hp == 0:
                build_xT(tile)
                yield
            yield from stage1(tile, hp, own)

        def pre2(tile, hp):
            yield from stage2(tile, hp, tile >= own_from)

        units = [(t, h) for t in range(ntiles) for h in range(4)]
        nu = len(units)
        for g_ in (pre1(*units[0]), pre2(*units[0])):
            for _ in g_:
                pass
        if nu > 1:
            for _ in pre1(*units[1]):
                pass
        for i, (t, h) in enumerate(units):
            gens = [stage3(t, h, t >= own_from)]
            if i + 1 < nu:
                gens.append(pre2(*units[i + 1]))
            if i + 2 < nu:
                gens.append(pre1(*units[i + 2]))
            interleave(*gens)
        for hp in range(4):
            p_ = nextpp()
            P.mm(p_, p_[:, 0:128], Sf[hp], Sf[hp][:], C.ident_f, C.ident_f[:])
            so = post["yr"]
            P.op("dve", lambda e: e.tensor_copy(so[:, 0:128], p_[:, 0:128]), reads=[p_], writes=[so])
            for hh in range(2):
                P.dma("sp", dout["wkv_p"][hp * 2 + hh, :, :], so[hh * 64:(hh + 1) * 64, hh * 64:(hh + 1) * 64], reads=[so])
        p_ = nextpp()
        P.mm(p_, p_[0:13, 0:128], shout, shout[:, 0:13], C.ident_f, C.ident_f[:])
        so = post["cen"]
        P.op("dve", lambda e: e.tensor_copy(so[0:13, 0:128], p_[0:13, 0:128]), reads=[p_], writes=[so])
        P.dma("sp", dout["shift_p"].rearrange("(c p) -> c p", p=128), so[0:13, 0:128], reads=[so])


CS_STRICT, CS_INCL, CS_STRICT_T, CS_ROW, CS_G1, CS_G23, NCS = 0, 128, 256, 384, 400, 532, 1048


def make_cmask_s():
    cs = np.zeros((128, NCS), np.float32)
    p = np.arange(128)
    h1, b1, t1 = p[:, None] // 64, (p[:, None] % 64) // 4, p[:, None] % 4
    h2, b2, t2 = p[None, :] // 64, (p[None, :] % 64) // 4, p[None, :] % 4
    same = (h1 == h2) & (b1 == b2)
    cs[:, CS_STRICT:CS_STRICT + 128] = (same & (t1 < t2))
    cs[:, CS_INCL:CS_INCL + 128] = (same & (t1 <= t2))
    cs[:, CS_STRICT_T:CS_STRICT_T + 128] = (same & (t1 > t2))
    cs[:, CS_ROW:CS_ROW + 16] = (((p[:, None] % 64) // 4) == np.arange(16)[None, :])
    t = p % 32
    g1 = np.zeros((128, 132), np.float32)
    r = np.arange(128)[None, :]
    g1[:, 0:128] = np.where(r >= t[:, None], 0.0, NEGM)
    u = np.arange(4)[None, :]
    g1[:, 128:132] = np.where(u <= t[:, None], 0.0, NEGM)
    g1[t >= 4] = 0.0
    cs[:, CS_G1:CS_G1 + 132] = g1
    g23 = np.zeros((128, 516), np.float32)
    c = (np.arange(512) // 128)[None, :]
    g23[:, 0:512] = np.where(c == t[:, None], 0.0, NEGM)
    g23[:, 512:516] = np.where(u == t[:, None], 0.0, NEGM)
    g23[t >= 4] = 0.0
    cs[:, CS_G23:CS_G23 + 516] = g23
    return cs


def make_colmask():
    col = np.arange(128)
    m = np.zeros((128, 16, 128), np.float32)
    for b in range(16):
        m[:, b, :] = (((col % 64) // 4) == b)[None, :]
    return m.reshape(128, 2048)


def emit_sample(P, C, din, dout, flags={}):
    yrS = P.sbuf("yrS", [128, 4, 64], BF16)
    yaS = P.sbuf("yaS", [128, 4, 64], BF16)
    xTs = P.sbuf("xTs", [128, 8, 64], BF16)
    cms = P.sbuf("cms", [128, NCS], F32)
    P.dma("sp", cms[:], din["cmask_s"][:, :], writes=[cms])
    with scope(P):
        xb = P.sbuf("xbS", [64, 1024], BF16)
        px = carve(C, 0, 0, 512, "psS_x", BF16)
        P.dma("pool", xb[:], din["xs"][:, :], writes=[xb])
        for k in range(8):
            P.tr(px, px[:, k * 64:(k + 1) * 64], xb, xb[:, k * 128:(k + 1) * 128], C.ident_b, C.ident_b[0:64, 0:64])
        P.op("dve", lambda e: e.tensor_copy(xTs[:].rearrange("p k t -> p (k t)"), px[:, 0:512]), reads=[px], writes=[xTs])
    if flags.get("s_rwkv", True):
        emit_rwkv_sample(P, C, din, dout, xTs, cms, yrS)
    if flags.get("s_attn", True):
        emit_attn_sample(P, C, din, dout, xTs, cms, yaS)
    if flags.get("s_out", True):
        emit_out_phase(P, C, din, xTs, 0, din["xs"], yrS, yaS, 64, dout["y_s"], "s")


def emit_rwkv_sample(P, C, din, dout, xTs, cms, yrS):
    W = 64
    with scope(P):
        w_r = P.sbuf("w_rS", [128, 8, RW_COLS], BF16)
        wl2 = P.sbuf("wl2S", [128, 512], BF16)
        bones_b = P.sbuf("bones_bS", [128, 128], BF16)
        scanm = P.sbuf("scanmS", [128, W], F32)
        colm = P.sbuf("colm", [128, 16, 128], BF16)
        shs = P.sbuf("shs", [16, SHIFT_COLS], F32)
        smu = P.sbuf("smu", [128, 13, 16], F32)
        shout = P.sbuf("shoutS", [128, 13, 16], F32)
        sho2 = P.sbuf("sho2", [16, SHIFT_COLS], F32)
        lor = P.sbuf("lorS", [128, W], BF16)
        wdad = P.sbuf("wdadS", [128, W], F32)
        bm = P.sbuf("bmS", [128, W], F32)
        tmp = {nm: P.sbuf("ts_" + nm, [128, W], F32) for nm in
               ("r", "k", "vz", "sg", "a", "cs", "eg", "eng", "kk", "rn", "km", "bv", "bonus", "gate", "gf", "yr", "cen", "sq", "rs")}
        tb = {nm: P.sbuf("tsb_" + nm, [128, W], BF16) for nm in ("rt", "kt", "bt", "at", "sqb", "ktg", "btg")}
        ex_ = {nm: P.sbuf("exs_" + nm, [128, 2, W], BF16) for nm in ("Lb", "Lk", "Ra", "Rr", "KH", "BH", "VB")}
        sq_ = {nm: P.sbuf("sqs_" + nm, [128, 128], BF16) for nm in
               ("N", "ak", "br", "kr", "NT", "T0", "P1T", "T", "KHt", "BHt", "Rat", "VBt", "Y", "W1", "W2", "Ub", "Ob")}
        W1b = P.sbuf("W1b", [128, 16, 128], BF16)
        Rrb = P.sbuf("Rrb", [128, 16, 128], BF16)
        KHtb = P.sbuf("KHtb", [128, 16, 128], BF16)
        BHtb = P.sbuf("BHtb", [128, 16, 128], BF16)
        Sv = P.sbuf("Sv", [128, 16, 64], F32)
        Svx = P.sbuf("Svx", [128, 16, 2, 64], F32)
        Sf = P.sbuf("SfS", [128, 16, 128], F32)
        Sb = P.sbuf("SbS", [128, 16, 128], BF16)
        So = P.sbuf("SoS", [128, 16, 128], F32)
        pp = [carve(C, i, 0, 512, "psS_p%d" % i) for i in range(2)]
        ptr = carve(C, 2, 0, 256, "psS_tr", BF16)
        pA = carve(C, 3, 0, 384, "psS_A")
        pB = carve(C, 2, 384, 512, "psS_B")
        pbig = [carve(C, 4 + i, 0, 512, "psS_big%d" % i) for i in range(4)]
        ppi = [0]

        def nextpp():
            b = pp[ppi[0] % 2]
            ppi[0] += 1
            return b

        def pv(col):
            return C.pv[:, col:col + 1]

        P.dma("pool", w_r[:], din["w_rw"].rearrange("(k p) c -> p k c", p=128), writes=[w_r])
        P.dma("pool", wl2[:], din["w_l2"][:, :], writes=[wl2])
        P.dma("pool", colm[:].rearrange("p b c -> p (b c)"), din["colmask"][:, :], writes=[colm])
        P.dma("sp", shs[:], din["shift_s"][:, :], writes=[shs])
        P.op("pool", lambda e: e.tensor_copy(bones_b[:], C.cm[:, CM_BONES:CM_BONES + 128]), reads=[C.cm], writes=[bones_b])
        P.op("pool", lambda e: e.memset(scanm[:], 1.0), writes=[scanm])
        P.op("pool", lambda e: e.memset(scanm[:, 0:W:4], 0.0), writes=[scanm])
        p_ = nextpp()
        for c in range(13):
            P.mm(p_, p_[:, c * 16:(c + 1) * 16], shs, shs[0:16, c * 128:(c + 1) * 128], C.ident_f, C.ident_f[0:16, 0:16])
        P.op("dve", lambda e: e.tensor_tensor(smu[:], p_[:, 0:208].rearrange("p (c b) -> p c b", b=16),
                                              bc(C.pv[:, PV_MU:PV_MU + 13], [128, 13, 16], [2]), ALU.mult),
             reads=[p_, C.pv], writes=[smu])

        def proj_shift(c, dst):
            q_ = nextpp()
            for k in range(8):
                P.mm(q_, q_[:, 0:W], w_r, w_r[:, k, c * 128:(c + 1) * 128], xTs, xTs[:, k, :], start=(k == 0), stop=(k == 7))
            P.op("act", lambda e: e.mul(bm[:], q_[:, 0:W], pv(PV_MU + c)), reads=[q_, C.pv], writes=[bm])
            q3 = q_[:, 0:W].rearrange("p (b t) -> p b t", t=4)
            d3 = dst[:].rearrange("p (b t) -> p b t", t=4)
            b3 = bm[:].rearrange("p (b t) -> p b t", t=4)
            P.op("dve", lambda e: e.scalar_tensor_tensor(d3[:, :, 1:4], q3[:, :, 1:4], pv(PV_OMM + c), b3[:, :, 0:3], ALU.mult, ALU.add),
                 reads=[q_, bm, C.pv], writes=[dst], partial=True)
            P.op("dve", lambda e: e.scalar_tensor_tensor(d3[:, :, 0:1], q3[:, :, 0:1], pv(PV_OMM + c), smu[:, c, :].unsqueeze(2), ALU.mult, ALU.add),
                 reads=[q_, smu, C.pv], writes=[dst], partial=True)
            P.op("act", lambda e: e.activation(shout[:, c, :].unsqueeze(2), q3[:, :, 3:4], AF.Copy), reads=[q_], writes=[shout], partial=True)

        proj_shift(12, wdad)
        P.op("act", lambda e: e.activation(lor[0:64, :], wdad[0:64, :], AF.Tanh), reads=[wdad], writes=[lor], partial=True)
        P.op("dve", lambda e: e.tensor_copy(lor[64:128, :], wdad[64:128, :]), reads=[wdad], writes=[lor], partial=True)
        hm = C.cm[:, CM_HM:CM_HM + 2]
        hm3 = bc(hm, [128, 2, W], [2])
        for hp in range(4):
            r, k, vz, sg, a, cs, eg, eng, kk, rn, km, bv = (tmp[n] for n in ("r", "k", "vz", "sg", "a", "cs", "eg", "eng", "kk", "rn", "km", "bv"))
            proj_shift(hp, r)
            proj_shift(4 + hp, k)
            proj_shift(8 + hp, vz)
            q_ = nextpp()
            P.mm(q_, q_[:, 0:W], wl2, wl2[0:64, hp * 128:(hp + 1) * 128], lor, lor[0:64, :])
            P.op("act", lambda e: e.activation(sg[:], q_[:, 0:W], AF.Sigmoid, bias=pv(PV_W0 + hp), scale=1.0), reads=[q_, C.pv], writes=[sg])
            q2 = nextpp()
            P.mm(q2, q2[:, 0:W], wl2, wl2[64:128, hp * 128:(hp + 1) * 128], lor, lor[64:128, :])
            P.op("act", lambda e: e.activation(a[:], q2[:, 0:W], AF.Sigmoid, bias=pv(PV_A0 + hp), scale=1.0), reads=[q2, C.pv], writes=[a])
            P.op("dve", lambda e: e.tensor_tensor_scan(cs[:], scanm[:], sg[:], 0.0, ALU.mult, ALU.add), reads=[scanm, sg], writes=[cs])
            P.op("act", lambda e: e.activation(eg[:], cs[:], AF.Exp, scale=-C0), reads=[cs], writes=[eg])
            P.op("act", lambda e: e.activation(eng[:], cs[:], AF.Exp, scale=C0), reads=[cs], writes=[eng])
            P.op("pool", lambda e: e.tensor_tensor(cs[:], cs[:], sg[:], ALU.subtract), reads=[cs, sg], writes=[cs])
            P.op("act", lambda e: e.activation(cs[:], cs[:], AF.Exp, scale=-C0), reads=[cs], writes=[cs])
            P.op("dve", lambda e: e.tensor_scalar(kk[:], k[:], pv(PV_KK + hp), None, ALU.mult), reads=[k, C.pv], writes=[kk])
            P.op("pool", lambda e: e.tensor_tensor(tb["sqb"][:], kk[:], kk[:], ALU.mult), reads=[kk], writes=[tb["sqb"]])
            q3_ = nextpp()
            P.mm(q3_, q3_[:, 0:W], bones_b, bones_b[:], tb["sqb"], tb["sqb"][:])
            P.op("dve", lambda e: e.tensor_scalar(rn[:], q3_[:, 0:W], 1e-24, None, ALU.max), reads=[q3_], writes=[rn])
            P.op("act", lambda e: e.activation(rn[:], rn[:], AF.Sqrt), reads=[rn], writes=[rn])
            P.op("dve", lambda e: e.reciprocal(rn[:], rn[:]), reads=[rn], writes=[rn])
            P.op("dve", lambda e: e.tensor_tensor(kk[:], kk[:], rn[:], ALU.mult), reads=[kk, rn], writes=[kk])
            P.op("dve", lambda e: e.tensor_scalar(km[:], a[:], -1.0, pv(PV_KA + hp), ALU.add, ALU.mult), reads=[a, C.pv], writes=[km])
            P.op("dve", lambda e: e.scalar_tensor_tensor(km[:], km[:], 1.0, k[:], ALU.add, ALU.mult), reads=[km, k], writes=[km])
            P.op("pool", lambda e: e.tensor_tensor(bv[:], kk[:], a[:], ALU.mult), reads=[kk, a], writes=[bv])
            P.op("pool", lambda e: e.tensor_tensor(tb["rt"][:], r[:], eg[:], ALU.mult), reads=[r, eg], writes=[tb["rt"]])
            P.op("dve", lambda e: e.tensor_tensor(tb["kt"][:], km[:], eng[:], ALU.mult), reads=[km, eng], writes=[tb["kt"]])
            P.op("pool", lambda e: e.tensor_tensor(tb["bt"][:], bv[:], eng[:], ALU.mult), reads=[bv, eng], writes=[tb["bt"]])
            P.op("dve", lambda e: e.scalar_tensor_tensor(tb["at"][:], kk[:], -1.0, cs[:], ALU.mult, ALU.mult), reads=[kk, cs], writes=[tb["at"]])
            P.op("dve", lambda e: e.scalar_tensor_tensor(tb["sqb"][:], r[:], pv(PV_RK + hp), km[:], ALU.mult, ALU.mult), reads=[r, km, C.pv], writes=[tb["sqb"]])
            q4 = nextpp()
            P.mm(q4, q4[:, 0:W], bones_b, bones_b[:], tb["sqb"], tb["sqb"][:])
            P.op("dve", lambda e: e.tensor_tensor(tmp["bonus"][:], q4[:, 0:W], vz[:], ALU.mult), reads=[q4, vz], writes=[tmp["bonus"]])
            q5 = nextpp()
            for kq in range(8):
                P.mm(q5, q5[:, 0:W], w_r, w_r[:, kq, (13 + hp) * 128:(14 + hp) * 128], xTs, xTs[:, kq, :], start=(kq == 0), stop=(kq == 7))
            P.op("act", lambda e: e.activation(tmp["gate"][:], q5[:, 0:W], AF.Silu), reads=[q5], writes=[tmp["gate"]])
            gcol = eg[:, 3:W:4]
            gf = tmp["gf"]
            P.op("pool", lambda e: e.tensor_copy(gf[:].rearrange("p (b t) -> p b t", t=4), bc(gcol, [128, 16, 4], [2])), reads=[eg], writes=[gf])
            P.op("dve", lambda e: e.tensor_tensor(tb["ktg"][:], tb["kt"][:], gf[:], ALU.mult), reads=[tb["kt"], gf], writes=[tb["ktg"]])
            P.op("pool", lambda e: e.tensor_tensor(tb["btg"][:], tb["bt"][:], gf[:], ALU.mult), reads=[tb["bt"], gf], writes=[tb["btg"]])

            def ex(x):
                return bc(x[:], [128, 2, W], [1])
            for i, (nm, src) in enumerate((("Lb", tb["bt"]), ("Lk", tb["kt"]), ("Ra", tb["at"]), ("Rr", tb["rt"]),
                                           ("KH", tb["ktg"]), ("BH", tb["btg"]), ("VB", vz))):
                P.op("dve" if i % 2 == 0 else "pool", lambda e: e.tensor_tensor(ex_[nm][:], ex(src), hm3, ALU.mult),
                     reads=[src, C.cm], writes=[ex_[nm]])

            def f2(nm):
                return ex_[nm][:].rearrange("p h s -> p (h s)")
            P.mm(pA, pA[:, 0:128], ex_["Lb"], f2("Lb"), ex_["Ra"], f2("Ra"))
            P.mm(pA, pA[:, 128:256], ex_["Lk"], f2("Lk"), ex_["Ra"], f2("Ra"))
            P.mm(pA, pA[:, 256:384], ex_["Ra"], f2("Ra"), ex_["Lb"], f2("Lb"))
            P.op("dve", lambda e: e.tensor_tensor(sq_["N"][:], pA[:, 0:128], cms[:, CS_STRICT:CS_STRICT + 128], ALU.mult), reads=[pA, cms], writes=[sq_["N"]])
            P.op("dve", lambda e: e.tensor_tensor(sq_["ak"][:], pA[:, 128:256], cms[:, CS_STRICT:CS_STRICT + 128], ALU.mult), reads=[pA, cms], writes=[sq_["ak"]])
            P.op("dve", lambda e: e.tensor_tensor(sq_["NT"][:], pA[:, 256:384], cms[:, CS_STRICT_T:CS_STRICT_T + 128], ALU.mult), reads=[pA, cms], writes=[sq_["NT"]])
            P.mm(pA, pA[:, 0:128], ex_["Lb"], f2("Lb"), ex_["Rr"], f2("Rr"))
            P.mm(pA, pA[:, 128:256], ex_["Lk"], f2("Lk"), ex_["Rr"], f2("Rr"))
            P.op("dve", lambda e: e.tensor_tensor(sq_["br"][:], pA[:, 0:128], cms[:, CS_INCL:CS_INCL + 128], ALU.mult), reads=[pA, cms], writes=[sq_["br"]])
            P.op("dve", lambda e: e.tensor_tensor(sq_["kr"][:], pA[:, 128:256], cms[:, CS_INCL:CS_INCL + 128], ALU.mult), reads=[pA, cms], writes=[sq_["kr"]])
            P.op("pool", lambda e: e.tensor_tensor(sq_["T0"][:], sq_["N"][:], C.cm[:, CM_EYE:CM_EYE + 128], ALU.add), reads=[sq_["N"], C.cm], writes=[sq_["T0"]])
            P.mm(pA, pA[:, 0:128], sq_["N"], sq_["N"][:], sq_["NT"], sq_["NT"][:])
            P.op("dve", lambda e: e.tensor_copy(sq_["P1T"][:], pA[:, 0:128]), reads=[pA], writes=[sq_["P1T"]])
            P.mm(pA, pA[:, 0:128], sq_["P1T"], sq_["P1T"][:], sq_["T0"], sq_["T0"][:])
            P.op("dve", lambda e: e.tensor_tensor(sq_["T"][:], pA[:, 0:128], sq_["T0"][:], ALU.add), reads=[pA, sq_["T0"]], writes=[sq_["T"]])
            for i, (src, dst) in enumerate((("KH", "KHt"), ("BH", "BHt"), ("Ra", "Rat"), ("VB", "VBt"))):
                P.tr(ptr, ptr[:, i * 128:(i + 1) * 128], ex_[src], f2(src), C.ident_b, C.ident_b[:])
            for i, dst in enumerate(("KHt", "BHt", "Rat", "VBt")):
                evac(P, 0, sq_[dst], sq_[dst][:], ptr, ptr[:, i * 128:(i + 1) * 128])
            P.mm(pA, pA[:, 0:128], sq_["ak"], sq_["ak"][:], sq_["VBt"], sq_["VBt"][:])
            P.op("act", lambda e: e.activation(sq_["Y"][:], pA[:, 0:128], AF.Copy), reads=[pA], writes=[sq_["Y"]])
            P.mm(pA, pA[:, 128:256], sq_["T"], sq_["T"][:], sq_["Y"], sq_["Y"][:])
            P.op("act", lambda e: e.activation(sq_["W2"][:], pA[:, 128:256], AF.Copy), reads=[pA], writes=[sq_["W2"]])
            P.mm(pA, pA[:, 256:384], sq_["Rat"], sq_["Rat"][:], sq_["T"], sq_["T"][:])
            P.op("dve", lambda e: e.tensor_copy(sq_["W1"][:], pA[:, 256:384]), reads=[pA], writes=[sq_["W1"]])
            P.dma("sp", Sv[:], din["wkv_s"][:, 2 * hp:2 * hp + 2, :, :].rearrange("b h v k -> (h v) b k"), writes=[Sv], partial=False)
            P.op("dve", lambda e: e.tensor_tensor(Svx[:], bc(Sv[:], [128, 16, 2, 64], [2]), bc(hm, [128, 16, 2, 64], [1, 3]), ALU.mult),
                 reads=[Sv, C.cm], writes=[Svx])
            for b in range(16):
                pb_ = pbig[b // 4]
                P.mm(pb_, pb_[:, (b % 4) * 128:(b % 4 + 1) * 128], Svx, Svx[:, b].rearrange("p h k -> p (h k)"), C.ident_f, C.ident_f[:])
            for i in range(4):
                P.op("dve", lambda e: e.tensor_copy(Sf[:, 4 * i:4 * i + 4, :].rearrange("p b c -> p (b c)"), pbig[i][:]), reads=[pbig[i]], writes=[Sf], partial=True)
                P.op("act", lambda e: e.activation(Sb[:, 4 * i:4 * i + 4, :].rearrange("p b c -> p (b c)"), pbig[i][:], AF.Copy), reads=[pbig[i]], writes=[Sb], partial=True)
            P.op("dve", lambda e: e.tensor_tensor(W1b[:], bc(sq_["W1"][:], [128, 16, 128], [1]), colm[:], ALU.mult), reads=[sq_["W1"], colm], writes=[W1b])
            P.op("pool", lambda e: e.tensor_tensor(Rrb[:], bc(f2("Rr"), [128, 16, 128], [1]), colm[:], ALU.mult), reads=[ex_["Rr"], colm], writes=[Rrb])
            rowm = bc(cms[:, CS_ROW:CS_ROW + 16], [128, 16, 128], [2])
            P.op("dve", lambda e: e.tensor_tensor(KHtb[:], bc(sq_["KHt"][:], [128, 16, 128], [1]), rowm, ALU.mult), reads=[sq_["KHt"], cms], writes=[KHtb])
            P.op("pool", lambda e: e.tensor_tensor(BHtb[:], bc(sq_["BHt"][:], [128, 16, 128], [1]), rowm, ALU.mult), reads=[sq_["BHt"], cms], writes=[BHtb])
            for b in range(16):
                P.mm(pB, pB[:, 0:128], W1b, W1b[:, b, :], Sb, Sb[:, b, :], start=(b == 0), stop=(b == 15))
            P.op("dve", lambda e: e.tensor_tensor(sq_["Ub"][:], pB[:, 0:128], sq_["W2"][:], ALU.add), reads=[pB, sq_["W2"]], writes=[sq_["Ub"]])
            for b in range(16):
                P.mm(pA, pA[:, 0:128], Rrb, Rrb[:, b, :], Sb, Sb[:, b, :], start=(b == 0), stop=False)
            P.mm(pA, pA[:, 0:128], sq_["br"], sq_["br"][:], sq_["Ub"], sq_["Ub"][:], start=False, stop=False)
            P.mm(pA, pA[:, 0:128], sq_["kr"], sq_["kr"][:], sq_["VBt"], sq_["VBt"][:], start=False, stop=True)
            P.op("act", lambda e: e.activation(sq_["Ob"][:], pA[:, 0:128], AF.Copy), reads=[pA], writes=[sq_["Ob"]])
            for b in range(16):
                pb_ = pbig[b // 4]
                sl = pb_[:, (b % 4) * 128:(b % 4 + 1) * 128]
                P.mm(pb_, sl, KHtb, KHtb[:, b, :], sq_["VBt"], sq_["VBt"][:], start=True, stop=False)
                P.mm(pb_, sl, BHtb, BHtb[:, b, :], sq_["Ub"], sq_["Ub"][:], start=False, stop=True)
            for i in range(4):
                sfv = Sf[:, 4 * i:4 * i + 4, :]
                P.op("dve", lambda e: e.tensor_tensor(sfv, sfv, bc(gcol[:, 4 * i:4 * i + 4], [128, 4, 128], [2]), ALU.mult), reads=[Sf, eg], writes=[Sf])
                P.op("dve", lambda e: e.tensor_tensor(sfv, sfv, pbig[i][:].rearrange("p (b c) -> p b c", b=4), ALU.add), reads=[Sf, pbig[i]], writes=[Sf])
            for b in range(16):
                pb_ = pbig[b // 4]
                P.mm(pb_, pb_[:, (b % 4) * 128:(b % 4 + 1) * 128], Sf, Sf[:, b, :], C.ident_f, C.ident_f[:])
            for i in range(4):
                evac(P, i, So, So[:, 4 * i:4 * i + 4, :].rearrange("p b c -> p (b c)"), pbig[i], pbig[i][:])
            for hh in range(2):
                P.dma("sp", dout["wkv_so"][:, 2 * hp + hh, :, :].rearrange("b v k -> v b k"),
                      So[hh * 64:(hh + 1) * 64, :, hh * 64:(hh + 1) * 64], reads=[So])
            py = nextpp()
            P.mm(py, py[:, 0:W], sq_["Ob"], sq_["Ob"][:], C.selb, C.selb[:])
            yr, cen, sq, rs = tmp["yr"], tmp["cen"], tmp["sq"], tmp["rs"]
            P.op("act", lambda e: e.activation(yr[:], py[:, 0:W], AF.Copy), reads=[py], writes=[yr])
            pm = nextpp()
            P.mm(pm, pm[:, 0:W], C.bones_f, C.bones_f[:], yr, yr[:])
            P.op("dve", lambda e: e.scalar_tensor_tensor(cen[:], pm[:, 0:W], -1.0 / 64.0, yr[:], ALU.mult, ALU.add), reads=[pm, yr], writes=[cen])
            P.op("pool", lambda e: e.tensor_tensor(sq[:], cen[:], cen[:], ALU.mult), reads=[cen], writes=[sq])
            pv_ = nextpp()
            P.mm(pv_, pv_[:, 0:W], C.bones_f, C.bones_f[:], sq, sq[:])
            P.op("dve", lambda e: e.tensor_scalar(rs[:], pv_[:, 0:W], 1.0 / 64.0, GN_EPS, ALU.mult, ALU.add), reads=[pv_], writes=[rs])
            P.op("act", lambda e: e.activation(rs[:], rs[:], AF.Sqrt), reads=[rs], writes=[rs])
            P.op("dve", lambda e: e.reciprocal(rs[:], rs[:]), reads=[rs], writes=[rs])
            P.op("dve", lambda e: e.tensor_tensor(cen[:], cen[:], rs[:], ALU.mult), reads=[cen, rs], writes=[cen])
            P.op("dve", lambda e: e.tensor_scalar(cen[:], cen[:], pv(PV_LG + hp), pv(PV_LB + hp), ALU.mult, ALU.add), reads=[cen, C.pv], writes=[cen])
            P.op("pool", lambda e: e.tensor_tensor(cen[:], cen[:], tmp["bonus"][:], ALU.add), reads=[cen, tmp["bonus"]], writes=[cen])
            P.op("pool", lambda e: e.tensor_tensor(yrS[:, hp, :], cen[:], tmp["gate"][:], ALU.mult), reads=[cen, tmp["gate"]], writes=[yrS], partial=True)
        for c0 in range(0, 13, 4):
            n = min(4, 13 - c0)
            q_ = nextpp()
            for c in range(n):
                P.mm(q_, q_[0:16, c * 128:(c + 1) * 128], shout, shout[:, c0 + c, :], C.ident_f, C.ident_f[:])
            P.op("dve", lambda e: e.tensor_copy(sho2[0:16, c0 * 128:(c0 + n) * 128], q_[0:16, 0:n * 128]), reads=[q_], writes=[sho2], partial=True)
        P.dma("sp", dout["shift_so"][:, :], sho2[:], reads=[sho2])


def emit_attn_sample(P, C, din, dout, xTs, cms, yaS):
    with scope(P):
        wq = P.sbuf("wqS", [128, 8, 5120], BF16)
        qTs = P.sbuf("qTs", [128, 12, 64], BF16)
        kvn = P.sbuf("kvn", [64, 3, 2, 512], F32)
        kvnb = P.sbuf("kvnb", [64, 3, 2, 512], BF16)
        szs = P.sbuf("szs", [128, 4, 64], F32)
        oaT = P.sbuf("oaT", [128, 4, 64], F32)
        kvt = [P.sbuf("kvtS%d" % i, [128, 4, 2, 512], BF16) for i in range(2)]
        kvnew = [P.sbuf("kvnw%d" % i, [4, 2, 512], BF16) for i in range(2)]
        KT = [P.sbuf("KTs%d" % i, [128, 4, 512], BF16) for i in range(2)]
        KTn = [P.sbuf("KTn%d" % i, [128, 4, 4], BF16) for i in range(2)]
        ss = [P.sbuf("ssS%d" % i, [128, 516], F32) for i in range(2)]
        pb = [P.sbuf("pbS%d" % i, [128, 516], BF16) for i in range(2)]
        PT = [P.sbuf("PTs%d" % i, [128, 5, 128], BF16) for i in range(2)]
        og = [P.sbuf("ogS%d" % i, [128, 3, 128], F32) for i in range(2)]
        stt = [P.sbuf("sttS%d" % i, [128, 24], F32) for i in range(2)]
        ob16 = [P.sbuf("ob16S%d" % i, [128, 128], BF16) for i in range(2)]
        pp = [carve(C, i, 0, 512, "psSA_p%d" % i) for i in range(2)]
        pkt = [carve(C, 2 + i, 0, 512, "psSA_kt%d" % i, BF16) for i in range(2)]
        pS = carve(C, 4, 0, 512, "psSA_S")
        pSn = carve(C, 5, 0, 16, "psSA_Sn")
        pKn = carve(C, 5, 16, 32, "psSA_Kn", BF16)
        pO = carve(C, 5, 128, 256, "psSA_O")
        pT = carve(C, 6, 0, 384, "psSA_T", BF16)
        pOT = carve(C, 7, 0, 128, "psSA_OT")
        P.dma("pool", wq[:], din["w_qkvz"].rearrange("(k p) c -> p k c", p=128), writes=[wq])
        P.op("dve", lambda e: e.memset(pS[:], 0.0), writes=[pS])
        P.op("dve", lambda e: e.memset(pSn[:], 0.0), writes=[pSn])
        P.op("dve", lambda e: e.memset(pO[:], 0.0), writes=[pO])
        ec = [0]
        for c in range(12):
            p_ = pp[c % 2]
            for k in range(8):
                P.mm(p_, p_[:, 0:64], wq, wq[:, k, c * 128:(c + 1) * 128], xTs, xTs[:, k, :], start=(k == 0), stop=(k == 7))
            evac(P, c, qTs, qTs[:, c, :], p_, p_[:, 0:64])
        for hh in range(4):
            p_ = pp[hh % 2]
            for k in range(8):
                P.mm(p_, p_[:, 0:64], wq, wq[:, k, 4608 + hh * 128:4608 + (hh + 1) * 128], xTs, xTs[:, k, :], start=(k == 0), stop=(k == 7))
            P.op("act", lambda e: e.activation(szs[:, hh, :], p_[:, 0:64], AF.Silu), reads=[p_], writes=[szs], partial=True)
        for g in range(3):
            for kv in range(2):
                p_ = pp[(g * 2 + kv) % 2]
                c0 = 1536 * (1 + kv) + g * 512
                for k in range(8):
                    P.mm(p_, p_[0:64, :], xTs, xTs[:, k, :], wq, wq[:, k, c0:c0 + 512], start=(k == 0), stop=(k == 7))
                evac(P, g * 2 + kv, kvn, kvn[:, g, kv, :], p_, p_[0:64, :])
        P.op("pool", lambda e: e.tensor_copy(kvnb[:], kvn[:]), reads=[kvn], writes=[kvnb])
        for g in range(3):
            P.dma("sp", dout["kvs%d" % (g + 1)].rearrange("b t kv h e -> (b t) kv (h e)"), kvn[:, g], reads=[kvn])
        it = 0
        for b in range(16):
            ogb, sb_ = og[b % 2], stt[b % 2]
            for g in range(3):
                i2 = it % 2
                it += 1
                ntile = 1 if g == 0 else 4
                nk = ntile * 128
                kt_, kn_, KT_, KTn_, ss_, pb_, PT_ = kvt[i2], kvnew[i2], KT[i2], KTn[i2], ss[i2], pb[i2], PT[i2]
                cache = din["cache%d" % (g + 1)]
                d = GROUPS[g][1]
                if g == 0:
                    P.dma("pool", kt_[:, 0].rearrange("p kv c -> p (kv c)"), cache[b].rearrange("r kv h e -> r (kv h e)"), writes=[kt_], partial=False)
                else:
                    for cl in range(4):
                        src = cache[b, cl:GROUPS[g][0]:d].rearrange("r kv h e -> r (kv h e)")
                        P.dma("pool", kt_[:, cl].rearrange("p kv c -> p (kv c)"), src, writes=[kt_], partial=(cl > 0))
                P.dma("sp", kn_[:], kvnb[4 * b:4 * b + 4, g], reads=[kvnb], writes=[kn_], partial=False)
                for h in range(4):
                    pk = pkt[h // 2]
                    for cl in range(ntile):
                        P.tr(pk, pk[:, (h % 2) * 512 + cl * 128:(h % 2) * 512 + (cl + 1) * 128], kt_, kt_[:, cl, 0, h * 128:(h + 1) * 128],
                             C.ident_b, C.ident_b[:])
                    P.tr(pKn, pKn[:, h * 4:(h + 1) * 4], kn_, kn_[0:4, 0, h * 128:(h + 1) * 128], C.ident_b, C.ident_b[0:4, 0:4])
                for j in range(2):
                    evac(P, ec[0], KT_, KT_[:, 2 * j:2 * j + 2, 0:nk], pkt[j], pkt[j][:].rearrange("p (h k) -> p h k", h=2)[:, :, 0:nk])
                    ec[0] += 1
                evac(P, ec[0], KTn_, KTn_[:].rearrange("p h k -> p (h k)"), pKn, pKn[:, 0:16])
                ec[0] += 1
                for h in range(4):
                    qv = qTs[:, g * 4 + h, 4 * b:4 * b + 4]
                    P.mm(pS, pS[32 * h:32 * h + 4, 0:nk], qTs, qv, KT_, KT_[:, h, 0:nk], tile_position=(0, 32 * h))
                    P.mm(pSn, pSn[32 * h:32 * h + 4, 0:4], qTs, qv, KTn_, KTn_[:, h, :], tile_position=(0, 32 * h))
                mk = cms[:, CS_G1:CS_G1 + 132] if g == 0 else cms[:, CS_G23:CS_G23 + 516]
                P.op("dve", lambda e: e.scalar_tensor_tensor(ss_[:, 0:nk], pS[:, 0:nk], SCALE, mk[:, 0:nk], ALU.mult, ALU.add),
                     reads=[pS, cms], writes=[ss_], partial=True)
                P.op("dve", lambda e: e.scalar_tensor_tensor(ss_[:, nk:nk + 4], pSn[:, 0:4], SCALE, mk[:, nk:nk + 4], ALU.mult, ALU.add),
                     reads=[pSn, cms], writes=[ss_], partial=True)
                c8 = g * 8
                P.op("dve", lambda e: e.tensor_reduce(sb_[:, c8:c8 + 1], ss_[:, 0:nk + 4], AX.X, ALU.max, negate=True), reads=[ss_], writes=[sb_], partial=True)
                P.op("act", lambda e: e.activation(pb_[:, 0:nk + 4], ss_[:, 0:nk + 4], AF.Exp, bias=sb_[:, c8:c8 + 1], scale=1.0,
                                                   accum_out=sb_[:, c8 + 1:c8 + 2]), reads=[ss_, sb_], writes=[pb_, sb_], partial=True)
                for cl in range(ntile):
                    P.tr(pT, pT[:, cl * 128:(cl + 1) * 128], pb_, pb_[:, cl * 128:(cl + 1) * 128], C.ident_b, C.ident_b[:])
                P.tr(pT, pT[0:4, 640:768], pb_, pb_[:, nk:nk + 4], C.ident_b, C.ident_b[:])
                evac(P, ec[0], PT_, PT_[:, 0:ntile, :].rearrange("p c q -> p (c q)"), pT, pT[:, 0:nk])
                ec[0] += 1
                evac(P, ec[0], PT_, PT_[0:4, 4, :], pT, pT[0:4, 640:768])
                ec[0] += 1
                for h in range(4):
                    for cl in range(ntile):
                        P.mm(pO, pO[32 * h:32 * h + 4, :], PT_, PT_[:, cl, 32 * h:32 * h + 4], kt_, kt_[:, cl, 1, h * 128:(h + 1) * 128],
                             start=(cl == 0), stop=False, tile_position=(0, 32 * h))
                    P.mm(pO, pO[32 * h:32 * h + 4, :], PT_, PT_[0:4, 4, 32 * h:32 * h + 4], kn_, kn_[0:4, 1, h * 128:(h + 1) * 128],
                         start=False, stop=True, tile_position=(0, 32 * h))
                P.op("dve", lambda e: e.reciprocal(sb_[:, c8 + 2:c8 + 3], sb_[:, c8 + 1:c8 + 2]), reads=[sb_], writes=[sb_], partial=True)
                P.op("dve", lambda e: e.tensor_scalar(ogb[:, g, :], pO[:], sb_[:, c8 + 2:c8 + 3], None, ALU.mult), reads=[pO, sb_], writes=[ogb], partial=True)
                P.op("act", lambda e: e.activation(sb_[:, c8 + 3:c8 + 4], sb_[:, c8 + 1:c8 + 2], AF.Ln), reads=[sb_], writes=[sb_], partial=True)
                P.op("dve", lambda e: e.tensor_tensor(sb_[:, c8 + 4:c8 + 5], sb_[:, c8 + 3:c8 + 4], sb_[:, c8:c8 + 1], ALU.subtract), reads=[sb_], writes=[sb_], partial=True)
            def col(i):
                return sb_[:, i:i + 1]
            P.op("dve", lambda e: e.tensor_tensor(col(5), col(4), col(12), ALU.max), reads=[sb_], writes=[sb_], partial=True)
            P.op("dve", lambda e: e.tensor_tensor(col(5), col(5), col(20), ALU.max), reads=[sb_], writes=[sb_], partial=True)
            for g in range(3):
                P.op("dve", lambda e: e.tensor_tensor(col(8 * g + 6), col(8 * g + 4), col(5), ALU.subtract), reads=[sb_], writes=[sb_], partial=True)
                P.op("act", lambda e: e.activation(col(8 * g + 6), col(8 * g + 6), AF.Exp), reads=[sb_], writes=[sb_], partial=True)
            P.op("dve", lambda e: e.tensor_tensor(col(7), col(6), col(14), ALU.add), reads=[sb_], writes=[sb_], partial=True)
            P.op("dve", lambda e: e.tensor_tensor(col(7), col(7), col(22), ALU.add), reads=[sb_], writes=[sb_], partial=True)
            P.op("dve", lambda e: e.reciprocal(col(7), col(7)), reads=[sb_], writes=[sb_], partial=True)
            for g in range(3):
                P.op("dve", lambda e: e.tensor_tensor(col(8 * g + 6), col(8 * g + 6), col(7), ALU.mult), reads=[sb_], writes=[sb_], partial=True)
            P.op("dve", lambda e: e.tensor_scalar(ogb[:, 0, :], ogb[:, 0, :], col(6), None, ALU.mult), reads=[ogb, sb_], writes=[ogb])
            P.op("dve", lambda e: e.scalar_tensor_tensor(ogb[:, 0, :], ogb[:, 1, :], col(14), ogb[:, 0, :], ALU.mult, ALU.add), reads=[ogb, sb_], writes=[ogb])
            P.op("dve", lambda e: e.scalar_tensor_tensor(ogb[:, 0, :], ogb[:, 2, :], col(22), ogb[:, 0, :], ALU.mult, ALU.add), reads=[ogb, sb_], writes=[ogb])
            P.mm(pOT, pOT[:], ogb, ogb[:, 0, :], C.ident_f, C.ident_f[:])
            evac(P, b, oaT, oaT[:, :, 4 * b:4 * b + 4], pOT, pOT[:].rearrange("p (h x) -> p h x", x=32)[:, :, 0:4])
        P.op("dve", lambda e: e.tensor_tensor(yaS[:], oaT[:], szs[:], ALU.mult), reads=[oaT, szs], writes=[yaS])
```
